# Optimizing a Trainium2 kernel written in Bass

```python
import jax
import jax.numpy as jnp
from jax import lax
import numpy as np

D_MODEL = 1024
BATCH = 4
SEQ = 8192
DEPTH = 4

N_EVEN = (DEPTH + 1) // 2
N_ODD = DEPTH // 2
NORM_EPS = 1e-6
F32 = jnp.float32

GLA_HEADS = 4
GLA_DK = D_MODEL // 16
GLA_DV = D_MODEL // 8
GLA_RANK = 16
GLA_TAU = 16.0
GLA_CHUNK = 64
GLA_SIZES = (GLA_HEADS * GLA_DK, GLA_HEADS * GLA_DK, GLA_HEADS * GLA_DV, GLA_HEADS * GLA_DV, GLA_RANK)
GLA_COLS = 2 * GLA_HEADS * GLA_DK + 2 * GLA_HEADS * GLA_DV + GLA_RANK

RWKV_HEADS = 8
RWKV_N = 64
RWKV_WIDTH = RWKV_HEADS * RWKV_N
RWKV_LORA_W = 64
RWKV_LORA_A = 64
RWKV_LORA_G = 128
RWKV_CHUNK = 16
RWKV_DECAY_SCALE = 0.6065306597126334
RWKV_GN_EPS = RWKV_N * 1e-5
RWKV_L2_EPS = 1e-12
RWKV_SIZES = (RWKV_WIDTH, RWKV_LORA_W, RWKV_WIDTH, RWKV_WIDTH, RWKV_LORA_A, RWKV_LORA_G)
RWKV_COLS = 3 * RWKV_WIDTH + RWKV_LORA_W + RWKV_LORA_A + RWKV_LORA_G
AB_COLS = GLA_COLS + RWKV_COLS

LRU_WIDTH = D_MODEL // 2
LRU_BLOCKS = 4
LRU_BLOCK = LRU_WIDTH // LRU_BLOCKS
LRU_C = 8.0
LRU_CONV = 4

MLSTM_HEADS = 4
MLSTM_DH = D_MODEL // 8
MLSTM_WIDTH = MLSTM_HEADS * MLSTM_DH
MLSTM_QKV_BLOCK = 4
MLSTM_NBLK = MLSTM_WIDTH // MLSTM_QKV_BLOCK
MLSTM_CONV = 4
MLSTM_CHUNK = 64
CD_SIZES = (LRU_WIDTH, LRU_WIDTH, MLSTM_WIDTH, MLSTM_WIDTH, 2 * MLSTM_HEADS)
CD_COLS = 2 * LRU_WIDTH + 2 * MLSTM_WIDTH + 2 * MLSTM_HEADS

D_MIX = GLA_HEADS * GLA_DV + RWKV_WIDTH

MEM_LEN = 256
XA_HEADS = 4
XA_DH = D_MODEL // XA_HEADS

D_FF = 2752
FFN_CONV = 3

kernel_name = 'hybrid_gla_rwkv7_rglru_mlstm_trunk'


def rms_norm(x, g):
    xf = x.astype(F32)
    y = xf * lax.rsqrt(jnp.mean(xf * xf, axis=-1, keepdims=True) + NORM_EPS)
    return (y * g.astype(F32)).astype(x.dtype)


def split_cols(x, sizes):
    outs, start = [], 0
    for s in sizes:
        outs.append(x[..., start:start + s])
        start += s
    return outs


def causal_dwconv(x, w, b):
    K, T = w.shape[0], x.shape[1]
    xp = jnp.pad(x, ((0, 0), (K - 1, 0), (0, 0)))
    y = b + xp[:, 0:T, :] * w[0]
    for j in range(1, K):
        y = y + xp[:, j:j + T, :] * w[j]
    return y


def to_chunks(t, chunk):
    B, T, H, d = t.shape
    return t.reshape(B, T // chunk, chunk, H, d).transpose(0, 3, 1, 2, 4)


def from_chunks(t):
    B, H, NC, L, d = t.shape
    return t.transpose(0, 2, 3, 1, 4).reshape(B, NC * L, H, d)


def linear_recurrence(a, u):
    def combine(left, right):
        a_l, u_l = left
        a_r, u_r = right
        return a_l * a_r, a_r * u_l + u_r
    _, h = lax.associative_scan(combine, (a, u), axis=1)
    return h


def gla_chunked(q, k, v, log_alpha):
    L = GLA_CHUNK
    q, k, v, la = (to_chunks(t, L) for t in (q, k, v, log_alpha))
    g = jnp.cumsum(la, axis=3)
    g_last = g[..., -1:, :]
    q_dec = q * jnp.exp(g)
    k_inv = k * jnp.exp(-g)
    k_end = k * jnp.exp(g_last - g)
    causal = jnp.tril(jnp.ones((L, L), dtype=bool))
    scores = jnp.where(causal, jnp.einsum('bhnlk,bhnsk->bhnls', q_dec, k_inv), 0.0)
    o_local = jnp.einsum('bhnls,bhnsv->bhnlv', scores, v)
    state_decay = jnp.exp(g_last[..., 0, :])

    def step(S, inp):
        q_c, ke_c, v_c, sd_c = inp
        o_state = jnp.einsum('bhlk,bhkv->bhlv', q_c, S)
        S = S * sd_c[..., None] + jnp.einsum('bhlk,bhlv->bhkv', ke_c, v_c)
        return S, o_state

    B, H = q.shape[:2]
    S0 = jnp.zeros((B, H, q.shape[-1], v.shape[-1]), F32)
    xs = tuple(jnp.moveaxis(t, 2, 0) for t in (q_dec, k_end, v, state_decay))
    _, o_state = lax.scan(step, S0, xs)
    return from_chunks(o_local + jnp.moveaxis(o_state, 0, 2))


def gla_group(p, w_alpha2, b_alpha, norm_g):
    B, T, _ = p.shape
    q, k, v, g, a_lr = split_cols(p.astype(F32), GLA_SIZES)
    log_alpha = jax.nn.log_sigmoid(a_lr @ w_alpha2 + b_alpha) / GLA_TAU
    hd = lambda t, d: t.reshape(B, T, GLA_HEADS, d)
    o = gla_chunked(hd(q, GLA_DK) * GLA_DK ** -0.5, hd(k, GLA_DK), hd(v, GLA_DV), hd(log_alpha, GLA_DK))
    o = o * lax.rsqrt(jnp.mean(o * o, axis=-1, keepdims=True) + NORM_EPS)
    return o.reshape(B, T, GLA_HEADS * GLA_DV) * norm_g * jax.nn.silu(g)


def rwkv7_chunked(r, log_w, k, v, a_vec, b_vec):
    L = RWKV_CHUNK
    r, log_w, k, v, a_vec, b_vec = (to_chunks(t, L) for t in (r, log_w, k, v, a_vec, b_vec))
    g_inc = jnp.cumsum(log_w, axis=3)
    g_last = g_inc[..., -1:, :]
    a_dec = a_vec * jnp.exp(g_inc - log_w)
    r_dec = r * jnp.exp(g_inc)
    b_inv = b_vec * jnp.exp(-g_inc)
    k_inv = k * jnp.exp(-g_inc)
    b_end = b_vec * jnp.exp(g_last - g_inc)
    k_end = k * jnp.exp(g_last - g_inc)
    strict = jnp.tril(jnp.ones((L, L), dtype=bool), -1)
    incl = jnp.tril(jnp.ones((L, L), dtype=bool))

    def pair(x, y, m):
        return jnp.where(m, jnp.einsum('bhnlk,bhnsk->bhnls', x, y), 0.0)

    A_ab = pair(a_dec, b_inv, strict)
    A_ak = pair(a_dec, k_inv, strict)
    A_rb = pair(r_dec, b_inv, incl)
    A_rk = pair(r_dec, k_inv, incl)
    N = r.shape[-1]
    rhs = jnp.concatenate([a_dec, jnp.einsum('bhnls,bhnsv->bhnlv', A_ak, v)], axis=-1)
    sol = lax.linalg.triangular_solve(jnp.eye(L, dtype=F32) - A_ab, rhs,
                                      left_side=True, lower=True, unit_diagonal=True)
    z_state, z_local = sol[..., :N], sol[..., N:]
    y_local = jnp.einsum('bhnls,bhnsv->bhnlv', A_rk, v)
    state_decay = jnp.exp(g_last[..., 0, :])

    def step(S, inp):
        zs, zl, rd, arb, yl, be, ke, vc, sd = inp
        z = jnp.einsum('bhlk,bhvk->bhlv', zs, S) + zl
        y = jnp.einsum('bhlk,bhvk->bhlv', rd, S) + jnp.einsum('bhls,bhsv->bhlv', arb, z) + yl
        S = (S * sd[:, :, None, :] + jnp.einsum('bhlv,bhlk->bhvk', z, be)
             + jnp.einsum('bhlv,bhlk->bhvk', vc, ke))
        return S, y

    B, H = r.shape[:2]
    S0 = jnp.zeros((B, H, N, N), F32)
    xs = tuple(jnp.moveaxis(t, 2, 0) for t in (z_state, z_local, r_dec, A_rb, y_local, b_end, k_end, v, state_decay))
    _, y = lax.scan(step, S0, xs)
    return from_chunks(jnp.moveaxis(y, 0, 2))


def rwkv7_group(p, mu, w0, w2, a0, a2, g2, k_k, k_a, r_k, ln_g, ln_b):
    B, T, _ = p.shape
    pf = p.astype(F32)
    prev = jnp.pad(pf, ((0, 0), (1, 0), (0, 0)))[:, :-1]
    pf = pf + (prev - pf) * mu
    r, w_lr, k, v, a_lr, g_lr = split_cols(pf, RWKV_SIZES)
    log_w = -RWKV_DECAY_SCALE * jax.nn.sigmoid(w0 + jnp.tanh(w_lr) @ w2)
    a = jax.nn.sigmoid(a0 + a_lr @ a2)
    g = jax.nn.sigmoid(g_lr) @ g2
    hd = lambda t: t.reshape(B, T, RWKV_HEADS, RWKV_N)
    kk = hd(k * k_k)
    kk = kk * lax.rsqrt(jnp.sum(kk * kk, axis=-1, keepdims=True) + RWKV_L2_EPS)
    k = k * (1.0 + (a - 1.0) * k_a)
    rh, kh, vh = hd(r), hd(k), hd(v)
    y = rwkv7_chunked(rh, hd(log_w), kh, vh, -kk, kk * hd(a))
    mu_y = jnp.mean(y, axis=-1, keepdims=True)
    var = jnp.mean(jnp.square(y - mu_y), axis=-1, keepdims=True)
    y = ((y - mu_y) * lax.rsqrt(var + RWKV_GN_EPS)).reshape(B, T, RWKV_WIDTH) * ln_g + ln_b
    bonus = jnp.sum(rh * kh * r_k.reshape(RWKV_HEADS, RWKV_N), axis=-1, keepdims=True) * vh
    y = y + bonus.reshape(B, T, RWKV_WIDTH)
    return y * g


def rglru_group(xb, gate, conv_w, conv_b, gate_w, gate_b, lam):
    B, T, _ = xb.shape
    xc = causal_dwconv(xb.astype(F32), conv_w, conv_b)
    xblk = xc.reshape(B, T, LRU_BLOCKS, LRU_BLOCK)
    gates = jnp.einsum('btnd,gnde->gbtne', xblk, gate_w).reshape(2, B, T, LRU_WIDTH) + gate_b[:, None, None, :]
    r_gate = jax.nn.sigmoid(gates[0])
    i_gate = jax.nn.sigmoid(gates[1])
    log_a = -LRU_C * r_gate * jax.nn.softplus(-lam)
    u = jnp.sqrt(-jnp.expm1(2.0 * log_a)) * (i_gate * xc)
    h = linear_recurrence(jnp.exp(log_a), u)
    return h * jax.nn.gelu(gate.astype(F32))


def mlstm_chunked(q, k, v, i_pre, log_f):
    L = MLSTM_CHUNK
    q, k, v = (to_chunks(t, L) for t in (q, k, v))
    i_pre, log_f = (to_chunks(t[..., None], L)[..., 0] for t in (i_pre, log_f))
    b = jnp.cumsum(log_f, axis=-1)
    b_last = b[..., -1]
    incl = jnp.tril(jnp.ones((L, L), dtype=bool))
    log_d = jnp.where(incl, b[..., :, None] - b[..., None, :] + i_pre[..., None, :], -jnp.inf)
    m_loc = jnp.max(log_d, axis=-1)
    w_loc = jnp.exp(log_d - m_loc[..., None]) * jnp.einsum('bhnld,bhnsd->bhnls', q, k)
    num_loc = jnp.einsum('bhnls,bhnsd->bhnld', w_loc, v)
    den_loc = jnp.sum(w_loc, axis=-1)
    log_e = b_last[..., None] - b + i_pre
    m_end = jnp.max(log_e, axis=-1)
    w_end = jnp.exp(log_e - m_end[..., None])

    def step(carry, inp):
        C, n, m = carry
        qc, kc, vc, bc, mlc, numc, denc, blc, mec, wec = inp
        m_t = jnp.maximum(bc + m[..., None], mlc)
        s_state = jnp.exp(bc + m[..., None] - m_t)
        s_loc = jnp.exp(mlc - m_t)
        num = s_state[..., None] * jnp.einsum('bhld,bhde->bhle', qc, C) + s_loc[..., None] * numc
        den = s_state * jnp.einsum('bhld,bhd->bhl', qc, n) + s_loc * denc
        h = num / jnp.maximum(jnp.abs(den), jnp.exp(-m_t))[..., None]
        m_new = jnp.maximum(blc + m, mec)
        c_state = jnp.exp(blc + m - m_new)
        c_in = jnp.exp(mec - m_new)[..., None] * wec
        C = c_state[..., None, None] * C + jnp.einsum('bhl,bhld,bhle->bhde', c_in, kc, vc)
        n = c_state[..., None] * n + jnp.einsum('bhl,bhld->bhd', c_in, kc)
        return (C, n, m_new), h

    B, H, _, _, DH = q.shape
    init = (jnp.zeros((B, H, DH, DH), F32), jnp.zeros((B, H, DH), F32), jnp.zeros((B, H), F32))
    xs = tuple(jnp.moveaxis(t, 2, 0) for t in (q, k, v, b, m_loc, num_loc, den_loc, b_last, m_end, w_end))
    _, h = lax.scan(step, init, xs)
    return from_chunks(jnp.moveaxis(h, 0, 2))


def mlstm_group(xm, o_pre, if_pre, conv_w, conv_b, qkv_w, b_if, norm_g):
    B, T, _ = xm.shape
    xmf = xm.astype(F32)
    xc = jax.nn.silu(causal_dwconv(xmf, conv_w, conv_b))

    def headwise(t, w):
        y = jnp.einsum('btnd,nde->btne', t.reshape(B, T, MLSTM_NBLK, MLSTM_QKV_BLOCK), w)
        return y.reshape(B, T, MLSTM_HEADS, MLSTM_DH)

    q = headwise(xc, qkv_w[0])
    k = headwise(xc, qkv_w[1]) * MLSTM_DH ** -0.5
    v = headwise(xmf, qkv_w[2])
    gates = if_pre.astype(F32) + b_if
    i_pre = gates[..., :MLSTM_HEADS]
    log_f = jax.nn.log_sigmoid(gates[..., MLSTM_HEADS:])
    h = mlstm_chunked(q, k, v, i_pre, log_f)
    h = h * lax.rsqrt(jnp.mean(h * h, axis=-1, keepdims=True) + NORM_EPS)
    return jax.nn.sigmoid(o_pre.astype(F32)) * h.reshape(B, T, MLSTM_WIDTH) * norm_g


def memory_cross_attention(xn, mem_n, wq, wkv, wo):
    B, T, _ = xn.shape
    M = mem_n.shape[1]
    q = (xn @ wq).reshape(B, T, XA_HEADS, XA_DH)
    kv = (mem_n @ wkv).reshape(B, M, 2, XA_HEADS, XA_DH)
    k, v = kv[:, :, 0], kv[:, :, 1]
    s = jnp.einsum('bthd,bmhd->bhtm', q, k).astype(F32) * XA_DH ** -0.5
    p = jax.nn.softmax(s, axis=-1).astype(v.dtype)
    o = jnp.einsum('bhtm,bmhd->bthd', p, v).reshape(B, T, XA_HEADS * XA_DH)
    return o @ wo


def conv_glu_ffn(xn, w_up, conv_w, conv_b, w_down):
    u, gt = split_cols(xn @ w_up, (D_FF, D_FF))
    u = causal_dwconv(u, conv_w, conv_b)
    return (jax.nn.silu(u) * gt) @ w_down


def setup_inputs(seed: int = 0) -> dict:
    key = jax.random.key(seed)
    ks = jax.random.split(key, 64)
    counter = [0]

    def nxt():
        counter[0] += 1
        return ks[counter[0] - 1]

    def nrm(shape, scale):
        return jax.random.normal(nxt(), shape, F32) * scale

    def gain(shape):
        return 1.0 + nrm(shape, 0.02)

    lam_s = jax.random.uniform(nxt(), (N_ODD, LRU_WIDTH), F32, 0.9, 0.999) ** (1.0 / LRU_C)
    lru_lambda = jnp.log(lam_s) - jnp.log1p(-lam_s)
    f_bias = jnp.linspace(3.0, 6.0, MLSTM_HEADS, dtype=F32)[None, :] + nrm((N_ODD, MLSTM_HEADS), 0.1)
    mlstm_b_if = jnp.concatenate([nrm((N_ODD, MLSTM_HEADS), 0.1), f_bias], axis=-1)
    return {
        'x': nrm((BATCH, SEQ, D_MODEL), 1.0),
        'mem': nrm((BATCH, MEM_LEN, D_MODEL), 1.0),
        'mem_norm_g': gain((D_MODEL,)),
        'norm_mix_g': gain((DEPTH, D_MODEL)),
        'ab_w_in': nrm((N_EVEN, D_MODEL, AB_COLS), D_MODEL ** -0.5),
        'gla_w_alpha2': nrm((N_EVEN, GLA_RANK, GLA_HEADS * GLA_DK), GLA_RANK ** -0.5),
        'gla_b_alpha': nrm((N_EVEN, GLA_HEADS * GLA_DK), 0.1),
        'gla_norm_g': gain((N_EVEN, GLA_HEADS * GLA_DV)),
        'rwkv_mu': jax.random.uniform(nxt(), (N_EVEN, RWKV_COLS), F32),
        'rwkv_w0': nrm((N_EVEN, RWKV_WIDTH), 0.5),
        'rwkv_w2': nrm((N_EVEN, RWKV_LORA_W, RWKV_WIDTH), RWKV_LORA_W ** -0.5),
        'rwkv_a0': nrm((N_EVEN, RWKV_WIDTH), 0.1),
        'rwkv_a2': nrm((N_EVEN, RWKV_LORA_A, RWKV_WIDTH), RWKV_LORA_A ** -0.5),
        'rwkv_g2': nrm((N_EVEN, RWKV_LORA_G, RWKV_WIDTH), RWKV_LORA_G ** -0.5),
        'rwkv_k_k': 0.85 + nrm((N_EVEN, RWKV_WIDTH), 0.02),
        'rwkv_k_a': gain((N_EVEN, RWKV_WIDTH)),
        'rwkv_r_k': nrm((N_EVEN, RWKV_WIDTH), 0.1),
        'rwkv_ln_g': gain((N_EVEN, RWKV_WIDTH)),
        'rwkv_ln_b': nrm((N_EVEN, RWKV_WIDTH), 0.02),
        'cd_w_in': nrm((N_ODD, D_MODEL, CD_COLS), D_MODEL ** -0.5),
        'lru_conv_w': nrm((N_ODD, LRU_CONV, LRU_WIDTH), LRU_CONV ** -0.5),
        'lru_conv_b': nrm((N_ODD, LRU_WIDTH), 0.02),
        'lru_gate_w': nrm((N_ODD, 2, LRU_BLOCKS, LRU_BLOCK, LRU_BLOCK), LRU_BLOCK ** -0.5),
        'lru_gate_b': nrm((N_ODD, 2, LRU_WIDTH), 0.02),
        'lru_lambda': lru_lambda,
        'mlstm_conv_w': nrm((N_ODD, MLSTM_CONV, MLSTM_WIDTH), MLSTM_CONV ** -0.5),
        'mlstm_conv_b': nrm((N_ODD, MLSTM_WIDTH), 0.02),
        'mlstm_qkv_w': nrm((N_ODD, 3, MLSTM_NBLK, MLSTM_QKV_BLOCK, MLSTM_QKV_BLOCK), MLSTM_QKV_BLOCK ** -0.5),
        'mlstm_b_if': mlstm_b_if,
        'mlstm_norm_g': gain((N_ODD, MLSTM_WIDTH)),
        'w_mix_out': nrm((DEPTH, D_MIX, D_MODEL), D_MIX ** -0.5),
        'norm_xattn_g': gain((DEPTH, D_MODEL)),
        'xattn_wq': nrm((DEPTH, D_MODEL, XA_HEADS * XA_DH), D_MODEL ** -0.5),
        'xattn_wkv': nrm((DEPTH, D_MODEL, 2 * XA_HEADS * XA_DH), D_MODEL ** -0.5),
        'xattn_wo': nrm((DEPTH, XA_HEADS * XA_DH, D_MODEL), D_MODEL ** -0.5),
        'norm_ffn_g': gain((DEPTH, D_MODEL)),
        'ffn_w_up': nrm((DEPTH, D_MODEL, 2 * D_FF), D_MODEL ** -0.5),
        'ffn_conv_w': nrm((DEPTH, FFN_CONV, D_FF), FFN_CONV ** -0.5),
        'ffn_conv_b': nrm((DEPTH, D_FF), 0.02),
        'ffn_w_down': nrm((DEPTH, D_FF, D_MODEL), D_FF ** -0.5),
        'final_norm_g': gain((D_MODEL,)),
    }


def reference(x, mem, mem_norm_g, norm_mix_g, ab_w_in, gla_w_alpha2, gla_b_alpha, gla_norm_g,
              rwkv_mu, rwkv_w0, rwkv_w2, rwkv_a0, rwkv_a2, rwkv_g2, rwkv_k_k, rwkv_k_a, rwkv_r_k,
              rwkv_ln_g, rwkv_ln_b, cd_w_in, lru_conv_w, lru_conv_b, lru_gate_w, lru_gate_b, lru_lambda,
              mlstm_conv_w, mlstm_conv_b, mlstm_qkv_w, mlstm_b_if, mlstm_norm_g, w_mix_out,
              norm_xattn_g, xattn_wq, xattn_wkv, xattn_wo, norm_ffn_g, ffn_w_up, ffn_conv_w, ffn_conv_b,
              ffn_w_down, final_norm_g):
    mem_n = rms_norm(mem, mem_norm_g)
    h = x
    for layer in range(DEPTH):
        j = layer // 2
        xn = rms_norm(h, norm_mix_g[layer])
        if layer % 2 == 0:
            proj = xn @ ab_w_in[j]
            y_a = gla_group(proj[..., :GLA_COLS], gla_w_alpha2[j], gla_b_alpha[j], gla_norm_g[j])
            y_b = rwkv7_group(proj[..., GLA_COLS:], rwkv_mu[j], rwkv_w0[j], rwkv_w2[j], rwkv_a0[j],
                              rwkv_a2[j], rwkv_g2[j], rwkv_k_k[j], rwkv_k_a[j], rwkv_r_k[j],
                              rwkv_ln_g[j], rwkv_ln_b[j])
        else:
            proj = xn @ cd_w_in[j]
            lru_x, lru_g, m_x, m_o, m_if = split_cols(proj, CD_SIZES)
            y_a = rglru_group(lru_x, lru_g, lru_conv_w[j], lru_conv_b[j], lru_gate_w[j], lru_gate_b[j],
                              lru_lambda[j])
            y_b = mlstm_group(m_x, m_o, m_if, mlstm_conv_w[j], mlstm_conv_b[j], mlstm_qkv_w[j],
                              mlstm_b_if[j], mlstm_norm_g[j])
        mixed = jnp.concatenate([y_a, y_b], axis=-1).astype(h.dtype)
        h = h + mixed @ w_mix_out[layer]
        h = h + memory_cross_attention(rms_norm(h, norm_xattn_g[layer]), mem_n,
                                       xattn_wq[layer], xattn_wkv[layer], xattn_wo[layer])
        h = h + conv_glu_ffn(rms_norm(h, norm_ffn_g[layer]), ffn_w_up[layer], ffn_conv_w[layer],
                             ffn_conv_b[layer], ffn_w_down[layer])
    return rms_norm(h, final_norm_g)
```

```python
import contextlib
import os
import numpy as np
import concourse.bass as bass
import concourse.mybir as mybir
from concourse.bass_utils import run_bass_kernel_spmd

F32 = mybir.dt.float32
BF16 = mybir.dt.bfloat16
AF = mybir.ActivationFunctionType
ALU = mybir.AluOpType
AX = mybir.AxisListType

ENGS = ("pe", "act", "dve", "pool", "sp")
D = 1024
DFF = 2752
NCH_FF = 22
EPS = 1e-6


class Res:
    __slots__ = ("name", "last_w", "readers", "dsem", "dcount", "psum")

    def __init__(self, name):
        self.name = name
        self.psum = False
        self.last_w = None
        self.readers = {}
        self.dsem = None
        self.dcount = 0


class Sched:
    def __init__(self, nc, stack):
        self.nc = nc
        self.stack = stack
        self.q = {e: [] for e in ENGS}
        self.cnt = {e: 0 for e in ENGS}
        self.pending = {e: False for e in ENGS}
        self.seen = {e: {} for e in ENGS}
        self.semh = {}
        for e in ENGS:
            self.semh["E_" + e] = stack.enter_context(nc.semaphore("sem_" + e))
        self.nsem = len(ENGS)
        self.ninstr = 0

    def _dsem(self, res):
        if res.dsem is None:
            key = "D_%d" % self.nsem
            self.semh[key] = self.stack.enter_context(self.nc.semaphore("d%d" % self.nsem))
            self.nsem += 1
            res.dsem = key
        return res.dsem

    def _wait(self, eng, deps):
        seen = self.seen[eng]
        for sk, v in deps.items():
            if seen.get(sk, 0) < v:
                seen[sk] = v
                h = self.semh[sk]
                self.q[eng].append(lambda e, h=h, v=v: e.wait_ge(h, v))
                self.ninstr += 1

    def _deps(self, own, reads, writes):
        deps = {}

        def add(ev):
            if ev[1] > deps.get(ev[0], 0):
                deps[ev[0]] = ev[1]
        for r in reads:
            if r.last_w is not None:
                add(r.last_w)
            if r.psum:
                for sk, v in r.readers.items():
                    if sk != own:
                        add((sk, v))
        skip_own = (own == "E_pe")
        for w in writes:
            if w.last_w is not None and not (skip_own and w.last_w[0] == own):
                add(w.last_w)
            for sk, v in w.readers.items():
                if not (skip_own and sk == own):
                    add((sk, v))
        return deps

    def _record(self, ev, reads, writes):
        for r in reads:
            if r.readers.get(ev[0], 0) < ev[1]:
                r.readers[ev[0]] = ev[1]
        for w in writes:
            w.last_w = ev
            w.readers = {}

    def op(self, eng, fn, reads=(), writes=(), signal=True):
        own = "E_" + eng
        self._wait(eng, self._deps(own, reads, writes))
        val = self.cnt[eng] + 1
        ev = (own, val)
        if signal:
            self.cnt[eng] = val
            self.pending[eng] = False
            h = self.semh[own]
            self.q[eng].append(lambda e, fn=fn, h=h: fn(e).then_inc(h, 1))
        else:
            self.pending[eng] = True
            self.q[eng].append(lambda e, fn=fn: fn(e))
        self.ninstr += 1
        self._record(ev, reads, writes)
        return ev

    def dma(self, qeng, out_ap, in_ap, sb_res, reads=(), writes=(), **kw):
        self._wait(qeng, self._deps("__none__", reads, writes))
        sk = self._dsem(sb_res)
        sb_res.dcount += 16
        ev = (sk, sb_res.dcount)
        h = self.semh[sk]
        self.q[qeng].append(
            lambda e, o=out_ap, i=in_ap, h=h, kw=kw: e.dma_start(out=o, in_=i, **kw).then_inc(h, 16))
        self.ninstr += 1
        self._record(ev, reads, writes)
        return ev

    def finish(self, final_events):
        for e in ENGS:
            assert not self.pending[e], "engine %s has unsignalled trailing ops" % e
        allev = {}
        for sk, v in list(final_events) + [("E_" + e, self.cnt[e]) for e in ENGS if self.cnt[e] > 0]:
            if v > allev.get(sk, 0):
                allev[sk] = v
        self._wait("sp", allev)
        nc = self.nc
        with nc.Block() as block:
            @block.tensor
            def _(eng):
                for f in self.q["pe"]:
                    f(eng)

            @block.scalar
            def _(eng):
                for f in self.q["act"]:
                    f(eng)

            @block.vector
            def _(eng):
                for f in self.q["dve"]:
                    f(eng)

            @block.gpsimd
            def _(eng):
                for f in self.q["pool"]:
                    f(eng)

            @block.sync
            def _(eng):
                for f in self.q["sp"]:
                    f(eng)


class T_:
    __slots__ = ("t", "r")

    def __init__(self, t, r):
        self.t = t
        self.r = r

    def __getitem__(self, k):
        return self.t[k]


ARENA_COLS = 84000


class KB:
    def __init__(self, T, NL, plan=None, final_norm=True):
        self.T = T
        self.NL = NL
        self.NB = T // 128
        self.plan = plan
        self.final_norm = final_norm
        self.nc = bass.Bass("TRN2", target_bir_lowering=False)
        self.din = {}
        self.uid = 0

    def inp(self, name, shape):
        self.din[name] = self.nc.dram_tensor(name, list(shape), F32, kind="ExternalInput").ap()
        return self.din[name]

    def sb(self, name, shape, dt=F32):
        t = self.st.enter_context(self.nc.sbuf_tensor(name, list(shape), dt))
        return T_(t, Res(name))

    def ps(self, name, shape, dt=F32):
        t = self.st.enter_context(self.nc.psum_tensor(name, list(shape), dt))
        r = Res(name)
        r.psum = True
        return T_(t, r)

    def psf(self):
        p = self.psf_pool[self.psf_i % len(self.psf_pool)]
        self.psf_i += 1
        return p

    def psb(self):
        p = self.psb_pool[self.psb_i % len(self.psb_pool)]
        self.psb_i += 1
        return p

    def arena_reset(self):
        carry = {}
        for r in self.arena_live:
            evs = dict(r.readers)
            if r.last_w is not None:
                evs[r.last_w[0]] = max(evs.get(r.last_w[0], 0), r.last_w[1])
            for k, v in evs.items():
                if v > carry.get(k, 0):
                    carry[k] = v
        for k, v in self.arena_carry.items():
            if v > carry.get(k, 0):
                carry[k] = v
        self.arena_carry = carry
        self.arena_live = []
        self.arena_off = 0

    def arena_alloc(self, name, kch, ncols):
        n = kch * ncols
        assert self.arena_off + n <= ARENA_COLS, (name, self.arena_off, n)
        ap = self.arena.t[:, self.arena_off:self.arena_off + n].rearrange("p (k n) -> p k n", k=kch)
        self.arena_off += n
        r = Res(name)
        r.readers = dict(self.arena_carry)
        self.arena_live.append(r)
        return T_(ap, r)

    def abuf(self, name, shape, dt=BF16):
        n = int(np.prod(shape[1:]))
        ncols = n if dt == BF16 else 2 * n
        self.arena_off += self.arena_off % 2
        assert self.arena_off + ncols <= ARENA_COLS, (name, self.arena_off, ncols)
        ap = self.arena.t[:, self.arena_off:self.arena_off + ncols]
        self.arena_off += ncols
        if dt != BF16:
            ap = ap.bitcast(dt)
        if len(shape) == 3:
            ap = ap.rearrange("p (k n) -> p k n", k=shape[1])
        elif len(shape) == 4:
            ap = ap.rearrange("p (a b n) -> p a b n", a=shape[1], b=shape[2])
        if shape[0] != 128:
            ap = ap[0:shape[0]]
        r = Res(name)
        r.readers = dict(self.arena_carry)
        self.arena_live.append(r)
        return T_(ap, r)

    def new_ht(self):
        ht = self.ht[self.ht_i % len(self.ht)]
        self.ht_i += 1
        return ht

    def load_w(self, src, dst, K, N, col0=0, scale=None, src_col0=0):
        S = self.S
        nk = (K + 127) // 128
        CB = 1024
        for kc in range(nk):
            rows = min(128, K - kc * 128)
            for c0 in range(0, N, CB):
                cw = min(CB, N - c0)
                stg = self.stg[self.stg_i % len(self.stg)]
                self.stg_i += 1
                S.dma("sp", stg.t[:rows, :cw], src[kc * 128:kc * 128 + rows, src_col0 + c0:src_col0 + c0 + cw],
                      stg.r, writes=[stg.r])
                o = dst.t[:rows, kc, col0 + c0:col0 + c0 + cw]
                eng = ("pool", "dve")[self.cast_i % 2] if self.cast_both else "pool"
                self.cast_i += 1
                if scale is None:
                    S.op(eng, lambda e, o=o, i=stg.t[:rows, :cw]: e.tensor_copy(out=o, in_=i),
                         reads=[stg.r], writes=[dst.r])
                else:
                    sc = scale.t[:rows, c0:c0 + cw]
                    S.op(eng, lambda e, o=o, i=stg.t[:rows, :cw], sc=sc: e.tensor_tensor(out=o, in0=i, in1=sc, op=ALU.mult),
                         reads=[stg.r, scale.r], writes=[dst.r])

    def load_small(self, src_ap, dst, dst_ap=None):
        self.S.dma("sp", dst.t[:] if dst_ap is None else dst_ap, src_ap, dst.r, writes=[dst.r])

    def load_h(self, src, blk, dst):
        self.S.dma("sp", dst.t[:, :], src[blk * 128:(blk + 1) * 128, :], dst.r,
                   reads=[self.hres[id(src)][blk]] if id(src) in self.hres else [], writes=[dst.r])

    def store_h(self, dstd, blk, src, ap=None):
        self.S.dma("sp", dstd[blk * 128:(blk + 1) * 128, :], src.t[:, :] if ap is None else ap, src.r,
                   reads=[src.r], writes=[self.hres[id(dstd)][blk]])

    def rstd_of(self, hap, hres):
        S = self.S
        ss = self.ss[self.ss_i % len(self.ss)]
        self.ss_i += 1
        junk = self.junk
        S.op("act", lambda e: e.activation(out=junk.t[:], in_=hap, func=AF.Square, accum_out=ss.t[:, 0:1]),
             reads=[hres], writes=[junk.r, ss.r])
        S.op("dve", lambda e: e.tensor_scalar(out=ss.t[:, 1:2], in0=ss.t[:, 0:1], scalar1=1.0 / D, scalar2=EPS,
                                              op0=ALU.mult, op1=ALU.add), reads=[ss.r], writes=[ss.r])
        S.op("act", lambda e: e.activation(out=ss.t[:, 2:3], in_=ss.t[:, 1:2], func=AF.Sqrt), reads=[ss.r], writes=[ss.r])
        S.op("dve", lambda e: e.reciprocal(out=ss.t[:, 3:4], in_=ss.t[:, 2:3]), reads=[ss.r], writes=[ss.r])
        return ss

    def norm_T(self, hap, hres, gb, xnT, col0):
        S = self.S
        ss = self.rstd_of(hap, hres)
        xn = self.xn[self.xn_i % len(self.xn)]
        self.xn_i += 1
        S.op("dve", lambda e: e.scalar_tensor_tensor(out=xn.t[:], in0=hap, scalar=ss.t[:, 3:4], in1=gb.t[:],
                                                     op0=ALU.mult, op1=ALU.mult),
             reads=[hres, ss.r, gb.r], writes=[xn.r])
        pT = self.psb()
        for kc in range(8):
            S.op("pe", lambda e, kc=kc: e.transpose(pT.t[:, kc * 128:(kc + 1) * 128], xn.t[:, kc * 128:(kc + 1) * 128],
                                                     self.identb.t[:]),
                 reads=[xn.r, self.identb.r], writes=[pT.r], signal=(kc == 7))
        S.op("act", lambda e: e.copy(out=xnT.t[:, :, col0:col0 + 128],
                                     in_=pT.t[:, 0:1024].rearrange("p (k n) -> p k n", k=8)),
             reads=[pT.r], writes=[xnT.r])
        return xn

    def mm_group(self, out_ap, out_res, pairs):
        n = len(pairs)
        for i, (l, r, rs) in enumerate(pairs):
            self.S.op("pe", lambda e, l=l, r=r, i=i: e.matmul(out_ap, lhsT=l, rhs=r, start=(i == 0), stop=(i == n - 1)),
                      reads=rs, writes=[out_res], signal=(i == n - 1))

    def phase_ffn(self, l, src, dst):
        S = self.S
        d = self.din
        TT = 256
        self.arena_reset()
        Wup = self.arena_alloc("wup", 8, 2 * DFF)
        Wdn = self.arena_alloc("wdn", NCH_FF, D)
        self.load_small(d["ffn_vec"][l], self.fvec)
        self.load_small(d["norm_ffn_g"][l:l + 1, :].partition_broadcast(128), self.gb)
        self.load_w(d["ffn_w_up"][l], Wup, D, 2 * DFF)
        self.load_w(d["ffn_w_down"][l], Wdn, DFF, D)
        xnT = self.abuf("xnT_f", [128, 8, TT + 2])
        actT = self.abuf("actT", [128, NCH_FF, TT])
        self.cv = [self.abuf("cv%d" % i, [128, 2 * TT], F32) for i in range(2)]
        self.cv_i = 0
        S.op("pool", lambda e: e.memset(xnT.t[:, :, 0:2], 0.0), writes=[xnT.r])
        fv = self.fvec
        for t0 in range(0, self.T, TT):
            nb = TT // 128
            hts = [self.new_ht() for b in range(nb)]
            for b in range(nb):
                self.load_h(src, t0 // 128 + b, hts[b])
                self.norm_T(hts[b].t[:, :], hts[b].r, self.gb, xnT, 2 + b * 128)
            for c in range(NCH_FF):
                cs = min(128, DFF - c * 128)
                pu = self.psf()
                pg = self.psf()
                self.mm_group(pu.t[:cs, 0:TT + 2], pu.r,
                              [(Wup.t[:, kc, c * 128:c * 128 + cs], xnT.t[:, kc, 0:TT + 2], [Wup.r, xnT.r]) for kc in range(8)])
                self.mm_group(pg.t[:cs, 0:TT + 2], pg.r,
                              [(Wup.t[:, kc, DFF + c * 128:DFF + c * 128 + cs], xnT.t[:, kc, 0:TT + 2], [Wup.r, xnT.r]) for kc in range(8)])
                cv = self.cv[self.cv_i % len(self.cv)]
                self.cv_i += 1
                w = lambda j, c=c, cs=cs: fv.t[:cs, c * 4 + j:c * 4 + j + 1]
                S.op("dve", lambda e, cs=cs, pu=pu, cv=cv, w=w: e.tensor_scalar(
                    out=cv.t[:cs, 0:TT], in0=pu.t[:cs, 2:TT + 2], scalar1=w(2), scalar2=w(3), op0=ALU.mult, op1=ALU.add),
                    reads=[pu.r, fv.r], writes=[cv.r])
                S.op("dve", lambda e, cs=cs, pu=pu, cv=cv, w=w: e.scalar_tensor_tensor(
                    out=cv.t[:cs, 0:TT], in0=pu.t[:cs, 1:TT + 1], scalar=w(1), in1=cv.t[:cs, 0:TT], op0=ALU.mult, op1=ALU.add),
                    reads=[pu.r, fv.r, cv.r], writes=[cv.r])
                S.op("dve", lambda e, cs=cs, pu=pu, cv=cv, w=w: e.scalar_tensor_tensor(
                    out=cv.t[:cs, 0:TT], in0=pu.t[:cs, 0:TT], scalar=w(0), in1=cv.t[:cs, 0:TT], op0=ALU.mult, op1=ALU.add),
                    reads=[pu.r, fv.r, cv.r], writes=[cv.r])
                S.op("act", lambda e, cs=cs, cv=cv: e.activation(out=cv.t[:cs, TT:2 * TT], in_=cv.t[:cs, 0:TT], func=AF.Silu),
                     reads=[cv.r], writes=[cv.r])
                S.op("dve", lambda e, cs=cs, cv=cv, pg=pg, c=c: e.tensor_tensor(
                    out=actT.t[:cs, c, 0:TT], in0=cv.t[:cs, TT:2 * TT], in1=pg.t[:cs, 2:TT + 2], op=ALU.mult),
                    reads=[cv.r, pg.r], writes=[actT.r])
            S.op("pool", lambda e: e.tensor_copy(out=xnT.t[:, :, 0:2], in_=xnT.t[:, :, TT:TT + 2]), reads=[xnT.r], writes=[xnT.r])
            for b in range(nb):
                for half in range(2):
                    po = self.psf()
                    prs = []
                    for c in range(NCH_FF):
                        cs = min(128, DFF - c * 128)
                        prs.append((actT.t[:cs, c, b * 128:(b + 1) * 128], Wdn.t[:cs, c, half * 512:(half + 1) * 512], [actT.r, Wdn.r]))
                    self.mm_group(po.t[:, :], po.r, prs)
                    ht = hts[b]
                    S.op("dve", lambda e, half=half, po=po, ht=ht: e.tensor_tensor(
                        out=ht.t[:, half * 512:(half + 1) * 512], in0=ht.t[:, half * 512:(half + 1) * 512], in1=po.t[:, :], op=ALU.add),
                        reads=[ht.r, po.r], writes=[ht.r])
                self.store_h(dst, t0 // 128 + b, hts[b])

    def prep_mem(self):
        d = self.din
        self.load_small(d["mem_norm_g"][0:1, :].partition_broadcast(128), self.gb)
        for b in range(2):
            ht = self.new_ht()
            self.S.dma("sp", ht.t[:, :], d["mem"][b * 128:(b + 1) * 128, :], ht.r, writes=[ht.r])
            self.norm_T(ht.t[:, :], ht.r, self.gb, self.memnT, b * 128)

    def phase_xattn(self, l, src, dst):
        S = self.S
        d = self.din
        TT = 512
        self.arena_reset()
        Wq = self.arena_alloc("wq", 8, D)
        Wkv = self.arena_alloc("wkv", 8, 2 * D)
        Wo = self.arena_alloc("wo", 8, D)
        self.load_small(d["norm_xattn_g"][l:l + 1, :].partition_broadcast(128), self.gb)
        self.load_w(d["xattn_wkv"][l], Wkv, D, 2 * D)
        self.load_w(d["xattn_wq"][l], Wq, D, D)
        self.load_w(d["xattn_wo"][l], Wo, D, D)
        memnT = self.memnT
        KT = self.abuf("KT", [128, 8, 256])
        Vr = self.abuf("Vr", [128, 2, D])
        xnT = self.abuf("xnT_x", [128, 8, TT])
        qT = self.abuf("qT", [128, 8, TT])
        pTs = self.abuf("pTs", [128, 8, TT])
        oT = self.abuf("oT", [128, 8, TT])
        self.ex = [self.abuf("ex%d" % i, [128, 512], F32) for i in range(2)]
        self.pb = [self.abuf("pb%d" % i, [128, 512]) for i in range(2)]
        self.ex_i = self.pb_i = 0
        for c in range(8):
            pk = self.psf()
            self.mm_group(pk.t[:, 0:256], pk.r, [(Wkv.t[:, kc, c * 128:(c + 1) * 128], memnT.t[:, kc, :], [Wkv.r, memnT.r]) for kc in range(8)])
            S.op("act", lambda e, c=c, pk=pk: e.copy(out=KT.t[:, c, :], in_=pk.t[:, 0:256]), reads=[pk.r], writes=[KT.r])
        for mb in range(2):
            for half in range(2):
                pv = self.psf()
                self.mm_group(pv.t[:, :], pv.r, [(memnT.t[:, kc, mb * 128:(mb + 1) * 128], Wkv.t[:, kc, D + half * 512:D + (half + 1) * 512],
                                                  [Wkv.r, memnT.r]) for kc in range(8)])
                S.op("act", lambda e, mb=mb, half=half, pv=pv: e.copy(out=Vr.t[:, mb, half * 512:(half + 1) * 512], in_=pv.t[:, :]),
                     reads=[pv.r], writes=[Vr.r])
        for t0 in range(0, self.T, TT):
            nb = min(TT, self.T - t0) // 128
            ntok = nb * 128
            hts = []
            for b in range(nb):
                ht = self.new_ht()
                hts.append(ht)
                self.load_h(src, t0 // 128 + b, ht)
                self.norm_T(ht.t[:, :], ht.r, self.gb, xnT, b * 128)
            for c in range(8):
                pq = self.psf()
                self.mm_group(pq.t[:, 0:ntok], pq.r, [(Wq.t[:, kc, c * 128:(c + 1) * 128], xnT.t[:, kc, 0:ntok], [Wq.r, xnT.r]) for kc in range(8)])
                S.op("act", lambda e, c=c, pq=pq: e.mul(out=qT.t[:, c, 0:ntok], in_=pq.t[:, 0:ntok], mul=1.0 / 16.0),
                     reads=[pq.r], writes=[qT.r])
            for b in range(nb):
                for hp in range(2):
                    psc = self.psf()
                    for hh in range(2):
                        h = hp * 2 + hh
                        self.mm_group(psc.t[:, hh * 256:(hh + 1) * 256], psc.r,
                                      [(qT.t[:, 2 * h + dd, b * 128:(b + 1) * 128], KT.t[:, 2 * h + dd, :], [qT.r, KT.r]) for dd in range(2)])
                    sm = self.smx[self.smx_i % len(self.smx)]
                    self.smx_i += 1
                    p3 = psc.t[:, :].rearrange("p (h m) -> p h m", h=2)
                    S.op("dve", lambda e, sm=sm, p3=p3: e.tensor_reduce(out=sm.t[:, 0:2], in_=p3, axis=AX.X, op=ALU.max),
                         reads=[psc.r], writes=[sm.r])
                    ex = self.ex[self.ex_i % len(self.ex)]
                    self.ex_i += 1
                    e3 = ex.t[:, :].rearrange("p (h m) -> p h m", h=2)
                    S.op("dve", lambda e, sm=sm, p3=p3, e3=e3: e.tensor_tensor(out=e3, in0=p3, in1=sm.t[:, 0:2].unsqueeze(2).to_broadcast([128, 2, 256]),
                                                                           op=ALU.subtract), reads=[psc.r, sm.r], writes=[ex.r])
                    S.op("act", lambda e, ex=ex: e.activation(out=ex.t[:, :], in_=ex.t[:, :], func=AF.Exp), reads=[ex.r], writes=[ex.r])
                    S.op("dve", lambda e, sm=sm, e3=e3: e.tensor_reduce(out=sm.t[:, 2:4], in_=e3, axis=AX.X, op=ALU.add),
                         reads=[ex.r], writes=[sm.r])
                    S.op("dve", lambda e, sm=sm: e.reciprocal(out=sm.t[:, 4:6], in_=sm.t[:, 2:4]), reads=[sm.r], writes=[sm.r])
                    pb = self.pb[self.pb_i % len(self.pb)]
                    self.pb_i += 1
                    S.op("dve", lambda e, sm=sm, e3=e3, pb=pb: e.tensor_tensor(
                        out=pb.t[:, :].rearrange("p (h m) -> p h m", h=2), in0=e3,
                        in1=sm.t[:, 4:6].unsqueeze(2).to_broadcast([128, 2, 256]), op=ALU.mult), reads=[ex.r, sm.r], writes=[pb.r])
                    pT = self.psb()
                    for i in range(4):
                        S.op("pe", lambda e, i=i, pT=pT, pb=pb: e.transpose(pT.t[:, i * 128:(i + 1) * 128], pb.t[:, i * 128:(i + 1) * 128], self.identb.t[:]),
                             reads=[pb.r, self.identb.r], writes=[pT.r], signal=(i == 3))
                    S.op("act", lambda e, pT=pT, hp=hp, b=b: e.copy(out=pTs.t[:, hp * 4:(hp + 1) * 4, b * 128:(b + 1) * 128],
                                                                   in_=pT.t[:, 0:512].rearrange("p (k n) -> p k n", k=4)),
                         reads=[pT.r], writes=[pTs.r])
            for h in range(4):
                for dd in range(2):
                    po = self.psf()
                    self.mm_group(po.t[:, 0:ntok], po.r, [(Vr.t[:, mb, h * 256 + dd * 128:h * 256 + (dd + 1) * 128], pTs.t[:, h * 2 + mb, 0:ntok],
                                                           [Vr.r, pTs.r]) for mb in range(2)])
                    S.op("act", lambda e, h=h, dd=dd, po=po: e.copy(out=oT.t[:, 2 * h + dd, 0:ntok], in_=po.t[:, 0:ntok]), reads=[po.r], writes=[oT.r])
            for b in range(nb):
                ht = hts[b]
                for half in range(2):
                    pw = self.psf()
                    self.mm_group(pw.t[:, :], pw.r, [(oT.t[:, kc, b * 128:(b + 1) * 128], Wo.t[:, kc, half * 512:(half + 1) * 512], [oT.r, Wo.r]) for kc in range(8)])
                    S.op("dve", lambda e, ht=ht, half=half, pw=pw: e.tensor_tensor(
                        out=ht.t[:, half * 512:(half + 1) * 512], in0=ht.t[:, half * 512:(half + 1) * 512], in1=pw.t[:, :], op=ALU.add),
                        reads=[ht.r, pw.r], writes=[ht.r])
                self.store_h(dst, t0 // 128 + b, ht)

    def phase_final(self, src, dst):
        S = self.S
        d = self.din
        self.load_small(d["final_norm_g"][0:1, :].partition_broadcast(128), self.gb)
        for blk in range(self.NB):
            ht = self.new_ht()
            self.load_h(src, blk, ht)
            ss = self.rstd_of(ht.t[:, :], ht.r)
            S.op("dve", lambda e, ht=ht, ss=ss: e.scalar_tensor_tensor(out=ht.t[:, :], in0=ht.t[:, :], scalar=ss.t[:, 3:4], in1=self.gb.t[:],
                                                                 op0=ALU.mult, op1=ALU.mult), reads=[ht.r, ss.r, self.gb.r], writes=[ht.r])
            self.store_h(dst, blk, ht)

    def build(self):
        nc = self.nc
        T, NL = self.T, self.NL
        inp = self.inp
        inp("x", [T, D]); inp("mem", [256, D]); inp("mem_norm_g", [1, D]); inp("final_norm_g", [1, D])
        inp("norm_mix_g", [NL, D]); inp("norm_xattn_g", [NL, D]); inp("norm_ffn_g", [NL, D])
        inp("xattn_wq", [NL, D, D]); inp("xattn_wkv", [NL, D, 2 * D]); inp("xattn_wo", [NL, D, D])
        inp("ffn_w_up", [NL, D, 2 * DFF]); inp("ffn_w_down", [NL, DFF, D]); inp("ffn_vec", [NL, 128, NCH_FF * 4])
        inp("w_mix_out", [NL, D, D])
        self.decl_mixer_inputs()
        out = nc.dram_tensor("out", [T, D], F32, kind="ExternalOutput").ap()
        hbuf = nc.dram_tensor("hbuf", [T, D], F32, kind="Internal").ap()
        hbuf2 = nc.dram_tensor("hbuf2", [T, D], F32, kind="Internal").ap()
        self.hres = {id(hbuf): [Res("h%d" % i) for i in range(self.NB)], id(hbuf2): [Res("g%d" % i) for i in range(self.NB)],
                     id(out): [Res("o%d" % i) for i in range(self.NB)]}
        with contextlib.ExitStack() as st:
            self.st = st
            self.S = S = Sched(nc, st)
            self.psf_pool = [self.ps("psf%d" % i, [128, 512]) for i in range(6)]
            self.psb_pool = [self.ps("psb%d" % i, [128, 1024], BF16) for i in range(2)]
            self.psf_i = self.psb_i = 0
            self.arena = self.sb("arena", [128, ARENA_COLS], BF16)
            self.arena_live, self.arena_carry, self.arena_off = [], {}, 0
            self.stg = [self.sb("stg%d" % i, [128, 1024]) for i in range(2)]
            self.stg_i = self.cast_i = 0
            self.cast_both = False
            self.gb = self.sb("gb", [128, D])
            self.ht = [self.sb("ht%d" % i, [128, D]) for i in range(4)]
            self.ht_i = 0
            self.ss = [self.sb("ss%d" % i, [128, 4]) for i in range(4)]
            self.ss_i = 0
            self.junk = self.sb("junk", [128, D], BF16)
            self.xn = [self.sb("xn%d" % i, [128, D], BF16) for i in range(2)]
            self.xn_i = 0
            self.identb = self.sb("identb", [128, 128], BF16)
            self.identf = self.sb("identf", [128, 128])
            S.op("pool", lambda e: e.memset(self.identf.t[:], 0.0), writes=[self.identf.r])
            S.op("pool", lambda e: e.affine_select(out=self.identf.t[:], in_=self.identf.t[:], pattern=[[-1, 128]], compare_op=ALU.not_equal,
                                                   fill=1.0, base=0, channel_multiplier=1), reads=[self.identf.r], writes=[self.identf.r])
            S.op("pool", lambda e: e.tensor_copy(out=self.identb.t[:], in_=self.identf.t[:]), reads=[self.identf.r], writes=[self.identb.r])
            self.memnT = self.sb("memnT", [128, 8, 256], BF16)
            self.fvec = self.sb("fvec", [128, NCH_FF * 4])
            self.smx = [self.sb("smx%d" % i, [128, 6]) for i in range(2)]
            self.smx_i = 0
            self.alloc_mixer_bufs()

            plan = self.plan
            if plan is None:
                plan = []
                for l in range(NL):
                    plan += [("M", l), ("X", l), ("F", l)]
            if any(p[0] == "X" for p in plan):
                self.prep_mem()
            cur = self.din["x"]
            for (kind, l) in plan:
                x_in = cur is self.din["x"]
                tgt = hbuf if x_in else cur
                if kind == "M" and l % 2 == 0:
                    oth = hbuf2 if cur is hbuf else hbuf
                    self.phase_gla(l, cur, oth)
                    if "norwkv" not in os.environ.get("KDBG", ""):
                        self.phase_rwkv(l, cur, oth)
                    tgt = oth
                elif kind == "M":
                    self.phase_mixer_odd(l, cur, tgt)
                elif kind == "X":
                    self.phase_xattn(l, cur, tgt)
                elif kind == "F":
                    self.phase_ffn(l, cur, tgt)
                cur = tgt
            if self.final_norm:
                self.phase_final(cur, out)
            else:
                for blk in range(self.NB):
                    ht = self.new_ht()
                    self.load_h(cur, blk, ht)
                    self.store_h(out, blk, ht)
            finals = []
            for r in [t.r for t in self.ht]:
                if r.dsem is not None:
                    finals.append((r.dsem, r.dcount))
            S.finish(finals)
            self.ninstr = S.ninstr
        return nc

    def phase_mixer_odd(self, l, src, dst):
        S = self.S
        d = self.din
        j = l // 2
        self.arena_reset()
        Wcd = self.arena_alloc("wcd", 8, 2056)
        Wout = self.arena_alloc("wout", 8, D)
        Wg = self.arena_alloc("wg", 1, 1024)
        Wbd = self.arena_alloc("wbd", 1, 1536)
        self.load_small(d["norm_mix_g"][l:l + 1, :].partition_broadcast(128), self.gb)
        self.load_w(d["cd_w_in"][j], Wcd, D, 2056)
        self.load_w(d["w_mix_out"][l], Wout, D, D)
        self.load_w(d["lru_gw"][j], Wg, 128, 8 * 128)
        self.load_w(d["ml_bd"][j], Wbd, 128, 12 * 128)
        Wg2 = T_(Wg.t[:, 0, :], Wg.r)
        Wbd2 = T_(Wbd.t[:, 0, :], Wbd.r)
        lv = self.abuf("lru_vec", [128, 4, 8], F32)
        mv = self.abuf("ml_vec", [128, 4, 5], F32)
        mrow = self.abuf("ml_row", [128, 8 + 512], F32)
        self.load_small(d["lru_vec"][j], lv)
        self.load_small(d["ml_vec"][j], mv)
        self.load_small(d["ml_row"][j], mrow)
        c8 = self.abuf("c8", [128, 4, 2], F32)
        tmp4 = self.abuf("tmp4", [128, 4], F32)
        S.op("act", lambda e: e.activation(out=tmp4.t[:, :], in_=lv.t[:, :, 7], func=AF.Exp, scale=-1.0), reads=[lv.r], writes=[tmp4.r])
        S.op("act", lambda e: e.activation(out=tmp4.t[:, :], in_=tmp4.t[:, :], func=AF.Ln, bias=1.0, scale=1.0), reads=[tmp4.r], writes=[tmp4.r])
        S.op("dve", lambda e: e.tensor_scalar(out=c8.t[:, :, 0], in0=tmp4.t[:, :], scalar1=-8.0, scalar2=None, op0=ALU.mult), reads=[tmp4.r], writes=[c8.r])
        S.op("dve", lambda e: e.tensor_scalar(out=c8.t[:, :, 1], in0=tmp4.t[:, :], scalar1=-16.0, scalar2=None, op0=ALU.mult), reads=[tmp4.r], writes=[c8.r])
        HL = 4
        xnT = self.abuf("xnT_m", [128, 8, HL + 128])
        mixedT = self.abuf("mixedT", [128, 8, 128])
        lcar = self.abuf("lcar", [128, 4], F32)
        Cst = self.abuf("Cst", [128, 4, 132], F32)
        Cb = self.abuf("Cb", [128, 4, 132])
        S.op("pool", lambda e: e.memset(xnT.t[:, :, 0:HL], 0.0), writes=[xnT.r])
        S.op("pool", lambda e: e.memset(lcar.t[:, :], 0.0), writes=[lcar.r])
        S.op("pool", lambda e: e.memset(Cst.t[:, :, :], 0.0), writes=[Cst.r])
        S.op("pool", lambda e: e.memset(Cb.t[:, :, :], 0.0), writes=[Cb.r])
        NT = 4
        f32t = [self.abuf("mo_f%d" % i, [128, 128], F32) for i in range(12)]
        bft = [self.abuf("mo_b%d" % i, [128, 128]) for i in range(10)]
        vaug = [self.abuf("vaug%d" % i, [128, 136]) for i in range(2)]
        nrow = [self.abuf("nrow%d" % i, [128, 132], F32) for i in range(2)]
        yrow = self.abuf("yrow", [128, 512])
        gt = self.abuf("gt", [128, 24], F32)
        opre = self.abuf("opre", [128, 512], F32)
        cnt = [0, 0, 0, 0]

        def F():
            cnt[0] += 1
            return f32t[cnt[0] % len(f32t)]

        def B():
            cnt[1] += 1
            return bft[cnt[1] % len(bft)]

        for blk in range(self.NB):
            ht = self.new_ht()
            self.load_h(src, blk, ht)
            self.norm_T(ht.t[:, :], ht.r, self.gb, xnT, HL)
            xw = xnT.t[:, :, 0:HL + 128]
            xc_ = xnT.t[:, :, HL:HL + 128]
            DBG = os.environ.get("KDBG", "")
            if "nolru" in DBG:
                S.op("pool", lambda e: e.memset(mixedT.t[:, 0:4, :], 0.0), writes=[mixedT.r])
            for n in range(0 if "nolru" in DBG else 4):
                px = self.psf()
                self.mm_group(px.t[:, 0:HL + 128], px.r, [(Wcd.t[:, kc, n * 128:(n + 1) * 128], xnT.t[:, kc, 0:HL + 128], [Wcd.r, xnT.r]) for kc in range(8)])
                xc = F()
                w = lambda q, n=n: lv.t[:, n, q:q + 1]
                S.op("dve", lambda e, px=px, xc=xc, w=w: e.tensor_scalar(out=xc.t[:, :], in0=px.t[:, HL:HL + 128], scalar1=w(3), scalar2=w(4), op0=ALU.mult, op1=ALU.add),
                     reads=[px.r, lv.r], writes=[xc.r])
                for q in range(3):
                    S.op("dve", lambda e, px=px, xc=xc, w=w, q=q: e.scalar_tensor_tensor(out=xc.t[:, :], in0=px.t[:, HL - 3 + q:HL - 3 + q + 128], scalar=w(q), in1=xc.t[:, :],
                                                                                       op0=ALU.mult, op1=ALU.add), reads=[px.r, lv.r, xc.r], writes=[xc.r])
                xcb = B()
                S.op("act", lambda e, xc=xc, xcb=xcb: e.copy(out=xcb.t[:, :], in_=xc.t[:, :]), reads=[xc.r], writes=[xcb.r])
                pr = self.psf()
                self.mm_group(pr.t[:, 0:128], pr.r, [(Wg2.t[:, (0 * 4 + n) * 128:(0 * 4 + n + 1) * 128], xcb.t[:, :], [Wg.r, xcb.r])])
                self.mm_group(pr.t[:, 128:256], pr.r, [(Wg2.t[:, (1 * 4 + n) * 128:(1 * 4 + n + 1) * 128], xcb.t[:, :], [Wg.r, xcb.r])])
                rg = F(); ig = F()
                S.op("act", lambda e, pr=pr, rg=rg, w=w: e.activation(out=rg.t[:, :], in_=pr.t[:, 0:128], func=AF.Sigmoid, bias=w(5), scale=1.0),
                     reads=[pr.r, lv.r], writes=[rg.r])
                S.op("act", lambda e, pr=pr, ig=ig, w=w: e.activation(out=ig.t[:, :], in_=pr.t[:, 128:256], func=AF.Sigmoid, bias=w(6), scale=1.0),
                     reads=[pr.r, lv.r], writes=[ig.r])
                a = F(); a2 = F()
                S.op("act", lambda e, rg=rg, a=a, n=n: e.activation(out=a.t[:, :], in_=rg.t[:, :], func=AF.Exp, scale=c8.t[:, n, 0:1]), reads=[rg.r, c8.r], writes=[a.r])
                S.op("act", lambda e, rg=rg, a2=a2, n=n: e.activation(out=a2.t[:, :], in_=rg.t[:, :], func=AF.Exp, scale=c8.t[:, n, 1:2]), reads=[rg.r, c8.r], writes=[a2.r])
                S.op("dve", lambda e, a2=a2: e.tensor_scalar(out=a2.t[:, :], in0=a2.t[:, :], scalar1=-1.0, scalar2=1.0, op0=ALU.mult, op1=ALU.add), reads=[a2.r], writes=[a2.r])
                S.op("act", lambda e, a2=a2: e.activation(out=a2.t[:, :], in_=a2.t[:, :], func=AF.Sqrt), reads=[a2.r], writes=[a2.r])
                S.op("dve", lambda e, ig=ig, xc=xc: e.tensor_tensor(out=ig.t[:, :], in0=ig.t[:, :], in1=xc.t[:, :], op=ALU.mult), reads=[ig.r, xc.r], writes=[ig.r])
                S.op("dve", lambda e, ig=ig, a2=a2: e.tensor_tensor(out=ig.t[:, :], in0=ig.t[:, :], in1=a2.t[:, :], op=ALU.mult), reads=[ig.r, a2.r], writes=[ig.r])
                hl = F()
                S.op("dve", lambda e, hl=hl, a=a, ig=ig, n=n: e.tensor_tensor_scan(out=hl.t[:, :], data0=a.t[:, :], data1=ig.t[:, :], initial=lcar.t[:, n:n + 1],
                                                                                 op0=ALU.mult, op1=ALU.add), reads=[a.r, ig.r, lcar.r], writes=[hl.r])
                S.op("pool", lambda e, hl=hl, n=n: e.tensor_copy(out=lcar.t[:, n:n + 1], in_=hl.t[:, 127:128]), reads=[hl.r], writes=[lcar.r])
                pg = self.psf()
                self.mm_group(pg.t[:, 0:128], pg.r, [(Wcd.t[:, kc, 512 + n * 128:512 + (n + 1) * 128], xnT.t[:, kc, HL:HL + 128], [Wcd.r, xnT.r]) for kc in range(8)])
                g = F(); g3 = F()
                S.op("act", lambda e, pg=pg, g=g: e.copy(out=g.t[:, :], in_=pg.t[:, 0:128]), reads=[pg.r], writes=[g.r])
                S.op("dve", lambda e, g=g, g3=g3: e.tensor_tensor(out=g3.t[:, :], in0=g.t[:, :], in1=g.t[:, :], op=ALU.mult), reads=[g.r], writes=[g3.r])
                S.op("dve", lambda e, g3=g3: e.tensor_scalar(out=g3.t[:, :], in0=g3.t[:, :], scalar1=0.044715, scalar2=1.0, op0=ALU.mult, op1=ALU.add), reads=[g3.r], writes=[g3.r])
                S.op("dve", lambda e, g=g, g3=g3: e.tensor_tensor(out=g3.t[:, :], in0=g3.t[:, :], in1=g.t[:, :], op=ALU.mult), reads=[g.r, g3.r], writes=[g3.r])
                S.op("act", lambda e, g3=g3: e.activation(out=g3.t[:, :], in_=g3.t[:, :], func=AF.Sigmoid, scale=1.5957691216), reads=[g3.r], writes=[g3.r])
                S.op("dve", lambda e, g=g, g3=g3: e.tensor_tensor(out=g3.t[:, :], in0=g3.t[:, :], in1=g.t[:, :], op=ALU.mult), reads=[g.r, g3.r], writes=[g3.r])
                S.op("dve", lambda e, hl=hl, g3=g3, n=n: e.tensor_tensor(out=mixedT.t[:, n, :], in0=g3.t[:, :], in1=hl.t[:, :], op=ALU.mult), reads=[hl.r, g3.r], writes=[mixedT.r])
            if "nomlstm" in DBG:
                S.op("pool", lambda e: e.memset(mixedT.t[:, 4:8, :], 0.0), writes=[mixedT.r])
                S.op("pool", lambda e: e.tensor_copy(out=xnT.t[:, :, 0:HL], in_=xnT.t[:, :, 128:128 + HL]), reads=[xnT.r], writes=[xnT.r])
                self.out_proj(mixedT, Wout, ht, dst, blk)
                continue
            pgt = self.psf()
            self.mm_group(pgt.t[:, 0:8], pgt.r, [(xnT.t[:, kc, HL:HL + 128], Wcd.t[:, kc, 2048:2056], [Wcd.r, xnT.r]) for kc in range(8)])
            S.op("dve", lambda e, pgt=pgt: e.tensor_tensor(out=gt.t[:, 0:8], in0=pgt.t[:, 0:8], in1=mrow.t[:, 0:8], op=ALU.add), reads=[pgt.r, mrow.r], writes=[gt.r])
            S.op("act", lambda e: e.activation(out=gt.t[:, 8:12], in_=gt.t[:, 4:8], func=AF.Exp, scale=-1.0), reads=[gt.r], writes=[gt.r])
            S.op("act", lambda e: e.activation(out=gt.t[:, 8:12], in_=gt.t[:, 8:12], func=AF.Ln, bias=1.0, scale=1.0), reads=[gt.r], writes=[gt.r])
            pc = self.psf()
            self.mm_group(pc.t[:, 0:4], pc.r, [(self.Uf.t[:, :], gt.t[:, 8:12], [self.Uf.r, gt.r])])
            self.mm_group(pc.t[:, 4:8], pc.r, [(self.onesf.t[:, :], gt.t[:, 8:12], [self.onesf.r, gt.r])])
            S.op("dve", lambda e, pc=pc: e.tensor_tensor(out=gt.t[:, 12:16], in0=pc.t[:, 0:4], in1=gt.t[:, 0:4], op=ALU.add), reads=[pc.r, gt.r], writes=[gt.r])
            S.op("act", lambda e: e.activation(out=gt.t[:, 12:16], in_=gt.t[:, 12:16], func=AF.Exp), reads=[gt.r], writes=[gt.r])
            S.op("act", lambda e, pc=pc: e.activation(out=gt.t[:, 16:24], in_=pc.t[:, 0:8], func=AF.Exp, scale=-1.0), reads=[pc.r, gt.r], writes=[gt.r])
            po = self.psf()
            self.mm_group(po.t[:, :], po.r, [(xnT.t[:, kc, HL:HL + 128], Wcd.t[:, kc, 1536:2048], [Wcd.r, xnT.r]) for kc in range(8)])
            S.op("act", lambda e, po=po: e.activation(out=opre.t[:, :], in_=po.t[:, :], func=AF.Sigmoid), reads=[po.r], writes=[opre.r])
            S.op("dve", lambda e: e.tensor_tensor(out=opre.t[:, :], in0=opre.t[:, :], in1=mrow.t[:, 8:520], op=ALU.mult), reads=[opre.r, mrow.r], writes=[opre.r])
            MS = float(os.environ.get("KMS", "9"))
            if MS < 9:
                S.op("pool", lambda e: e.memset(yrow.t[:, :], 0.0), writes=[yrow.r])
            for n in range(4 if MS >= 2 else 0):
                px = self.psf()
                self.mm_group(px.t[:, 0:HL + 128], px.r, [(Wcd.t[:, kc, 1024 + n * 128:1024 + (n + 1) * 128], xnT.t[:, kc, 0:HL + 128], [Wcd.r, xnT.r]) for kc in range(8)])
                xc = F()
                w = lambda q, n=n: mv.t[:, n, q:q + 1]
                S.op("dve", lambda e, px=px, xc=xc, w=w: e.tensor_scalar(out=xc.t[:, :], in0=px.t[:, HL:HL + 128], scalar1=w(3), scalar2=w(4), op0=ALU.mult, op1=ALU.add),
                     reads=[px.r, mv.r], writes=[xc.r])
                for q in range(3):
                    S.op("dve", lambda e, px=px, xc=xc, w=w, q=q: e.scalar_tensor_tensor(out=xc.t[:, :], in0=px.t[:, HL - 3 + q:HL - 3 + q + 128], scalar=w(q), in1=xc.t[:, :],
                                                                                       op0=ALU.mult, op1=ALU.add), reads=[px.r, mv.r, xc.r], writes=[xc.r])
                xcm = B(); mxb = B()
                S.op("act", lambda e, xc=xc, xcm=xcm: e.activation(out=xcm.t[:, :], in_=xc.t[:, :], func=AF.Silu), reads=[xc.r], writes=[xcm.r])
                S.op("act", lambda e, px=px, mxb=mxb: e.copy(out=mxb.t[:, :], in_=px.t[:, HL:HL + 128]), reads=[px.r], writes=[mxb.r])
                if MS < 2.2:
                    continue
                wq = Wbd2.t[:, (0 * 4 + n) * 128:(0 * 4 + n + 1) * 128]
                wk = Wbd2.t[:, (1 * 4 + n) * 128:(1 * 4 + n + 1) * 128]
                wv = Wbd2.t[:, (2 * 4 + n) * 128:(2 * 4 + n + 1) * 128]
                pqk = self.psf()
                self.mm_group(pqk.t[:, 0:128], pqk.r, [(wq, xcm.t[:, :], [Wbd.r, xcm.r])])
                self.mm_group(pqk.t[:, 128:256], pqk.r, [(wk, xcm.t[:, :], [Wbd.r, xcm.r])])
                self.mm_group(pqk.t[:, 256:384], pqk.r, [(xcm.t[:, :], wk, [Wbd.r, xcm.r])])
                self.mm_group(pqk.t[:, 384:512], pqk.r, [(mxb.t[:, :], wv, [Wbd.r, mxb.r])])
                qT = B(); kT = B(); kr = B()
                S.op("act", lambda e, pqk=pqk, qT=qT: e.copy(out=qT.t[:, :], in_=pqk.t[:, 0:128]), reads=[pqk.r], writes=[qT.r])
                S.op("act", lambda e, pqk=pqk, kT=kT: e.mul(out=kT.t[:, :], in_=pqk.t[:, 128:256], mul=128.0 ** -0.5), reads=[pqk.r], writes=[kT.r])
                S.op("act", lambda e, pqk=pqk, kr=kr: e.mul(out=kr.t[:, :], in_=pqk.t[:, 256:384], mul=128.0 ** -0.5), reads=[pqk.r], writes=[kr.r])
                if MS < 2.3:
                    continue
                va = vaug[cnt[2] % 2]; cnt[2] += 1
                KVA = os.environ.get("KVA", "12")
                if "1" in KVA:
                  S.op("act", lambda e, pqk=pqk, va=va, n=n: e.activation(out=va.t[:, 0:128], in_=pqk.t[:, 384:512], func=AF.Identity, bias=0.0, scale=gt.t[:, 12 + n:13 + n]),
                       reads=[pqk.r, gt.r], writes=[va.r])
                if "2" in KVA:
                  S.op("dve", lambda e, va=va, n=n: e.tensor_tensor(out=va.t[:, 128:136], in0=self.onesf.t[:, 0:8], in1=gt.t[:, 12 + n:13 + n].to_broadcast([128, 8]), op=ALU.mult),
                     reads=[gt.r, self.onesf.r], writes=[va.r])
                if MS < 3:
                    continue
                psT = self.psf()
                self.mm_group(psT.t[:, 0:128], psT.r, [(kT.t[:, :], qT.t[:, :], [kT.r, qT.r])])
                scT = B()
                S.op("dve", lambda e, psT=psT, scT=scT: e.tensor_tensor(out=scT.t[:, :], in0=psT.t[:, 0:128], in1=self.Uf.t[:, :], op=ALU.mult),
                     reads=[psT.r, self.Uf.r], writes=[scT.r])
                pn = self.psf()
                self.mm_group(pn.t[:, 0:130], pn.r, [(scT.t[:, :], va.t[:, 0:130], [scT.r, va.r]), (qT.t[:, :], Cb.t[:, n, 0:130], [qT.r, Cb.r])])
                nr = nrow[cnt[3] % 2]; cnt[3] += 1
                S.op("dve", lambda e, pn=pn, nr=nr, n=n: e.tensor_tensor(out=nr.t[:, 0:129], in0=pn.t[:, 0:129], in1=gt.t[:, 16 + n:17 + n].to_broadcast([128, 129]), op=ALU.mult),
                     reads=[pn.r, gt.r], writes=[nr.r])
                if MS < 4:
                    continue
                S.op("act", lambda e, nr=nr: e.activation(out=nr.t[:, 129:130], in_=nr.t[:, 128:129], func=AF.Abs), reads=[nr.r], writes=[nr.r])
                S.op("dve", lambda e, nr=nr: e.tensor_scalar(out=nr.t[:, 129:130], in0=nr.t[:, 129:130], scalar1=1.0, scalar2=None, op0=ALU.max),
                     reads=[nr.r], writes=[nr.r])
                S.op("dve", lambda e, nr=nr: e.reciprocal(out=nr.t[:, 130:131], in_=nr.t[:, 129:130]), reads=[nr.r], writes=[nr.r])
                hh = F()
                S.op("dve", lambda e, nr=nr, hh=hh: e.tensor_tensor(out=hh.t[:, :], in0=nr.t[:, 0:128], in1=nr.t[:, 130:131].to_broadcast([128, 128]), op=ALU.mult),
                     reads=[nr.r], writes=[hh.r])
                jk = F()
                S.op("act", lambda e, hh=hh, jk=jk, nr=nr: e.activation(out=jk.t[:, :], in_=hh.t[:, :], func=AF.Square, accum_out=nr.t[:, 131:132]),
                     reads=[hh.r], writes=[jk.r, nr.r])
                S.op("dve", lambda e, nr=nr: e.tensor_scalar(out=nr.t[:, 129:130], in0=nr.t[:, 131:132], scalar1=1.0 / 128.0, scalar2=EPS, op0=ALU.mult, op1=ALU.add),
                     reads=[nr.r], writes=[nr.r])
                S.op("act", lambda e, nr=nr: e.activation(out=nr.t[:, 129:130], in_=nr.t[:, 129:130], func=AF.Sqrt), reads=[nr.r], writes=[nr.r])
                S.op("dve", lambda e, nr=nr: e.reciprocal(out=nr.t[:, 130:131], in_=nr.t[:, 129:130]), reads=[nr.r], writes=[nr.r])
                S.op("dve", lambda e, nr=nr, hh=hh, n=n: e.scalar_tensor_tensor(out=yrow.t[:, n * 128:(n + 1) * 128], in0=hh.t[:, :], scalar=nr.t[:, 130:131],
                                                                             in1=opre.t[:, n * 128:(n + 1) * 128], op0=ALU.mult, op1=ALU.mult),
                     reads=[nr.r, hh.r, opre.r], writes=[yrow.r])
                if MS < 5:
                    continue
                pC = self.psf()
                self.mm_group(pC.t[:, 0:130], pC.r, [(kr.t[:, :], va.t[:, 0:130], [kr.r, va.r])])
                S.op("dve", lambda e, pC=pC, n=n: e.tensor_tensor(out=Cst.t[:, n, 0:129], in0=Cst.t[:, n, 0:129], in1=pC.t[:, 0:129], op=ALU.add),
                     reads=[pC.r, Cst.r], writes=[Cst.r])
                S.op("dve", lambda e, n=n: e.tensor_tensor(out=Cst.t[:, n, 0:129], in0=Cst.t[:, n, 0:129], in1=gt.t[:, 20 + n:21 + n].to_broadcast([128, 129]), op=ALU.mult),
                     reads=[Cst.r, gt.r], writes=[Cst.r])
                S.op("act", lambda e, n=n: e.copy(out=Cb.t[:, n, 0:130], in_=Cst.t[:, n, 0:130]), reads=[Cst.r], writes=[Cb.r])
            pT = self.psb()
            for n in range(4):
                S.op("pe", lambda e, n=n, pT=pT: e.transpose(pT.t[:, n * 128:(n + 1) * 128], yrow.t[:, n * 128:(n + 1) * 128], self.identb.t[:]),
                     reads=[yrow.r, self.identb.r], writes=[pT.r], signal=(n == 3))
            S.op("act", lambda e, pT=pT: e.copy(out=mixedT.t[:, 4:8, :], in_=pT.t[:, 0:512].rearrange("p (k n) -> p k n", k=4)), reads=[pT.r], writes=[mixedT.r])
            S.op("pool", lambda e: e.tensor_copy(out=xnT.t[:, :, 0:HL], in_=xnT.t[:, :, 128:128 + HL]), reads=[xnT.r], writes=[xnT.r])
            self.out_proj(mixedT, Wout, ht, dst, blk)

    def out_proj(self, mixedT, Wout, ht, dst, blk, nk=8):
        S = self.S
        for half in range(2):
            pw = self.psf()
            self.mm_group(pw.t[:, :], pw.r, [(mixedT.t[:, kc, :], Wout.t[:, kc, half * 512:(half + 1) * 512], [mixedT.r, Wout.r]) for kc in range(nk)])
            S.op("dve", lambda e, ht=ht, half=half, pw=pw: e.tensor_tensor(
                out=ht.t[:, half * 512:(half + 1) * 512], in0=ht.t[:, half * 512:(half + 1) * 512], in1=pw.t[:, :], op=ALU.add),
                reads=[ht.r, pw.r], writes=[ht.r])
        self.store_h(dst, blk, ht)

    def phase_mixer(self, l, src, dst):
        if l % 2 == 1:
            self.phase_mixer_odd(l, src, dst)
        else:
            self.phase_mixer_even(l, src, dst)

    def decl_mixer_inputs(self):
        NE, NO = (self.NL + 1) // 2, self.NL // 2
        inp = self.inp
        inp("cd_w_in", [NO, D, 2056]); inp("lru_gw", [NO, 128, 1024]); inp("ml_bd", [NO, 128, 12 * 128])
        inp("lru_vec", [NO, 128, 32]); inp("ml_vec", [NO, 128, 20]); inp("ml_row", [NO, 128, 520])
        self.decl_even_inputs(NE)

    def alloc_mixer_bufs(self):
        S = self.S
        self.Uf = self.sb("Uf", [128, 128])
        self.onesf = self.sb("onesf", [128, 128])
        S.op("pool", lambda e: e.memset(self.onesf.t[:], 1.0), writes=[self.onesf.r])
        S.op("pool", lambda e: e.memset(self.Uf.t[:], 1.0), writes=[self.Uf.r])
        S.op("pool", lambda e: e.affine_select(out=self.Uf.t[:], in_=self.Uf.t[:], pattern=[[1, 128]], compare_op=ALU.is_ge,
                                               fill=0.0, base=0, channel_multiplier=-1), reads=[self.Uf.r], writes=[self.Uf.r])
        self.alloc_even_bufs()

    def decl_even_inputs(self, NE):
        inp = self.inp
        inp("ab_w_in", [NE, D, 3344]); inp("gla_wa", [NE, 16, 256]); inp("gla_vec", [NE, 64, 4]); inp("gla_ng", [NE, 128, 4])
        inp("rw_w2", [NE, 64, 512]); inp("rw_a2", [NE, 64, 512]); inp("rw_g2", [NE, 128, 512])
        inp("rw_vec", [NE, 64, 64]); inp("rw_mu_s", [NE, 128, 4]); inp("rw_row", [NE, 128, 1024])

    def alloc_even_bufs(self):
        S = self.S
        self.Us = self.sb("Us", [128, 128])
        self.Ls = self.sb("Ls", [128, 128])
        S.op("pool", lambda e: e.memset(self.Us.t[:], 1.0), writes=[self.Us.r])
        S.op("pool", lambda e: e.affine_select(out=self.Us.t[:], in_=self.Us.t[:], pattern=[[1, 128]], compare_op=ALU.is_gt,
                                               fill=0.0, base=0, channel_multiplier=-1), reads=[self.Us.r], writes=[self.Us.r])
        S.op("pool", lambda e: e.memset(self.Ls.t[:], 1.0), writes=[self.Ls.r])
        S.op("pool", lambda e: e.affine_select(out=self.Ls.t[:], in_=self.Ls.t[:], pattern=[[-1, 128]], compare_op=ALU.is_gt,
                                               fill=0.0, base=0, channel_multiplier=1), reads=[self.Ls.r], writes=[self.Ls.r])

    def phase_gla(self, l, src, dst):
        S = self.S
        d = self.din
        j = l // 2
        DBG = os.environ.get("KDBG", "")
        self.arena_reset()
        Wg = self.arena_alloc("w_gla", 8, 1552)
        Wout = self.arena_alloc("wout_a", 4, D)
        Wa = self.arena_alloc("w_alpha", 1, 256)
        self.load_small(d["norm_mix_g"][l:l + 1, :].partition_broadcast(128), self.gb)
        self.load_w(d["ab_w_in"][j], Wg, D, 1552)
        self.load_w(d["w_mix_out"][l][0:512, :], Wout, 512, D)
        self.load_w(d["gla_wa"][j], Wa, 16, 256)
        gvec = self.abuf("gla_vec", [64, 4], F32)
        gng = self.abuf("gla_ng", [128, 4], F32)
        self.load_small(d["gla_vec"][j], gvec)
        self.load_small(d["gla_ng"][j], gng)
        xnT = self.abuf("xnT_e", [128, 8, 128])
        mixedT = self.abuf("mixedT", [128, 4, 128])
        Sg = self.abuf("Sg", [64, 4, 128], F32)
        Sgb = self.abuf("Sgb", [64, 4, 128])
        S.op("pool", lambda e: e.memset(Sg.t[:, :, :], 0.0), writes=[Sg.r])
        S.op("pool", lambda e: e.memset(Sgb.t[:, :, :], 0.0), writes=[Sgb.r])
        gq = self.abuf("gq", [64, 4, 128], F32)
        gk = self.abuf("gk", [64, 4, 128], F32)
        gx = self.abuf("gx", [64, 4, 128], F32)
        gcs = self.abuf("gcs", [64, 4, 128], F32)
        ge = self.abuf("ge", [64, 4, 128], F32)
        gsd = self.abuf("gsd", [64, 8], F32)
        qdec = self.abuf("qdec", [64, 4, 128])
        kinv = self.abuf("kinv", [64, 4, 128])
        kend = self.abuf("kend", [64, 4, 128])
        kendr = self.abuf("kendr", [128, 256])
        vrow = self.abuf("g_vrow", [128, 512])
        alrT = self.abuf("alrT", [16, 128])
        scT = self.abuf("g_scT", [128, 4, 128])
        osq = self.abuf("g_osq", [128, 512], F32)
        orst = self.abuf("g_orst", [128, 512], F32)
        gsil = self.abuf("g_sil", [128, 4, 128], F32)
        for blk in range(self.NB):
            ht = self.new_ht()
            self.load_h(src, blk, ht)
            xn_cur = self.norm_T(ht.t[:, :], ht.r, self.gb, xnT, 0)
            xw = [Wg.r, xnT.r]
            pq = self.psf(); pk = self.psf()
            for h in range(4):
                self.mm_group(pq.t[0:64, h * 128:(h + 1) * 128], pq.r, [(Wg.t[:, kc, h * 64:(h + 1) * 64], xnT.t[:, kc, :], xw) for kc in range(8)])
            for h in range(4):
                self.mm_group(pk.t[0:64, h * 128:(h + 1) * 128], pk.r, [(Wg.t[:, kc, 256 + h * 64:256 + (h + 1) * 64], xnT.t[:, kc, :], xw) for kc in range(8)])
            S.op("act", lambda e, pq=pq: e.copy(out=gq.t[:, :, :], in_=pq.t[0:64, :].rearrange("p (h n) -> p h n", h=4)), reads=[pq.r], writes=[gq.r])
            S.op("act", lambda e, pk=pk: e.copy(out=gk.t[:, :, :], in_=pk.t[0:64, :].rearrange("p (h n) -> p h n", h=4)), reads=[pk.r], writes=[gk.r])
            pv = self.psf()
            self.mm_group(pv.t[:, :], pv.r, [(xnT.t[:, kc, :], Wg.t[:, kc, 512:1024], xw) for kc in range(8)])
            S.op("act", lambda e, pv=pv: e.copy(out=vrow.t[:, :], in_=pv.t[:, :]), reads=[pv.r], writes=[vrow.r])
            pg = self.psf()
            for h in range(4):
                self.mm_group(pg.t[:, h * 128:(h + 1) * 128], pg.r, [(Wg.t[:, kc, 1024 + h * 128:1024 + (h + 1) * 128], xnT.t[:, kc, :], xw) for kc in range(8)])
            S.op("act", lambda e, pg=pg: e.activation(out=gsil.t[:, :, :], in_=pg.t[:, :].rearrange("p (h n) -> p h n", h=4), func=AF.Silu), reads=[pg.r], writes=[gsil.r])
            pa = self.psf()
            self.mm_group(pa.t[0:16, 0:128], pa.r, [(Wg.t[:, kc, 1536:1552], xnT.t[:, kc, :], xw) for kc in range(8)])
            S.op("act", lambda e, pa=pa: e.copy(out=alrT.t[:, :], in_=pa.t[0:16, 0:128]), reads=[pa.r], writes=[alrT.r])
            px = self.psf()
            for h in range(4):
                self.mm_group(px.t[0:64, h * 128:(h + 1) * 128], px.r, [(Wa.t[0:16, 0, h * 64:(h + 1) * 64], alrT.t[:, :], [Wa.r, alrT.r])])
            S.op("dve", lambda e, px=px: e.tensor_tensor(out=gx.t[:, :, :], in0=px.t[0:64, :].rearrange("p (h n) -> p h n", h=4),
                                                         in1=gvec.t[:, 0:4].unsqueeze(2).to_broadcast([64, 4, 128]), op=ALU.add), reads=[px.r, gvec.r], writes=[gx.r])
            S.op("act", lambda e: e.activation(out=gx.t[:, :, :], in_=gx.t[:, :, :], func=AF.Exp, scale=-1.0), reads=[gx.r], writes=[gx.r])
            S.op("act", lambda e: e.activation(out=gx.t[:, :, :], in_=gx.t[:, :, :], func=AF.Ln, bias=1.0, scale=1.0), reads=[gx.r], writes=[gx.r])
            for h in range(4):
                S.op("dve", lambda e, h=h: e.tensor_tensor_scan(out=gcs.t[:, h, :], data0=self.onesf.t[0:64, :], data1=gx.t[:, h, :], initial=0.0,
                                                                op0=ALU.mult, op1=ALU.add), reads=[gx.r, self.onesf.r], writes=[gcs.r])
            S.op("act", lambda e: e.activation(out=ge.t[:, :, :], in_=gcs.t[:, :, :], func=AF.Exp, scale=-1.0 / 16.0), reads=[gcs.r], writes=[ge.r])
            S.op("dve", lambda e: e.scalar_tensor_tensor(out=qdec.t[:, :, :], in0=gq.t[:, :, :], scalar=0.125, in1=ge.t[:, :, :], op0=ALU.mult, op1=ALU.mult),
                 reads=[gq.r, ge.r], writes=[qdec.r])
            S.op("act", lambda e: e.copy(out=gsd.t[:, 0:4], in_=ge.t[:, :, 127]), reads=[ge.r], writes=[gsd.r])
            S.op("act", lambda e: e.activation(out=ge.t[:, :, :], in_=gcs.t[:, :, :], func=AF.Exp, scale=1.0 / 16.0), reads=[gcs.r], writes=[ge.r])
            S.op("dve", lambda e: e.tensor_tensor(out=kinv.t[:, :, :], in0=gk.t[:, :, :], in1=ge.t[:, :, :], op=ALU.mult), reads=[gk.r, ge.r], writes=[kinv.r])
            S.op("dve", lambda e: e.tensor_tensor(out=gx.t[:, :, :], in0=gcs.t[:, :, :], in1=gcs.t[:, :, 127:128].to_broadcast([64, 4, 128]), op=ALU.subtract),
                 reads=[gcs.r], writes=[gx.r])
            S.op("act", lambda e: e.activation(out=ge.t[:, :, :], in_=gx.t[:, :, :], func=AF.Exp, scale=1.0 / 16.0), reads=[gx.r], writes=[ge.r])
            S.op("dve", lambda e: e.tensor_tensor(out=kend.t[:, :, :], in0=gk.t[:, :, :], in1=ge.t[:, :, :], op=ALU.mult), reads=[gk.r, ge.r], writes=[kend.r])
            pT = self.psb()
            for h in range(4):
                S.op("pe", lambda e, h=h, pT=pT: e.transpose(pT.t[:, h * 64:(h + 1) * 64], kend.t[:, h, :], self.identb.t[0:64, 0:64]),
                     reads=[kend.r, self.identb.r], writes=[pT.r], signal=(h == 3))
            S.op("act", lambda e, pT=pT: e.copy(out=kendr.t[:, :], in_=pT.t[:, 0:256]), reads=[pT.r], writes=[kendr.r])
            psc = self.psf()
            for h in range(4):
                self.mm_group(psc.t[:, h * 128:(h + 1) * 128], psc.r, [(kinv.t[:, h, :], qdec.t[:, h, :], [kinv.r, qdec.r])])
            S.op("dve", lambda e, psc=psc: e.tensor_tensor(out=scT.t[:, :, :], in0=psc.t[:, :].rearrange("p (h n) -> p h n", h=4),
                                                           in1=self.Uf.t[:, :].unsqueeze(1).to_broadcast([128, 4, 128]), op=ALU.mult),
                 reads=[psc.r, self.Uf.r], writes=[scT.r])
            po = self.psf()
            for h in range(4):
                self.mm_group(po.t[:, h * 128:(h + 1) * 128], po.r, [(vrow.t[:, h * 128:(h + 1) * 128], scT.t[:, h, :], [vrow.r, scT.r]),
                                                                  (Sgb.t[:, h, :], qdec.t[:, h, :], [Sgb.r, qdec.r])])
            pS = self.psf()
            for h in range(4):
                self.mm_group(pS.t[0:64, h * 128:(h + 1) * 128], pS.r, [(kendr.t[:, h * 64:(h + 1) * 64], vrow.t[:, h * 128:(h + 1) * 128], [kendr.r, vrow.r])])
            S.op("dve", lambda e: e.tensor_tensor(out=Sg.t[:, :, :], in0=Sg.t[:, :, :], in1=gsd.t[:, 0:4].unsqueeze(2).to_broadcast([64, 4, 128]), op=ALU.mult),
                 reads=[Sg.r, gsd.r], writes=[Sg.r])
            S.op("dve", lambda e, pS=pS: e.tensor_tensor(out=Sg.t[:, :, :], in0=Sg.t[:, :, :], in1=pS.t[0:64, :].rearrange("p (h n) -> p h n", h=4), op=ALU.add),
                 reads=[Sg.r, pS.r], writes=[Sg.r])
            S.op("act", lambda e: e.copy(out=Sgb.t[:, :, :], in_=Sg.t[:, :, :]), reads=[Sg.r], writes=[Sgb.r])
            S.op("act", lambda e, po=po: e.activation(out=osq.t[:, :], in_=po.t[:, :], func=AF.Square), reads=[po.r], writes=[osq.r])
            pss = self.psf()
            self.mm_group(pss.t[:, :], pss.r, [(self.onesf.t[:, :], osq.t[:, :], [self.onesf.r, osq.r])])
            S.op("dve", lambda e, pss=pss: e.tensor_scalar(out=orst.t[:, :], in0=pss.t[:, :], scalar1=1.0 / 128.0, scalar2=EPS, op0=ALU.mult, op1=ALU.add),
                 reads=[pss.r], writes=[orst.r])
            S.op("act", lambda e: e.activation(out=orst.t[:, :], in_=orst.t[:, :], func=AF.Sqrt), reads=[orst.r], writes=[orst.r])
            S.op("dve", lambda e: e.reciprocal(out=orst.t[:, :], in_=orst.t[:, :]), reads=[orst.r], writes=[orst.r])
            S.op("act", lambda e, po=po: e.copy(out=osq.t[:, :], in_=po.t[:, :]), reads=[po.r, osq.r], writes=[osq.r])
            S.op("dve", lambda e: e.tensor_tensor(out=osq.t[:, :], in0=osq.t[:, :], in1=orst.t[:, :], op=ALU.mult), reads=[osq.r, orst.r], writes=[osq.r])
            S.op("dve", lambda e: e.tensor_tensor(out=gsil.t[:, :, :], in0=gsil.t[:, :, :], in1=gng.t[:, 0:4].unsqueeze(2).to_broadcast([128, 4, 128]), op=ALU.mult),
                 reads=[gsil.r, gng.r], writes=[gsil.r])
            S.op("dve", lambda e: e.tensor_tensor(out=mixedT.t[:, 0:4, :], in0=osq.t[:, :].rearrange("p (h n) -> p h n", h=4), in1=gsil.t[:, :, :], op=ALU.mult),
                 reads=[osq.r, gsil.r], writes=[mixedT.r])
            self.out_proj(mixedT, Wout, ht, dst, blk, nk=4)

    def phase_rwkv(self, l, src, resbuf):
        S = self.S
        d = self.din
        j = l // 2
        CW = 0.6065306597126334
        self.arena_reset()
        Wr = self.arena_alloc("w_rwkv", 8, 1792)
        Wout = self.arena_alloc("wout_b", 4, D)
        W2b = self.arena_alloc("rw_w2", 1, 512)
        A2b = self.arena_alloc("rw_a2", 1, 512)
        G2b = self.arena_alloc("rw_g2", 1, 512)
        self.load_small(d["norm_mix_g"][l:l + 1, :].partition_broadcast(128), self.gb)
        self.load_w(d["ab_w_in"][j], Wr, D, 1792, src_col0=1552)
        self.load_w(d["w_mix_out"][l][512:1024, :], Wout, 512, D)
        self.load_w(d["rw_w2"][j], W2b, 64, 512)
        self.load_w(d["rw_a2"][j], A2b, 64, 512)
        self.load_w(d["rw_g2"][j], G2b, 128, 512)
        rvec = self.abuf("rw_vec", [64, 8, 8], F32)
        mus = self.abuf("rw_mus", [128, 4], F32)
        rrow = self.abuf("rw_row", [128, 1024], F32)
        self.load_small(d["rw_vec"][j], rvec)
        self.load_small(d["rw_mu_s"][j], mus)
        self.load_small(d["rw_row"][j], rrow)
        vb = lambda i: rvec.t[:, i, :].unsqueeze(2).to_broadcast([64, 8, 128])
        xnT = self.abuf("xnT_r", [128, 8, 128])
        mixedT = self.abuf("mixedT_r", [128, 4, 128])
        K3 = lambda name: self.abuf(name, [64, 8, 128], F32)
        Pr = self.abuf("Pr", [64, 8, 129], F32); Pk = self.abuf("Pk", [64, 8, 129], F32); Pv = self.abuf("Pv", [64, 8, 129], F32)
        Pl = self.abuf("Pl", [128, 3, 129], F32)
        for P in (Pr, Pk, Pv, Pl):
            S.op("pool", lambda e, P=P: e.memset(P.t[:, :, :], 0.0), writes=[P.r])
        R = K3("R"); Kt = K3("K"); KK = K3("KK"); AS = K3("AS"); LW = K3("LW"); CS = K3("CS"); E = K3("E"); T1 = K3("T1")
        BI = K3("BI"); KI = K3("KI")
        AR = self.abuf("AR", [64, 8, 256], F32)
        ltmp = self.abuf("ltmp", [128, 128], F32)
        lor = self.abuf("lor", [128, 3, 128])
        sd = self.abuf("rsd", [64, 8], F32)
        Vrow = self.abuf("Vrow", [128, 512], F32); BEr = self.abuf("BEr", [128, 512], F32); KEr = self.abuf("KEr", [128, 512], F32)
        X = self.abuf("X", [128, 8, 128], F32)
        AZ = self.abuf("AZ", [128, 4, 128], F32)
        Zr = self.abuf("Zr", [128, 512], F32)
        Y = self.abuf("Y", [128, 8, 64], F32); Yc = self.abuf("Yc", [128, 8, 64], F32)
        GR = self.abuf("GR", [128, 512], F32)
        yb = self.abuf("yb", [128, 512])
        st8 = self.abuf("st8", [128, 32], F32)
        ST = self.abuf("ST", [128, 8, 64], F32)
        Mall = self.abuf("Mall", [128, 4, 512], F32)
        ch = [self.abuf("ch%d" % i, [128, 4, 128], F32) for i in range(6)]
        MASK4 = self.abuf("MASK4", [128, 512], F32)
        S.op("pool", lambda e: e.memset(ST.t[:, :, :], 0.0), writes=[ST.r])
        S.op("pool", lambda e: e.tensor_copy(out=ST.t[64:128, :, :], in_=self.identf.t[64:128, 64:128].unsqueeze(1).to_broadcast([64, 8, 64])),
             reads=[self.identf.r, ST.r], writes=[ST.r])
        for q, msk in enumerate((self.Us, self.Uf, self.Us, self.Uf)):
            S.op("pool", lambda e, q=q, msk=msk: e.tensor_copy(out=MASK4.t[:, q * 128:(q + 1) * 128], in_=msk.t[:, :]), reads=[msk.r, MASK4.r], writes=[MASK4.r])
        onesK = self.onesf.t[0:64, 0:64]
        v3 = lambda t: t.t[:, :, :]

        for blk in range(self.NB):
            htA = self.new_ht()
            self.load_h(src, blk, htA)
            self.norm_T(htA.t[:, :], htA.r, self.gb, xnT, 0)
            ht = self.new_ht()
            self.load_h(resbuf, blk, ht)
            xw = [Wr.r, xnT.r]
            for (c0, P) in ((0, Pr), (576, Pk), (1088, Pv)):
                for hh in range(2):
                    pp = self.psf()
                    for hl in range(4):
                        h = hh * 4 + hl
                        self.mm_group(pp.t[0:64, hl * 128:(hl + 1) * 128], pp.r, [(Wr.t[:, kc, c0 + h * 64:c0 + (h + 1) * 64], xnT.t[:, kc, :], xw) for kc in range(8)])
                    S.op("act", lambda e, pp=pp, P=P, hh=hh: e.copy(out=P.t[:, hh * 4:(hh + 1) * 4, 1:129], in_=pp.t[0:64, :].rearrange("p (h n) -> p h n", h=4)),
                         reads=[pp.r], writes=[P.r])
            pl = self.psf()
            self.mm_group(pl.t[0:64, 0:128], pl.r, [(Wr.t[:, kc, 512:576], xnT.t[:, kc, :], xw) for kc in range(8)])
            self.mm_group(pl.t[0:64, 128:256], pl.r, [(Wr.t[:, kc, 1600:1664], xnT.t[:, kc, :], xw) for kc in range(8)])
            self.mm_group(pl.t[:, 256:384], pl.r, [(Wr.t[:, kc, 1664:1792], xnT.t[:, kc, :], xw) for kc in range(8)])
            S.op("act", lambda e, pl=pl: e.copy(out=Pl.t[0:64, 0:2, 1:129], in_=pl.t[0:64, 0:256].rearrange("p (h n) -> p h n", h=2)), reads=[pl.r], writes=[Pl.r])
            S.op("act", lambda e, pl=pl: e.copy(out=Pl.t[:, 2, 1:129], in_=pl.t[:, 256:384]), reads=[pl.r, Pl.r], writes=[Pl.r])
            for (P, i, out) in ((Pr, 5, R), (Pk, 6, Kt), (Pv, 7, E)):
                S.op("dve", lambda e, P=P: e.tensor_tensor(out=T1.t[:, :, :], in0=P.t[:, :, 0:128], in1=P.t[:, :, 1:129], op=ALU.subtract), reads=[P.r], writes=[T1.r])
                S.op("dve", lambda e, i=i: e.tensor_tensor(out=T1.t[:, :, :], in0=T1.t[:, :, :], in1=vb(i), op=ALU.mult), reads=[T1.r, rvec.r], writes=[T1.r])
                S.op("dve", lambda e, P=P, out=out: e.tensor_tensor(out=out.t[:, :, :], in0=T1.t[:, :, :], in1=P.t[:, :, 1:129], op=ALU.add), reads=[T1.r, P.r], writes=[out.r])
                S.op("act", lambda e, P=P: e.copy(out=P.t[:, :, 0:1], in_=P.t[:, :, 128:129]), reads=[P.r], writes=[P.r])
            pt = self.psf()
            for h in range(8):
                S.op("pe", lambda e, h=h, pt=pt: e.transpose(pt.t[:, h * 64:(h + 1) * 64], E.t[:, h, :], self.identf.t[0:64, 0:64]),
                     reads=[E.r, self.identf.r], writes=[pt.r], signal=(h == 7))
            S.op("act", lambda e, pt=pt: e.copy(out=Vrow.t[:, :], in_=pt.t[:, :]), reads=[pt.r], writes=[Vrow.r])
            for i, (rows, fn) in enumerate(((64, AF.Tanh), (64, AF.Identity), (128, AF.Sigmoid))):
                S.op("dve", lambda e, i=i, rows=rows: e.tensor_tensor(out=ltmp.t[0:rows, :], in0=Pl.t[0:rows, i, 0:128], in1=Pl.t[0:rows, i, 1:129], op=ALU.subtract),
                     reads=[Pl.r], writes=[ltmp.r])
                S.op("dve", lambda e, i=i, rows=rows: e.scalar_tensor_tensor(out=ltmp.t[0:rows, :], in0=ltmp.t[0:rows, :], scalar=mus.t[0:rows, i:i + 1],
                                                                         in1=Pl.t[0:rows, i, 1:129], op0=ALU.mult, op1=ALU.add),
                     reads=[ltmp.r, mus.r, Pl.r], writes=[ltmp.r])
                S.op("act", lambda e, i=i, rows=rows, fn=fn: e.activation(out=lor.t[0:rows, i, :], in_=ltmp.t[0:rows, :], func=fn), reads=[ltmp.r], writes=[lor.r])
            S.op("act", lambda e: e.copy(out=Pl.t[:, :, 0:1], in_=Pl.t[:, :, 128:129]), reads=[Pl.r], writes=[Pl.r])
            for (Wl, li, vi, OUT) in ((W2b, 0, 0, LW), (A2b, 1, 1, AS)):
                for hh in range(2):
                    pp = self.psf()
                    for hl in range(4):
                        h = hh * 4 + hl
                        self.mm_group(pp.t[0:64, hl * 128:(hl + 1) * 128], pp.r, [(Wl.t[0:64, 0, h * 64:(h + 1) * 64], lor.t[0:64, li, :], [Wl.r, lor.r])])
                    S.op("dve", lambda e, pp=pp, hh=hh, vi=vi, OUT=OUT: e.tensor_tensor(
                        out=OUT.t[:, hh * 4:(hh + 1) * 4, :], in0=pp.t[0:64, :].rearrange("p (h n) -> p h n", h=4),
                        in1=rvec.t[:, vi, hh * 4:(hh + 1) * 4].unsqueeze(2).to_broadcast([64, 4, 128]), op=ALU.add), reads=[pp.r, rvec.r], writes=[OUT.r])
                S.op("act", lambda e, OUT=OUT: e.activation(out=OUT.t[:, :, :], in_=OUT.t[:, :, :], func=AF.Sigmoid), reads=[OUT.r], writes=[OUT.r])
            pgr = self.psf()
            self.mm_group(pgr.t[:, :], pgr.r, [(lor.t[:, 2, :], G2b.t[:, 0, :], [lor.r, G2b.r])])
            S.op("act", lambda e, pgr=pgr: e.copy(out=GR.t[:, :], in_=pgr.t[:, :]), reads=[pgr.r], writes=[GR.r])
            for h in range(8):
                S.op("dve", lambda e, h=h: e.tensor_tensor_scan(out=CS.t[:, h, :], data0=self.onesf.t[0:64, :], data1=LW.t[:, h, :], initial=0.0,
                                                                op0=ALU.mult, op1=ALU.add), reads=[LW.r, self.onesf.r], writes=[CS.r])
            S.op("dve", lambda e: e.tensor_tensor(out=v3(KK), in0=v3(Kt), in1=vb(2), op=ALU.mult), reads=[Kt.r, rvec.r], writes=[KK.r])
            S.op("dve", lambda e: e.tensor_tensor(out=v3(T1), in0=v3(KK), in1=v3(KK), op=ALU.mult), reads=[KK.r], writes=[T1.r])
            for hh in range(2):
                pp = self.psf()
                self.mm_group(pp.t[0:64, :], pp.r, [(onesK, T1.t[:, hh * 4:(hh + 1) * 4, :], [self.onesf.r, T1.r])])
                S.op("act", lambda e, pp=pp, hh=hh: e.activation(out=E.t[:, hh * 4:(hh + 1) * 4, :], in_=pp.t[0:64, :].rearrange("p (h n) -> p h n", h=4), func=AF.Sqrt),
                     reads=[pp.r], writes=[E.r])
            S.op("dve", lambda e: e.tensor_scalar(out=v3(E), in0=v3(E), scalar1=1e-6, scalar2=None, op0=ALU.max), reads=[E.r], writes=[E.r])
            S.op("dve", lambda e: e.reciprocal(out=v3(E), in_=v3(E)), reads=[E.r], writes=[E.r])
            S.op("dve", lambda e: e.tensor_tensor(out=v3(KK), in0=v3(KK), in1=v3(E), op=ALU.mult), reads=[KK.r, E.r], writes=[KK.r])
            S.op("dve", lambda e: e.tensor_scalar(out=v3(T1), in0=v3(AS), scalar1=-1.0, scalar2=None, op0=ALU.add), reads=[AS.r], writes=[T1.r])
            S.op("dve", lambda e: e.tensor_tensor(out=v3(T1), in0=v3(T1), in1=vb(3), op=ALU.mult), reads=[T1.r, rvec.r], writes=[T1.r])
            S.op("dve", lambda e: e.tensor_scalar(out=v3(T1), in0=v3(T1), scalar1=1.0, scalar2=None, op0=ALU.add), reads=[T1.r], writes=[T1.r])
            S.op("dve", lambda e: e.tensor_tensor(out=v3(Kt), in0=v3(Kt), in1=v3(T1), op=ALU.mult), reads=[Kt.r, T1.r], writes=[Kt.r])
            S.op("dve", lambda e: e.tensor_tensor(out=v3(T1), in0=v3(R), in1=v3(Kt), op=ALU.mult), reads=[R.r, Kt.r], writes=[T1.r])
            S.op("dve", lambda e: e.tensor_tensor(out=v3(T1), in0=v3(T1), in1=vb(4), op=ALU.mult), reads=[T1.r, rvec.r], writes=[T1.r])
            pb = self.psf()
            for h in range(8):
                self.mm_group(pb.t[:, h * 2:h * 2 + 2], pb.r, [(T1.t[:, h, :], self.onesf.t[0:64, 0:2], [T1.r, self.onesf.r])])
            S.op("act", lambda e, pb=pb: e.copy(out=st8.t[:, 16:32], in_=pb.t[:, 0:16]), reads=[pb.r], writes=[st8.r])
            S.op("dve", lambda e: e.tensor_tensor(out=v3(AS), in0=v3(AS), in1=v3(KK), op=ALU.mult), reads=[AS.r, KK.r], writes=[AS.r])
            S.op("act", lambda e: e.activation(out=v3(E), in_=v3(CS), func=AF.Exp, scale=-CW), reads=[CS.r], writes=[E.r])
            S.op("dve", lambda e: e.tensor_tensor(out=AR.t[:, :, 128:256], in0=v3(R), in1=v3(E), op=ALU.mult), reads=[R.r, E.r], writes=[AR.r])
            S.op("dve", lambda e: e.tensor_tensor(out=v3(T1), in0=v3(CS), in1=v3(LW), op=ALU.subtract), reads=[CS.r, LW.r], writes=[T1.r])
            S.op("act", lambda e: e.activation(out=v3(E), in_=v3(T1), func=AF.Exp, scale=-CW), reads=[T1.r], writes=[E.r])
            S.op("dve", lambda e: e.scalar_tensor_tensor(out=AR.t[:, :, 0:128], in0=v3(KK), scalar=-1.0, in1=v3(E), op0=ALU.mult, op1=ALU.mult),
                 reads=[KK.r, E.r], writes=[AR.r])
            S.op("act", lambda e: e.activation(out=v3(E), in_=v3(CS), func=AF.Exp, scale=CW), reads=[CS.r], writes=[E.r])
            S.op("dve", lambda e: e.tensor_tensor(out=v3(BI), in0=v3(AS), in1=v3(E), op=ALU.mult), reads=[AS.r, E.r], writes=[BI.r])
            S.op("dve", lambda e: e.tensor_tensor(out=v3(KI), in0=v3(Kt), in1=v3(E), op=ALU.mult), reads=[Kt.r, E.r], writes=[KI.r])
            S.op("act", lambda e: e.activation(out=sd.t[:, :], in_=CS.t[:, :, 127], func=AF.Exp, scale=-CW), reads=[CS.r], writes=[sd.r])
            S.op("dve", lambda e: e.tensor_tensor(out=v3(T1), in0=v3(CS), in1=CS.t[:, :, 127:128].to_broadcast([64, 8, 128]), op=ALU.subtract), reads=[CS.r], writes=[T1.r])
            S.op("act", lambda e: e.activation(out=v3(E), in_=v3(T1), func=AF.Exp, scale=CW), reads=[T1.r], writes=[E.r])
            S.op("dve", lambda e: e.tensor_tensor(out=v3(AS), in0=v3(AS), in1=v3(E), op=ALU.mult), reads=[AS.r, E.r], writes=[AS.r])
            S.op("dve", lambda e: e.tensor_tensor(out=v3(Kt), in0=v3(Kt), in1=v3(E), op=ALU.mult), reads=[Kt.r, E.r], writes=[Kt.r])
            for (srcap, srcr, dstap, dstt) in ((lambda h: AR.t[:, h, 0:128], AR.r, X.t[:, :, 0:64], X),
                                              (lambda h: AS.t[:, h, :], AS.r, BEr.t[:, :].rearrange("p (h n) -> p h n", h=8), BEr),
                                              (lambda h: Kt.t[:, h, :], Kt.r, KEr.t[:, :].rearrange("p (h n) -> p h n", h=8), KEr)):
                pt = self.psf()
                for h in range(8):
                    S.op("pe", lambda e, h=h, pt=pt, srcap=srcap: e.transpose(pt.t[:, h * 64:(h + 1) * 64], srcap(h), self.identf.t[0:64, 0:64]),
                         reads=[srcr, self.identf.r], writes=[pt.r], signal=(h == 7))
                S.op("act", lambda e, pt=pt, dstap=dstap: e.copy(out=dstap, in_=pt.t[:, :].rearrange("p (h n) -> p h n", h=8)), reads=[pt.r], writes=[dstt.r])
            for hh in range(2):
                for hl in range(4):
                    h = hh * 4 + hl
                    pM = self.psf()
                    self.mm_group(pM.t[:, 0:256], pM.r, [(BI.t[:, h, :], AR.t[:, h, :], [BI.r, AR.r])])
                    self.mm_group(pM.t[:, 256:512], pM.r, [(KI.t[:, h, :], AR.t[:, h, :], [KI.r, AR.r])])
                    S.op("dve", lambda e, pM=pM, hl=hl: e.tensor_tensor(out=Mall.t[:, hl, :], in0=pM.t[:, :], in1=MASK4.t[:, :], op=ALU.mult),
                         reads=[pM.r, MASK4.r], writes=[Mall.r])
                pA = self.psf()
                for hl in range(4):
                    h = hh * 4 + hl
                    self.mm_group(pA.t[:, hl * 128:(hl + 1) * 128], pA.r, [(AR.t[:, h, 0:128], BI.t[:, h, :], [AR.r, BI.r])])
                A0 = ch[0]
                S.op("dve", lambda e, pA=pA: e.tensor_tensor(out=A0.t[:, :, :], in0=pA.t[:, :].rearrange("p (h n) -> p h n", h=4),
                                                             in1=self.Ls.t[:, :].unsqueeze(1).to_broadcast([128, 4, 128]), op=ALU.mult), reads=[pA.r, self.Ls.r], writes=[A0.r])
                Ap, Apt = A0, T_(Mall.t[:, :, 0:128], Mall.r)
                Q = ch[1]
                S.op("dve", lambda e, Q=Q: e.tensor_tensor(out=Q.t[:, :, :], in0=Mall.t[:, :, 0:128], in1=self.identf.t[:, :].unsqueeze(1).to_broadcast([128, 4, 128]), op=ALU.add),
                     reads=[Mall.r, self.identf.r], writes=[Q.r])
                free = [ch[2], ch[3], ch[4], ch[5]]
                for step in range(6):
                    An = free.pop(0)
                    pN = self.psf()
                    for hl in range(4):
                        self.mm_group(pN.t[:, hl * 128:(hl + 1) * 128], pN.r, [(Apt.t[:, hl, :], Ap.t[:, hl, :], [Apt.r, Ap.r])])
                    S.op("act", lambda e, pN=pN, An=An: e.copy(out=An.t[:, :, :], in_=pN.t[:, :].rearrange("p (h n) -> p h n", h=4)), reads=[pN.r], writes=[An.r])
                    Ant = None
                    if step < 5:
                        Ant = free.pop(0)
                        pNt = self.psf()
                        for hl in range(4):
                            self.mm_group(pNt.t[:, hl * 128:(hl + 1) * 128], pNt.r, [(Ap.t[:, hl, :], Apt.t[:, hl, :], [Apt.r, Ap.r])])
                        S.op("act", lambda e, pNt=pNt, Ant=Ant: e.copy(out=Ant.t[:, :, :], in_=pNt.t[:, :].rearrange("p (h n) -> p h n", h=4)), reads=[pNt.r], writes=[Ant.r])
                    pQ = self.psf()
                    for hl in range(4):
                        self.mm_group(pQ.t[:, hl * 128:(hl + 1) * 128], pQ.r, [(An.t[:, hl, :], Q.t[:, hl, :], [An.r, Q.r])])
                    Qn = free.pop(0)
                    S.op("dve", lambda e, pQ=pQ, Q=Q, Qn=Qn: e.tensor_tensor(out=Qn.t[:, :, :], in0=Q.t[:, :, :], in1=pQ.t[:, :].rearrange("p (h n) -> p h n", h=4), op=ALU.add),
                         reads=[pQ.r, Q.r], writes=[Qn.r])
                    free.append(Ap)
                    if Apt.r is not Mall.r:
                        free.append(Apt)
                    free.append(Q)
                    Ap, Apt, Q = An, Ant, Qn
                pK = self.psf()
                for hl in range(4):
                    h = hh * 4 + hl
                    self.mm_group(pK.t[:, hl * 64:(hl + 1) * 64], pK.r, [(Mall.t[:, hl, 256:384], Vrow.t[:, h * 64:(h + 1) * 64], [Mall.r, Vrow.r])])
                S.op("act", lambda e, pK=pK, hh=hh: e.copy(out=X.t[:, hh * 4:(hh + 1) * 4, 64:128], in_=pK.t[:, 0:256].rearrange("p (h n) -> p h n", h=4)),
                     reads=[pK.r], writes=[X.r])
                pZ = self.psf()
                for hl in range(4):
                    h = hh * 4 + hl
                    self.mm_group(pZ.t[:, hl * 128:(hl + 1) * 128], pZ.r, [(X.t[:, h, :], Q.t[:, hl, :], [X.r, Q.r])])
                S.op("act", lambda e, pZ=pZ: e.copy(out=AZ.t[:, :, :], in_=pZ.t[:, :].rearrange("p (h n) -> p h n", h=4)), reads=[pZ.r], writes=[AZ.r])
                pZr = self.psf()
                for hl in range(4):
                    h = hh * 4 + hl
                    self.mm_group(pZr.t[:, hl * 64:(hl + 1) * 64], pZr.r, [(AZ.t[:, hl, :], ST.t[:, h, :], [AZ.r, ST.r])])
                S.op("dve", lambda e, pZr=pZr, hh=hh: e.tensor_copy(out=Zr.t[:, hh * 256:(hh + 1) * 256], in_=pZr.t[:, 0:256]), reads=[pZr.r], writes=[Zr.r])
                pY = self.psf()
                for hl in range(4):
                    h = hh * 4 + hl
                    hs = slice(h * 64, (h + 1) * 64)
                    self.mm_group(pY.t[:, hl * 64:(hl + 1) * 64], pY.r, [(AR.t[:, h, 128:256], ST.t[0:64, h, :], [AR.r, ST.r]),
                                                                      (Mall.t[:, hl, 128:256], Zr.t[:, hs], [Mall.r, Zr.r]),
                                                                      (Mall.t[:, hl, 384:512], Vrow.t[:, hs], [Mall.r, Vrow.r])])
                S.op("act", lambda e, pY=pY, hh=hh: e.copy(out=Y.t[:, hh * 4:(hh + 1) * 4, :], in_=pY.t[:, 0:256].rearrange("p (h n) -> p h n", h=4)),
                     reads=[pY.r], writes=[Y.r])
                pD = self.psf()
                for hl in range(4):
                    h = hh * 4 + hl
                    hs = slice(h * 64, (h + 1) * 64)
                    self.mm_group(pD.t[0:64, hl * 64:(hl + 1) * 64], pD.r, [(BEr.t[:, hs], Zr.t[:, hs], [BEr.r, Zr.r]), (KEr.t[:, hs], Vrow.t[:, hs], [KEr.r, Vrow.r])])
                S.op("dve", lambda e, hh=hh: e.tensor_tensor(out=ST.t[0:64, hh * 4:(hh + 1) * 4, :], in0=ST.t[0:64, hh * 4:(hh + 1) * 4, :],
                                                             in1=sd.t[:, hh * 4:(hh + 1) * 4].unsqueeze(2).to_broadcast([64, 4, 64]), op=ALU.mult), reads=[ST.r, sd.r], writes=[ST.r])
                S.op("dve", lambda e, hh=hh, pD=pD: e.tensor_tensor(out=ST.t[0:64, hh * 4:(hh + 1) * 4, :], in0=ST.t[0:64, hh * 4:(hh + 1) * 4, :],
                                                                    in1=pD.t[0:64, 0:256].rearrange("p (h n) -> p h n", h=4), op=ALU.add), reads=[ST.r, pD.r], writes=[ST.r])
            b8 = lambda c0: st8.t[:, c0:c0 + 8].unsqueeze(2).to_broadcast([128, 8, 64])
            S.op("dve", lambda e: e.tensor_reduce(out=st8.t[:, 0:8], in_=Y.t[:, :, :], axis=AX.X, op=ALU.add), reads=[Y.r], writes=[st8.r])
            S.op("dve", lambda e: e.tensor_scalar(out=st8.t[:, 0:8], in0=st8.t[:, 0:8], scalar1=1.0 / 64.0, scalar2=None, op0=ALU.mult), reads=[st8.r], writes=[st8.r])
            S.op("dve", lambda e: e.tensor_tensor(out=Yc.t[:, :, :], in0=Y.t[:, :, :], in1=b8(0), op=ALU.subtract), reads=[Y.r, st8.r], writes=[Yc.r])
            S.op("dve", lambda e: e.tensor_tensor(out=Y.t[:, :, :], in0=Yc.t[:, :, :], in1=Yc.t[:, :, :], op=ALU.mult), reads=[Yc.r], writes=[Y.r])
            S.op("dve", lambda e: e.tensor_reduce(out=st8.t[:, 8:16], in_=Y.t[:, :, :], axis=AX.X, op=ALU.add), reads=[Y.r], writes=[st8.r])
            S.op("dve", lambda e: e.tensor_scalar(out=st8.t[:, 8:16], in0=st8.t[:, 8:16], scalar1=1.0 / 64.0, scalar2=64e-5, op0=ALU.mult, op1=ALU.add), reads=[st8.r], writes=[st8.r])
            S.op("act", lambda e: e.activation(out=st8.t[:, 8:16], in_=st8.t[:, 8:16], func=AF.Sqrt), reads=[st8.r], writes=[st8.r])
            S.op("dve", lambda e: e.reciprocal(out=st8.t[:, 8:16], in_=st8.t[:, 8:16]), reads=[st8.r], writes=[st8.r])
            S.op("dve", lambda e: e.tensor_tensor(out=Yc.t[:, :, :], in0=Yc.t[:, :, :], in1=b8(8), op=ALU.mult), reads=[Yc.r, st8.r], writes=[Yc.r])
            Yc2 = Yc.t[:, :, :].rearrange("p h n -> p (h n)")
            S.op("dve", lambda e: e.tensor_tensor(out=Yc2, in0=Yc2, in1=rrow.t[:, 0:512], op=ALU.mult), reads=[Yc.r, rrow.r], writes=[Yc.r])
            S.op("dve", lambda e: e.tensor_tensor(out=Yc2, in0=Yc2, in1=rrow.t[:, 512:1024], op=ALU.add), reads=[Yc.r, rrow.r], writes=[Yc.r])
            S.op("dve", lambda e: e.tensor_tensor(out=Y.t[:, :, :], in0=Vrow.t[:, :].rearrange("p (h n) -> p h n", h=8),
                                                  in1=st8.t[:, 16:32:2].unsqueeze(2).to_broadcast([128, 8, 64]), op=ALU.mult), reads=[Vrow.r, st8.r], writes=[Y.r])
            S.op("dve", lambda e: e.tensor_tensor(out=Yc.t[:, :, :], in0=Yc.t[:, :, :], in1=Y.t[:, :, :], op=ALU.add), reads=[Yc.r, Y.r], writes=[Yc.r])
            S.op("dve", lambda e: e.tensor_tensor(out=yb.t[:, :], in0=Yc2, in1=GR.t[:, :], op=ALU.mult), reads=[Yc.r, GR.r], writes=[yb.r])
            pT = self.psb()
            for n in range(4):
                S.op("pe", lambda e, n=n, pT=pT: e.transpose(pT.t[:, n * 128:(n + 1) * 128], yb.t[:, n * 128:(n + 1) * 128], self.identb.t[:]),
                     reads=[yb.r, self.identb.r], writes=[pT.r], signal=(n == 3))
            S.op("act", lambda e, pT=pT: e.copy(out=mixedT.t[:, :, :], in_=pT.t[:, 0:512].rearrange("p (k n) -> p k n", k=4)), reads=[pT.r], writes=[mixedT.r])
            self.out_proj(mixedT, Wout, ht, resbuf, blk, nk=4)


def host_layout(inputs, NL):
    f = lambda a: np.ascontiguousarray(np.asarray(a, dtype=np.float32))
    m = {}
    for k in ("norm_mix_g", "norm_xattn_g", "norm_ffn_g", "xattn_wq", "xattn_wkv", "xattn_wo", "ffn_w_up", "ffn_w_down", "w_mix_out"):
        m[k] = f(inputs[k])
    m["mem_norm_g"] = f(inputs["mem_norm_g"]).reshape(1, D)
    m["final_norm_g"] = f(inputs["final_norm_g"]).reshape(1, D)
    cw = f(inputs["ffn_conv_w"])
    cb = f(inputs["ffn_conv_b"])
    pad = NCH_FF * 128 - DFF
    v = np.concatenate([cw, cb[:, None, :]], axis=1)
    v = np.pad(v, ((0, 0), (0, 0), (0, pad)))
    v = v.reshape(NL, 4, NCH_FF, 128).transpose(0, 3, 2, 1)
    m["ffn_vec"] = np.ascontiguousarray(v.reshape(NL, 128, NCH_FF * 4))
    NE = (NL + 1) // 2
    m["ab_w_in"] = f(inputs["ab_w_in"])
    m["gla_wa"] = f(inputs["gla_w_alpha2"])
    m["gla_vec"] = np.ascontiguousarray(f(inputs["gla_b_alpha"]).reshape(NE, 4, 64).transpose(0, 2, 1))
    m["gla_ng"] = np.ascontiguousarray(f(inputs["gla_norm_g"]).reshape(NE, 4, 128).transpose(0, 2, 1))
    m["rw_w2"] = f(inputs["rwkv_w2"]); m["rw_a2"] = f(inputs["rwkv_a2"]); m["rw_g2"] = f(inputs["rwkv_g2"])
    mu = f(inputs["rwkv_mu"])
    kht = lambda a: a.reshape(NE, 8, 64).transpose(0, 2, 1)
    vecs = [inputs["rwkv_w0"], inputs["rwkv_a0"], inputs["rwkv_k_k"], inputs["rwkv_k_a"], inputs["rwkv_r_k"],
            mu[:, 0:512], mu[:, 576:1088], mu[:, 1088:1600]]
    m["rw_vec"] = np.ascontiguousarray(np.stack([kht(f(v)) for v in vecs], axis=2).reshape(NE, 64, 64))
    ms = np.zeros((NE, 128, 4), np.float32)
    ms[:, 0:64, 0] = mu[:, 512:576]; ms[:, 0:64, 1] = mu[:, 1600:1664]; ms[:, :, 2] = mu[:, 1664:1792]
    m["rw_mu_s"] = ms
    row = np.concatenate([f(inputs["rwkv_ln_g"]), f(inputs["rwkv_ln_b"])], axis=1)
    m["rw_row"] = np.ascontiguousarray(np.broadcast_to(row[:, None, :], (NE, 128, 1024)))
    NO = NL // 2
    m["cd_w_in"] = f(inputs["cd_w_in"])
    gw = f(inputs["lru_gate_w"])
    m["lru_gw"] = np.ascontiguousarray(gw.transpose(0, 3, 1, 2, 4).reshape(NO, 128, 1024))
    qkv = f(inputs["mlstm_qkv_w"])
    bd = np.zeros((NO, 3, 4, 128, 128), np.float32)
    q5 = qkv.reshape(NO, 3, 4, 32, 4, 4)
    for b in range(32):
        bd[:, :, :, 4 * b:4 * b + 4, 4 * b:4 * b + 4] = q5[:, :, :, b]
    m["ml_bd"] = np.ascontiguousarray(bd.transpose(0, 3, 1, 2, 4).reshape(NO, 128, 12 * 128))
    fm = lambda a: a.reshape(NO, -1, 4, 128).transpose(0, 3, 2, 1)
    lcw = f(inputs["lru_conv_w"]); lcb = f(inputs["lru_conv_b"]); lgb = f(inputs["lru_gate_b"]); lam = f(inputs["lru_lambda"])
    lv = np.concatenate([lcw, lcb[:, None], lgb, lam[:, None]], axis=1)
    m["lru_vec"] = np.ascontiguousarray(fm(lv).reshape(NO, 128, 32))
    mcw = f(inputs["mlstm_conv_w"]); mcb = f(inputs["mlstm_conv_b"])
    mv = np.concatenate([mcw, mcb[:, None]], axis=1)
    m["ml_vec"] = np.ascontiguousarray(fm(mv).reshape(NO, 128, 20))
    row = np.concatenate([f(inputs["mlstm_b_if"]), f(inputs["mlstm_norm_g"])], axis=1)
    m["ml_row"] = np.ascontiguousarray(np.broadcast_to(row[:, None, :], (NO, 128, 520)))
    return m


_CACHE = {}


def kernel(**inputs):
    x = np.asarray(inputs["x"], dtype=np.float32)
    mem = np.asarray(inputs["mem"], dtype=np.float32)
    B, T, _ = x.shape
    NL = inputs["norm_mix_g"].shape[0]
    key = (T, NL)
    if key not in _CACHE:
        _CACHE[key] = KB(T, NL).build()
    nc = _CACHE[key]
    shared = host_layout(inputs, NL)
    in_maps = []
    for b in range(B):
        mm = dict(shared)
        mm["x"] = np.ascontiguousarray(x[b])
        mm["mem"] = np.ascontiguousarray(mem[b])
        in_maps.append(mm)
    res = run_bass_kernel_spmd(nc, in_maps, core_ids=list(range(B)))
    return np.stack([np.asarray(r["out"], dtype=np.float32) for r in res.results], axis=0)
```

```python
import contextlib
import os
import numpy as np
import concourse.bass as bass
import concourse.mybir as mybir
from concourse.bass_utils import run_bass_kernel_spmd

F32 = mybir.dt.float32
BF16 = mybir.dt.bfloat16
AF = mybir.ActivationFunctionType
ALU = mybir.AluOpType
AX = mybir.AxisListType

ENGS = ("pe", "act", "dve", "pool", "sp")
D = 1024
DFF = 2752
NCH_FF = 22
EPS = 1e-6


class Res:
    __slots__ = ("name", "last_w", "readers", "dsem", "dcount", "psum")

    def __init__(self, name):
        self.name = name
        self.psum = False
        self.last_w = None
        self.readers = {}
        self.dsem = None
        self.dcount = 0


class Sched:
    def __init__(self, nc, stack):
        self.nc = nc
        self.stack = stack
        self.q = {e: [] for e in ENGS}
        self.cnt = {e: 0 for e in ENGS}
        self.pending = {e: False for e in ENGS}
        self.seen = {e: {} for e in ENGS}
        self.semh = {}
        for e in ENGS:
            self.semh["E_" + e] = stack.enter_context(nc.semaphore("sem_" + e))
        self.nsem = len(ENGS)
        self.ninstr = 0
        self.rec = None

    def _dsem(self, res):
        if res.dsem is None:
            key = "D_%d" % self.nsem
            self.semh[key] = self.stack.enter_context(self.nc.semaphore("d%d" % self.nsem))
            self.nsem += 1
            res.dsem = key
        return res.dsem

    def _wait(self, eng, deps):
        seen = self.seen[eng]
        for sk, v in deps.items():
            if seen.get(sk, 0) < v:
                seen[sk] = v
                h = self.semh[sk]
                self.q[eng].append(lambda e, h=h, v=v: e.wait_ge(h, v))
                self.ninstr += 1

    def _deps(self, own, reads, writes):
        deps = {}

        def add(ev):
            if ev[1] > deps.get(ev[0], 0):
                deps[ev[0]] = ev[1]
        for r in reads:
            if r.last_w is not None:
                add(r.last_w)
            if r.psum:
                for sk, v in r.readers.items():
                    if sk != own:
                        add((sk, v))
        skip_own = (own == "E_pe")
        for w in writes:
            if w.last_w is not None and not (skip_own and w.last_w[0] == own):
                add(w.last_w)
            for sk, v in w.readers.items():
                if not (skip_own and sk == own):
                    add((sk, v))
        return deps

    def _record(self, ev, reads, writes):
        for r in reads:
            if r.readers.get(ev[0], 0) < ev[1]:
                r.readers[ev[0]] = ev[1]
        for w in writes:
            w.last_w = ev
            w.readers = {}

    def begin_record(self):
        self.rec = []

    def end_record(self):
        r, self.rec = self.rec, None
        units, cur = [], []
        for it in r:
            cur.append(it)
            if it[0] == "dma" or it[5]:
                units.append(cur)
                cur = []
        assert not cur
        return units

    def emit_interleaved(self, lists):
        idx = [0] * len(lists)
        while True:
            best, bf = -1, 2.0
            for i, l in enumerate(lists):
                if idx[i] < len(l):
                    fr = idx[i] / len(l)
                    if fr < bf:
                        best, bf = i, fr
            if best < 0:
                break
            for it in lists[best][idx[best]]:
                if it[0] == "op":
                    self.op(it[1], it[2], it[3], it[4], it[5])
                else:
                    self.dma(it[1], it[2], it[3], it[4], it[5], it[6], **it[7])
            idx[best] += 1

    def op(self, eng, fn, reads=(), writes=(), signal=True):
        if self.rec is not None:
            self.rec.append(("op", eng, fn, tuple(reads), tuple(writes), signal))
            return None
        own = "E_" + eng
        self._wait(eng, self._deps(own, reads, writes))
        val = self.cnt[eng] + 1
        ev = (own, val)
        if signal:
            self.cnt[eng] = val
            self.pending[eng] = False
            h = self.semh[own]
            self.q[eng].append(lambda e, fn=fn, h=h: fn(e).then_inc(h, 1))
        else:
            self.pending[eng] = True
            self.q[eng].append(lambda e, fn=fn: fn(e))
        self.ninstr += 1
        self._record(ev, reads, writes)
        return ev

    def dma(self, qeng, out_ap, in_ap, sb_res, reads=(), writes=(), **kw):
        if self.rec is not None:
            self.rec.append(("dma", qeng, out_ap, in_ap, sb_res, tuple(reads), tuple(writes), kw))
            return None
        self._wait(qeng, self._deps("__none__", reads, writes))
        sk = self._dsem(sb_res)
        sb_res.dcount += 16
        ev = (sk, sb_res.dcount)
        h = self.semh[sk]
        self.q[qeng].append(
            lambda e, o=out_ap, i=in_ap, h=h, kw=kw: e.dma_start(out=o, in_=i, **kw).then_inc(h, 16))
        self.ninstr += 1
        self._record(ev, reads, writes)
        return ev

    def finish(self, final_events):
        for e in ENGS:
            assert not self.pending[e], "engine %s has unsignalled trailing ops" % e
        allev = {}
        for sk, v in list(final_events) + [("E_" + e, self.cnt[e]) for e in ENGS if self.cnt[e] > 0]:
            if v > allev.get(sk, 0):
                allev[sk] = v
        self._wait("sp", allev)
        nc = self.nc
        with nc.Block() as block:
            @block.tensor
            def _(eng):
                for f in self.q["pe"]:
                    f(eng)

            @block.scalar
            def _(eng):
                for f in self.q["act"]:
                    f(eng)

            @block.vector
            def _(eng):
                for f in self.q["dve"]:
                    f(eng)

            @block.gpsimd
            def _(eng):
                for f in self.q["pool"]:
                    f(eng)

            @block.sync
            def _(eng):
                for f in self.q["sp"]:
                    f(eng)


class T_:
    __slots__ = ("t", "r")

    def __init__(self, t, r):
        self.t = t
        self.r = r

    def __getitem__(self, k):
        return self.t[k]


ARENA_COLS = 84000


class KB:
    def __init__(self, T, NL, plan=None, final_norm=True):
        self.T = T
        self.NL = NL
        self.NB = T // 128
        self.plan = plan
        self.final_norm = final_norm
        self.nc = bass.Bass("TRN2", target_bir_lowering=False)
        self.din = {}
        self.uid = 0

    def inp(self, name, shape):
        self.din[name] = self.nc.dram_tensor(name, list(shape), F32, kind="ExternalInput").ap()
        return self.din[name]

    def sb(self, name, shape, dt=F32):
        t = self.st.enter_context(self.nc.sbuf_tensor(name, list(shape), dt))
        return T_(t, Res(name))

    def ps(self, name, shape, dt=F32):
        t = self.st.enter_context(self.nc.psum_tensor(name, list(shape), dt))
        r = Res(name)
        r.psum = True
        return T_(t, r)

    def psf(self):
        pool = self.psf_sub if getattr(self, "psf_sub", None) else self.psf_pool
        self.psf_i += 1
        return pool[self.psf_i % len(pool)]

    def psb(self):
        p = self.psb_pool[self.psb_i % len(self.psb_pool)]
        self.psb_i += 1
        return p

    def arena_reset(self):
        carry = {}
        for r in self.arena_live:
            evs = dict(r.readers)
            if r.last_w is not None:
                evs[r.last_w[0]] = max(evs.get(r.last_w[0], 0), r.last_w[1])
            for k, v in evs.items():
                if v > carry.get(k, 0):
                    carry[k] = v
        for k, v in self.arena_carry.items():
            if v > carry.get(k, 0):
                carry[k] = v
        self.arena_carry = carry
        self.arena_live = []
        self.arena_off = 0

    def arena_alloc(self, name, kch, ncols):
        n = kch * ncols
        assert self.arena_off + n <= ARENA_COLS, (name, self.arena_off, n)
        ap = self.arena.t[:, self.arena_off:self.arena_off + n].rearrange("p (k n) -> p k n", k=kch)
        self.arena_off += n
        r = Res(name)
        r.readers = dict(self.arena_carry)
        self.arena_live.append(r)
        return T_(ap, r)

    def abuf(self, name, shape, dt=BF16):
        n = int(np.prod(shape[1:]))
        ncols = n if dt == BF16 else 2 * n
        self.arena_off += self.arena_off % 2
        assert self.arena_off + ncols <= ARENA_COLS, (name, self.arena_off, ncols)
        ap = self.arena.t[:, self.arena_off:self.arena_off + ncols]
        self.arena_off += ncols
        if dt != BF16:
            ap = ap.bitcast(dt)
        if len(shape) == 3:
            ap = ap.rearrange("p (k n) -> p k n", k=shape[1])
        elif len(shape) == 4:
            ap = ap.rearrange("p (a b n) -> p a b n", a=shape[1], b=shape[2])
        if shape[0] != 128:
            ap = ap[0:shape[0]]
        r = Res(name)
        r.readers = dict(self.arena_carry)
        self.arena_live.append(r)
        return T_(ap, r)

    def new_ht(self):
        ht = self.ht[self.ht_i % len(self.ht)]
        self.ht_i += 1
        return ht

    def load_w(self, src, dst, K, N, col0=0, scale=None, src_col0=0):
        S = self.S
        nk = (K + 127) // 128
        CB = 1024
        for kc in range(nk):
            rows = min(128, K - kc * 128)
            for c0 in range(0, N, CB):
                cw = min(CB, N - c0)
                stg = self.stg[self.stg_i % len(self.stg)]
                self.stg_i += 1
                S.dma("sp", stg.t[:rows, :cw], src[kc * 128:kc * 128 + rows, src_col0 + c0:src_col0 + c0 + cw],
                      stg.r, writes=[stg.r])
                o = dst.t[:rows, kc, col0 + c0:col0 + c0 + cw]
                eng = ("pool", "dve")[self.cast_i % 2] if self.cast_both else "pool"
                self.cast_i += 1
                if scale is None:
                    S.op(eng, lambda e, o=o, i=stg.t[:rows, :cw]: e.tensor_copy(out=o, in_=i),
                         reads=[stg.r], writes=[dst.r])
                else:
                    sc = scale.t[:rows, c0:c0 + cw]
                    S.op(eng, lambda e, o=o, i=stg.t[:rows, :cw], sc=sc: e.tensor_tensor(out=o, in0=i, in1=sc, op=ALU.mult),
                         reads=[stg.r, scale.r], writes=[dst.r])

    def load_small(self, src_ap, dst, dst_ap=None):
        self.S.dma("sp", dst.t[:] if dst_ap is None else dst_ap, src_ap, dst.r, writes=[dst.r])

    def load_h(self, src, blk, dst):
        self.S.dma("sp", dst.t[:, :], src[blk * 128:(blk + 1) * 128, :], dst.r,
                   reads=[self.hres[id(src)][blk]] if id(src) in self.hres else [], writes=[dst.r])

    def store_h(self, dstd, blk, src, ap=None):
        self.S.dma("sp", dstd[blk * 128:(blk + 1) * 128, :], src.t[:, :] if ap is None else ap, src.r,
                   reads=[src.r], writes=[self.hres[id(dstd)][blk]])

    def rstd_of(self, hap, hres):
        S = self.S
        ss = self.ss[self.ss_i % len(self.ss)]
        self.ss_i += 1
        junk = self.junk
        S.op("act", lambda e: e.activation(out=junk.t[:], in_=hap, func=AF.Square, accum_out=ss.t[:, 0:1]),
             reads=[hres], writes=[junk.r, ss.r])
        S.op("dve", lambda e: e.tensor_scalar(out=ss.t[:, 1:2], in0=ss.t[:, 0:1], scalar1=1.0 / D, scalar2=EPS,
                                              op0=ALU.mult, op1=ALU.add), reads=[ss.r], writes=[ss.r])
        S.op("act", lambda e: e.activation(out=ss.t[:, 2:3], in_=ss.t[:, 1:2], func=AF.Sqrt), reads=[ss.r], writes=[ss.r])
        S.op("dve", lambda e: e.reciprocal(out=ss.t[:, 3:4], in_=ss.t[:, 2:3]), reads=[ss.r], writes=[ss.r])
        return ss

    def norm_T(self, hap, hres, gb, xnT, col0):
        S = self.S
        ss = self.rstd_of(hap, hres)
        xn = self.xn[self.xn_i % len(self.xn)]
        self.xn_i += 1
        S.op("dve", lambda e: e.scalar_tensor_tensor(out=xn.t[:], in0=hap, scalar=ss.t[:, 3:4], in1=gb.t[:],
                                                     op0=ALU.mult, op1=ALU.mult),
             reads=[hres, ss.r, gb.r], writes=[xn.r])
        pT = self.psb()
        for kc in range(8):
            S.op("pe", lambda e, kc=kc: e.transpose(pT.t[:, kc * 128:(kc + 1) * 128], xn.t[:, kc * 128:(kc + 1) * 128],
                                                     self.identb.t[:]),
                 reads=[xn.r, self.identb.r], writes=[pT.r], signal=(kc == 7))
        S.op("act", lambda e: e.copy(out=xnT.t[:, :, col0:col0 + 128],
                                     in_=pT.t[:, 0:1024].rearrange("p (k n) -> p k n", k=8)),
             reads=[pT.r], writes=[xnT.r])
        return xn

    def mm_group(self, out_ap, out_res, pairs):
        n = len(pairs)
        for i, (l, r, rs) in enumerate(pairs):
            self.S.op("pe", lambda e, l=l, r=r, i=i: e.matmul(out_ap, lhsT=l, rhs=r, start=(i == 0), stop=(i == n - 1)),
                      reads=rs, writes=[out_res], signal=(i == n - 1))

    def phase_ffn(self, l, src, dst):
        S = self.S
        d = self.din
        TT = 256
        self.arena_reset()
        Wup = self.arena_alloc("wup", 8, 2 * DFF)
        Wdn = self.arena_alloc("wdn", NCH_FF, D)
        self.load_small(d["ffn_vec"][l], self.fvec)
        self.load_small(d["norm_ffn_g"][l:l + 1, :].partition_broadcast(128), self.gb)
        self.load_w(d["ffn_w_up"][l], Wup, D, 2 * DFF)
        self.load_w(d["ffn_w_down"][l], Wdn, DFF, D)
        xnT = self.abuf("xnT_f", [128, 8, TT + 2])
        actT = self.abuf("actT", [128, NCH_FF, TT])
        self.cv = [self.abuf("cv%d" % i, [128, 2 * TT], F32) for i in range(2)]
        self.cv_i = 0
        S.op("pool", lambda e: e.memset(xnT.t[:, :, 0:2], 0.0), writes=[xnT.r])
        fv = self.fvec
        for t0 in range(0, self.T, TT):
            nb = TT // 128
            hts = [self.new_ht() for b in range(nb)]
            for b in range(nb):
                self.load_h(src, t0 // 128 + b, hts[b])
                self.norm_T(hts[b].t[:, :], hts[b].r, self.gb, xnT, 2 + b * 128)
            for c in range(NCH_FF):
                cs = min(128, DFF - c * 128)
                pu = self.psf()
                pg = self.psf()
                self.mm_group(pu.t[:cs, 0:TT + 2], pu.r,
                              [(Wup.t[:, kc, c * 128:c * 128 + cs], xnT.t[:, kc, 0:TT + 2], [Wup.r, xnT.r]) for kc in range(8)])
                self.mm_group(pg.t[:cs, 0:TT + 2], pg.r,
                              [(Wup.t[:, kc, DFF + c * 128:DFF + c * 128 + cs], xnT.t[:, kc, 0:TT + 2], [Wup.r, xnT.r]) for kc in range(8)])
                cv = self.cv[self.cv_i % len(self.cv)]
                self.cv_i += 1
                w = lambda j, c=c, cs=cs: fv.t[:cs, c * 4 + j:c * 4 + j + 1]
                S.op("dve", lambda e, cs=cs, pu=pu, cv=cv, w=w: e.tensor_scalar(
                    out=cv.t[:cs, 0:TT], in0=pu.t[:cs, 2:TT + 2], scalar1=w(2), scalar2=w(3), op0=ALU.mult, op1=ALU.add),
                    reads=[pu.r, fv.r], writes=[cv.r])
                S.op("dve", lambda e, cs=cs, pu=pu, cv=cv, w=w: e.scalar_tensor_tensor(
                    out=cv.t[:cs, 0:TT], in0=pu.t[:cs, 1:TT + 1], scalar=w(1), in1=cv.t[:cs, 0:TT], op0=ALU.mult, op1=ALU.add),
                    reads=[pu.r, fv.r, cv.r], writes=[cv.r])
                S.op("dve", lambda e, cs=cs, pu=pu, cv=cv, w=w: e.scalar_tensor_tensor(
                    out=cv.t[:cs, 0:TT], in0=pu.t[:cs, 0:TT], scalar=w(0), in1=cv.t[:cs, 0:TT], op0=ALU.mult, op1=ALU.add),
                    reads=[pu.r, fv.r, cv.r], writes=[cv.r])
                S.op("act", lambda e, cs=cs, cv=cv: e.activation(out=cv.t[:cs, TT:2 * TT], in_=cv.t[:cs, 0:TT], func=AF.Silu),
                     reads=[cv.r], writes=[cv.r])
                S.op("dve", lambda e, cs=cs, cv=cv, pg=pg, c=c: e.tensor_tensor(
                    out=actT.t[:cs, c, 0:TT], in0=cv.t[:cs, TT:2 * TT], in1=pg.t[:cs, 2:TT + 2], op=ALU.mult),
                    reads=[cv.r, pg.r], writes=[actT.r])
            S.op("pool", lambda e: e.tensor_copy(out=xnT.t[:, :, 0:2], in_=xnT.t[:, :, TT:TT + 2]), reads=[xnT.r], writes=[xnT.r])
            for b in range(nb):
                for half in range(2):
                    po = self.psf()
                    prs = []
                    for c in range(NCH_FF):
                        cs = min(128, DFF - c * 128)
                        prs.append((actT.t[:cs, c, b * 128:(b + 1) * 128], Wdn.t[:cs, c, half * 512:(half + 1) * 512], [actT.r, Wdn.r]))
                    self.mm_group(po.t[:, :], po.r, prs)
                    ht = hts[b]
                    S.op("dve", lambda e, half=half, po=po, ht=ht: e.tensor_tensor(
                        out=ht.t[:, half * 512:(half + 1) * 512], in0=ht.t[:, half * 512:(half + 1) * 512], in1=po.t[:, :], op=ALU.add),
                        reads=[ht.r, po.r], writes=[ht.r])
                self.store_h(dst, t0 // 128 + b, hts[b])

    def prep_mem(self):
        d = self.din
        self.load_small(d["mem_norm_g"][0:1, :].partition_broadcast(128), self.gb)
        for b in range(2):
            ht = self.new_ht()
            self.S.dma("sp", ht.t[:, :], d["mem"][b * 128:(b + 1) * 128, :], ht.r, writes=[ht.r])
            self.norm_T(ht.t[:, :], ht.r, self.gb, self.memnT, b * 128)

    def phase_xattn(self, l, src, dst):
        S = self.S
        d = self.din
        TT = 512
        self.arena_reset()
        Wq = self.arena_alloc("wq", 8, D)
        Wkv = self.arena_alloc("wkv", 8, 2 * D)
        Wo = self.arena_alloc("wo", 8, D)
        self.load_small(d["norm_xattn_g"][l:l + 1, :].partition_broadcast(128), self.gb)
        self.load_w(d["xattn_wkv"][l], Wkv, D, 2 * D)
        self.load_w(d["xattn_wq"][l], Wq, D, D)
        self.load_w(d["xattn_wo"][l], Wo, D, D)
        memnT = self.memnT
        KT = self.abuf("KT", [128, 8, 256])
        Vr = self.abuf("Vr", [128, 2, D])
        xnT = self.abuf("xnT_x", [128, 8, TT])
        qT = self.abuf("qT", [128, 8, TT])
        pTs = self.abuf("pTs", [128, 8, TT])
        oT = self.abuf("oT", [128, 8, TT])
        self.ex = [self.abuf("ex%d" % i, [128, 512], F32) for i in range(2)]
        self.pb = [self.abuf("pb%d" % i, [128, 512]) for i in range(2)]
        self.ex_i = self.pb_i = 0
        for c in range(8):
            pk = self.psf()
            self.mm_group(pk.t[:, 0:256], pk.r, [(Wkv.t[:, kc, c * 128:(c + 1) * 128], memnT.t[:, kc, :], [Wkv.r, memnT.r]) for kc in range(8)])
            S.op("act", lambda e, c=c, pk=pk: e.copy(out=KT.t[:, c, :], in_=pk.t[:, 0:256]), reads=[pk.r], writes=[KT.r])
        for mb in range(2):
            for half in range(2):
                pv = self.psf()
                self.mm_group(pv.t[:, :], pv.r, [(memnT.t[:, kc, mb * 128:(mb + 1) * 128], Wkv.t[:, kc, D + half * 512:D + (half + 1) * 512],
                                                  [Wkv.r, memnT.r]) for kc in range(8)])
                S.op("act", lambda e, mb=mb, half=half, pv=pv: e.copy(out=Vr.t[:, mb, half * 512:(half + 1) * 512], in_=pv.t[:, :]),
                     reads=[pv.r], writes=[Vr.r])
        for t0 in range(0, self.T, TT):
            nb = min(TT, self.T - t0) // 128
            ntok = nb * 128
            hts = []
            for b in range(nb):
                ht = self.new_ht()
                hts.append(ht)
                self.load_h(src, t0 // 128 + b, ht)
                self.norm_T(ht.t[:, :], ht.r, self.gb, xnT, b * 128)
            for c in range(8):
                pq = self.psf()
                self.mm_group(pq.t[:, 0:ntok], pq.r, [(Wq.t[:, kc, c * 128:(c + 1) * 128], xnT.t[:, kc, 0:ntok], [Wq.r, xnT.r]) for kc in range(8)])
                S.op("act", lambda e, c=c, pq=pq: e.mul(out=qT.t[:, c, 0:ntok], in_=pq.t[:, 0:ntok], mul=1.0 / 16.0),
                     reads=[pq.r], writes=[qT.r])
            for b in range(nb):
                for hp in range(2):
                    psc = self.psf()
                    for hh in range(2):
                        h = hp * 2 + hh
                        self.mm_group(psc.t[:, hh * 256:(hh + 1) * 256], psc.r,
                                      [(qT.t[:, 2 * h + dd, b * 128:(b + 1) * 128], KT.t[:, 2 * h + dd, :], [qT.r, KT.r]) for dd in range(2)])
                    sm = self.smx[self.smx_i % len(self.smx)]
                    self.smx_i += 1
                    p3 = psc.t[:, :].rearrange("p (h m) -> p h m", h=2)
                    S.op("dve", lambda e, sm=sm, p3=p3: e.tensor_reduce(out=sm.t[:, 0:2], in_=p3, axis=AX.X, op=ALU.max),
                         reads=[psc.r], writes=[sm.r])
                    ex = self.ex[self.ex_i % len(self.ex)]
                    self.ex_i += 1
                    e3 = ex.t[:, :].rearrange("p (h m) -> p h m", h=2)
                    S.op("dve", lambda e, sm=sm, p3=p3, e3=e3: e.tensor_tensor(out=e3, in0=p3, in1=sm.t[:, 0:2].unsqueeze(2).to_broadcast([128, 2, 256]),
                                                                           op=ALU.subtract), reads=[psc.r, sm.r], writes=[ex.r])
                    S.op("act", lambda e, ex=ex: e.activation(out=ex.t[:, :], in_=ex.t[:, :], func=AF.Exp), reads=[ex.r], writes=[ex.r])
                    S.op("dve", lambda e, sm=sm, e3=e3: e.tensor_reduce(out=sm.t[:, 2:4], in_=e3, axis=AX.X, op=ALU.add),
                         reads=[ex.r], writes=[sm.r])
                    S.op("dve", lambda e, sm=sm: e.reciprocal(out=sm.t[:, 4:6], in_=sm.t[:, 2:4]), reads=[sm.r], writes=[sm.r])
                    pb = self.pb[self.pb_i % len(self.pb)]
                    self.pb_i += 1
                    S.op("dve", lambda e, sm=sm, e3=e3, pb=pb: e.tensor_tensor(
                        out=pb.t[:, :].rearrange("p (h m) -> p h m", h=2), in0=e3,
                        in1=sm.t[:, 4:6].unsqueeze(2).to_broadcast([128, 2, 256]), op=ALU.mult), reads=[ex.r, sm.r], writes=[pb.r])
                    pT = self.psb()
                    for i in range(4):
                        S.op("pe", lambda e, i=i, pT=pT, pb=pb: e.transpose(pT.t[:, i * 128:(i + 1) * 128], pb.t[:, i * 128:(i + 1) * 128], self.identb.t[:]),
                             reads=[pb.r, self.identb.r], writes=[pT.r], signal=(i == 3))
                    S.op("act", lambda e, pT=pT, hp=hp, b=b: e.copy(out=pTs.t[:, hp * 4:(hp + 1) * 4, b * 128:(b + 1) * 128],
                                                                   in_=pT.t[:, 0:512].rearrange("p (k n) -> p k n", k=4)),
                         reads=[pT.r], writes=[pTs.r])
            for h in range(4):
                for dd in range(2):
                    po = self.psf()
                    self.mm_group(po.t[:, 0:ntok], po.r, [(Vr.t[:, mb, h * 256 + dd * 128:h * 256 + (dd + 1) * 128], pTs.t[:, h * 2 + mb, 0:ntok],
                                                           [Vr.r, pTs.r]) for mb in range(2)])
                    S.op("act", lambda e, h=h, dd=dd, po=po: e.copy(out=oT.t[:, 2 * h + dd, 0:ntok], in_=po.t[:, 0:ntok]), reads=[po.r], writes=[oT.r])
            for b in range(nb):
                ht = hts[b]
                for half in range(2):
                    pw = self.psf()
                    self.mm_group(pw.t[:, :], pw.r, [(oT.t[:, kc, b * 128:(b + 1) * 128], Wo.t[:, kc, half * 512:(half + 1) * 512], [oT.r, Wo.r]) for kc in range(8)])
                    S.op("dve", lambda e, ht=ht, half=half, pw=pw: e.tensor_tensor(
                        out=ht.t[:, half * 512:(half + 1) * 512], in0=ht.t[:, half * 512:(half + 1) * 512], in1=pw.t[:, :], op=ALU.add),
                        reads=[ht.r, pw.r], writes=[ht.r])
                self.store_h(dst, t0 // 128 + b, ht)

    def phase_final(self, src, dst):
        S = self.S
        d = self.din
        self.load_small(d["final_norm_g"][0:1, :].partition_broadcast(128), self.gb)
        for blk in range(self.NB):
            ht = self.new_ht()
            self.load_h(src, blk, ht)
            ss = self.rstd_of(ht.t[:, :], ht.r)
            S.op("dve", lambda e, ht=ht, ss=ss: e.scalar_tensor_tensor(out=ht.t[:, :], in0=ht.t[:, :], scalar=ss.t[:, 3:4], in1=self.gb.t[:],
                                                                 op0=ALU.mult, op1=ALU.mult), reads=[ht.r, ss.r, self.gb.r], writes=[ht.r])
            self.store_h(dst, blk, ht)

    def build(self):
        nc = self.nc
        T, NL = self.T, self.NL
        inp = self.inp
        inp("x", [T, D]); inp("mem", [256, D]); inp("mem_norm_g", [1, D]); inp("final_norm_g", [1, D])
        inp("norm_mix_g", [NL, D]); inp("norm_xattn_g", [NL, D]); inp("norm_ffn_g", [NL, D])
        inp("xattn_wq", [NL, D, D]); inp("xattn_wkv", [NL, D, 2 * D]); inp("xattn_wo", [NL, D, D])
        inp("ffn_w_up", [NL, D, 2 * DFF]); inp("ffn_w_down", [NL, DFF, D]); inp("ffn_vec", [NL, 128, NCH_FF * 4])
        inp("w_mix_out", [NL, D, D])
        self.decl_mixer_inputs()
        out = nc.dram_tensor("out", [T, D], F32, kind="ExternalOutput").ap()
        hbuf = nc.dram_tensor("hbuf", [T, D], F32, kind="Internal").ap()
        hbuf2 = nc.dram_tensor("hbuf2", [T, D], F32, kind="Internal").ap()
        self.hres = {id(hbuf): [Res("h%d" % i) for i in range(self.NB)], id(hbuf2): [Res("g%d" % i) for i in range(self.NB)],
                     id(out): [Res("o%d" % i) for i in range(self.NB)]}
        with contextlib.ExitStack() as st:
            self.st = st
            self.S = S = Sched(nc, st)
            self.psf_pool = [self.ps("psf%d" % i, [128, 512]) for i in range(6)]
            self.psb_pool = [self.ps("psb%d" % i, [128, 1024], BF16) for i in range(2)]
            self.psf_i = self.psb_i = 0
            self.arena = self.sb("arena", [128, ARENA_COLS], BF16)
            self.arena_live, self.arena_carry, self.arena_off = [], {}, 0
            self.stg = [self.sb("stg%d" % i, [128, 1024]) for i in range(2)]
            self.stg_i = self.cast_i = 0
            self.cast_both = False
            self.gb = self.sb("gb", [128, D])
            self.ht = [self.sb("ht%d" % i, [128, D]) for i in range(4)]
            self.ht_i = 0
            self.ss = [self.sb("ss%d" % i, [128, 4]) for i in range(4)]
            self.ss_i = 0
            self.junk = self.sb("junk", [128, D], BF16)
            self.xn = [self.sb("xn%d" % i, [128, D], BF16) for i in range(2)]
            self.xn_i = 0
            self.identb = self.sb("identb", [128, 128], BF16)
            self.identf = self.sb("identf", [128, 128])
            S.op("pool", lambda e: e.memset(self.identf.t[:], 0.0), writes=[self.identf.r])
            S.op("pool", lambda e: e.affine_select(out=self.identf.t[:], in_=self.identf.t[:], pattern=[[-1, 128]], compare_op=ALU.not_equal,
                                                   fill=1.0, base=0, channel_multiplier=1), reads=[self.identf.r], writes=[self.identf.r])
            S.op("pool", lambda e: e.tensor_copy(out=self.identb.t[:], in_=self.identf.t[:]), reads=[self.identf.r], writes=[self.identb.r])
            self.memnT = self.sb("memnT", [128, 8, 256], BF16)
            self.fvec = self.sb("fvec", [128, NCH_FF * 4])
            self.smx = [self.sb("smx%d" % i, [128, 6]) for i in range(2)]
            self.smx_i = 0
            self.alloc_mixer_bufs()

            plan = self.plan
            if plan is None:
                plan = []
                for l in range(NL):
                    plan += [("M", l), ("X", l), ("F", l)]
            if any(p[0] == "X" for p in plan):
                self.prep_mem()
            cur = self.din["x"]
            for (kind, l) in plan:
                x_in = cur is self.din["x"]
                tgt = hbuf if x_in else cur
                if kind == "M" and l % 2 == 0:
                    oth = hbuf2 if cur is hbuf else hbuf
                    self.phase_gla(l, cur, oth)
                    if "norwkv" not in os.environ.get("KDBG", ""):
                        self.phase_rwkv(l, cur, oth)
                    tgt = oth
                elif kind == "M":
                    self.phase_mixer_odd(l, cur, tgt)
                elif kind == "X":
                    self.phase_xattn(l, cur, tgt)
                elif kind == "F":
                    self.phase_ffn(l, cur, tgt)
                cur = tgt
            if self.final_norm:
                self.phase_final(cur, out)
            else:
                for blk in range(self.NB):
                    ht = self.new_ht()
                    self.load_h(cur, blk, ht)
                    self.store_h(out, blk, ht)
            finals = []
            for r in [t.r for t in self.ht]:
                if r.dsem is not None:
                    finals.append((r.dsem, r.dcount))
            S.finish(finals)
            self.ninstr = S.ninstr
        return nc

    def phase_mixer_odd(self, l, src, dst):
        S = self.S
        d = self.din
        j = l // 2
        self.arena_reset()
        Wcd = self.arena_alloc("wcd", 8, 2056)
        Wout = self.arena_alloc("wout", 8, D)
        Wg = self.arena_alloc("wg", 1, 1024)
        Wbd = self.arena_alloc("wbd", 1, 1536)
        self.load_small(d["norm_mix_g"][l:l + 1, :].partition_broadcast(128), self.gb)
        self.load_w(d["cd_w_in"][j], Wcd, D, 2056)
        self.load_w(d["w_mix_out"][l], Wout, D, D)
        self.load_w(d["lru_gw"][j], Wg, 128, 8 * 128)
        self.load_w(d["ml_bd"][j], Wbd, 128, 12 * 128)
        Wg2 = T_(Wg.t[:, 0, :], Wg.r)
        Wbd2 = T_(Wbd.t[:, 0, :], Wbd.r)
        lv = self.abuf("lru_vec", [128, 4, 8], F32)
        mv = self.abuf("ml_vec", [128, 4, 5], F32)
        mrow = self.abuf("ml_row", [128, 8 + 512], F32)
        self.load_small(d["lru_vec"][j], lv)
        self.load_small(d["ml_vec"][j], mv)
        self.load_small(d["ml_row"][j], mrow)
        c8 = self.abuf("c8", [128, 4, 2], F32)
        tmp4 = self.abuf("tmp4", [128, 4], F32)
        S.op("act", lambda e: e.activation(out=tmp4.t[:, :], in_=lv.t[:, :, 7], func=AF.Exp, scale=-1.0), reads=[lv.r], writes=[tmp4.r])
        S.op("act", lambda e: e.activation(out=tmp4.t[:, :], in_=tmp4.t[:, :], func=AF.Ln, bias=1.0, scale=1.0), reads=[tmp4.r], writes=[tmp4.r])
        S.op("dve", lambda e: e.tensor_scalar(out=c8.t[:, :, 0], in0=tmp4.t[:, :], scalar1=-8.0, scalar2=None, op0=ALU.mult), reads=[tmp4.r], writes=[c8.r])
        S.op("dve", lambda e: e.tensor_scalar(out=c8.t[:, :, 1], in0=tmp4.t[:, :], scalar1=-16.0, scalar2=None, op0=ALU.mult), reads=[tmp4.r], writes=[c8.r])
        HL = 4
        xnT = self.abuf("xnT_m", [128, 8, HL + 128])
        mixedT = self.abuf("mixedT", [128, 8, 128])
        lcar = self.abuf("lcar", [128, 4], F32)
        Cst = self.abuf("Cst", [128, 4, 132], F32)
        Cb = self.abuf("Cb", [128, 4, 132])
        S.op("pool", lambda e: e.memset(xnT.t[:, :, 0:HL], 0.0), writes=[xnT.r])
        S.op("pool", lambda e: e.memset(lcar.t[:, :], 0.0), writes=[lcar.r])
        S.op("pool", lambda e: e.memset(Cst.t[:, :, :], 0.0), writes=[Cst.r])
        S.op("pool", lambda e: e.memset(Cb.t[:, :, :], 0.0), writes=[Cb.r])
        NT = 4
        f32t = [self.abuf("mo_f%d" % i, [128, 128], F32) for i in range(12)]
        bft = [self.abuf("mo_b%d" % i, [128, 128]) for i in range(10)]
        vaug = [self.abuf("vaug%d" % i, [128, 136]) for i in range(2)]
        nrow = [self.abuf("nrow%d" % i, [128, 132], F32) for i in range(2)]
        yrow = self.abuf("yrow", [128, 512])
        gt = self.abuf("gt", [128, 24], F32)
        opre = self.abuf("opre", [128, 512], F32)
        cnt = [0, 0, 0, 0]

        pools = {"F": f32t[0:8], "B": bft[0:2]}

        def F():
            cnt[0] += 1
            return pools["F"][cnt[0] % len(pools["F"])]

        def B():
            cnt[1] += 1
            return pools["B"][cnt[1] % len(pools["B"])]

        for blk in range(self.NB):
            ht = self.new_ht()
            self.load_h(src, blk, ht)
            self.norm_T(ht.t[:, :], ht.r, self.gb, xnT, HL)
            xw = xnT.t[:, :, 0:HL + 128]
            xc_ = xnT.t[:, :, HL:HL + 128]
            DBG = os.environ.get("KDBG", "")
            if "nolru" in DBG:
                S.op("pool", lambda e: e.memset(mixedT.t[:, 0:4, :], 0.0), writes=[mixedT.r])
            recA = []
            for n in range(0 if "nolru" in DBG else 4):
                S.begin_record()
                self.psf_sub = self.psf_pool[0:3]
                pools["F"], pools["B"] = f32t[0:8], bft[0:2]
                px = self.psf()
                self.mm_group(px.t[:, 0:HL + 128], px.r, [(Wcd.t[:, kc, n * 128:(n + 1) * 128], xnT.t[:, kc, 0:HL + 128], [Wcd.r, xnT.r]) for kc in range(8)])
                xc = F()
                w = lambda q, n=n: lv.t[:, n, q:q + 1]
                S.op("dve", lambda e, px=px, xc=xc, w=w: e.tensor_scalar(out=xc.t[:, :], in0=px.t[:, HL:HL + 128], scalar1=w(3), scalar2=w(4), op0=ALU.mult, op1=ALU.add),
                     reads=[px.r, lv.r], writes=[xc.r])
                for q in range(3):
                    S.op("dve", lambda e, px=px, xc=xc, w=w, q=q: e.scalar_tensor_tensor(out=xc.t[:, :], in0=px.t[:, HL - 3 + q:HL - 3 + q + 128], scalar=w(q), in1=xc.t[:, :],
                                                                                       op0=ALU.mult, op1=ALU.add), reads=[px.r, lv.r, xc.r], writes=[xc.r])
                xcb = B()
                S.op("act", lambda e, xc=xc, xcb=xcb: e.copy(out=xcb.t[:, :], in_=xc.t[:, :]), reads=[xc.r], writes=[xcb.r])
                pr = self.psf()
                self.mm_group(pr.t[:, 0:128], pr.r, [(Wg2.t[:, (0 * 4 + n) * 128:(0 * 4 + n + 1) * 128], xcb.t[:, :], [Wg.r, xcb.r])])
                self.mm_group(pr.t[:, 128:256], pr.r, [(Wg2.t[:, (1 * 4 + n) * 128:(1 * 4 + n + 1) * 128], xcb.t[:, :], [Wg.r, xcb.r])])
                rg = F(); ig = F()
                S.op("act", lambda e, pr=pr, rg=rg, w=w: e.activation(out=rg.t[:, :], in_=pr.t[:, 0:128], func=AF.Sigmoid, bias=w(5), scale=1.0),
                     reads=[pr.r, lv.r], writes=[rg.r])
                S.op("act", lambda e, pr=pr, ig=ig, w=w: e.activation(out=ig.t[:, :], in_=pr.t[:, 128:256], func=AF.Sigmoid, bias=w(6), scale=1.0),
                     reads=[pr.r, lv.r], writes=[ig.r])
                a = F(); a2 = F()
                S.op("act", lambda e, rg=rg, a=a, n=n: e.activation(out=a.t[:, :], in_=rg.t[:, :], func=AF.Exp, scale=c8.t[:, n, 0:1]), reads=[rg.r, c8.r], writes=[a.r])
                S.op("act", lambda e, rg=rg, a2=a2, n=n: e.activation(out=a2.t[:, :], in_=rg.t[:, :], func=AF.Exp, scale=c8.t[:, n, 1:2]), reads=[rg.r, c8.r], writes=[a2.r])
                S.op("dve", lambda e, a2=a2: e.tensor_scalar(out=a2.t[:, :], in0=a2.t[:, :], scalar1=-1.0, scalar2=1.0, op0=ALU.mult, op1=ALU.add), reads=[a2.r], writes=[a2.r])
                S.op("act", lambda e, a2=a2: e.activation(out=a2.t[:, :], in_=a2.t[:, :], func=AF.Sqrt), reads=[a2.r], writes=[a2.r])
                S.op("dve", lambda e, ig=ig, xc=xc: e.tensor_tensor(out=ig.t[:, :], in0=ig.t[:, :], in1=xc.t[:, :], op=ALU.mult), reads=[ig.r, xc.r], writes=[ig.r])
                S.op("dve", lambda e, ig=ig, a2=a2: e.tensor_tensor(out=ig.t[:, :], in0=ig.t[:, :], in1=a2.t[:, :], op=ALU.mult), reads=[ig.r, a2.r], writes=[ig.r])
                hl = F()
                S.op("dve", lambda e, hl=hl, a=a, ig=ig, n=n: e.tensor_tensor_scan(out=hl.t[:, :], data0=a.t[:, :], data1=ig.t[:, :], initial=lcar.t[:, n:n + 1],
                                                                                 op0=ALU.mult, op1=ALU.add), reads=[a.r, ig.r, lcar.r], writes=[hl.r])
                S.op("pool", lambda e, hl=hl, n=n: e.tensor_copy(out=lcar.t[:, n:n + 1], in_=hl.t[:, 127:128]), reads=[hl.r], writes=[lcar.r])
                pg = self.psf()
                self.mm_group(pg.t[:, 0:128], pg.r, [(Wcd.t[:, kc, 512 + n * 128:512 + (n + 1) * 128], xnT.t[:, kc, HL:HL + 128], [Wcd.r, xnT.r]) for kc in range(8)])
                g = F(); g3 = F()
                S.op("act", lambda e, pg=pg, g=g: e.copy(out=g.t[:, :], in_=pg.t[:, 0:128]), reads=[pg.r], writes=[g.r])
                S.op("dve", lambda e, g=g, g3=g3: e.tensor_tensor(out=g3.t[:, :], in0=g.t[:, :], in1=g.t[:, :], op=ALU.mult), reads=[g.r], writes=[g3.r])
                S.op("dve", lambda e, g3=g3: e.tensor_scalar(out=g3.t[:, :], in0=g3.t[:, :], scalar1=0.044715, scalar2=1.0, op0=ALU.mult, op1=ALU.add), reads=[g3.r], writes=[g3.r])
                S.op("dve", lambda e, g=g, g3=g3: e.tensor_tensor(out=g3.t[:, :], in0=g3.t[:, :], in1=g.t[:, :], op=ALU.mult), reads=[g.r, g3.r], writes=[g3.r])
                S.op("act", lambda e, g3=g3: e.activation(out=g3.t[:, :], in_=g3.t[:, :], func=AF.Sigmoid, scale=1.5957691216), reads=[g3.r], writes=[g3.r])
                S.op("dve", lambda e, g=g, g3=g3: e.tensor_tensor(out=g3.t[:, :], in0=g3.t[:, :], in1=g.t[:, :], op=ALU.mult), reads=[g.r, g3.r], writes=[g3.r])
                S.op("dve", lambda e, hl=hl, g3=g3, n=n: e.tensor_tensor(out=mixedT.t[:, n, :], in0=g3.t[:, :], in1=hl.t[:, :], op=ALU.mult), reads=[hl.r, g3.r], writes=[mixedT.r])
                recA.append(S.end_record())
                self.psf_sub = None
            pools["F"], pools["B"] = f32t[8:12], bft[2:10]
            if "nomlstm" in DBG:
                for rA in recA:
                    S.emit_interleaved([rA])
                S.op("pool", lambda e: e.memset(mixedT.t[:, 4:8, :], 0.0), writes=[mixedT.r])
                S.op("pool", lambda e: e.tensor_copy(out=xnT.t[:, :, 0:HL], in_=xnT.t[:, :, 128:128 + HL]), reads=[xnT.r], writes=[xnT.r])
                self.out_proj(mixedT, Wout, ht, dst, blk)
                continue
            pgt = self.psf()
            self.mm_group(pgt.t[:, 0:8], pgt.r, [(xnT.t[:, kc, HL:HL + 128], Wcd.t[:, kc, 2048:2056], [Wcd.r, xnT.r]) for kc in range(8)])
            S.op("dve", lambda e, pgt=pgt: e.tensor_tensor(out=gt.t[:, 0:8], in0=pgt.t[:, 0:8], in1=mrow.t[:, 0:8], op=ALU.add), reads=[pgt.r, mrow.r], writes=[gt.r])
            S.op("act", lambda e: e.activation(out=gt.t[:, 8:12], in_=gt.t[:, 4:8], func=AF.Exp, scale=-1.0), reads=[gt.r], writes=[gt.r])
            S.op("act", lambda e: e.activation(out=gt.t[:, 8:12], in_=gt.t[:, 8:12], func=AF.Ln, bias=1.0, scale=1.0), reads=[gt.r], writes=[gt.r])
            pc = self.psf()
            self.mm_group(pc.t[:, 0:4], pc.r, [(self.Uf.t[:, :], gt.t[:, 8:12], [self.Uf.r, gt.r])])
            self.mm_group(pc.t[:, 4:8], pc.r, [(self.onesf.t[:, :], gt.t[:, 8:12], [self.onesf.r, gt.r])])
            S.op("dve", lambda e, pc=pc: e.tensor_tensor(out=gt.t[:, 12:16], in0=pc.t[:, 0:4], in1=gt.t[:, 0:4], op=ALU.add), reads=[pc.r, gt.r], writes=[gt.r])
            S.op("act", lambda e: e.activation(out=gt.t[:, 12:16], in_=gt.t[:, 12:16], func=AF.Exp), reads=[gt.r], writes=[gt.r])
            S.op("act", lambda e, pc=pc: e.activation(out=gt.t[:, 16:24], in_=pc.t[:, 0:8], func=AF.Exp, scale=-1.0), reads=[pc.r, gt.r], writes=[gt.r])
            po = self.psf()
            self.mm_group(po.t[:, :], po.r, [(xnT.t[:, kc, HL:HL + 128], Wcd.t[:, kc, 1536:2048], [Wcd.r, xnT.r]) for kc in range(8)])
            S.op("act", lambda e, po=po: e.activation(out=opre.t[:, :], in_=po.t[:, :], func=AF.Sigmoid), reads=[po.r], writes=[opre.r])
            S.op("dve", lambda e: e.tensor_tensor(out=opre.t[:, :], in0=opre.t[:, :], in1=mrow.t[:, 8:520], op=ALU.mult), reads=[opre.r, mrow.r], writes=[opre.r])
            for n in range(4):
                S.begin_record()
                self.psf_sub = self.psf_pool[3:6]
                px = self.psf()
                self.mm_group(px.t[:, 0:HL + 128], px.r, [(Wcd.t[:, kc, 1024 + n * 128:1024 + (n + 1) * 128], xnT.t[:, kc, 0:HL + 128], [Wcd.r, xnT.r]) for kc in range(8)])
                xc = F()
                w = lambda q, n=n: mv.t[:, n, q:q + 1]
                S.op("dve", lambda e, px=px, xc=xc, w=w: e.tensor_scalar(out=xc.t[:, :], in0=px.t[:, HL:HL + 128], scalar1=w(3), scalar2=w(4), op0=ALU.mult, op1=ALU.add),
                     reads=[px.r, mv.r], writes=[xc.r])
                for q in range(3):
                    S.op("dve", lambda e, px=px, xc=xc, w=w, q=q: e.scalar_tensor_tensor(out=xc.t[:, :], in0=px.t[:, HL - 3 + q:HL - 3 + q + 128], scalar=w(q), in1=xc.t[:, :],
                                                                                       op0=ALU.mult, op1=ALU.add), reads=[px.r, mv.r, xc.r], writes=[xc.r])
                xcm = B(); mxb = B()
                S.op("act", lambda e, xc=xc, xcm=xcm: e.activation(out=xcm.t[:, :], in_=xc.t[:, :], func=AF.Silu), reads=[xc.r], writes=[xcm.r])
                S.op("act", lambda e, px=px, mxb=mxb: e.copy(out=mxb.t[:, :], in_=px.t[:, HL:HL + 128]), reads=[px.r], writes=[mxb.r])
                wq = Wbd2.t[:, (0 * 4 + n) * 128:(0 * 4 + n + 1) * 128]
                wk = Wbd2.t[:, (1 * 4 + n) * 128:(1 * 4 + n + 1) * 128]
                wv = Wbd2.t[:, (2 * 4 + n) * 128:(2 * 4 + n + 1) * 128]
                pqk = self.psf()
                self.mm_group(pqk.t[:, 0:128], pqk.r, [(wq, xcm.t[:, :], [Wbd.r, xcm.r])])
                self.mm_group(pqk.t[:, 128:256], pqk.r, [(wk, xcm.t[:, :], [Wbd.r, xcm.r])])
                self.mm_group(pqk.t[:, 256:384], pqk.r, [(xcm.t[:, :], wk, [Wbd.r, xcm.r])])
                self.mm_group(pqk.t[:, 384:512], pqk.r, [(mxb.t[:, :], wv, [Wbd.r, mxb.r])])
                qT = B(); kT = B(); kr = B()
                S.op("act", lambda e, pqk=pqk, qT=qT: e.copy(out=qT.t[:, :], in_=pqk.t[:, 0:128]), reads=[pqk.r], writes=[qT.r])
                S.op("act", lambda e, pqk=pqk, kT=kT: e.mul(out=kT.t[:, :], in_=pqk.t[:, 128:256], mul=128.0 ** -0.5), reads=[pqk.r], writes=[kT.r])
                S.op("act", lambda e, pqk=pqk, kr=kr: e.mul(out=kr.t[:, :], in_=pqk.t[:, 256:384], mul=128.0 ** -0.5), reads=[pqk.r], writes=[kr.r])
                va = vaug[cnt[2] % 2]; cnt[2] += 1
                S.op("act", lambda e, pqk=pqk, va=va, n=n: e.activation(out=va.t[:, 0:128], in_=pqk.t[:, 384:512], func=AF.Identity, bias=0.0, scale=gt.t[:, 12 + n:13 + n]),
                     reads=[pqk.r, gt.r], writes=[va.r])
                S.op("dve", lambda e, va=va, n=n: e.tensor_tensor(out=va.t[:, 128:136], in0=self.onesf.t[:, 0:8], in1=gt.t[:, 12 + n:13 + n].to_broadcast([128, 8]), op=ALU.mult),
                     reads=[gt.r, self.onesf.r], writes=[va.r])
                psT = self.psf()
                self.mm_group(psT.t[:, 0:128], psT.r, [(kT.t[:, :], qT.t[:, :], [kT.r, qT.r])])
                scT = B()
                S.op("dve", lambda e, psT=psT, scT=scT: e.tensor_tensor(out=scT.t[:, :], in0=psT.t[:, 0:128], in1=self.Uf.t[:, :], op=ALU.mult),
                     reads=[psT.r, self.Uf.r], writes=[scT.r])
                pn = self.psf()
                self.mm_group(pn.t[:, 0:130], pn.r, [(scT.t[:, :], va.t[:, 0:130], [scT.r, va.r]), (qT.t[:, :], Cb.t[:, n, 0:130], [qT.r, Cb.r])])
                nr = nrow[cnt[3] % 2]; cnt[3] += 1
                S.op("dve", lambda e, pn=pn, nr=nr, n=n: e.tensor_tensor(out=nr.t[:, 0:129], in0=pn.t[:, 0:129], in1=gt.t[:, 16 + n:17 + n].to_broadcast([128, 129]), op=ALU.mult),
                     reads=[pn.r, gt.r], writes=[nr.r])
                S.op("act", lambda e, nr=nr: e.activation(out=nr.t[:, 129:130], in_=nr.t[:, 128:129], func=AF.Abs), reads=[nr.r], writes=[nr.r])
                S.op("dve", lambda e, nr=nr: e.tensor_scalar(out=nr.t[:, 129:130], in0=nr.t[:, 129:130], scalar1=1.0, scalar2=None, op0=ALU.max),
                     reads=[nr.r], writes=[nr.r])
                S.op("dve", lambda e, nr=nr: e.reciprocal(out=nr.t[:, 130:131], in_=nr.t[:, 129:130]), reads=[nr.r], writes=[nr.r])
                hh = F()
                S.op("dve", lambda e, nr=nr, hh=hh: e.tensor_tensor(out=hh.t[:, :], in0=nr.t[:, 0:128], in1=nr.t[:, 130:131].to_broadcast([128, 128]), op=ALU.mult),
                     reads=[nr.r], writes=[hh.r])
                jk = F()
                S.op("act", lambda e, hh=hh, jk=jk, nr=nr: e.activation(out=jk.t[:, :], in_=hh.t[:, :], func=AF.Square, accum_out=nr.t[:, 131:132]),
                     reads=[hh.r], writes=[jk.r, nr.r])
                S.op("dve", lambda e, nr=nr: e.tensor_scalar(out=nr.t[:, 129:130], in0=nr.t[:, 131:132], scalar1=1.0 / 128.0, scalar2=EPS, op0=ALU.mult, op1=ALU.add),
                     reads=[nr.r], writes=[nr.r])
                S.op("act", lambda e, nr=nr: e.activation(out=nr.t[:, 129:130], in_=nr.t[:, 129:130], func=AF.Sqrt), reads=[nr.r], writes=[nr.r])
                S.op("dve", lambda e, nr=nr: e.reciprocal(out=nr.t[:, 130:131], in_=nr.t[:, 129:130]), reads=[nr.r], writes=[nr.r])
                S.op("dve", lambda e, nr=nr, hh=hh, n=n: e.scalar_tensor_tensor(out=yrow.t[:, n * 128:(n + 1) * 128], in0=hh.t[:, :], scalar=nr.t[:, 130:131],
                                                                             in1=opre.t[:, n * 128:(n + 1) * 128], op0=ALU.mult, op1=ALU.mult),
                     reads=[nr.r, hh.r, opre.r], writes=[yrow.r])
                pC = self.psf()
                self.mm_group(pC.t[:, 0:130], pC.r, [(kr.t[:, :], va.t[:, 0:130], [kr.r, va.r])])
                S.op("dve", lambda e, pC=pC, n=n: e.tensor_tensor(out=Cst.t[:, n, 0:129], in0=Cst.t[:, n, 0:129], in1=pC.t[:, 0:129], op=ALU.add),
                     reads=[pC.r, Cst.r], writes=[Cst.r])
                S.op("dve", lambda e, n=n: e.tensor_tensor(out=Cst.t[:, n, 0:129], in0=Cst.t[:, n, 0:129], in1=gt.t[:, 20 + n:21 + n].to_broadcast([128, 129]), op=ALU.mult),
                     reads=[Cst.r, gt.r], writes=[Cst.r])
                S.op("act", lambda e, n=n: e.copy(out=Cb.t[:, n, 0:130], in_=Cst.t[:, n, 0:130]), reads=[Cst.r], writes=[Cb.r])
                recB = S.end_record()
                self.psf_sub = None
                S.emit_interleaved([recA[n], recB] if n < len(recA) else [recB])
            pT = self.psb()
            for n in range(4):
                S.op("pe", lambda e, n=n, pT=pT: e.transpose(pT.t[:, n * 128:(n + 1) * 128], yrow.t[:, n * 128:(n + 1) * 128], self.identb.t[:]),
                     reads=[yrow.r, self.identb.r], writes=[pT.r], signal=(n == 3))
            S.op("act", lambda e, pT=pT: e.copy(out=mixedT.t[:, 4:8, :], in_=pT.t[:, 0:512].rearrange("p (k n) -> p k n", k=4)), reads=[pT.r], writes=[mixedT.r])
            S.op("pool", lambda e: e.tensor_copy(out=xnT.t[:, :, 0:HL], in_=xnT.t[:, :, 128:128 + HL]), reads=[xnT.r], writes=[xnT.r])
            self.out_proj(mixedT, Wout, ht, dst, blk)

    def out_proj(self, mixedT, Wout, ht, dst, blk, nk=8):
        S = self.S
        for half in range(2):
            pw = self.psf()
            self.mm_group(pw.t[:, :], pw.r, [(mixedT.t[:, kc, :], Wout.t[:, kc, half * 512:(half + 1) * 512], [mixedT.r, Wout.r]) for kc in range(nk)])
            S.op("dve", lambda e, ht=ht, half=half, pw=pw: e.tensor_tensor(
                out=ht.t[:, half * 512:(half + 1) * 512], in0=ht.t[:, half * 512:(half + 1) * 512], in1=pw.t[:, :], op=ALU.add),
                reads=[ht.r, pw.r], writes=[ht.r])
        self.store_h(dst, blk, ht)

    def phase_mixer(self, l, src, dst):
        if l % 2 == 1:
            self.phase_mixer_odd(l, src, dst)
        else:
            self.phase_mixer_even(l, src, dst)

    def decl_mixer_inputs(self):
        NE, NO = (self.NL + 1) // 2, self.NL // 2
        inp = self.inp
        inp("cd_w_in", [NO, D, 2056]); inp("lru_gw", [NO, 128, 1024]); inp("ml_bd", [NO, 128, 12 * 128])
        inp("lru_vec", [NO, 128, 32]); inp("ml_vec", [NO, 128, 20]); inp("ml_row", [NO, 128, 520])
        self.decl_even_inputs(NE)

    def alloc_mixer_bufs(self):
        S = self.S
        self.Uf = self.sb("Uf", [128, 128])
        self.onesf = self.sb("onesf", [128, 128])
        S.op("pool", lambda e: e.memset(self.onesf.t[:], 1.0), writes=[self.onesf.r])
        S.op("pool", lambda e: e.memset(self.Uf.t[:], 1.0), writes=[self.Uf.r])
        S.op("pool", lambda e: e.affine_select(out=self.Uf.t[:], in_=self.Uf.t[:], pattern=[[1, 128]], compare_op=ALU.is_ge,
                                               fill=0.0, base=0, channel_multiplier=-1), reads=[self.Uf.r], writes=[self.Uf.r])
        self.alloc_even_bufs()

    def decl_even_inputs(self, NE):
        inp = self.inp
        inp("ab_w_in", [NE, D, 3344]); inp("gla_wa", [NE, 16, 256]); inp("gla_vec", [NE, 64, 4]); inp("gla_ng", [NE, 128, 4])
        inp("rw_w2", [NE, 64, 512]); inp("rw_a2", [NE, 64, 512]); inp("rw_g2", [NE, 128, 512])
        inp("rw_vec", [NE, 64, 64]); inp("rw_mu_s", [NE, 128, 4]); inp("rw_row", [NE, 128, 1024])

    def alloc_even_bufs(self):
        S = self.S
        self.Us = self.sb("Us", [128, 128])
        self.Ls = self.sb("Ls", [128, 128])
        S.op("pool", lambda e: e.memset(self.Us.t[:], 1.0), writes=[self.Us.r])
        S.op("pool", lambda e: e.affine_select(out=self.Us.t[:], in_=self.Us.t[:], pattern=[[1, 128]], compare_op=ALU.is_gt,
                                               fill=0.0, base=0, channel_multiplier=-1), reads=[self.Us.r], writes=[self.Us.r])
        S.op("pool", lambda e: e.memset(self.Ls.t[:], 1.0), writes=[self.Ls.r])
        S.op("pool", lambda e: e.affine_select(out=self.Ls.t[:], in_=self.Ls.t[:], pattern=[[-1, 128]], compare_op=ALU.is_gt,
                                               fill=0.0, base=0, channel_multiplier=1), reads=[self.Ls.r], writes=[self.Ls.r])

    def phase_gla(self, l, src, dst):
        S = self.S
        d = self.din
        j = l // 2
        DBG = os.environ.get("KDBG", "")
        self.arena_reset()
        Wg = self.arena_alloc("w_gla", 8, 1552)
        Wout = self.arena_alloc("wout_a", 4, D)
        Wa = self.arena_alloc("w_alpha", 1, 256)
        self.load_small(d["norm_mix_g"][l:l + 1, :].partition_broadcast(128), self.gb)
        self.load_w(d["ab_w_in"][j], Wg, D, 1552)
        self.load_w(d["w_mix_out"][l][0:512, :], Wout, 512, D)
        self.load_w(d["gla_wa"][j], Wa, 16, 256)
        gvec = self.abuf("gla_vec", [64, 4], F32)
        gng = self.abuf("gla_ng", [128, 4], F32)
        self.load_small(d["gla_vec"][j], gvec)
        self.load_small(d["gla_ng"][j], gng)
        xnT = self.abuf("xnT_e", [128, 8, 128])
        mixedT = self.abuf("mixedT", [128, 4, 128])
        Sg = self.abuf("Sg", [64, 4, 128], F32)
        Sgb = self.abuf("Sgb", [64, 4, 128])
        S.op("pool", lambda e: e.memset(Sg.t[:, :, :], 0.0), writes=[Sg.r])
        S.op("pool", lambda e: e.memset(Sgb.t[:, :, :], 0.0), writes=[Sgb.r])
        gq = self.abuf("gq", [64, 4, 128], F32)
        gk = self.abuf("gk", [64, 4, 128], F32)
        gx = self.abuf("gx", [64, 4, 128], F32)
        gcs = self.abuf("gcs", [64, 4, 128], F32)
        ge = self.abuf("ge", [64, 4, 128], F32)
        gsd = self.abuf("gsd", [64, 8], F32)
        qdec = self.abuf("qdec", [64, 4, 128])
        kinv = self.abuf("kinv", [64, 4, 128])
        kend = self.abuf("kend", [64, 4, 128])
        kendr = self.abuf("kendr", [128, 256])
        vrow = self.abuf("g_vrow", [128, 512])
        alrT = self.abuf("alrT", [16, 128])
        scT = self.abuf("g_scT", [128, 4, 128])
        osq = self.abuf("g_osq", [128, 512], F32)
        orst = self.abuf("g_orst", [128, 512], F32)
        gsil = self.abuf("g_sil", [128, 4, 128], F32)
        for blk in range(self.NB):
            ht = self.new_ht()
            self.load_h(src, blk, ht)
            xn_cur = self.norm_T(ht.t[:, :], ht.r, self.gb, xnT, 0)
            xw = [Wg.r, xnT.r]
            pq = self.psf(); pk = self.psf()
            for h in range(4):
                self.mm_group(pq.t[0:64, h * 128:(h + 1) * 128], pq.r, [(Wg.t[:, kc, h * 64:(h + 1) * 64], xnT.t[:, kc, :], xw) for kc in range(8)])
            for h in range(4):
                self.mm_group(pk.t[0:64, h * 128:(h + 1) * 128], pk.r, [(Wg.t[:, kc, 256 + h * 64:256 + (h + 1) * 64], xnT.t[:, kc, :], xw) for kc in range(8)])
            S.op("act", lambda e, pq=pq: e.copy(out=gq.t[:, :, :], in_=pq.t[0:64, :].rearrange("p (h n) -> p h n", h=4)), reads=[pq.r], writes=[gq.r])
            S.op("act", lambda e, pk=pk: e.copy(out=gk.t[:, :, :], in_=pk.t[0:64, :].rearrange("p (h n) -> p h n", h=4)), reads=[pk.r], writes=[gk.r])
            pv = self.psf()
            self.mm_group(pv.t[:, :], pv.r, [(xnT.t[:, kc, :], Wg.t[:, kc, 512:1024], xw) for kc in range(8)])
            S.op("act", lambda e, pv=pv: e.copy(out=vrow.t[:, :], in_=pv.t[:, :]), reads=[pv.r], writes=[vrow.r])
            pg = self.psf()
            for h in range(4):
                self.mm_group(pg.t[:, h * 128:(h + 1) * 128], pg.r, [(Wg.t[:, kc, 1024 + h * 128:1024 + (h + 1) * 128], xnT.t[:, kc, :], xw) for kc in range(8)])
            S.op("act", lambda e, pg=pg: e.activation(out=gsil.t[:, :, :], in_=pg.t[:, :].rearrange("p (h n) -> p h n", h=4), func=AF.Silu), reads=[pg.r], writes=[gsil.r])
            pa = self.psf()
            self.mm_group(pa.t[0:16, 0:128], pa.r, [(Wg.t[:, kc, 1536:1552], xnT.t[:, kc, :], xw) for kc in range(8)])
            S.op("act", lambda e, pa=pa: e.copy(out=alrT.t[:, :], in_=pa.t[0:16, 0:128]), reads=[pa.r], writes=[alrT.r])
            px = self.psf()
            for h in range(4):
                self.mm_group(px.t[0:64, h * 128:(h + 1) * 128], px.r, [(Wa.t[0:16, 0, h * 64:(h + 1) * 64], alrT.t[:, :], [Wa.r, alrT.r])])
            S.op("dve", lambda e, px=px: e.tensor_tensor(out=gx.t[:, :, :], in0=px.t[0:64, :].rearrange("p (h n) -> p h n", h=4),
                                                         in1=gvec.t[:, 0:4].unsqueeze(2).to_broadcast([64, 4, 128]), op=ALU.add), reads=[px.r, gvec.r], writes=[gx.r])
            S.op("act", lambda e: e.activation(out=gx.t[:, :, :], in_=gx.t[:, :, :], func=AF.Exp, scale=-1.0), reads=[gx.r], writes=[gx.r])
            S.op("act", lambda e: e.activation(out=gx.t[:, :, :], in_=gx.t[:, :, :], func=AF.Ln, bias=1.0, scale=1.0), reads=[gx.r], writes=[gx.r])
            for h in range(4):
                S.op("dve", lambda e, h=h: e.tensor_tensor_scan(out=gcs.t[:, h, :], data0=self.onesf.t[0:64, :], data1=gx.t[:, h, :], initial=0.0,
                                                                op0=ALU.mult, op1=ALU.add), reads=[gx.r, self.onesf.r], writes=[gcs.r])
            S.op("act", lambda e: e.activation(out=ge.t[:, :, :], in_=gcs.t[:, :, :], func=AF.Exp, scale=-1.0 / 16.0), reads=[gcs.r], writes=[ge.r])
            S.op("dve", lambda e: e.scalar_tensor_tensor(out=qdec.t[:, :, :], in0=gq.t[:, :, :], scalar=0.125, in1=ge.t[:, :, :], op0=ALU.mult, op1=ALU.mult),
                 reads=[gq.r, ge.r], writes=[qdec.r])
            S.op("act", lambda e: e.copy(out=gsd.t[:, 0:4], in_=ge.t[:, :, 127]), reads=[ge.r], writes=[gsd.r])
            S.op("act", lambda e: e.activation(out=ge.t[:, :, :], in_=gcs.t[:, :, :], func=AF.Exp, scale=1.0 / 16.0), reads=[gcs.r], writes=[ge.r])
            S.op("dve", lambda e: e.tensor_tensor(out=kinv.t[:, :, :], in0=gk.t[:, :, :], in1=ge.t[:, :, :], op=ALU.mult), reads=[gk.r, ge.r], writes=[kinv.r])
            S.op("dve", lambda e: e.tensor_tensor(out=gx.t[:, :, :], in0=gcs.t[:, :, :], in1=gcs.t[:, :, 127:128].to_broadcast([64, 4, 128]), op=ALU.subtract),
                 reads=[gcs.r], writes=[gx.r])
            S.op("act", lambda e: e.activation(out=ge.t[:, :, :], in_=gx.t[:, :, :], func=AF.Exp, scale=1.0 / 16.0), reads=[gx.r], writes=[ge.r])
            S.op("dve", lambda e: e.tensor_tensor(out=kend.t[:, :, :], in0=gk.t[:, :, :], in1=ge.t[:, :, :], op=ALU.mult), reads=[gk.r, ge.r], writes=[kend.r])
            pT = self.psb()
            for h in range(4):
                S.op("pe", lambda e, h=h, pT=pT: e.transpose(pT.t[:, h * 64:(h + 1) * 64], kend.t[:, h, :], self.identb.t[0:64, 0:64]),
                     reads=[kend.r, self.identb.r], writes=[pT.r], signal=(h == 3))
            S.op("act", lambda e, pT=pT: e.copy(out=kendr.t[:, :], in_=pT.t[:, 0:256]), reads=[pT.r], writes=[kendr.r])
            psc = self.psf()
            for h in range(4):
                self.mm_group(psc.t[:, h * 128:(h + 1) * 128], psc.r, [(kinv.t[:, h, :], qdec.t[:, h, :], [kinv.r, qdec.r])])
            S.op("dve", lambda e, psc=psc: e.tensor_tensor(out=scT.t[:, :, :], in0=psc.t[:, :].rearrange("p (h n) -> p h n", h=4),
                                                           in1=self.Uf.t[:, :].unsqueeze(1).to_broadcast([128, 4, 128]), op=ALU.mult),
                 reads=[psc.r, self.Uf.r], writes=[scT.r])
            po = self.psf()
            for h in range(4):
                self.mm_group(po.t[:, h * 128:(h + 1) * 128], po.r, [(vrow.t[:, h * 128:(h + 1) * 128], scT.t[:, h, :], [vrow.r, scT.r]),
                                                                  (Sgb.t[:, h, :], qdec.t[:, h, :], [Sgb.r, qdec.r])])
            pS = self.psf()
            for h in range(4):
                self.mm_group(pS.t[0:64, h * 128:(h + 1) * 128], pS.r, [(kendr.t[:, h * 64:(h + 1) * 64], vrow.t[:, h * 128:(h + 1) * 128], [kendr.r, vrow.r])])
            S.op("dve", lambda e: e.tensor_tensor(out=Sg.t[:, :, :], in0=Sg.t[:, :, :], in1=gsd.t[:, 0:4].unsqueeze(2).to_broadcast([64, 4, 128]), op=ALU.mult),
                 reads=[Sg.r, gsd.r], writes=[Sg.r])
            S.op("dve", lambda e, pS=pS: e.tensor_tensor(out=Sg.t[:, :, :], in0=Sg.t[:, :, :], in1=pS.t[0:64, :].rearrange("p (h n) -> p h n", h=4), op=ALU.add),
                 reads=[Sg.r, pS.r], writes=[Sg.r])
            S.op("act", lambda e: e.copy(out=Sgb.t[:, :, :], in_=Sg.t[:, :, :]), reads=[Sg.r], writes=[Sgb.r])
            S.op("act", lambda e, po=po: e.activation(out=osq.t[:, :], in_=po.t[:, :], func=AF.Square), reads=[po.r], writes=[osq.r])
            pss = self.psf()
            self.mm_group(pss.t[:, :], pss.r, [(self.onesf.t[:, :], osq.t[:, :], [self.onesf.r, osq.r])])
            S.op("dve", lambda e, pss=pss: e.tensor_scalar(out=orst.t[:, :], in0=pss.t[:, :], scalar1=1.0 / 128.0, scalar2=EPS, op0=ALU.mult, op1=ALU.add),
                 reads=[pss.r], writes=[orst.r])
            S.op("act", lambda e: e.activation(out=orst.t[:, :], in_=orst.t[:, :], func=AF.Sqrt), reads=[orst.r], writes=[orst.r])
            S.op("dve", lambda e: e.reciprocal(out=orst.t[:, :], in_=orst.t[:, :]), reads=[orst.r], writes=[orst.r])
            S.op("act", lambda e, po=po: e.copy(out=osq.t[:, :], in_=po.t[:, :]), reads=[po.r, osq.r], writes=[osq.r])
            S.op("dve", lambda e: e.tensor_tensor(out=osq.t[:, :], in0=osq.t[:, :], in1=orst.t[:, :], op=ALU.mult), reads=[osq.r, orst.r], writes=[osq.r])
            S.op("dve", lambda e: e.tensor_tensor(out=gsil.t[:, :, :], in0=gsil.t[:, :, :], in1=gng.t[:, 0:4].unsqueeze(2).to_broadcast([128, 4, 128]), op=ALU.mult),
                 reads=[gsil.r, gng.r], writes=[gsil.r])
            S.op("dve", lambda e: e.tensor_tensor(out=mixedT.t[:, 0:4, :], in0=osq.t[:, :].rearrange("p (h n) -> p h n", h=4), in1=gsil.t[:, :, :], op=ALU.mult),
                 reads=[osq.r, gsil.r], writes=[mixedT.r])
            self.out_proj(mixedT, Wout, ht, dst, blk, nk=4)

    def phase_rwkv(self, l, src, resbuf):
        S = self.S
        d = self.din
        j = l // 2
        CW = 0.6065306597126334
        self.arena_reset()
        Wr = self.arena_alloc("w_rwkv", 8, 1792)
        Wout = self.arena_alloc("wout_b", 4, D)
        W2b = self.arena_alloc("rw_w2", 1, 512)
        A2b = self.arena_alloc("rw_a2", 1, 512)
        G2b = self.arena_alloc("rw_g2", 1, 512)
        self.load_small(d["norm_mix_g"][l:l + 1, :].partition_broadcast(128), self.gb)
        self.load_w(d["ab_w_in"][j], Wr, D, 1792, src_col0=1552)
        self.load_w(d["w_mix_out"][l][512:1024, :], Wout, 512, D)
        self.load_w(d["rw_w2"][j], W2b, 64, 512)
        self.load_w(d["rw_a2"][j], A2b, 64, 512)
        self.load_w(d["rw_g2"][j], G2b, 128, 512)
        rvec = self.abuf("rw_vec", [64, 8, 8], F32)
        mus = self.abuf("rw_mus", [128, 4], F32)
        rrow = self.abuf("rw_row", [128, 1024], F32)
        self.load_small(d["rw_vec"][j], rvec)
        self.load_small(d["rw_mu_s"][j], mus)
        self.load_small(d["rw_row"][j], rrow)
        vb = lambda i: rvec.t[:, i, :].unsqueeze(2).to_broadcast([64, 8, 128])
        xnT = self.abuf("xnT_r", [128, 8, 128])
        mixedT = self.abuf("mixedT_r", [128, 4, 128])
        K3 = lambda name: self.abuf(name, [64, 8, 128], F32)
        Pr = self.abuf("Pr", [64, 8, 129], F32); Pk = self.abuf("Pk", [64, 8, 129], F32); Pv = self.abuf("Pv", [64, 8, 129], F32)
        Pl = self.abuf("Pl", [128, 3, 129], F32)
        for P in (Pr, Pk, Pv, Pl):
            S.op("pool", lambda e, P=P: e.memset(P.t[:, :, :], 0.0), writes=[P.r])
        R = K3("R"); Kt = K3("K"); KK = K3("KK"); AS = K3("AS"); LW = K3("LW"); CS = K3("CS"); E = K3("E"); T1 = K3("T1")
        BI = K3("BI"); KI = K3("KI")
        AR = self.abuf("AR", [64, 8, 256], F32)
        ltmp = self.abuf("ltmp", [128, 128], F32)
        lor = self.abuf("lor", [128, 3, 128])
        sd = self.abuf("rsd", [64, 8], F32)
        Vrow = self.abuf("Vrow", [128, 512], F32); BEr = self.abuf("BEr", [128, 512], F32); KEr = self.abuf("KEr", [128, 512], F32)
        X = self.abuf("X", [128, 8, 128], F32)
        AZs = [self.abuf("AZ%d" % i, [128, 4, 128], F32) for i in range(2)]
        Zr = self.abuf("Zr", [128, 512], F32)
        Y = self.abuf("Y", [128, 8, 64], F32); Yc = self.abuf("Yc", [128, 8, 64], F32)
        GR = self.abuf("GR", [128, 512], F32)
        yb = self.abuf("yb", [128, 512])
        st8 = self.abuf("st8", [128, 32], F32)
        ST = self.abuf("ST", [128, 8, 64], F32)
        Malls = [self.abuf("Mall%d" % i, [128, 4, 512], F32) for i in range(2)]
        chs = [[self.abuf("ch%d_%d" % (k, i), [128, 4, 128], F32) for i in range(2)] for k in range(2)]
        Xh = [T_(X.t, Res("Xh%d" % i)) for i in range(2)]
        Zrh = [T_(Zr.t, Res("Zrh%d" % i)) for i in range(2)]
        Yh = [T_(Y.t, Res("Yh%d" % i)) for i in range(2)]
        STh = [T_(ST.t, Res("STh%d" % i)) for i in range(2)]
        MASK4 = self.abuf("MASK4", [128, 512], F32)
        S.op("pool", lambda e: e.memset(ST.t[:, :, :], 0.0), writes=[STh[0].r, STh[1].r])
        S.op("pool", lambda e: e.tensor_copy(out=ST.t[64:128, :, :], in_=self.identf.t[64:128, 64:128].unsqueeze(1).to_broadcast([64, 8, 64])),
             reads=[self.identf.r, STh[0].r, STh[1].r], writes=[STh[0].r, STh[1].r])
        for q, msk in enumerate((self.Us, self.Uf, self.Us, self.Uf)):
            S.op("pool", lambda e, q=q, msk=msk: e.tensor_copy(out=MASK4.t[:, q * 128:(q + 1) * 128], in_=msk.t[:, :]), reads=[msk.r, MASK4.r], writes=[MASK4.r])
        onesK = self.onesf.t[0:64, 0:64]
        v3 = lambda t: t.t[:, :, :]

        for blk in range(self.NB):
            htA = self.new_ht()
            self.load_h(src, blk, htA)
            self.norm_T(htA.t[:, :], htA.r, self.gb, xnT, 0)
            ht = self.new_ht()
            self.load_h(resbuf, blk, ht)
            xw = [Wr.r, xnT.r]
            for (c0, P) in ((0, Pr), (576, Pk), (1088, Pv)):
                for hh in range(2):
                    pp = self.psf()
                    for hl in range(4):
                        h = hh * 4 + hl
                        self.mm_group(pp.t[0:64, hl * 128:(hl + 1) * 128], pp.r, [(Wr.t[:, kc, c0 + h * 64:c0 + (h + 1) * 64], xnT.t[:, kc, :], xw) for kc in range(8)])
                    S.op("act", lambda e, pp=pp, P=P, hh=hh: e.copy(out=P.t[:, hh * 4:(hh + 1) * 4, 1:129], in_=pp.t[0:64, :].rearrange("p (h n) -> p h n", h=4)),
                         reads=[pp.r], writes=[P.r])
            pl = self.psf()
            self.mm_group(pl.t[0:64, 0:128], pl.r, [(Wr.t[:, kc, 512:576], xnT.t[:, kc, :], xw) for kc in range(8)])
            self.mm_group(pl.t[0:64, 128:256], pl.r, [(Wr.t[:, kc, 1600:1664], xnT.t[:, kc, :], xw) for kc in range(8)])
            self.mm_group(pl.t[:, 256:384], pl.r, [(Wr.t[:, kc, 1664:1792], xnT.t[:, kc, :], xw) for kc in range(8)])
            S.op("act", lambda e, pl=pl: e.copy(out=Pl.t[0:64, 0:2, 1:129], in_=pl.t[0:64, 0:256].rearrange("p (h n) -> p h n", h=2)), reads=[pl.r], writes=[Pl.r])
            S.op("act", lambda e, pl=pl: e.copy(out=Pl.t[:, 2, 1:129], in_=pl.t[:, 256:384]), reads=[pl.r, Pl.r], writes=[Pl.r])
            for (P, i, out) in ((Pr, 5, R), (Pk, 6, Kt), (Pv, 7, E)):
                S.op("dve", lambda e, P=P: e.tensor_tensor(out=T1.t[:, :, :], in0=P.t[:, :, 0:128], in1=P.t[:, :, 1:129], op=ALU.subtract), reads=[P.r], writes=[T1.r])
                S.op("dve", lambda e, i=i: e.tensor_tensor(out=T1.t[:, :, :], in0=T1.t[:, :, :], in1=vb(i), op=ALU.mult), reads=[T1.r, rvec.r], writes=[T1.r])
                S.op("dve", lambda e, P=P, out=out: e.tensor_tensor(out=out.t[:, :, :], in0=T1.t[:, :, :], in1=P.t[:, :, 1:129], op=ALU.add), reads=[T1.r, P.r], writes=[out.r])
                S.op("act", lambda e, P=P: e.copy(out=P.t[:, :, 0:1], in_=P.t[:, :, 128:129]), reads=[P.r], writes=[P.r])
            pt = self.psf()
            for h in range(8):
                S.op("pe", lambda e, h=h, pt=pt: e.transpose(pt.t[:, h * 64:(h + 1) * 64], E.t[:, h, :], self.identf.t[0:64, 0:64]),
                     reads=[E.r, self.identf.r], writes=[pt.r], signal=(h == 7))
            S.op("act", lambda e, pt=pt: e.copy(out=Vrow.t[:, :], in_=pt.t[:, :]), reads=[pt.r], writes=[Vrow.r])
            for i, (rows, fn) in enumerate(((64, AF.Tanh), (64, AF.Identity), (128, AF.Sigmoid))):
                S.op("dve", lambda e, i=i, rows=rows: e.tensor_tensor(out=ltmp.t[0:rows, :], in0=Pl.t[0:rows, i, 0:128], in1=Pl.t[0:rows, i, 1:129], op=ALU.subtract),
                     reads=[Pl.r], writes=[ltmp.r])
                S.op("dve", lambda e, i=i, rows=rows: e.scalar_tensor_tensor(out=ltmp.t[0:rows, :], in0=ltmp.t[0:rows, :], scalar=mus.t[0:rows, i:i + 1],
                                                                         in1=Pl.t[0:rows, i, 1:129], op0=ALU.mult, op1=ALU.add),
                     reads=[ltmp.r, mus.r, Pl.r], writes=[ltmp.r])
                S.op("act", lambda e, i=i, rows=rows, fn=fn: e.activation(out=lor.t[0:rows, i, :], in_=ltmp.t[0:rows, :], func=fn), reads=[ltmp.r], writes=[lor.r])
            S.op("act", lambda e: e.copy(out=Pl.t[:, :, 0:1], in_=Pl.t[:, :, 128:129]), reads=[Pl.r], writes=[Pl.r])
            for (Wl, li, vi, OUT) in ((W2b, 0, 0, LW), (A2b, 1, 1, AS)):
                for hh in range(2):
                    pp = self.psf()
                    for hl in range(4):
                        h = hh * 4 + hl
                        self.mm_group(pp.t[0:64, hl * 128:(hl + 1) * 128], pp.r, [(Wl.t[0:64, 0, h * 64:(h + 1) * 64], lor.t[0:64, li, :], [Wl.r, lor.r])])
                    S.op("dve", lambda e, pp=pp, hh=hh, vi=vi, OUT=OUT: e.tensor_tensor(
                        out=OUT.t[:, hh * 4:(hh + 1) * 4, :], in0=pp.t[0:64, :].rearrange("p (h n) -> p h n", h=4),
                        in1=rvec.t[:, vi, hh * 4:(hh + 1) * 4].unsqueeze(2).to_broadcast([64, 4, 128]), op=ALU.add), reads=[pp.r, rvec.r], writes=[OUT.r])
                S.op("act", lambda e, OUT=OUT: e.activation(out=OUT.t[:, :, :], in_=OUT.t[:, :, :], func=AF.Sigmoid), reads=[OUT.r], writes=[OUT.r])
            pgr = self.psf()
            self.mm_group(pgr.t[:, :], pgr.r, [(lor.t[:, 2, :], G2b.t[:, 0, :], [lor.r, G2b.r])])
            S.op("act", lambda e, pgr=pgr: e.copy(out=GR.t[:, :], in_=pgr.t[:, :]), reads=[pgr.r], writes=[GR.r])
            for h in range(8):
                S.op("dve", lambda e, h=h: e.tensor_tensor_scan(out=CS.t[:, h, :], data0=self.onesf.t[0:64, :], data1=LW.t[:, h, :], initial=0.0,
                                                                op0=ALU.mult, op1=ALU.add), reads=[LW.r, self.onesf.r], writes=[CS.r])
            S.op("dve", lambda e: e.tensor_tensor(out=v3(KK), in0=v3(Kt), in1=vb(2), op=ALU.mult), reads=[Kt.r, rvec.r], writes=[KK.r])
            S.op("dve", lambda e: e.tensor_tensor(out=v3(T1), in0=v3(KK), in1=v3(KK), op=ALU.mult), reads=[KK.r], writes=[T1.r])
            for hh in range(2):
                pp = self.psf()
                self.mm_group(pp.t[0:64, :], pp.r, [(onesK, T1.t[:, hh * 4:(hh + 1) * 4, :], [self.onesf.r, T1.r])])
                S.op("act", lambda e, pp=pp, hh=hh: e.activation(out=E.t[:, hh * 4:(hh + 1) * 4, :], in_=pp.t[0:64, :].rearrange("p (h n) -> p h n", h=4), func=AF.Sqrt),
                     reads=[pp.r], writes=[E.r])
            S.op("dve", lambda e: e.tensor_scalar(out=v3(E), in0=v3(E), scalar1=1e-6, scalar2=None, op0=ALU.max), reads=[E.r], writes=[E.r])
            S.op("dve", lambda e: e.reciprocal(out=v3(E), in_=v3(E)), reads=[E.r], writes=[E.r])
            S.op("dve", lambda e: e.tensor_tensor(out=v3(KK), in0=v3(KK), in1=v3(E), op=ALU.mult), reads=[KK.r, E.r], writes=[KK.r])
            S.op("dve", lambda e: e.tensor_scalar(out=v3(T1), in0=v3(AS), scalar1=-1.0, scalar2=None, op0=ALU.add), reads=[AS.r], writes=[T1.r])
            S.op("dve", lambda e: e.tensor_tensor(out=v3(T1), in0=v3(T1), in1=vb(3), op=ALU.mult), reads=[T1.r, rvec.r], writes=[T1.r])
            S.op("dve", lambda e: e.tensor_scalar(out=v3(T1), in0=v3(T1), scalar1=1.0, scalar2=None, op0=ALU.add), reads=[T1.r], writes=[T1.r])
            S.op("dve", lambda e: e.tensor_tensor(out=v3(Kt), in0=v3(Kt), in1=v3(T1), op=ALU.mult), reads=[Kt.r, T1.r], writes=[Kt.r])
            S.op("dve", lambda e: e.tensor_tensor(out=v3(T1), in0=v3(R), in1=v3(Kt), op=ALU.mult), reads=[R.r, Kt.r], writes=[T1.r])
            S.op("dve", lambda e: e.tensor_tensor(out=v3(T1), in0=v3(T1), in1=vb(4), op=ALU.mult), reads=[T1.r, rvec.r], writes=[T1.r])
            pb = self.psf()
            for h in range(8):
                self.mm_group(pb.t[:, h * 2:h * 2 + 2], pb.r, [(T1.t[:, h, :], self.onesf.t[0:64, 0:2], [T1.r, self.onesf.r])])
            S.op("act", lambda e, pb=pb: e.copy(out=st8.t[:, 16:32], in_=pb.t[:, 0:16]), reads=[pb.r], writes=[st8.r])
            S.op("dve", lambda e: e.tensor_tensor(out=v3(AS), in0=v3(AS), in1=v3(KK), op=ALU.mult), reads=[AS.r, KK.r], writes=[AS.r])
            S.op("act", lambda e: e.activation(out=v3(E), in_=v3(CS), func=AF.Exp, scale=-CW), reads=[CS.r], writes=[E.r])
            S.op("dve", lambda e: e.tensor_tensor(out=AR.t[:, :, 128:256], in0=v3(R), in1=v3(E), op=ALU.mult), reads=[R.r, E.r], writes=[AR.r])
            S.op("dve", lambda e: e.tensor_tensor(out=v3(T1), in0=v3(CS), in1=v3(LW), op=ALU.subtract), reads=[CS.r, LW.r], writes=[T1.r])
            S.op("act", lambda e: e.activation(out=v3(E), in_=v3(T1), func=AF.Exp, scale=-CW), reads=[T1.r], writes=[E.r])
            S.op("dve", lambda e: e.scalar_tensor_tensor(out=AR.t[:, :, 0:128], in0=v3(KK), scalar=-1.0, in1=v3(E), op0=ALU.mult, op1=ALU.mult),
                 reads=[KK.r, E.r], writes=[AR.r])
            S.op("act", lambda e: e.activation(out=v3(E), in_=v3(CS), func=AF.Exp, scale=CW), reads=[CS.r], writes=[E.r])
            S.op("dve", lambda e: e.tensor_tensor(out=v3(BI), in0=v3(AS), in1=v3(E), op=ALU.mult), reads=[AS.r, E.r], writes=[BI.r])
            S.op("dve", lambda e: e.tensor_tensor(out=v3(KI), in0=v3(Kt), in1=v3(E), op=ALU.mult), reads=[Kt.r, E.r], writes=[KI.r])
            S.op("act", lambda e: e.activation(out=sd.t[:, :], in_=CS.t[:, :, 127], func=AF.Exp, scale=-CW), reads=[CS.r], writes=[sd.r])
            S.op("dve", lambda e: e.tensor_tensor(out=v3(T1), in0=v3(CS), in1=CS.t[:, :, 127:128].to_broadcast([64, 8, 128]), op=ALU.subtract), reads=[CS.r], writes=[T1.r])
            S.op("act", lambda e: e.activation(out=v3(E), in_=v3(T1), func=AF.Exp, scale=CW), reads=[T1.r], writes=[E.r])
            S.op("dve", lambda e: e.tensor_tensor(out=v3(AS), in0=v3(AS), in1=v3(E), op=ALU.mult), reads=[AS.r, E.r], writes=[AS.r])
            S.op("dve", lambda e: e.tensor_tensor(out=v3(Kt), in0=v3(Kt), in1=v3(E), op=ALU.mult), reads=[Kt.r, E.r], writes=[Kt.r])
            for (srcap, srcr, dstap, dstt) in ((lambda h: AR.t[:, h, 0:128], AR.r, X.t[:, :, 0:64], None),
                                              (lambda h: AS.t[:, h, :], AS.r, BEr.t[:, :].rearrange("p (h n) -> p h n", h=8), BEr),
                                              (lambda h: Kt.t[:, h, :], Kt.r, KEr.t[:, :].rearrange("p (h n) -> p h n", h=8), KEr)):
                pt = self.psf()
                for h in range(8):
                    S.op("pe", lambda e, h=h, pt=pt, srcap=srcap: e.transpose(pt.t[:, h * 64:(h + 1) * 64], srcap(h), self.identf.t[0:64, 0:64]),
                         reads=[srcr, self.identf.r], writes=[pt.r], signal=(h == 7))
                S.op("act", lambda e, pt=pt, dstap=dstap: e.copy(out=dstap, in_=pt.t[:, :].rearrange("p (h n) -> p h n", h=8)), reads=[pt.r],
                     writes=[dstt.r] if dstt is not None else [Xh[0].r, Xh[1].r])
            recs = []
            for hh in range(2):
                S.begin_record()
                self.psf_sub = self.psf_pool[3 * hh:3 * hh + 3]
                Mall, ch, AZ = Malls[hh], chs[hh], AZs[hh]
                X_, Zr_, Y_, ST_ = Xh[hh], Zrh[hh], Yh[hh], STh[hh]
                for hl in range(4):
                    h = hh * 4 + hl
                    pM = self.psf()
                    self.mm_group(pM.t[:, 0:256], pM.r, [(BI.t[:, h, :], AR.t[:, h, :], [BI.r, AR.r])])
                    self.mm_group(pM.t[:, 256:512], pM.r, [(KI.t[:, h, :], AR.t[:, h, :], [KI.r, AR.r])])
                    S.op("dve", lambda e, pM=pM, hl=hl, Mall=Mall: e.tensor_tensor(out=Mall.t[:, hl, :], in0=pM.t[:, :], in1=MASK4.t[:, :], op=ALU.mult),
                         reads=[pM.r, MASK4.r], writes=[Mall.r])
                pA = self.psf()
                for hl in range(4):
                    h = hh * 4 + hl
                    self.mm_group(pA.t[:, hl * 128:(hl + 1) * 128], pA.r, [(AR.t[:, h, 0:128], BI.t[:, h, :], [AR.r, BI.r])])
                A0 = ch[0]
                S.op("dve", lambda e, pA=pA, A0=A0: e.tensor_tensor(out=A0.t[:, :, :], in0=pA.t[:, :].rearrange("p (h n) -> p h n", h=4),
                                                             in1=self.Ls.t[:, :].unsqueeze(1).to_broadcast([128, 4, 128]), op=ALU.mult), reads=[pA.r, self.Ls.r], writes=[A0.r])
                Ap, Apt = A0, T_(Mall.t[:, :, 0:128], Mall.r)
                Q = ch[1]
                S.op("dve", lambda e, Q=Q, Mall=Mall: e.tensor_tensor(out=Q.t[:, :, :], in0=Mall.t[:, :, 0:128], in1=self.identf.t[:, :].unsqueeze(1).to_broadcast([128, 4, 128]), op=ALU.add),
                     reads=[Mall.r, self.identf.r], writes=[Q.r])
                for step in range(6):
                    pN = self.psf()
                    for hl in range(4):
                        self.mm_group(pN.t[:, hl * 128:(hl + 1) * 128], pN.r, [(Apt.t[:, hl, :], Ap.t[:, hl, :], [Apt.r, Ap.r])])
                    if step < 5:
                        pNt = self.psf()
                        for hl in range(4):
                            self.mm_group(pNt.t[:, hl * 128:(hl + 1) * 128], pNt.r, [(Ap.t[:, hl, :], Apt.t[:, hl, :], [Apt.r, Ap.r])])
                    S.op("act", lambda e, pN=pN, Ap=Ap: e.copy(out=Ap.t[:, :, :], in_=pN.t[:, :].rearrange("p (h n) -> p h n", h=4)), reads=[pN.r], writes=[Ap.r])
                    if step < 5:
                        S.op("act", lambda e, pNt=pNt, Apt=Apt: e.copy(out=Apt.t[:, :, :], in_=pNt.t[:, :].rearrange("p (h n) -> p h n", h=4)), reads=[pNt.r], writes=[Apt.r])
                    pQ = self.psf()
                    for hl in range(4):
                        self.mm_group(pQ.t[:, hl * 128:(hl + 1) * 128], pQ.r, [(Ap.t[:, hl, :], Q.t[:, hl, :], [Ap.r, Q.r])])
                    S.op("dve", lambda e, pQ=pQ, Q=Q: e.tensor_tensor(out=Q.t[:, :, :], in0=Q.t[:, :, :], in1=pQ.t[:, :].rearrange("p (h n) -> p h n", h=4), op=ALU.add),
                         reads=[pQ.r, Q.r], writes=[Q.r])
                pK = self.psf()
                for hl in range(4):
                    h = hh * 4 + hl
                    self.mm_group(pK.t[:, hl * 64:(hl + 1) * 64], pK.r, [(Mall.t[:, hl, 256:384], Vrow.t[:, h * 64:(h + 1) * 64], [Mall.r, Vrow.r])])
                S.op("act", lambda e, pK=pK, hh=hh: e.copy(out=X.t[:, hh * 4:(hh + 1) * 4, 64:128], in_=pK.t[:, 0:256].rearrange("p (h n) -> p h n", h=4)),
                     reads=[pK.r], writes=[X_.r])
                pZ = self.psf()
                for hl in range(4):
                    h = hh * 4 + hl
                    self.mm_group(pZ.t[:, hl * 128:(hl + 1) * 128], pZ.r, [(X.t[:, h, :], Q.t[:, hl, :], [X_.r, Q.r])])
                S.op("act", lambda e, pZ=pZ, AZ=AZ: e.copy(out=AZ.t[:, :, :], in_=pZ.t[:, :].rearrange("p (h n) -> p h n", h=4)), reads=[pZ.r], writes=[AZ.r])
                pZr = self.psf()
                for hl in range(4):
                    h = hh * 4 + hl
                    self.mm_group(pZr.t[:, hl * 64:(hl + 1) * 64], pZr.r, [(AZ.t[:, hl, :], ST.t[:, h, :], [AZ.r, ST_.r])])
                S.op("dve", lambda e, pZr=pZr, hh=hh: e.tensor_copy(out=Zr.t[:, hh * 256:(hh + 1) * 256], in_=pZr.t[:, 0:256]), reads=[pZr.r], writes=[Zr_.r])
                pY = self.psf()
                for hl in range(4):
                    h = hh * 4 + hl
                    hs = slice(h * 64, (h + 1) * 64)
                    self.mm_group(pY.t[:, hl * 64:(hl + 1) * 64], pY.r, [(AR.t[:, h, 128:256], ST.t[0:64, h, :], [AR.r, ST_.r]),
                                                                      (Mall.t[:, hl, 128:256], Zr.t[:, hs], [Mall.r, Zr_.r]),
                                                                      (Mall.t[:, hl, 384:512], Vrow.t[:, hs], [Mall.r, Vrow.r])])
                S.op("act", lambda e, pY=pY, hh=hh: e.copy(out=Y.t[:, hh * 4:(hh + 1) * 4, :], in_=pY.t[:, 0:256].rearrange("p (h n) -> p h n", h=4)),
                     reads=[pY.r], writes=[Y_.r])
                pD = self.psf()
                for hl in range(4):
                    h = hh * 4 + hl
                    hs = slice(h * 64, (h + 1) * 64)
                    self.mm_group(pD.t[0:64, hl * 64:(hl + 1) * 64], pD.r, [(BEr.t[:, hs], Zr.t[:, hs], [BEr.r, Zr_.r]), (KEr.t[:, hs], Vrow.t[:, hs], [KEr.r, Vrow.r])])
                S.op("dve", lambda e, hh=hh: e.tensor_tensor(out=ST.t[0:64, hh * 4:(hh + 1) * 4, :], in0=ST.t[0:64, hh * 4:(hh + 1) * 4, :],
                                                             in1=sd.t[:, hh * 4:(hh + 1) * 4].unsqueeze(2).to_broadcast([64, 4, 64]), op=ALU.mult), reads=[ST_.r, sd.r], writes=[ST_.r])
                S.op("dve", lambda e, hh=hh, pD=pD: e.tensor_tensor(out=ST.t[0:64, hh * 4:(hh + 1) * 4, :], in0=ST.t[0:64, hh * 4:(hh + 1) * 4, :],
                                                                    in1=pD.t[0:64, 0:256].rearrange("p (h n) -> p h n", h=4), op=ALU.add), reads=[ST_.r, pD.r], writes=[ST_.r])
                recs.append(S.end_record())
                self.psf_sub = None
            S.emit_interleaved(recs)
            b8 = lambda c0: st8.t[:, c0:c0 + 8].unsqueeze(2).to_broadcast([128, 8, 64])
            S.op("dve", lambda e: e.tensor_reduce(out=st8.t[:, 0:8], in_=Y.t[:, :, :], axis=AX.X, op=ALU.add), reads=[Yh[0].r, Yh[1].r], writes=[st8.r])
            S.op("dve", lambda e: e.tensor_scalar(out=st8.t[:, 0:8], in0=st8.t[:, 0:8], scalar1=1.0 / 64.0, scalar2=None, op0=ALU.mult), reads=[st8.r], writes=[st8.r])
            S.op("dve", lambda e: e.tensor_tensor(out=Yc.t[:, :, :], in0=Y.t[:, :, :], in1=b8(0), op=ALU.subtract), reads=[Yh[0].r, Yh[1].r, st8.r], writes=[Yc.r])
            S.op("dve", lambda e: e.tensor_tensor(out=Y.t[:, :, :], in0=Yc.t[:, :, :], in1=Yc.t[:, :, :], op=ALU.mult), reads=[Yc.r], writes=[Yh[0].r, Yh[1].r])
            S.op("dve", lambda e: e.tensor_reduce(out=st8.t[:, 8:16], in_=Y.t[:, :, :], axis=AX.X, op=ALU.add), reads=[Yh[0].r, Yh[1].r], writes=[st8.r])
            S.op("dve", lambda e: e.tensor_scalar(out=st8.t[:, 8:16], in0=st8.t[:, 8:16], scalar1=1.0 / 64.0, scalar2=64e-5, op0=ALU.mult, op1=ALU.add), reads=[st8.r], writes=[st8.r])
            S.op("act", lambda e: e.activation(out=st8.t[:, 8:16], in_=st8.t[:, 8:16], func=AF.Sqrt), reads=[st8.r], writes=[st8.r])
            S.op("dve", lambda e: e.reciprocal(out=st8.t[:, 8:16], in_=st8.t[:, 8:16]), reads=[st8.r], writes=[st8.r])
            S.op("dve", lambda e: e.tensor_tensor(out=Yc.t[:, :, :], in0=Yc.t[:, :, :], in1=b8(8), op=ALU.mult), reads=[Yc.r, st8.r], writes=[Yc.r])
            Yc2 = Yc.t[:, :, :].rearrange("p h n -> p (h n)")
            S.op("dve", lambda e: e.tensor_tensor(out=Yc2, in0=Yc2, in1=rrow.t[:, 0:512], op=ALU.mult), reads=[Yc.r, rrow.r], writes=[Yc.r])
            S.op("dve", lambda e: e.tensor_tensor(out=Yc2, in0=Yc2, in1=rrow.t[:, 512:1024], op=ALU.add), reads=[Yc.r, rrow.r], writes=[Yc.r])
            S.op("dve", lambda e: e.tensor_tensor(out=Y.t[:, :, :], in0=Vrow.t[:, :].rearrange("p (h n) -> p h n", h=8),
                                                  in1=st8.t[:, 16:32:2].unsqueeze(2).to_broadcast([128, 8, 64]), op=ALU.mult), reads=[Vrow.r, st8.r], writes=[Yh[0].r, Yh[1].r])
            S.op("dve", lambda e: e.tensor_tensor(out=Yc.t[:, :, :], in0=Yc.t[:, :, :], in1=Y.t[:, :, :], op=ALU.add), reads=[Yc.r, Yh[0].r, Yh[1].r], writes=[Yc.r])
            S.op("dve", lambda e: e.tensor_tensor(out=yb.t[:, :], in0=Yc2, in1=GR.t[:, :], op=ALU.mult), reads=[Yc.r, GR.r], writes=[yb.r])
            pT = self.psb()
            for n in range(4):
                S.op("pe", lambda e, n=n, pT=pT: e.transpose(pT.t[:, n * 128:(n + 1) * 128], yb.t[:, n * 128:(n + 1) * 128], self.identb.t[:]),
                     reads=[yb.r, self.identb.r], writes=[pT.r], signal=(n == 3))
            S.op("act", lambda e, pT=pT: e.copy(out=mixedT.t[:, :, :], in_=pT.t[:, 0:512].rearrange("p (k n) -> p k n", k=4)), reads=[pT.r], writes=[mixedT.r])
            self.out_proj(mixedT, Wout, ht, resbuf, blk, nk=4)


def host_layout(inputs, NL):
    f = lambda a: np.ascontiguousarray(np.asarray(a, dtype=np.float32))
    m = {}
    for k in ("norm_mix_g", "norm_xattn_g", "norm_ffn_g", "xattn_wq", "xattn_wkv", "xattn_wo", "ffn_w_up", "ffn_w_down", "w_mix_out"):
        m[k] = f(inputs[k])
    m["mem_norm_g"] = f(inputs["mem_norm_g"]).reshape(1, D)
    m["final_norm_g"] = f(inputs["final_norm_g"]).reshape(1, D)
    cw = f(inputs["ffn_conv_w"])
    cb = f(inputs["ffn_conv_b"])
    pad = NCH_FF * 128 - DFF
    v = np.concatenate([cw, cb[:, None, :]], axis=1)
    v = np.pad(v, ((0, 0), (0, 0), (0, pad)))
    v = v.reshape(NL, 4, NCH_FF, 128).transpose(0, 3, 2, 1)
    m["ffn_vec"] = np.ascontiguousarray(v.reshape(NL, 128, NCH_FF * 4))
    NE = (NL + 1) // 2
    m["ab_w_in"] = f(inputs["ab_w_in"])
    m["gla_wa"] = f(inputs["gla_w_alpha2"])
    m["gla_vec"] = np.ascontiguousarray(f(inputs["gla_b_alpha"]).reshape(NE, 4, 64).transpose(0, 2, 1))
    m["gla_ng"] = np.ascontiguousarray(f(inputs["gla_norm_g"]).reshape(NE, 4, 128).transpose(0, 2, 1))
    m["rw_w2"] = f(inputs["rwkv_w2"]); m["rw_a2"] = f(inputs["rwkv_a2"]); m["rw_g2"] = f(inputs["rwkv_g2"])
    mu = f(inputs["rwkv_mu"])
    kht = lambda a: a.reshape(NE, 8, 64).transpose(0, 2, 1)
    vecs = [inputs["rwkv_w0"], inputs["rwkv_a0"], inputs["rwkv_k_k"], inputs["rwkv_k_a"], inputs["rwkv_r_k"],
            mu[:, 0:512], mu[:, 576:1088], mu[:, 1088:1600]]
    m["rw_vec"] = np.ascontiguousarray(np.stack([kht(f(v)) for v in vecs], axis=2).reshape(NE, 64, 64))
    ms = np.zeros((NE, 128, 4), np.float32)
    ms[:, 0:64, 0] = mu[:, 512:576]; ms[:, 0:64, 1] = mu[:, 1600:1664]; ms[:, :, 2] = mu[:, 1664:1792]
    m["rw_mu_s"] = ms
    row = np.concatenate([f(inputs["rwkv_ln_g"]), f(inputs["rwkv_ln_b"])], axis=1)
    m["rw_row"] = np.ascontiguousarray(np.broadcast_to(row[:, None, :], (NE, 128, 1024)))
    NO = NL // 2
    m["cd_w_in"] = f(inputs["cd_w_in"])
    gw = f(inputs["lru_gate_w"])
    m["lru_gw"] = np.ascontiguousarray(gw.transpose(0, 3, 1, 2, 4).reshape(NO, 128, 1024))
    qkv = f(inputs["mlstm_qkv_w"])
    bd = np.zeros((NO, 3, 4, 128, 128), np.float32)
    q5 = qkv.reshape(NO, 3, 4, 32, 4, 4)
    for b in range(32):
        bd[:, :, :, 4 * b:4 * b + 4, 4 * b:4 * b + 4] = q5[:, :, :, b]
    m["ml_bd"] = np.ascontiguousarray(bd.transpose(0, 3, 1, 2, 4).reshape(NO, 128, 12 * 128))
    fm = lambda a: a.reshape(NO, -1, 4, 128).transpose(0, 3, 2, 1)
    lcw = f(inputs["lru_conv_w"]); lcb = f(inputs["lru_conv_b"]); lgb = f(inputs["lru_gate_b"]); lam = f(inputs["lru_lambda"])
    lv = np.concatenate([lcw, lcb[:, None], lgb, lam[:, None]], axis=1)
    m["lru_vec"] = np.ascontiguousarray(fm(lv).reshape(NO, 128, 32))
    mcw = f(inputs["mlstm_conv_w"]); mcb = f(inputs["mlstm_conv_b"])
    mv = np.concatenate([mcw, mcb[:, None]], axis=1)
    m["ml_vec"] = np.ascontiguousarray(fm(mv).reshape(NO, 128, 20))
    row = np.concatenate([f(inputs["mlstm_b_if"]), f(inputs["mlstm_norm_g"])], axis=1)
    m["ml_row"] = np.ascontiguousarray(np.broadcast_to(row[:, None, :], (NO, 128, 520)))
    return m


_CACHE = {}


def kernel(**inputs):
    x = np.asarray(inputs["x"], dtype=np.float32)
    mem = np.asarray(inputs["mem"], dtype=np.float32)
    B, T, _ = x.shape
    NL = inputs["norm_mix_g"].shape[0]
    key = (T, NL)
    if key not in _CACHE:
        _CACHE[key] = KB(T, NL).build()
    nc = _CACHE[key]
    shared = host_layout(inputs, NL)
    in_maps = []
    for b in range(B):
        mm = dict(shared)
        mm["x"] = np.ascontiguousarray(x[b])
        mm["mem"] = np.ascontiguousarray(mem[b])
        in_maps.append(mm)
    res = run_bass_kernel_spmd(nc, in_maps, core_ids=list(range(B)))
    return np.stack([np.asarray(r["out"], dtype=np.float32) for r in res.results], axis=0)
```

```python
import contextlib
import os
import numpy as np
import concourse.bass as bass
import concourse.mybir as mybir
from concourse.bass_utils import run_bass_kernel_spmd

F32 = mybir.dt.float32
BF16 = mybir.dt.bfloat16
AF = mybir.ActivationFunctionType
ALU = mybir.AluOpType
AX = mybir.AxisListType

ENGS = ("pe", "act", "dve", "pool", "sp")
D = 1024
DFF = 2752
NCH_FF = 22
EPS = 1e-6


class Res:
    __slots__ = ("name", "last_w", "readers", "dsem", "dcount", "psum")

    def __init__(self, name):
        self.name = name
        self.psum = False
        self.last_w = None
        self.readers = {}
        self.dsem = None
        self.dcount = 0


class Sched:
    def __init__(self, nc, stack):
        self.nc = nc
        self.stack = stack
        self.q = {e: [] for e in ENGS}
        self.cnt = {e: 0 for e in ENGS}
        self.pending = {e: False for e in ENGS}
        self.seen = {e: {} for e in ENGS}
        self.semh = {}
        for e in ENGS:
            self.semh["E_" + e] = stack.enter_context(nc.semaphore("sem_" + e))
        self.nsem = len(ENGS)
        self.ninstr = 0
        self.rec = None

    def _dsem(self, res):
        if res.dsem is None:
            key = "D_%d" % self.nsem
            self.semh[key] = self.stack.enter_context(self.nc.semaphore("d%d" % self.nsem))
            self.nsem += 1
            res.dsem = key
        return res.dsem

    def _wait(self, eng, deps):
        seen = self.seen[eng]
        for sk, v in deps.items():
            if seen.get(sk, 0) < v:
                seen[sk] = v
                h = self.semh[sk]
                self.q[eng].append(lambda e, h=h, v=v: e.wait_ge(h, v))
                self.ninstr += 1

    def _deps(self, own, reads, writes):
        deps = {}

        def add(ev):
            if ev[1] > deps.get(ev[0], 0):
                deps[ev[0]] = ev[1]
        for r in reads:
            if r.last_w is not None:
                add(r.last_w)
            if r.psum:
                for sk, v in r.readers.items():
                    if sk != own:
                        add((sk, v))
        skip_own = (own == "E_pe")
        for w in writes:
            if w.last_w is not None and not (skip_own and w.last_w[0] == own):
                add(w.last_w)
            for sk, v in w.readers.items():
                if not (skip_own and sk == own):
                    add((sk, v))
        return deps

    def _record(self, ev, reads, writes):
        for r in reads:
            if r.readers.get(ev[0], 0) < ev[1]:
                r.readers[ev[0]] = ev[1]
        for w in writes:
            w.last_w = ev
            w.readers = {}

    def begin_record(self):
        self.rec = []

    def end_record(self):
        r, self.rec = self.rec, None
        units, cur = [], []
        for it in r:
            cur.append(it)
            if it[0] == "dma" or it[5]:
                units.append(cur)
                cur = []
        assert not cur
        return units

    def emit_interleaved(self, lists):
        idx = [0] * len(lists)
        while True:
            best, bf = -1, 2.0
            for i, l in enumerate(lists):
                if idx[i] < len(l):
                    fr = idx[i] / len(l)
                    if fr < bf:
                        best, bf = i, fr
            if best < 0:
                break
            for it in lists[best][idx[best]]:
                if it[0] == "op":
                    self.op(it[1], it[2], it[3], it[4], it[5])
                else:
                    self.dma(it[1], it[2], it[3], it[4], it[5], it[6], **it[7])
            idx[best] += 1

    def op(self, eng, fn, reads=(), writes=(), signal=True):
        if self.rec is not None:
            self.rec.append(("op", eng, fn, tuple(reads), tuple(writes), signal))
            return None
        own = "E_" + eng
        self._wait(eng, self._deps(own, reads, writes))
        val = self.cnt[eng] + 1
        ev = (own, val)
        if signal:
            self.cnt[eng] = val
            self.pending[eng] = False
            h = self.semh[own]
            self.q[eng].append(lambda e, fn=fn, h=h: fn(e).then_inc(h, 1))
        else:
            self.pending[eng] = True
            self.q[eng].append(lambda e, fn=fn: fn(e))
        self.ninstr += 1
        self._record(ev, reads, writes)
        return ev

    def dma(self, qeng, out_ap, in_ap, sb_res, reads=(), writes=(), **kw):
        if self.rec is not None:
            self.rec.append(("dma", qeng, out_ap, in_ap, sb_res, tuple(reads), tuple(writes), kw))
            return None
        self._wait(qeng, self._deps("__none__", reads, writes))
        sk = self._dsem(sb_res)
        sb_res.dcount += 16
        ev = (sk, sb_res.dcount)
        h = self.semh[sk]
        self.q[qeng].append(
            lambda e, o=out_ap, i=in_ap, h=h, kw=kw: e.dma_start(out=o, in_=i, **kw).then_inc(h, 16))
        self.ninstr += 1
        self._record(ev, reads, writes)
        return ev

    def finish(self, final_events):
        for e in ENGS:
            assert not self.pending[e], "engine %s has unsignalled trailing ops" % e
        allev = {}
        for sk, v in list(final_events) + [("E_" + e, self.cnt[e]) for e in ENGS if self.cnt[e] > 0]:
            if v > allev.get(sk, 0):
                allev[sk] = v
        self._wait("sp", allev)
        nc = self.nc
        with nc.Block() as block:
            @block.tensor
            def _(eng):
                for f in self.q["pe"]:
                    f(eng)

            @block.scalar
            def _(eng):
                for f in self.q["act"]:
                    f(eng)

            @block.vector
            def _(eng):
                for f in self.q["dve"]:
                    f(eng)

            @block.gpsimd
            def _(eng):
                for f in self.q["pool"]:
                    f(eng)

            @block.sync
            def _(eng):
                for f in self.q["sp"]:
                    f(eng)


class T_:
    __slots__ = ("t", "r")

    def __init__(self, t, r):
        self.t = t
        self.r = r

    def __getitem__(self, k):
        return self.t[k]


ARENA_COLS = 84000


class KB:
    def __init__(self, T, NL, plan=None, final_norm=True):
        self.T = T
        self.NL = NL
        self.NB = T // 128
        self.plan = plan
        self.final_norm = final_norm
        self.nc = bass.Bass("TRN2", target_bir_lowering=False)
        self.din = {}
        self.uid = 0

    def inp(self, name, shape):
        self.din[name] = self.nc.dram_tensor(name, list(shape), F32, kind="ExternalInput").ap()
        return self.din[name]

    def sb(self, name, shape, dt=F32):
        t = self.st.enter_context(self.nc.sbuf_tensor(name, list(shape), dt))
        return T_(t, Res(name))

    def ps(self, name, shape, dt=F32):
        t = self.st.enter_context(self.nc.psum_tensor(name, list(shape), dt))
        r = Res(name)
        r.psum = True
        return T_(t, r)

    def psf(self):
        pool = self.psf_sub if getattr(self, "psf_sub", None) else self.psf_pool
        self.psf_i += 1
        return pool[self.psf_i % len(pool)]

    def psb(self):
        p = self.psb_pool[self.psb_i % len(self.psb_pool)]
        self.psb_i += 1
        return p

    def arena_reset(self):
        carry = {}
        for r in self.arena_live:
            evs = dict(r.readers)
            if r.last_w is not None:
                evs[r.last_w[0]] = max(evs.get(r.last_w[0], 0), r.last_w[1])
            for k, v in evs.items():
                if v > carry.get(k, 0):
                    carry[k] = v
        for k, v in self.arena_carry.items():
            if v > carry.get(k, 0):
                carry[k] = v
        self.arena_carry = carry
        self.arena_live = []
        self.arena_off = 0

    def arena_alloc(self, name, kch, ncols):
        n = kch * ncols
        assert self.arena_off + n <= ARENA_COLS, (name, self.arena_off, n)
        ap = self.arena.t[:, self.arena_off:self.arena_off + n].rearrange("p (k n) -> p k n", k=kch)
        self.arena_off += n
        r = Res(name)
        r.readers = dict(self.arena_carry)
        self.arena_live.append(r)
        return T_(ap, r)

    def abuf(self, name, shape, dt=BF16):
        n = int(np.prod(shape[1:]))
        ncols = n if dt == BF16 else 2 * n
        self.arena_off += self.arena_off % 2
        assert self.arena_off + ncols <= ARENA_COLS, (name, self.arena_off, ncols)
        ap = self.arena.t[:, self.arena_off:self.arena_off + ncols]
        self.arena_off += ncols
        if dt != BF16:
            ap = ap.bitcast(dt)
        if len(shape) == 3:
            ap = ap.rearrange("p (k n) -> p k n", k=shape[1])
        elif len(shape) == 4:
            ap = ap.rearrange("p (a b n) -> p a b n", a=shape[1], b=shape[2])
        if shape[0] != 128:
            ap = ap[0:shape[0]]
        r = Res(name)
        r.readers = dict(self.arena_carry)
        self.arena_live.append(r)
        return T_(ap, r)

    def new_ht(self):
        ht = self.ht[self.ht_i % len(self.ht)]
        self.ht_i += 1
        return ht

    def load_w(self, src, dst, K, N, col0=0, scale=None, src_col0=0):
        S = self.S
        nk = (K + 127) // 128
        CB = 1024
        for kc in range(nk):
            rows = min(128, K - kc * 128)
            for c0 in range(0, N, CB):
                cw = min(CB, N - c0)
                stg = self.stg[self.stg_i % len(self.stg)]
                self.stg_i += 1
                S.dma("sp", stg.t[:rows, :cw], src[kc * 128:kc * 128 + rows, src_col0 + c0:src_col0 + c0 + cw],
                      stg.r, writes=[stg.r])
                o = dst.t[:rows, kc, col0 + c0:col0 + c0 + cw]
                eng = ("pool", "dve")[self.cast_i % 2] if self.cast_both else "pool"
                self.cast_i += 1
                if scale is None:
                    S.op(eng, lambda e, o=o, i=stg.t[:rows, :cw]: e.tensor_copy(out=o, in_=i),
                         reads=[stg.r], writes=[dst.r])
                else:
                    sc = scale.t[:rows, c0:c0 + cw]
                    S.op(eng, lambda e, o=o, i=stg.t[:rows, :cw], sc=sc: e.tensor_tensor(out=o, in0=i, in1=sc, op=ALU.mult),
                         reads=[stg.r, scale.r], writes=[dst.r])

    def load_small(self, src_ap, dst, dst_ap=None):
        self.S.dma("sp", dst.t[:] if dst_ap is None else dst_ap, src_ap, dst.r, writes=[dst.r])

    def load_h(self, src, blk, dst):
        self.S.dma("sp", dst.t[:, :], src[blk * 128:(blk + 1) * 128, :], dst.r,
                   reads=[self.hres[id(src)][blk]] if id(src) in self.hres else [], writes=[dst.r])

    def store_h(self, dstd, blk, src, ap=None):
        self.S.dma("act", dstd[blk * 128:(blk + 1) * 128, :], src.t[:, :] if ap is None else ap, src.r,
                   reads=[src.r], writes=[self.hres[id(dstd)][blk]])

    def rstd_of(self, hap, hres):
        S = self.S
        ss = self.ss[self.ss_i % len(self.ss)]
        self.ss_i += 1
        junk = self.junk
        S.op("act", lambda e: e.activation(out=junk.t[:], in_=hap, func=AF.Square, accum_out=ss.t[:, 0:1]),
             reads=[hres], writes=[junk.r, ss.r])
        S.op("dve", lambda e: e.tensor_scalar(out=ss.t[:, 1:2], in0=ss.t[:, 0:1], scalar1=1.0 / D, scalar2=EPS,
                                              op0=ALU.mult, op1=ALU.add), reads=[ss.r], writes=[ss.r])
        S.op("act", lambda e: e.activation(out=ss.t[:, 2:3], in_=ss.t[:, 1:2], func=AF.Sqrt), reads=[ss.r], writes=[ss.r])
        S.op("dve", lambda e: e.reciprocal(out=ss.t[:, 3:4], in_=ss.t[:, 2:3]), reads=[ss.r], writes=[ss.r])
        return ss

    def norm_T(self, hap, hres, gb, xnT, col0):
        S = self.S
        ss = self.rstd_of(hap, hres)
        xn = self.xn[self.xn_i % len(self.xn)]
        self.xn_i += 1
        S.op("dve", lambda e: e.scalar_tensor_tensor(out=xn.t[:], in0=hap, scalar=ss.t[:, 3:4], in1=gb.t[:],
                                                     op0=ALU.mult, op1=ALU.mult),
             reads=[hres, ss.r, gb.r], writes=[xn.r])
        pT = self.psb()
        for kc in range(8):
            S.op("pe", lambda e, kc=kc: e.transpose(pT.t[:, kc * 128:(kc + 1) * 128], xn.t[:, kc * 128:(kc + 1) * 128],
                                                     self.identb.t[:]),
                 reads=[xn.r, self.identb.r], writes=[pT.r], signal=(kc == 7))
        S.op("act", lambda e: e.copy(out=xnT.t[:, :, col0:col0 + 128],
                                     in_=pT.t[:, 0:1024].rearrange("p (k n) -> p k n", k=8)),
             reads=[pT.r], writes=[xnT.r])
        return xn

    def mm_group(self, out_ap, out_res, pairs):
        n = len(pairs)
        for i, (l, r, rs) in enumerate(pairs):
            self.S.op("pe", lambda e, l=l, r=r, i=i: e.matmul(out_ap, lhsT=l, rhs=r, start=(i == 0), stop=(i == n - 1)),
                      reads=rs, writes=[out_res], signal=(i == n - 1))

    def phase_ffn(self, l, src, dst):
        S = self.S
        d = self.din
        TT = 256
        self.arena_reset()
        Wup = self.arena_alloc("wup", 8, 2 * DFF)
        Wdn = self.arena_alloc("wdn", NCH_FF, D)
        self.load_small(d["ffn_vec"][l], self.fvec)
        self.load_small(d["norm_ffn_g"][l:l + 1, :].partition_broadcast(128), self.gb)
        self.load_w(d["ffn_w_up"][l], Wup, D, 2 * DFF)
        self.load_w(d["ffn_w_down"][l], Wdn, DFF, D)
        xnT = self.abuf("xnT_f", [128, 8, TT + 2])
        actT = self.abuf("actT", [128, NCH_FF, TT])
        self.cv = [self.abuf("cv%d" % i, [128, 2 * TT], F32) for i in range(2)]
        self.cv_i = 0
        S.op("pool", lambda e: e.memset(xnT.t[:, :, 0:2], 0.0), writes=[xnT.r])
        fv = self.fvec
        for t0 in range(0, self.T, TT):
            nb = TT // 128
            hts = [self.new_ht() for b in range(nb)]
            for b in range(nb):
                self.load_h(src, t0 // 128 + b, hts[b])
                self.norm_T(hts[b].t[:, :], hts[b].r, self.gb, xnT, 2 + b * 128)
            for c in range(NCH_FF):
                cs = min(128, DFF - c * 128)
                pu = self.psf()
                pg = self.psf()
                self.mm_group(pu.t[:cs, 0:TT + 2], pu.r,
                              [(Wup.t[:, kc, c * 128:c * 128 + cs], xnT.t[:, kc, 0:TT + 2], [Wup.r, xnT.r]) for kc in range(8)])
                self.mm_group(pg.t[:cs, 0:TT + 2], pg.r,
                              [(Wup.t[:, kc, DFF + c * 128:DFF + c * 128 + cs], xnT.t[:, kc, 0:TT + 2], [Wup.r, xnT.r]) for kc in range(8)])
                cv = self.cv[self.cv_i % len(self.cv)]
                self.cv_i += 1
                w = lambda j, c=c, cs=cs: fv.t[:cs, c * 4 + j:c * 4 + j + 1]
                S.op("dve", lambda e, cs=cs, pu=pu, cv=cv, w=w: e.tensor_scalar(
                    out=cv.t[:cs, 0:TT], in0=pu.t[:cs, 2:TT + 2], scalar1=w(2), scalar2=w(3), op0=ALU.mult, op1=ALU.add),
                    reads=[pu.r, fv.r], writes=[cv.r])
                S.op("dve", lambda e, cs=cs, pu=pu, cv=cv, w=w: e.scalar_tensor_tensor(
                    out=cv.t[:cs, 0:TT], in0=pu.t[:cs, 1:TT + 1], scalar=w(1), in1=cv.t[:cs, 0:TT], op0=ALU.mult, op1=ALU.add),
                    reads=[pu.r, fv.r, cv.r], writes=[cv.r])
                S.op("dve", lambda e, cs=cs, pu=pu, cv=cv, w=w: e.scalar_tensor_tensor(
                    out=cv.t[:cs, 0:TT], in0=pu.t[:cs, 0:TT], scalar=w(0), in1=cv.t[:cs, 0:TT], op0=ALU.mult, op1=ALU.add),
                    reads=[pu.r, fv.r, cv.r], writes=[cv.r])
                S.op("act", lambda e, cs=cs, cv=cv: e.activation(out=cv.t[:cs, TT:2 * TT], in_=cv.t[:cs, 0:TT], func=AF.Silu),
                     reads=[cv.r], writes=[cv.r])
                S.op("dve", lambda e, cs=cs, cv=cv, pg=pg, c=c: e.tensor_tensor(
                    out=actT.t[:cs, c, 0:TT], in0=cv.t[:cs, TT:2 * TT], in1=pg.t[:cs, 2:TT + 2], op=ALU.mult),
                    reads=[cv.r, pg.r], writes=[actT.r])
            S.op("pool", lambda e: e.tensor_copy(out=xnT.t[:, :, 0:2], in_=xnT.t[:, :, TT:TT + 2]), reads=[xnT.r], writes=[xnT.r])
            for b in range(nb):
                for half in range(2):
                    po = self.psf()
                    prs = []
                    for c in range(NCH_FF):
                        cs = min(128, DFF - c * 128)
                        prs.append((actT.t[:cs, c, b * 128:(b + 1) * 128], Wdn.t[:cs, c, half * 512:(half + 1) * 512], [actT.r, Wdn.r]))
                    self.mm_group(po.t[:, :], po.r, prs)
                    ht = hts[b]
                    S.op("dve", lambda e, half=half, po=po, ht=ht: e.tensor_tensor(
                        out=ht.t[:, half * 512:(half + 1) * 512], in0=ht.t[:, half * 512:(half + 1) * 512], in1=po.t[:, :], op=ALU.add),
                        reads=[ht.r, po.r], writes=[ht.r])
                self.store_h(dst, t0 // 128 + b, hts[b])

    def prep_mem(self):
        d = self.din
        self.load_small(d["mem_norm_g"][0:1, :].partition_broadcast(128), self.gb)
        for b in range(2):
            ht = self.new_ht()
            self.S.dma("sp", ht.t[:, :], d["mem"][b * 128:(b + 1) * 128, :], ht.r, writes=[ht.r])
            self.norm_T(ht.t[:, :], ht.r, self.gb, self.memnT, b * 128)

    def phase_xattn(self, l, src, dst):
        S = self.S
        d = self.din
        TT = 512
        self.arena_reset()
        Wq = self.arena_alloc("wq", 8, D)
        Wkv = self.arena_alloc("wkv", 8, 2 * D)
        Wo = self.arena_alloc("wo", 8, D)
        self.load_small(d["norm_xattn_g"][l:l + 1, :].partition_broadcast(128), self.gb)
        self.load_w(d["xattn_wkv"][l], Wkv, D, 2 * D)
        self.load_w(d["xattn_wq"][l], Wq, D, D)
        self.load_w(d["xattn_wo"][l], Wo, D, D)
        memnT = self.memnT
        KT = self.abuf("KT", [128, 8, 256])
        Vr = self.abuf("Vr", [128, 2, D])
        xnT = self.abuf("xnT_x", [128, 8, TT])
        qT = self.abuf("qT", [128, 8, TT])
        pTs = self.abuf("pTs", [128, 8, TT])
        oT = self.abuf("oT", [128, 8, TT])
        self.ex = [self.abuf("ex%d" % i, [128, 512], F32) for i in range(2)]
        self.pb = [self.abuf("pb%d" % i, [128, 512]) for i in range(2)]
        self.ex_i = self.pb_i = 0
        for c in range(8):
            pk = self.psf()
            self.mm_group(pk.t[:, 0:256], pk.r, [(Wkv.t[:, kc, c * 128:(c + 1) * 128], memnT.t[:, kc, :], [Wkv.r, memnT.r]) for kc in range(8)])
            S.op("act", lambda e, c=c, pk=pk: e.copy(out=KT.t[:, c, :], in_=pk.t[:, 0:256]), reads=[pk.r], writes=[KT.r])
        for mb in range(2):
            for half in range(2):
                pv = self.psf()
                self.mm_group(pv.t[:, :], pv.r, [(memnT.t[:, kc, mb * 128:(mb + 1) * 128], Wkv.t[:, kc, D + half * 512:D + (half + 1) * 512],
                                                  [Wkv.r, memnT.r]) for kc in range(8)])
                S.op("act", lambda e, mb=mb, half=half, pv=pv: e.copy(out=Vr.t[:, mb, half * 512:(half + 1) * 512], in_=pv.t[:, :]),
                     reads=[pv.r], writes=[Vr.r])
        for t0 in range(0, self.T, TT):
            nb = min(TT, self.T - t0) // 128
            ntok = nb * 128
            hts = []
            for b in range(nb):
                ht = self.new_ht()
                hts.append(ht)
                self.load_h(src, t0 // 128 + b, ht)
                self.norm_T(ht.t[:, :], ht.r, self.gb, xnT, b * 128)
            for c in range(8):
                pq = self.psf()
                self.mm_group(pq.t[:, 0:ntok], pq.r, [(Wq.t[:, kc, c * 128:(c + 1) * 128], xnT.t[:, kc, 0:ntok], [Wq.r, xnT.r]) for kc in range(8)])
                S.op("act", lambda e, c=c, pq=pq: e.mul(out=qT.t[:, c, 0:ntok], in_=pq.t[:, 0:ntok], mul=1.0 / 16.0),
                     reads=[pq.r], writes=[qT.r])
            for b in range(nb):
                for hp in range(2):
                    psc = self.psf()
                    for hh in range(2):
                        h = hp * 2 + hh
                        self.mm_group(psc.t[:, hh * 256:(hh + 1) * 256], psc.r,
                                      [(qT.t[:, 2 * h + dd, b * 128:(b + 1) * 128], KT.t[:, 2 * h + dd, :], [qT.r, KT.r]) for dd in range(2)])
                    sm = self.smx[self.smx_i % len(self.smx)]
                    self.smx_i += 1
                    p3 = psc.t[:, :].rearrange("p (h m) -> p h m", h=2)
                    S.op("dve", lambda e, sm=sm, p3=p3: e.tensor_reduce(out=sm.t[:, 0:2], in_=p3, axis=AX.X, op=ALU.max),
                         reads=[psc.r], writes=[sm.r])
                    ex = self.ex[self.ex_i % len(self.ex)]
                    self.ex_i += 1
                    e3 = ex.t[:, :].rearrange("p (h m) -> p h m", h=2)
                    S.op("dve", lambda e, sm=sm, p3=p3, e3=e3: e.tensor_tensor(out=e3, in0=p3, in1=sm.t[:, 0:2].unsqueeze(2).to_broadcast([128, 2, 256]),
                                                                           op=ALU.subtract), reads=[psc.r, sm.r], writes=[ex.r])
                    S.op("act", lambda e, ex=ex: e.activation(out=ex.t[:, :], in_=ex.t[:, :], func=AF.Exp), reads=[ex.r], writes=[ex.r])
                    S.op("dve", lambda e, sm=sm, e3=e3: e.tensor_reduce(out=sm.t[:, 2:4], in_=e3, axis=AX.X, op=ALU.add),
                         reads=[ex.r], writes=[sm.r])
                    S.op("dve", lambda e, sm=sm: e.reciprocal(out=sm.t[:, 4:6], in_=sm.t[:, 2:4]), reads=[sm.r], writes=[sm.r])
                    pb = self.pb[self.pb_i % len(self.pb)]
                    self.pb_i += 1
                    S.op("dve", lambda e, sm=sm, e3=e3, pb=pb: e.tensor_tensor(
                        out=pb.t[:, :].rearrange("p (h m) -> p h m", h=2), in0=e3,
                        in1=sm.t[:, 4:6].unsqueeze(2).to_broadcast([128, 2, 256]), op=ALU.mult), reads=[ex.r, sm.r], writes=[pb.r])
                    pT = self.psb()
                    for i in range(4):
                        S.op("pe", lambda e, i=i, pT=pT, pb=pb: e.transpose(pT.t[:, i * 128:(i + 1) * 128], pb.t[:, i * 128:(i + 1) * 128], self.identb.t[:]),
                             reads=[pb.r, self.identb.r], writes=[pT.r], signal=(i == 3))
                    S.op("act", lambda e, pT=pT, hp=hp, b=b: e.copy(out=pTs.t[:, hp * 4:(hp + 1) * 4, b * 128:(b + 1) * 128],
                                                                   in_=pT.t[:, 0:512].rearrange("p (k n) -> p k n", k=4)),
                         reads=[pT.r], writes=[pTs.r])
            for h in range(4):
                for dd in range(2):
                    po = self.psf()
                    self.mm_group(po.t[:, 0:ntok], po.r, [(Vr.t[:, mb, h * 256 + dd * 128:h * 256 + (dd + 1) * 128], pTs.t[:, h * 2 + mb, 0:ntok],
                                                           [Vr.r, pTs.r]) for mb in range(2)])
                    S.op("act", lambda e, h=h, dd=dd, po=po: e.copy(out=oT.t[:, 2 * h + dd, 0:ntok], in_=po.t[:, 0:ntok]), reads=[po.r], writes=[oT.r])
            for b in range(nb):
                ht = hts[b]
                for half in range(2):
                    pw = self.psf()
                    self.mm_group(pw.t[:, :], pw.r, [(oT.t[:, kc, b * 128:(b + 1) * 128], Wo.t[:, kc, half * 512:(half + 1) * 512], [oT.r, Wo.r]) for kc in range(8)])
                    S.op("dve", lambda e, ht=ht, half=half, pw=pw: e.tensor_tensor(
                        out=ht.t[:, half * 512:(half + 1) * 512], in0=ht.t[:, half * 512:(half + 1) * 512], in1=pw.t[:, :], op=ALU.add),
                        reads=[ht.r, pw.r], writes=[ht.r])
                self.store_h(dst, t0 // 128 + b, ht)

    def phase_final(self, src, dst):
        S = self.S
        d = self.din
        self.load_small(d["final_norm_g"][0:1, :].partition_broadcast(128), self.gb)
        for blk in range(self.NB):
            ht = self.new_ht()
            self.load_h(src, blk, ht)
            ss = self.rstd_of(ht.t[:, :], ht.r)
            S.op("dve", lambda e, ht=ht, ss=ss: e.scalar_tensor_tensor(out=ht.t[:, :], in0=ht.t[:, :], scalar=ss.t[:, 3:4], in1=self.gb.t[:],
                                                                 op0=ALU.mult, op1=ALU.mult), reads=[ht.r, ss.r, self.gb.r], writes=[ht.r])
            self.store_h(dst, blk, ht)

    def build(self):
        nc = self.nc
        T, NL = self.T, self.NL
        inp = self.inp
        inp("x", [T, D]); inp("mem", [256, D]); inp("mem_norm_g", [1, D]); inp("final_norm_g", [1, D])
        inp("norm_mix_g", [NL, D]); inp("norm_xattn_g", [NL, D]); inp("norm_ffn_g", [NL, D])
        inp("xattn_wq", [NL, D, D]); inp("xattn_wkv", [NL, D, 2 * D]); inp("xattn_wo", [NL, D, D])
        inp("ffn_w_up", [NL, D, 2 * DFF]); inp("ffn_w_down", [NL, DFF, D]); inp("ffn_vec", [NL, 128, NCH_FF * 4])
        inp("w_mix_out", [NL, D, D])
        self.decl_mixer_inputs()
        out = nc.dram_tensor("out", [T, D], F32, kind="ExternalOutput").ap()
        hbuf = nc.dram_tensor("hbuf", [T, D], F32, kind="Internal").ap()
        hbuf2 = nc.dram_tensor("hbuf2", [T, D], F32, kind="Internal").ap()
        self.hres = {id(hbuf): [Res("h%d" % i) for i in range(self.NB)], id(hbuf2): [Res("g%d" % i) for i in range(self.NB)],
                     id(out): [Res("o%d" % i) for i in range(self.NB)]}
        with contextlib.ExitStack() as st:
            self.st = st
            self.S = S = Sched(nc, st)
            self.psf_pool = [self.ps("psf%d" % i, [128, 512]) for i in range(6)]
            self.psb_pool = [self.ps("psb%d" % i, [128, 1024], BF16) for i in range(2)]
            self.psf_i = self.psb_i = 0
            self.arena = self.sb("arena", [128, ARENA_COLS], BF16)
            self.arena_live, self.arena_carry, self.arena_off = [], {}, 0
            self.stg = [self.sb("stg%d" % i, [128, 1024]) for i in range(2)]
            self.stg_i = self.cast_i = 0
            self.cast_both = False
            self.gb = self.sb("gb", [128, D])
            self.ht = [self.sb("ht%d" % i, [128, D]) for i in range(4)]
            self.ht_i = 0
            self.ss = [self.sb("ss%d" % i, [128, 4]) for i in range(4)]
            self.ss_i = 0
            self.junk = self.sb("junk", [128, D], BF16)
            self.xn = [self.sb("xn%d" % i, [128, D], BF16) for i in range(2)]
            self.xn_i = 0
            self.identb = self.sb("identb", [128, 128], BF16)
            self.identf = self.sb("identf", [128, 128])
            S.op("pool", lambda e: e.memset(self.identf.t[:], 0.0), writes=[self.identf.r])
            S.op("pool", lambda e: e.affine_select(out=self.identf.t[:], in_=self.identf.t[:], pattern=[[-1, 128]], compare_op=ALU.not_equal,
                                                   fill=1.0, base=0, channel_multiplier=1), reads=[self.identf.r], writes=[self.identf.r])
            S.op("pool", lambda e: e.tensor_copy(out=self.identb.t[:], in_=self.identf.t[:]), reads=[self.identf.r], writes=[self.identb.r])
            self.memnT = self.sb("memnT", [128, 8, 256], BF16)
            self.fvec = self.sb("fvec", [128, NCH_FF * 4])
            self.smx = [self.sb("smx%d" % i, [128, 6]) for i in range(2)]
            self.smx_i = 0
            self.alloc_mixer_bufs()

            plan = self.plan
            if plan is None:
                plan = []
                for l in range(NL):
                    plan += [("M", l), ("X", l), ("F", l)]
            if any(p[0] == "X" for p in plan):
                self.prep_mem()
            cur = self.din["x"]
            for (kind, l) in plan:
                x_in = cur is self.din["x"]
                tgt = hbuf if x_in else cur
                if kind == "M" and l % 2 == 0:
                    oth = hbuf2 if cur is hbuf else hbuf
                    self.phase_gla(l, cur, oth)
                    if "norwkv" not in os.environ.get("KDBG", ""):
                        self.phase_rwkv(l, cur, oth)
                    tgt = oth
                elif kind == "M":
                    self.phase_mixer_odd(l, cur, tgt)
                elif kind == "X":
                    self.phase_xattn(l, cur, tgt)
                elif kind == "F":
                    self.phase_ffn(l, cur, tgt)
                cur = tgt
            if self.final_norm:
                self.phase_final(cur, out)
            else:
                for blk in range(self.NB):
                    ht = self.new_ht()
                    self.load_h(cur, blk, ht)
                    self.store_h(out, blk, ht)
            finals = []
            for r in [t.r for t in self.ht]:
                if r.dsem is not None:
                    finals.append((r.dsem, r.dcount))
            S.finish(finals)
            self.ninstr = S.ninstr
        return nc

    def phase_mixer_odd(self, l, src, dst):
        S = self.S
        d = self.din
        j = l // 2
        self.arena_reset()
        Wcd = self.arena_alloc("wcd", 8, 2056)
        Wout = self.arena_alloc("wout", 8, D)
        Wg = self.arena_alloc("wg", 1, 1024)
        Wbd = self.arena_alloc("wbd", 1, 1536)
        self.load_small(d["norm_mix_g"][l:l + 1, :].partition_broadcast(128), self.gb)
        self.load_w(d["cd_w_in"][j], Wcd, D, 2056)
        self.load_w(d["w_mix_out"][l], Wout, D, D)
        self.load_w(d["lru_gw"][j], Wg, 128, 8 * 128)
        self.load_w(d["ml_bd"][j], Wbd, 128, 12 * 128)
        Wg2 = T_(Wg.t[:, 0, :], Wg.r)
        Wbd2 = T_(Wbd.t[:, 0, :], Wbd.r)
        lv = self.abuf("lru_vec", [128, 4, 8], F32)
        mv = self.abuf("ml_vec", [128, 4, 5], F32)
        mrow = self.abuf("ml_row", [128, 8 + 512], F32)
        self.load_small(d["lru_vec"][j], lv)
        self.load_small(d["ml_vec"][j], mv)
        self.load_small(d["ml_row"][j], mrow)
        c8 = self.abuf("c8", [128, 4, 2], F32)
        tmp4 = self.abuf("tmp4", [128, 4], F32)
        S.op("act", lambda e: e.activation(out=tmp4.t[:, :], in_=lv.t[:, :, 7], func=AF.Exp, scale=-1.0), reads=[lv.r], writes=[tmp4.r])
        S.op("act", lambda e: e.activation(out=tmp4.t[:, :], in_=tmp4.t[:, :], func=AF.Ln, bias=1.0, scale=1.0), reads=[tmp4.r], writes=[tmp4.r])
        S.op("dve", lambda e: e.tensor_scalar(out=c8.t[:, :, 0], in0=tmp4.t[:, :], scalar1=-8.0, scalar2=None, op0=ALU.mult), reads=[tmp4.r], writes=[c8.r])
        S.op("dve", lambda e: e.tensor_scalar(out=c8.t[:, :, 1], in0=tmp4.t[:, :], scalar1=-16.0, scalar2=None, op0=ALU.mult), reads=[tmp4.r], writes=[c8.r])
        HL = 4
        xnT = self.abuf("xnT_m", [128, 8, HL + 128])
        mixedT = self.abuf("mixedT", [128, 8, 128])
        lcar = self.abuf("lcar", [128, 4], F32)
        Cst = self.abuf("Cst", [128, 4, 132], F32)
        Cb = self.abuf("Cb", [128, 4, 132])
        S.op("pool", lambda e: e.memset(xnT.t[:, :, 0:HL], 0.0), writes=[xnT.r])
        S.op("pool", lambda e: e.memset(lcar.t[:, :], 0.0), writes=[lcar.r])
        S.op("pool", lambda e: e.memset(Cst.t[:, :, :], 0.0), writes=[Cst.r])
        S.op("pool", lambda e: e.memset(Cb.t[:, :, :], 0.0), writes=[Cb.r])
        NT = 4
        f32t = [self.abuf("mo_f%d" % i, [128, 128], F32) for i in range(12)]
        bft = [self.abuf("mo_b%d" % i, [128, 128]) for i in range(10)]
        vaug = [self.abuf("vaug%d" % i, [128, 136]) for i in range(2)]
        nrow = [self.abuf("nrow%d" % i, [128, 132], F32) for i in range(2)]
        yrow = self.abuf("yrow", [128, 512])
        gt = self.abuf("gt", [128, 24], F32)
        opre = self.abuf("opre", [128, 512], F32)
        cnt = [0, 0, 0, 0]

        pools = {"F": f32t[0:8], "B": bft[0:2]}

        def F():
            cnt[0] += 1
            return pools["F"][cnt[0] % len(pools["F"])]

        def B():
            cnt[1] += 1
            return pools["B"][cnt[1] % len(pools["B"])]

        for blk in range(self.NB):
            ht = self.new_ht()
            self.load_h(src, blk, ht)
            self.norm_T(ht.t[:, :], ht.r, self.gb, xnT, HL)
            xw = xnT.t[:, :, 0:HL + 128]
            xc_ = xnT.t[:, :, HL:HL + 128]
            DBG = os.environ.get("KDBG", "")
            if "nolru" in DBG:
                S.op("pool", lambda e: e.memset(mixedT.t[:, 0:4, :], 0.0), writes=[mixedT.r])
            recA = []
            for n in range(0 if "nolru" in DBG else 4):
                S.begin_record()
                self.psf_sub = self.psf_pool[0:3]
                pools["F"], pools["B"] = f32t[0:8], bft[0:2]
                px = self.psf()
                self.mm_group(px.t[:, 0:HL + 128], px.r, [(Wcd.t[:, kc, n * 128:(n + 1) * 128], xnT.t[:, kc, 0:HL + 128], [Wcd.r, xnT.r]) for kc in range(8)])
                xc = F()
                w = lambda q, n=n: lv.t[:, n, q:q + 1]
                S.op("dve", lambda e, px=px, xc=xc, w=w: e.tensor_scalar(out=xc.t[:, :], in0=px.t[:, HL:HL + 128], scalar1=w(3), scalar2=w(4), op0=ALU.mult, op1=ALU.add),
                     reads=[px.r, lv.r], writes=[xc.r])
                for q in range(3):
                    S.op("dve", lambda e, px=px, xc=xc, w=w, q=q: e.scalar_tensor_tensor(out=xc.t[:, :], in0=px.t[:, HL - 3 + q:HL - 3 + q + 128], scalar=w(q), in1=xc.t[:, :],
                                                                                       op0=ALU.mult, op1=ALU.add), reads=[px.r, lv.r, xc.r], writes=[xc.r])
                xcb = B()
                S.op("act", lambda e, xc=xc, xcb=xcb: e.copy(out=xcb.t[:, :], in_=xc.t[:, :]), reads=[xc.r], writes=[xcb.r])
                pr = self.psf()
                self.mm_group(pr.t[:, 0:128], pr.r, [(Wg2.t[:, (0 * 4 + n) * 128:(0 * 4 + n + 1) * 128], xcb.t[:, :], [Wg.r, xcb.r])])
                self.mm_group(pr.t[:, 128:256], pr.r, [(Wg2.t[:, (1 * 4 + n) * 128:(1 * 4 + n + 1) * 128], xcb.t[:, :], [Wg.r, xcb.r])])
                rg = F(); ig = F()
                S.op("act", lambda e, pr=pr, rg=rg, w=w: e.activation(out=rg.t[:, :], in_=pr.t[:, 0:128], func=AF.Sigmoid, bias=w(5), scale=1.0),
                     reads=[pr.r, lv.r], writes=[rg.r])
                S.op("act", lambda e, pr=pr, ig=ig, w=w: e.activation(out=ig.t[:, :], in_=pr.t[:, 128:256], func=AF.Sigmoid, bias=w(6), scale=1.0),
                     reads=[pr.r, lv.r], writes=[ig.r])
                a = F(); a2 = F()
                S.op("act", lambda e, rg=rg, a=a, n=n: e.activation(out=a.t[:, :], in_=rg.t[:, :], func=AF.Exp, scale=c8.t[:, n, 0:1]), reads=[rg.r, c8.r], writes=[a.r])
                S.op("act", lambda e, rg=rg, a2=a2, n=n: e.activation(out=a2.t[:, :], in_=rg.t[:, :], func=AF.Exp, scale=c8.t[:, n, 1:2]), reads=[rg.r, c8.r], writes=[a2.r])
                S.op("dve", lambda e, a2=a2: e.tensor_scalar(out=a2.t[:, :], in0=a2.t[:, :], scalar1=-1.0, scalar2=1.0, op0=ALU.mult, op1=ALU.add), reads=[a2.r], writes=[a2.r])
                S.op("act", lambda e, a2=a2: e.activation(out=a2.t[:, :], in_=a2.t[:, :], func=AF.Sqrt), reads=[a2.r], writes=[a2.r])
                S.op("dve", lambda e, ig=ig, xc=xc: e.tensor_tensor(out=ig.t[:, :], in0=ig.t[:, :], in1=xc.t[:, :], op=ALU.mult), reads=[ig.r, xc.r], writes=[ig.r])
                S.op("dve", lambda e, ig=ig, a2=a2: e.tensor_tensor(out=ig.t[:, :], in0=ig.t[:, :], in1=a2.t[:, :], op=ALU.mult), reads=[ig.r, a2.r], writes=[ig.r])
                hl = F()
                S.op("dve", lambda e, hl=hl, a=a, ig=ig, n=n: e.tensor_tensor_scan(out=hl.t[:, :], data0=a.t[:, :], data1=ig.t[:, :], initial=lcar.t[:, n:n + 1],
                                                                                 op0=ALU.mult, op1=ALU.add), reads=[a.r, ig.r, lcar.r], writes=[hl.r])
                S.op("pool", lambda e, hl=hl, n=n: e.tensor_copy(out=lcar.t[:, n:n + 1], in_=hl.t[:, 127:128]), reads=[hl.r], writes=[lcar.r])
                pg = self.psf()
                self.mm_group(pg.t[:, 0:128], pg.r, [(Wcd.t[:, kc, 512 + n * 128:512 + (n + 1) * 128], xnT.t[:, kc, HL:HL + 128], [Wcd.r, xnT.r]) for kc in range(8)])
                g = F(); g3 = F()
                S.op("act", lambda e, pg=pg, g=g: e.copy(out=g.t[:, :], in_=pg.t[:, 0:128]), reads=[pg.r], writes=[g.r])
                S.op("dve", lambda e, g=g, g3=g3: e.tensor_tensor(out=g3.t[:, :], in0=g.t[:, :], in1=g.t[:, :], op=ALU.mult), reads=[g.r], writes=[g3.r])
                S.op("dve", lambda e, g3=g3: e.tensor_scalar(out=g3.t[:, :], in0=g3.t[:, :], scalar1=0.044715, scalar2=1.0, op0=ALU.mult, op1=ALU.add), reads=[g3.r], writes=[g3.r])
                S.op("dve", lambda e, g=g, g3=g3: e.tensor_tensor(out=g3.t[:, :], in0=g3.t[:, :], in1=g.t[:, :], op=ALU.mult), reads=[g.r, g3.r], writes=[g3.r])
                S.op("act", lambda e, g3=g3: e.activation(out=g3.t[:, :], in_=g3.t[:, :], func=AF.Sigmoid, scale=1.5957691216), reads=[g3.r], writes=[g3.r])
                S.op("dve", lambda e, g=g, g3=g3: e.tensor_tensor(out=g3.t[:, :], in0=g3.t[:, :], in1=g.t[:, :], op=ALU.mult), reads=[g.r, g3.r], writes=[g3.r])
                S.op("dve", lambda e, hl=hl, g3=g3, n=n: e.tensor_tensor(out=mixedT.t[:, n, :], in0=g3.t[:, :], in1=hl.t[:, :], op=ALU.mult), reads=[hl.r, g3.r], writes=[mixedT.r])
                recA.append(S.end_record())
                self.psf_sub = None
            pools["F"], pools["B"] = f32t[8:12], bft[2:10]
            if "nomlstm" in DBG:
                for rA in recA:
                    S.emit_interleaved([rA])
                S.op("pool", lambda e: e.memset(mixedT.t[:, 4:8, :], 0.0), writes=[mixedT.r])
                S.op("pool", lambda e: e.tensor_copy(out=xnT.t[:, :, 0:HL], in_=xnT.t[:, :, 128:128 + HL]), reads=[xnT.r], writes=[xnT.r])
                self.out_proj(mixedT, Wout, ht, dst, blk)
                continue
            pgt = self.psf()
            self.mm_group(pgt.t[:, 0:8], pgt.r, [(xnT.t[:, kc, HL:HL + 128], Wcd.t[:, kc, 2048:2056], [Wcd.r, xnT.r]) for kc in range(8)])
            S.op("dve", lambda e, pgt=pgt: e.tensor_tensor(out=gt.t[:, 0:8], in0=pgt.t[:, 0:8], in1=mrow.t[:, 0:8], op=ALU.add), reads=[pgt.r, mrow.r], writes=[gt.r])
            S.op("act", lambda e: e.activation(out=gt.t[:, 8:12], in_=gt.t[:, 4:8], func=AF.Exp, scale=-1.0), reads=[gt.r], writes=[gt.r])
            S.op("act", lambda e: e.activation(out=gt.t[:, 8:12], in_=gt.t[:, 8:12], func=AF.Ln, bias=1.0, scale=1.0), reads=[gt.r], writes=[gt.r])
            pc = self.psf()
            self.mm_group(pc.t[:, 0:4], pc.r, [(self.Uf.t[:, :], gt.t[:, 8:12], [self.Uf.r, gt.r])])
            self.mm_group(pc.t[:, 4:8], pc.r, [(self.onesf.t[:, :], gt.t[:, 8:12], [self.onesf.r, gt.r])])
            S.op("dve", lambda e, pc=pc: e.tensor_tensor(out=gt.t[:, 12:16], in0=pc.t[:, 0:4], in1=gt.t[:, 0:4], op=ALU.add), reads=[pc.r, gt.r], writes=[gt.r])
            S.op("act", lambda e: e.activation(out=gt.t[:, 12:16], in_=gt.t[:, 12:16], func=AF.Exp), reads=[gt.r], writes=[gt.r])
            S.op("act", lambda e, pc=pc: e.activation(out=gt.t[:, 16:24], in_=pc.t[:, 0:8], func=AF.Exp, scale=-1.0), reads=[pc.r, gt.r], writes=[gt.r])
            po = self.psf()
            self.mm_group(po.t[:, :], po.r, [(xnT.t[:, kc, HL:HL + 128], Wcd.t[:, kc, 1536:2048], [Wcd.r, xnT.r]) for kc in range(8)])
            S.op("act", lambda e, po=po: e.activation(out=opre.t[:, :], in_=po.t[:, :], func=AF.Sigmoid), reads=[po.r], writes=[opre.r])
            S.op("dve", lambda e: e.tensor_tensor(out=opre.t[:, :], in0=opre.t[:, :], in1=mrow.t[:, 8:520], op=ALU.mult), reads=[opre.r, mrow.r], writes=[opre.r])
            for n in range(4):
                S.begin_record()
                self.psf_sub = self.psf_pool[3:6]
                px = self.psf()
                self.mm_group(px.t[:, 0:HL + 128], px.r, [(Wcd.t[:, kc, 1024 + n * 128:1024 + (n + 1) * 128], xnT.t[:, kc, 0:HL + 128], [Wcd.r, xnT.r]) for kc in range(8)])
                xc = F()
                w = lambda q, n=n: mv.t[:, n, q:q + 1]
                S.op("dve", lambda e, px=px, xc=xc, w=w: e.tensor_scalar(out=xc.t[:, :], in0=px.t[:, HL:HL + 128], scalar1=w(3), scalar2=w(4), op0=ALU.mult, op1=ALU.add),
                     reads=[px.r, mv.r], writes=[xc.r])
                for q in range(3):
                    S.op("dve", lambda e, px=px, xc=xc, w=w, q=q: e.scalar_tensor_tensor(out=xc.t[:, :], in0=px.t[:, HL - 3 + q:HL - 3 + q + 128], scalar=w(q), in1=xc.t[:, :],
                                                                                       op0=ALU.mult, op1=ALU.add), reads=[px.r, mv.r, xc.r], writes=[xc.r])
                xcm = B(); mxb = B()
                S.op("act", lambda e, xc=xc, xcm=xcm: e.activation(out=xcm.t[:, :], in_=xc.t[:, :], func=AF.Silu), reads=[xc.r], writes=[xcm.r])
                S.op("act", lambda e, px=px, mxb=mxb: e.copy(out=mxb.t[:, :], in_=px.t[:, HL:HL + 128]), reads=[px.r], writes=[mxb.r])
                wq = Wbd2.t[:, (0 * 4 + n) * 128:(0 * 4 + n + 1) * 128]
                wk = Wbd2.t[:, (1 * 4 + n) * 128:(1 * 4 + n + 1) * 128]
                wv = Wbd2.t[:, (2 * 4 + n) * 128:(2 * 4 + n + 1) * 128]
                pqk = self.psf()
                self.mm_group(pqk.t[:, 0:128], pqk.r, [(wq, xcm.t[:, :], [Wbd.r, xcm.r])])
                self.mm_group(pqk.t[:, 128:256], pqk.r, [(wk, xcm.t[:, :], [Wbd.r, xcm.r])])
                self.mm_group(pqk.t[:, 256:384], pqk.r, [(xcm.t[:, :], wk, [Wbd.r, xcm.r])])
                self.mm_group(pqk.t[:, 384:512], pqk.r, [(mxb.t[:, :], wv, [Wbd.r, mxb.r])])
                qT = B(); kT = B(); kr = B()
                S.op("act", lambda e, pqk=pqk, qT=qT: e.copy(out=qT.t[:, :], in_=pqk.t[:, 0:128]), reads=[pqk.r], writes=[qT.r])
                S.op("act", lambda e, pqk=pqk, kT=kT: e.mul(out=kT.t[:, :], in_=pqk.t[:, 128:256], mul=128.0 ** -0.5), reads=[pqk.r], writes=[kT.r])
                S.op("act", lambda e, pqk=pqk, kr=kr: e.mul(out=kr.t[:, :], in_=pqk.t[:, 256:384], mul=128.0 ** -0.5), reads=[pqk.r], writes=[kr.r])
                va = vaug[cnt[2] % 2]; cnt[2] += 1
                S.op("act", lambda e, pqk=pqk, va=va, n=n: e.activation(out=va.t[:, 0:128], in_=pqk.t[:, 384:512], func=AF.Identity, bias=0.0, scale=gt.t[:, 12 + n:13 + n]),
                     reads=[pqk.r, gt.r], writes=[va.r])
                S.op("dve", lambda e, va=va, n=n: e.tensor_tensor(out=va.t[:, 128:136], in0=self.onesf.t[:, 0:8], in1=gt.t[:, 12 + n:13 + n].to_broadcast([128, 8]), op=ALU.mult),
                     reads=[gt.r, self.onesf.r], writes=[va.r])
                psT = self.psf()
                self.mm_group(psT.t[:, 0:128], psT.r, [(kT.t[:, :], qT.t[:, :], [kT.r, qT.r])])
                scT = B()
                S.op("dve", lambda e, psT=psT, scT=scT: e.tensor_tensor(out=scT.t[:, :], in0=psT.t[:, 0:128], in1=self.Uf.t[:, :], op=ALU.mult),
                     reads=[psT.r, self.Uf.r], writes=[scT.r])
                pn = self.psf()
                self.mm_group(pn.t[:, 0:130], pn.r, [(scT.t[:, :], va.t[:, 0:130], [scT.r, va.r]), (qT.t[:, :], Cb.t[:, n, 0:130], [qT.r, Cb.r])])
                nr = nrow[cnt[3] % 2]; cnt[3] += 1
                S.op("dve", lambda e, pn=pn, nr=nr, n=n: e.tensor_tensor(out=nr.t[:, 0:129], in0=pn.t[:, 0:129], in1=gt.t[:, 16 + n:17 + n].to_broadcast([128, 129]), op=ALU.mult),
                     reads=[pn.r, gt.r], writes=[nr.r])
                S.op("act", lambda e, nr=nr: e.activation(out=nr.t[:, 129:130], in_=nr.t[:, 128:129], func=AF.Abs), reads=[nr.r], writes=[nr.r])
                S.op("dve", lambda e, nr=nr: e.tensor_scalar(out=nr.t[:, 129:130], in0=nr.t[:, 129:130], scalar1=1.0, scalar2=None, op0=ALU.max),
                     reads=[nr.r], writes=[nr.r])
                S.op("dve", lambda e, nr=nr: e.reciprocal(out=nr.t[:, 130:131], in_=nr.t[:, 129:130]), reads=[nr.r], writes=[nr.r])
                hh = F()
                S.op("dve", lambda e, nr=nr, hh=hh: e.tensor_tensor(out=hh.t[:, :], in0=nr.t[:, 0:128], in1=nr.t[:, 130:131].to_broadcast([128, 128]), op=ALU.mult),
                     reads=[nr.r], writes=[hh.r])
                jk = F()
                S.op("act", lambda e, hh=hh, jk=jk, nr=nr: e.activation(out=jk.t[:, :], in_=hh.t[:, :], func=AF.Square, accum_out=nr.t[:, 131:132]),
                     reads=[hh.r], writes=[jk.r, nr.r])
                S.op("dve", lambda e, nr=nr: e.tensor_scalar(out=nr.t[:, 129:130], in0=nr.t[:, 131:132], scalar1=1.0 / 128.0, scalar2=EPS, op0=ALU.mult, op1=ALU.add),
                     reads=[nr.r], writes=[nr.r])
                S.op("act", lambda e, nr=nr: e.activation(out=nr.t[:, 129:130], in_=nr.t[:, 129:130], func=AF.Sqrt), reads=[nr.r], writes=[nr.r])
                S.op("dve", lambda e, nr=nr: e.reciprocal(out=nr.t[:, 130:131], in_=nr.t[:, 129:130]), reads=[nr.r], writes=[nr.r])
                S.op("dve", lambda e, nr=nr, hh=hh, n=n: e.scalar_tensor_tensor(out=yrow.t[:, n * 128:(n + 1) * 128], in0=hh.t[:, :], scalar=nr.t[:, 130:131],
                                                                             in1=opre.t[:, n * 128:(n + 1) * 128], op0=ALU.mult, op1=ALU.mult),
                     reads=[nr.r, hh.r, opre.r], writes=[yrow.r])
                pC = self.psf()
                self.mm_group(pC.t[:, 0:130], pC.r, [(kr.t[:, :], va.t[:, 0:130], [kr.r, va.r])])
                S.op("dve", lambda e, pC=pC, n=n: e.tensor_tensor(out=Cst.t[:, n, 0:129], in0=Cst.t[:, n, 0:129], in1=pC.t[:, 0:129], op=ALU.add),
                     reads=[pC.r, Cst.r], writes=[Cst.r])
                S.op("dve", lambda e, n=n: e.tensor_tensor(out=Cst.t[:, n, 0:129], in0=Cst.t[:, n, 0:129], in1=gt.t[:, 20 + n:21 + n].to_broadcast([128, 129]), op=ALU.mult),
                     reads=[Cst.r, gt.r], writes=[Cst.r])
                S.op("act", lambda e, n=n: e.copy(out=Cb.t[:, n, 0:130], in_=Cst.t[:, n, 0:130]), reads=[Cst.r], writes=[Cb.r])
                recB = S.end_record()
                self.psf_sub = None
                S.emit_interleaved([recA[n], recB] if n < len(recA) else [recB])
            pT = self.psb()
            for n in range(4):
                S.op("pe", lambda e, n=n, pT=pT: e.transpose(pT.t[:, n * 128:(n + 1) * 128], yrow.t[:, n * 128:(n + 1) * 128], self.identb.t[:]),
                     reads=[yrow.r, self.identb.r], writes=[pT.r], signal=(n == 3))
            S.op("act", lambda e, pT=pT: e.copy(out=mixedT.t[:, 4:8, :], in_=pT.t[:, 0:512].rearrange("p (k n) -> p k n", k=4)), reads=[pT.r], writes=[mixedT.r])
            S.op("pool", lambda e: e.tensor_copy(out=xnT.t[:, :, 0:HL], in_=xnT.t[:, :, 128:128 + HL]), reads=[xnT.r], writes=[xnT.r])
            self.out_proj(mixedT, Wout, ht, dst, blk)

    def out_proj(self, mixedT, Wout, ht, dst, blk, nk=8):
        S = self.S
        for half in range(2):
            pw = self.psf()
            self.mm_group(pw.t[:, :], pw.r, [(mixedT.t[:, kc, :], Wout.t[:, kc, half * 512:(half + 1) * 512], [mixedT.r, Wout.r]) for kc in range(nk)])
            S.op("dve", lambda e, ht=ht, half=half, pw=pw: e.tensor_tensor(
                out=ht.t[:, half * 512:(half + 1) * 512], in0=ht.t[:, half * 512:(half + 1) * 512], in1=pw.t[:, :], op=ALU.add),
                reads=[ht.r, pw.r], writes=[ht.r])
        self.store_h(dst, blk, ht)

    def phase_mixer(self, l, src, dst):
        if l % 2 == 1:
            self.phase_mixer_odd(l, src, dst)
        else:
            self.phase_mixer_even(l, src, dst)

    def decl_mixer_inputs(self):
        NE, NO = (self.NL + 1) // 2, self.NL // 2
        inp = self.inp
        inp("cd_w_in", [NO, D, 2056]); inp("lru_gw", [NO, 128, 1024]); inp("ml_bd", [NO, 128, 12 * 128])
        inp("lru_vec", [NO, 128, 32]); inp("ml_vec", [NO, 128, 20]); inp("ml_row", [NO, 128, 520])
        self.decl_even_inputs(NE)

    def alloc_mixer_bufs(self):
        S = self.S
        self.Uf = self.sb("Uf", [128, 128])
        self.onesf = self.sb("onesf", [128, 128])
        S.op("pool", lambda e: e.memset(self.onesf.t[:], 1.0), writes=[self.onesf.r])
        S.op("pool", lambda e: e.memset(self.Uf.t[:], 1.0), writes=[self.Uf.r])
        S.op("pool", lambda e: e.affine_select(out=self.Uf.t[:], in_=self.Uf.t[:], pattern=[[1, 128]], compare_op=ALU.is_ge,
                                               fill=0.0, base=0, channel_multiplier=-1), reads=[self.Uf.r], writes=[self.Uf.r])
        self.alloc_even_bufs()

    def decl_even_inputs(self, NE):
        inp = self.inp
        inp("ab_w_in", [NE, D, 3344]); inp("gla_wa", [NE, 16, 256]); inp("gla_vec", [NE, 64, 4]); inp("gla_ng", [NE, 128, 4])
        inp("rw_w2", [NE, 64, 512]); inp("rw_a2", [NE, 64, 512]); inp("rw_g2", [NE, 128, 512])
        inp("rw_vec", [NE, 64, 64]); inp("rw_mu_s", [NE, 128, 4]); inp("rw_row", [NE, 128, 1024])

    def alloc_even_bufs(self):
        S = self.S
        self.Us = self.sb("Us", [128, 128])
        self.Ls = self.sb("Ls", [128, 128])
        S.op("pool", lambda e: e.memset(self.Us.t[:], 1.0), writes=[self.Us.r])
        S.op("pool", lambda e: e.affine_select(out=self.Us.t[:], in_=self.Us.t[:], pattern=[[1, 128]], compare_op=ALU.is_gt,
                                               fill=0.0, base=0, channel_multiplier=-1), reads=[self.Us.r], writes=[self.Us.r])
        S.op("pool", lambda e: e.memset(self.Ls.t[:], 1.0), writes=[self.Ls.r])
        S.op("pool", lambda e: e.affine_select(out=self.Ls.t[:], in_=self.Ls.t[:], pattern=[[-1, 128]], compare_op=ALU.is_gt,
                                               fill=0.0, base=0, channel_multiplier=1), reads=[self.Ls.r], writes=[self.Ls.r])

    def phase_gla(self, l, src, dst):
        S = self.S
        d = self.din
        j = l // 2
        DBG = os.environ.get("KDBG", "")
        self.arena_reset()
        Wg = self.arena_alloc("w_gla", 8, 1552)
        Wout = self.arena_alloc("wout_a", 4, D)
        Wa = self.arena_alloc("w_alpha", 1, 256)
        self.load_small(d["norm_mix_g"][l:l + 1, :].partition_broadcast(128), self.gb)
        self.load_w(d["ab_w_in"][j], Wg, D, 1552)
        self.load_w(d["w_mix_out"][l][0:512, :], Wout, 512, D)
        self.load_w(d["gla_wa"][j], Wa, 16, 256)
        gvec = self.abuf("gla_vec", [64, 4], F32)
        gng = self.abuf("gla_ng", [128, 4], F32)
        self.load_small(d["gla_vec"][j], gvec)
        self.load_small(d["gla_ng"][j], gng)
        xnT = self.abuf("xnT_e", [128, 8, 128])
        mixedT = self.abuf("mixedT", [128, 4, 128])
        Sg = self.abuf("Sg", [64, 4, 128], F32)
        Sgb = self.abuf("Sgb", [64, 4, 128])
        S.op("pool", lambda e: e.memset(Sg.t[:, :, :], 0.0), writes=[Sg.r])
        S.op("pool", lambda e: e.memset(Sgb.t[:, :, :], 0.0), writes=[Sgb.r])
        gq = self.abuf("gq", [64, 4, 128], F32)
        gk = self.abuf("gk", [64, 4, 128], F32)
        gx = self.abuf("gx", [64, 4, 128], F32)
        gcs = self.abuf("gcs", [64, 4, 128], F32)
        ge = self.abuf("ge", [64, 4, 128], F32)
        gsd = self.abuf("gsd", [64, 8], F32)
        qdec = self.abuf("qdec", [64, 4, 128])
        kinv = self.abuf("kinv", [64, 4, 128])
        kend = self.abuf("kend", [64, 4, 128])
        kendr = self.abuf("kendr", [128, 256])
        vrow = self.abuf("g_vrow", [128, 512])
        alrT = self.abuf("alrT", [16, 128])
        scT = self.abuf("g_scT", [128, 4, 128])
        osq = self.abuf("g_osq", [128, 512], F32)
        orst = self.abuf("g_orst", [128, 512], F32)
        gsil = self.abuf("g_sil", [128, 4, 128], F32)
        for blk in range(self.NB):
            ht = self.new_ht()
            self.load_h(src, blk, ht)
            xn_cur = self.norm_T(ht.t[:, :], ht.r, self.gb, xnT, 0)
            xw = [Wg.r, xnT.r]
            pq = self.psf(); pk = self.psf()
            for h in range(4):
                self.mm_group(pq.t[0:64, h * 128:(h + 1) * 128], pq.r, [(Wg.t[:, kc, h * 64:(h + 1) * 64], xnT.t[:, kc, :], xw) for kc in range(8)])
            for h in range(4):
                self.mm_group(pk.t[0:64, h * 128:(h + 1) * 128], pk.r, [(Wg.t[:, kc, 256 + h * 64:256 + (h + 1) * 64], xnT.t[:, kc, :], xw) for kc in range(8)])
            S.op("act", lambda e, pq=pq: e.copy(out=gq.t[:, :, :], in_=pq.t[0:64, :].rearrange("p (h n) -> p h n", h=4)), reads=[pq.r], writes=[gq.r])
            S.op("act", lambda e, pk=pk: e.copy(out=gk.t[:, :, :], in_=pk.t[0:64, :].rearrange("p (h n) -> p h n", h=4)), reads=[pk.r], writes=[gk.r])
            pv = self.psf()
            self.mm_group(pv.t[:, :], pv.r, [(xnT.t[:, kc, :], Wg.t[:, kc, 512:1024], xw) for kc in range(8)])
            S.op("act", lambda e, pv=pv: e.copy(out=vrow.t[:, :], in_=pv.t[:, :]), reads=[pv.r], writes=[vrow.r])
            pg = self.psf()
            for h in range(4):
                self.mm_group(pg.t[:, h * 128:(h + 1) * 128], pg.r, [(Wg.t[:, kc, 1024 + h * 128:1024 + (h + 1) * 128], xnT.t[:, kc, :], xw) for kc in range(8)])
            S.op("act", lambda e, pg=pg: e.activation(out=gsil.t[:, :, :], in_=pg.t[:, :].rearrange("p (h n) -> p h n", h=4), func=AF.Silu), reads=[pg.r], writes=[gsil.r])
            pa = self.psf()
            self.mm_group(pa.t[0:16, 0:128], pa.r, [(Wg.t[:, kc, 1536:1552], xnT.t[:, kc, :], xw) for kc in range(8)])
            S.op("act", lambda e, pa=pa: e.copy(out=alrT.t[:, :], in_=pa.t[0:16, 0:128]), reads=[pa.r], writes=[alrT.r])
            px = self.psf()
            for h in range(4):
                self.mm_group(px.t[0:64, h * 128:(h + 1) * 128], px.r, [(Wa.t[0:16, 0, h * 64:(h + 1) * 64], alrT.t[:, :], [Wa.r, alrT.r])])
            S.op("dve", lambda e, px=px: e.tensor_tensor(out=gx.t[:, :, :], in0=px.t[0:64, :].rearrange("p (h n) -> p h n", h=4),
                                                         in1=gvec.t[:, 0:4].unsqueeze(2).to_broadcast([64, 4, 128]), op=ALU.add), reads=[px.r, gvec.r], writes=[gx.r])
            S.op("act", lambda e: e.activation(out=gx.t[:, :, :], in_=gx.t[:, :, :], func=AF.Exp, scale=-1.0), reads=[gx.r], writes=[gx.r])
            S.op("act", lambda e: e.activation(out=gx.t[:, :, :], in_=gx.t[:, :, :], func=AF.Ln, bias=1.0, scale=1.0), reads=[gx.r], writes=[gx.r])
            for h in range(4):
                S.op("dve", lambda e, h=h: e.tensor_tensor_scan(out=gcs.t[:, h, :], data0=self.onesf.t[0:64, :], data1=gx.t[:, h, :], initial=0.0,
                                                                op0=ALU.mult, op1=ALU.add), reads=[gx.r, self.onesf.r], writes=[gcs.r])
            S.op("act", lambda e: e.activation(out=ge.t[:, :, :], in_=gcs.t[:, :, :], func=AF.Exp, scale=-1.0 / 16.0), reads=[gcs.r], writes=[ge.r])
            S.op("dve", lambda e: e.scalar_tensor_tensor(out=qdec.t[:, :, :], in0=gq.t[:, :, :], scalar=0.125, in1=ge.t[:, :, :], op0=ALU.mult, op1=ALU.mult),
                 reads=[gq.r, ge.r], writes=[qdec.r])
            S.op("act", lambda e: e.copy(out=gsd.t[:, 0:4], in_=ge.t[:, :, 127]), reads=[ge.r], writes=[gsd.r])
            S.op("act", lambda e: e.activation(out=ge.t[:, :, :], in_=gcs.t[:, :, :], func=AF.Exp, scale=1.0 / 16.0), reads=[gcs.r], writes=[ge.r])
            S.op("dve", lambda e: e.tensor_tensor(out=kinv.t[:, :, :], in0=gk.t[:, :, :], in1=ge.t[:, :, :], op=ALU.mult), reads=[gk.r, ge.r], writes=[kinv.r])
            S.op("dve", lambda e: e.tensor_tensor(out=gx.t[:, :, :], in0=gcs.t[:, :, :], in1=gcs.t[:, :, 127:128].to_broadcast([64, 4, 128]), op=ALU.subtract),
                 reads=[gcs.r], writes=[gx.r])
            S.op("act", lambda e: e.activation(out=ge.t[:, :, :], in_=gx.t[:, :, :], func=AF.Exp, scale=1.0 / 16.0), reads=[gx.r], writes=[ge.r])
            S.op("dve", lambda e: e.tensor_tensor(out=kend.t[:, :, :], in0=gk.t[:, :, :], in1=ge.t[:, :, :], op=ALU.mult), reads=[gk.r, ge.r], writes=[kend.r])
            pT = self.psb()
            for h in range(4):
                S.op("pe", lambda e, h=h, pT=pT: e.transpose(pT.t[:, h * 64:(h + 1) * 64], kend.t[:, h, :], self.identb.t[0:64, 0:64]),
                     reads=[kend.r, self.identb.r], writes=[pT.r], signal=(h == 3))
            S.op("act", lambda e, pT=pT: e.copy(out=kendr.t[:, :], in_=pT.t[:, 0:256]), reads=[pT.r], writes=[kendr.r])
            psc = self.psf()
            for h in range(4):
                self.mm_group(psc.t[:, h * 128:(h + 1) * 128], psc.r, [(kinv.t[:, h, :], qdec.t[:, h, :], [kinv.r, qdec.r])])
            S.op("dve", lambda e, psc=psc: e.tensor_tensor(out=scT.t[:, :, :], in0=psc.t[:, :].rearrange("p (h n) -> p h n", h=4),
                                                           in1=self.Uf.t[:, :].unsqueeze(1).to_broadcast([128, 4, 128]), op=ALU.mult),
                 reads=[psc.r, self.Uf.r], writes=[scT.r])
            po = self.psf()
            for h in range(4):
                self.mm_group(po.t[:, h * 128:(h + 1) * 128], po.r, [(vrow.t[:, h * 128:(h + 1) * 128], scT.t[:, h, :], [vrow.r, scT.r]),
                                                                  (Sgb.t[:, h, :], qdec.t[:, h, :], [Sgb.r, qdec.r])])
            pS = self.psf()
            for h in range(4):
                self.mm_group(pS.t[0:64, h * 128:(h + 1) * 128], pS.r, [(kendr.t[:, h * 64:(h + 1) * 64], vrow.t[:, h * 128:(h + 1) * 128], [kendr.r, vrow.r])])
            S.op("dve", lambda e: e.tensor_tensor(out=Sg.t[:, :, :], in0=Sg.t[:, :, :], in1=gsd.t[:, 0:4].unsqueeze(2).to_broadcast([64, 4, 128]), op=ALU.mult),
                 reads=[Sg.r, gsd.r], writes=[Sg.r])
            S.op("dve", lambda e, pS=pS: e.tensor_tensor(out=Sg.t[:, :, :], in0=Sg.t[:, :, :], in1=pS.t[0:64, :].rearrange("p (h n) -> p h n", h=4), op=ALU.add),
                 reads=[Sg.r, pS.r], writes=[Sg.r])
            S.op("act", lambda e: e.copy(out=Sgb.t[:, :, :], in_=Sg.t[:, :, :]), reads=[Sg.r], writes=[Sgb.r])
            S.op("act", lambda e, po=po: e.activation(out=osq.t[:, :], in_=po.t[:, :], func=AF.Square), reads=[po.r], writes=[osq.r])
            pss = self.psf()
            self.mm_group(pss.t[:, :], pss.r, [(self.onesf.t[:, :], osq.t[:, :], [self.onesf.r, osq.r])])
            S.op("dve", lambda e, pss=pss: e.tensor_scalar(out=orst.t[:, :], in0=pss.t[:, :], scalar1=1.0 / 128.0, scalar2=EPS, op0=ALU.mult, op1=ALU.add),
                 reads=[pss.r], writes=[orst.r])
            S.op("act", lambda e: e.activation(out=orst.t[:, :], in_=orst.t[:, :], func=AF.Sqrt), reads=[orst.r], writes=[orst.r])
            S.op("dve", lambda e: e.reciprocal(out=orst.t[:, :], in_=orst.t[:, :]), reads=[orst.r], writes=[orst.r])
            S.op("act", lambda e, po=po: e.copy(out=osq.t[:, :], in_=po.t[:, :]), reads=[po.r, osq.r], writes=[osq.r])
            S.op("dve", lambda e: e.tensor_tensor(out=osq.t[:, :], in0=osq.t[:, :], in1=orst.t[:, :], op=ALU.mult), reads=[osq.r, orst.r], writes=[osq.r])
            S.op("dve", lambda e: e.tensor_tensor(out=gsil.t[:, :, :], in0=gsil.t[:, :, :], in1=gng.t[:, 0:4].unsqueeze(2).to_broadcast([128, 4, 128]), op=ALU.mult),
                 reads=[gsil.r, gng.r], writes=[gsil.r])
            S.op("dve", lambda e: e.tensor_tensor(out=mixedT.t[:, 0:4, :], in0=osq.t[:, :].rearrange("p (h n) -> p h n", h=4), in1=gsil.t[:, :, :], op=ALU.mult),
                 reads=[osq.r, gsil.r], writes=[mixedT.r])
            self.out_proj(mixedT, Wout, ht, dst, blk, nk=4)

    def phase_rwkv(self, l, src, resbuf):
        S = self.S
        d = self.din
        j = l // 2
        CW = 0.6065306597126334
        self.arena_reset()
        Wr = self.arena_alloc("w_rwkv", 8, 1792)
        Wout = self.arena_alloc("wout_b", 4, D)
        W2b = self.arena_alloc("rw_w2", 1, 512)
        A2b = self.arena_alloc("rw_a2", 1, 512)
        G2b = self.arena_alloc("rw_g2", 1, 512)
        self.load_small(d["norm_mix_g"][l:l + 1, :].partition_broadcast(128), self.gb)
        self.load_w(d["ab_w_in"][j], Wr, D, 1792, src_col0=1552)
        self.load_w(d["w_mix_out"][l][512:1024, :], Wout, 512, D)
        self.load_w(d["rw_w2"][j], W2b, 64, 512)
        self.load_w(d["rw_a2"][j], A2b, 64, 512)
        self.load_w(d["rw_g2"][j], G2b, 128, 512)
        rvec = self.abuf("rw_vec", [64, 8, 8], F32)
        mus = self.abuf("rw_mus", [128, 4], F32)
        rrow = self.abuf("rw_row", [128, 1024], F32)
        self.load_small(d["rw_vec"][j], rvec)
        self.load_small(d["rw_mu_s"][j], mus)
        self.load_small(d["rw_row"][j], rrow)
        vb = lambda i: rvec.t[:, i, :].unsqueeze(2).to_broadcast([64, 8, 128])
        xnT = self.abuf("xnT_r", [128, 8, 128])
        mixedT = self.abuf("mixedT_r", [128, 4, 128])
        K3 = lambda name: self.abuf(name, [64, 8, 128], F32)
        Pr = self.abuf("Pr", [64, 8, 129], F32); Pk = self.abuf("Pk", [64, 8, 129], F32); Pv = self.abuf("Pv", [64, 8, 129], F32)
        Pl = self.abuf("Pl", [128, 3, 129], F32)
        for P in (Pr, Pk, Pv, Pl):
            S.op("pool", lambda e, P=P: e.memset(P.t[:, :, :], 0.0), writes=[P.r])
        R = K3("R"); Kt = K3("K"); KK = K3("KK"); AS = K3("AS"); LW = K3("LW"); CS = K3("CS"); E = K3("E"); T1 = K3("T1")
        BI = K3("BI"); KI = K3("KI")
        AR = self.abuf("AR", [64, 8, 256], F32)
        ltmp = self.abuf("ltmp", [128, 128], F32)
        lor = self.abuf("lor", [128, 3, 128])
        sd = self.abuf("rsd", [64, 8], F32)
        Vrow = self.abuf("Vrow", [128, 512], F32); BEr = self.abuf("BEr", [128, 512], F32); KEr = self.abuf("KEr", [128, 512], F32)
        X = self.abuf("X", [128, 8, 128], F32)
        AZs = [self.abuf("AZ%d" % i, [128, 4, 128], F32) for i in range(2)]
        Zr = self.abuf("Zr", [128, 512], F32)
        Y = self.abuf("Y", [128, 8, 64], F32); Yc = self.abuf("Yc", [128, 8, 64], F32)
        GR = self.abuf("GR", [128, 512], F32)
        yb = self.abuf("yb", [128, 512])
        st8 = self.abuf("st8", [128, 32], F32)
        ST = self.abuf("ST", [128, 8, 64], F32)
        Malls = [self.abuf("Mall%d" % i, [128, 4, 512], F32) for i in range(2)]
        chs = [[self.abuf("ch%d_%d" % (k, i), [128, 4, 128], F32) for i in range(2)] for k in range(2)]
        Xh = [T_(X.t, Res("Xh%d" % i)) for i in range(2)]
        Zrh = [T_(Zr.t, Res("Zrh%d" % i)) for i in range(2)]
        Yh = [T_(Y.t, Res("Yh%d" % i)) for i in range(2)]
        STh = [T_(ST.t, Res("STh%d" % i)) for i in range(2)]
        MASK4 = self.abuf("MASK4", [128, 512], F32)
        S.op("pool", lambda e: e.memset(ST.t[:, :, :], 0.0), writes=[STh[0].r, STh[1].r])
        S.op("pool", lambda e: e.tensor_copy(out=ST.t[64:128, :, :], in_=self.identf.t[64:128, 64:128].unsqueeze(1).to_broadcast([64, 8, 64])),
             reads=[self.identf.r, STh[0].r, STh[1].r], writes=[STh[0].r, STh[1].r])
        for q, msk in enumerate((self.Us, self.Uf, self.Us, self.Uf)):
            S.op("pool", lambda e, q=q, msk=msk: e.tensor_copy(out=MASK4.t[:, q * 128:(q + 1) * 128], in_=msk.t[:, :]), reads=[msk.r, MASK4.r], writes=[MASK4.r])
        onesK = self.onesf.t[0:64, 0:64]
        v3 = lambda t: t.t[:, :, :]

        for blk in range(self.NB):
            htA = self.new_ht()
            self.load_h(src, blk, htA)
            self.norm_T(htA.t[:, :], htA.r, self.gb, xnT, 0)
            ht = self.new_ht()
            self.load_h(resbuf, blk, ht)
            xw = [Wr.r, xnT.r]
            for (c0, P) in ((0, Pr), (576, Pk), (1088, Pv)):
                for hh in range(2):
                    pp = self.psf()
                    for hl in range(4):
                        h = hh * 4 + hl
                        self.mm_group(pp.t[0:64, hl * 128:(hl + 1) * 128], pp.r, [(Wr.t[:, kc, c0 + h * 64:c0 + (h + 1) * 64], xnT.t[:, kc, :], xw) for kc in range(8)])
                    S.op("act", lambda e, pp=pp, P=P, hh=hh: e.copy(out=P.t[:, hh * 4:(hh + 1) * 4, 1:129], in_=pp.t[0:64, :].rearrange("p (h n) -> p h n", h=4)),
                         reads=[pp.r], writes=[P.r])
            pl = self.psf()
            self.mm_group(pl.t[0:64, 0:128], pl.r, [(Wr.t[:, kc, 512:576], xnT.t[:, kc, :], xw) for kc in range(8)])
            self.mm_group(pl.t[0:64, 128:256], pl.r, [(Wr.t[:, kc, 1600:1664], xnT.t[:, kc, :], xw) for kc in range(8)])
            self.mm_group(pl.t[:, 256:384], pl.r, [(Wr.t[:, kc, 1664:1792], xnT.t[:, kc, :], xw) for kc in range(8)])
            S.op("act", lambda e, pl=pl: e.copy(out=Pl.t[0:64, 0:2, 1:129], in_=pl.t[0:64, 0:256].rearrange("p (h n) -> p h n", h=2)), reads=[pl.r], writes=[Pl.r])
            S.op("act", lambda e, pl=pl: e.copy(out=Pl.t[:, 2, 1:129], in_=pl.t[:, 256:384]), reads=[pl.r, Pl.r], writes=[Pl.r])
            for (P, i, out) in ((Pr, 5, R), (Pk, 6, Kt), (Pv, 7, E)):
                S.op("dve", lambda e, P=P: e.tensor_tensor(out=T1.t[:, :, :], in0=P.t[:, :, 0:128], in1=P.t[:, :, 1:129], op=ALU.subtract), reads=[P.r], writes=[T1.r])
                S.op("dve", lambda e, i=i: e.tensor_tensor(out=T1.t[:, :, :], in0=T1.t[:, :, :], in1=vb(i), op=ALU.mult), reads=[T1.r, rvec.r], writes=[T1.r])
                S.op("dve", lambda e, P=P, out=out: e.tensor_tensor(out=out.t[:, :, :], in0=T1.t[:, :, :], in1=P.t[:, :, 1:129], op=ALU.add), reads=[T1.r, P.r], writes=[out.r])
                S.op("act", lambda e, P=P: e.copy(out=P.t[:, :, 0:1], in_=P.t[:, :, 128:129]), reads=[P.r], writes=[P.r])
            pt = self.psf()
            for h in range(8):
                S.op("pe", lambda e, h=h, pt=pt: e.transpose(pt.t[:, h * 64:(h + 1) * 64], E.t[:, h, :], self.identf.t[0:64, 0:64]),
                     reads=[E.r, self.identf.r], writes=[pt.r], signal=(h == 7))
            S.op("act", lambda e, pt=pt: e.copy(out=Vrow.t[:, :], in_=pt.t[:, :]), reads=[pt.r], writes=[Vrow.r])
            for i, (rows, fn) in enumerate(((64, AF.Tanh), (64, AF.Identity), (128, AF.Sigmoid))):
                S.op("dve", lambda e, i=i, rows=rows: e.tensor_tensor(out=ltmp.t[0:rows, :], in0=Pl.t[0:rows, i, 0:128], in1=Pl.t[0:rows, i, 1:129], op=ALU.subtract),
                     reads=[Pl.r], writes=[ltmp.r])
                S.op("dve", lambda e, i=i, rows=rows: e.scalar_tensor_tensor(out=ltmp.t[0:rows, :], in0=ltmp.t[0:rows, :], scalar=mus.t[0:rows, i:i + 1],
                                                                         in1=Pl.t[0:rows, i, 1:129], op0=ALU.mult, op1=ALU.add),
                     reads=[ltmp.r, mus.r, Pl.r], writes=[ltmp.r])
                S.op("act", lambda e, i=i, rows=rows, fn=fn: e.activation(out=lor.t[0:rows, i, :], in_=ltmp.t[0:rows, :], func=fn), reads=[ltmp.r], writes=[lor.r])
            S.op("act", lambda e: e.copy(out=Pl.t[:, :, 0:1], in_=Pl.t[:, :, 128:129]), reads=[Pl.r], writes=[Pl.r])
            for (Wl, li, vi, OUT) in ((W2b, 0, 0, LW), (A2b, 1, 1, AS)):
                for hh in range(2):
                    pp = self.psf()
                    for hl in range(4):
                        h = hh * 4 + hl
                        self.mm_group(pp.t[0:64, hl * 128:(hl + 1) * 128], pp.r, [(Wl.t[0:64, 0, h * 64:(h + 1) * 64], lor.t[0:64, li, :], [Wl.r, lor.r])])
                    S.op("dve", lambda e, pp=pp, hh=hh, vi=vi, OUT=OUT: e.tensor_tensor(
                        out=OUT.t[:, hh * 4:(hh + 1) * 4, :], in0=pp.t[0:64, :].rearrange("p (h n) -> p h n", h=4),
                        in1=rvec.t[:, vi, hh * 4:(hh + 1) * 4].unsqueeze(2).to_broadcast([64, 4, 128]), op=ALU.add), reads=[pp.r, rvec.r], writes=[OUT.r])
                S.op("act", lambda e, OUT=OUT: e.activation(out=OUT.t[:, :, :], in_=OUT.t[:, :, :], func=AF.Sigmoid), reads=[OUT.r], writes=[OUT.r])
            pgr = self.psf()
            self.mm_group(pgr.t[:, :], pgr.r, [(lor.t[:, 2, :], G2b.t[:, 0, :], [lor.r, G2b.r])])
            S.op("act", lambda e, pgr=pgr: e.copy(out=GR.t[:, :], in_=pgr.t[:, :]), reads=[pgr.r], writes=[GR.r])
            for h in range(8):
                S.op("dve", lambda e, h=h: e.tensor_tensor_scan(out=CS.t[:, h, :], data0=self.onesf.t[0:64, :], data1=LW.t[:, h, :], initial=0.0,
                                                                op0=ALU.mult, op1=ALU.add), reads=[LW.r, self.onesf.r], writes=[CS.r])
            S.op("dve", lambda e: e.tensor_tensor(out=v3(KK), in0=v3(Kt), in1=vb(2), op=ALU.mult), reads=[Kt.r, rvec.r], writes=[KK.r])
            S.op("dve", lambda e: e.tensor_tensor(out=v3(T1), in0=v3(KK), in1=v3(KK), op=ALU.mult), reads=[KK.r], writes=[T1.r])
            for hh in range(2):
                pp = self.psf()
                self.mm_group(pp.t[0:64, :], pp.r, [(onesK, T1.t[:, hh * 4:(hh + 1) * 4, :], [self.onesf.r, T1.r])])
                S.op("act", lambda e, pp=pp, hh=hh: e.activation(out=E.t[:, hh * 4:(hh + 1) * 4, :], in_=pp.t[0:64, :].rearrange("p (h n) -> p h n", h=4), func=AF.Sqrt),
                     reads=[pp.r], writes=[E.r])
            S.op("dve", lambda e: e.tensor_scalar(out=v3(E), in0=v3(E), scalar1=1e-6, scalar2=None, op0=ALU.max), reads=[E.r], writes=[E.r])
            S.op("dve", lambda e: e.reciprocal(out=v3(E), in_=v3(E)), reads=[E.r], writes=[E.r])
            S.op("dve", lambda e: e.tensor_tensor(out=v3(KK), in0=v3(KK), in1=v3(E), op=ALU.mult), reads=[KK.r, E.r], writes=[KK.r])
            S.op("dve", lambda e: e.tensor_scalar(out=v3(T1), in0=v3(AS), scalar1=-1.0, scalar2=None, op0=ALU.add), reads=[AS.r], writes=[T1.r])
            S.op("dve", lambda e: e.tensor_tensor(out=v3(T1), in0=v3(T1), in1=vb(3), op=ALU.mult), reads=[T1.r, rvec.r], writes=[T1.r])
            S.op("dve", lambda e: e.tensor_scalar(out=v3(T1), in0=v3(T1), scalar1=1.0, scalar2=None, op0=ALU.add), reads=[T1.r], writes=[T1.r])
            S.op("dve", lambda e: e.tensor_tensor(out=v3(Kt), in0=v3(Kt), in1=v3(T1), op=ALU.mult), reads=[Kt.r, T1.r], writes=[Kt.r])
            S.op("dve", lambda e: e.tensor_tensor(out=v3(T1), in0=v3(R), in1=v3(Kt), op=ALU.mult), reads=[R.r, Kt.r], writes=[T1.r])
            S.op("dve", lambda e: e.tensor_tensor(out=v3(T1), in0=v3(T1), in1=vb(4), op=ALU.mult), reads=[T1.r, rvec.r], writes=[T1.r])
            pb = self.psf()
            for h in range(8):
                self.mm_group(pb.t[:, h * 2:h * 2 + 2], pb.r, [(T1.t[:, h, :], self.onesf.t[0:64, 0:2], [T1.r, self.onesf.r])])
            S.op("act", lambda e, pb=pb: e.copy(out=st8.t[:, 16:32], in_=pb.t[:, 0:16]), reads=[pb.r], writes=[st8.r])
            S.op("dve", lambda e: e.tensor_tensor(out=v3(AS), in0=v3(AS), in1=v3(KK), op=ALU.mult), reads=[AS.r, KK.r], writes=[AS.r])
            S.op("act", lambda e: e.activation(out=v3(E), in_=v3(CS), func=AF.Exp, scale=-CW), reads=[CS.r], writes=[E.r])
            S.op("dve", lambda e: e.tensor_tensor(out=AR.t[:, :, 128:256], in0=v3(R), in1=v3(E), op=ALU.mult), reads=[R.r, E.r], writes=[AR.r])
            S.op("dve", lambda e: e.tensor_tensor(out=v3(T1), in0=v3(CS), in1=v3(LW), op=ALU.subtract), reads=[CS.r, LW.r], writes=[T1.r])
            S.op("act", lambda e: e.activation(out=v3(E), in_=v3(T1), func=AF.Exp, scale=-CW), reads=[T1.r], writes=[E.r])
            S.op("dve", lambda e: e.scalar_tensor_tensor(out=AR.t[:, :, 0:128], in0=v3(KK), scalar=-1.0, in1=v3(E), op0=ALU.mult, op1=ALU.mult),
                 reads=[KK.r, E.r], writes=[AR.r])
            S.op("act", lambda e: e.activation(out=v3(E), in_=v3(CS), func=AF.Exp, scale=CW), reads=[CS.r], writes=[E.r])
            S.op("dve", lambda e: e.tensor_tensor(out=v3(BI), in0=v3(AS), in1=v3(E), op=ALU.mult), reads=[AS.r, E.r], writes=[BI.r])
            S.op("dve", lambda e: e.tensor_tensor(out=v3(KI), in0=v3(Kt), in1=v3(E), op=ALU.mult), reads=[Kt.r, E.r], writes=[KI.r])
            S.op("act", lambda e: e.activation(out=sd.t[:, :], in_=CS.t[:, :, 127], func=AF.Exp, scale=-CW), reads=[CS.r], writes=[sd.r])
            S.op("dve", lambda e: e.tensor_tensor(out=v3(T1), in0=v3(CS), in1=CS.t[:, :, 127:128].to_broadcast([64, 8, 128]), op=ALU.subtract), reads=[CS.r], writes=[T1.r])
            S.op("act", lambda e: e.activation(out=v3(E), in_=v3(T1), func=AF.Exp, scale=CW), reads=[T1.r], writes=[E.r])
            S.op("dve", lambda e: e.tensor_tensor(out=v3(AS), in0=v3(AS), in1=v3(E), op=ALU.mult), reads=[AS.r, E.r], writes=[AS.r])
            S.op("dve", lambda e: e.tensor_tensor(out=v3(Kt), in0=v3(Kt), in1=v3(E), op=ALU.mult), reads=[Kt.r, E.r], writes=[Kt.r])
            for (srcap, srcr, dstap, dstt) in ((lambda h: AR.t[:, h, 0:128], AR.r, X.t[:, :, 0:64], None),
                                              (lambda h: AS.t[:, h, :], AS.r, BEr.t[:, :].rearrange("p (h n) -> p h n", h=8), BEr),
                                              (lambda h: Kt.t[:, h, :], Kt.r, KEr.t[:, :].rearrange("p (h n) -> p h n", h=8), KEr)):
                pt = self.psf()
                for h in range(8):
                    S.op("pe", lambda e, h=h, pt=pt, srcap=srcap: e.transpose(pt.t[:, h * 64:(h + 1) * 64], srcap(h), self.identf.t[0:64, 0:64]),
                         reads=[srcr, self.identf.r], writes=[pt.r], signal=(h == 7))
                S.op("act", lambda e, pt=pt, dstap=dstap: e.copy(out=dstap, in_=pt.t[:, :].rearrange("p (h n) -> p h n", h=8)), reads=[pt.r],
                     writes=[dstt.r] if dstt is not None else [Xh[0].r, Xh[1].r])
            recs = []
            for hh in range(2):
                S.begin_record()
                self.psf_sub = self.psf_pool[3 * hh:3 * hh + 3]
                Mall, ch, AZ = Malls[hh], chs[hh], AZs[hh]
                X_, Zr_, Y_, ST_ = Xh[hh], Zrh[hh], Yh[hh], STh[hh]
                for hl in range(4):
                    h = hh * 4 + hl
                    pM = self.psf()
                    self.mm_group(pM.t[:, 0:256], pM.r, [(BI.t[:, h, :], AR.t[:, h, :], [BI.r, AR.r])])
                    self.mm_group(pM.t[:, 256:512], pM.r, [(KI.t[:, h, :], AR.t[:, h, :], [KI.r, AR.r])])
                    S.op("dve", lambda e, pM=pM, hl=hl, Mall=Mall: e.tensor_tensor(out=Mall.t[:, hl, :], in0=pM.t[:, :], in1=MASK4.t[:, :], op=ALU.mult),
                         reads=[pM.r, MASK4.r], writes=[Mall.r])
                pA = self.psf()
                for hl in range(4):
                    h = hh * 4 + hl
                    self.mm_group(pA.t[:, hl * 128:(hl + 1) * 128], pA.r, [(AR.t[:, h, 0:128], BI.t[:, h, :], [AR.r, BI.r])])
                A0 = ch[0]
                S.op("dve", lambda e, pA=pA, A0=A0: e.tensor_tensor(out=A0.t[:, :, :], in0=pA.t[:, :].rearrange("p (h n) -> p h n", h=4),
                                                             in1=self.Ls.t[:, :].unsqueeze(1).to_broadcast([128, 4, 128]), op=ALU.mult), reads=[pA.r, self.Ls.r], writes=[A0.r])
                Ap, Apt = A0, T_(Mall.t[:, :, 0:128], Mall.r)
                Q = ch[1]
                S.op("dve", lambda e, Q=Q, Mall=Mall: e.tensor_tensor(out=Q.t[:, :, :], in0=Mall.t[:, :, 0:128], in1=self.identf.t[:, :].unsqueeze(1).to_broadcast([128, 4, 128]), op=ALU.add),
                     reads=[Mall.r, self.identf.r], writes=[Q.r])
                for step in range(6):
                    pN = self.psf()
                    for hl in range(4):
                        self.mm_group(pN.t[:, hl * 128:(hl + 1) * 128], pN.r, [(Apt.t[:, hl, :], Ap.t[:, hl, :], [Apt.r, Ap.r])])
                    if step < 5:
                        pNt = self.psf()
                        for hl in range(4):
                            self.mm_group(pNt.t[:, hl * 128:(hl + 1) * 128], pNt.r, [(Ap.t[:, hl, :], Apt.t[:, hl, :], [Apt.r, Ap.r])])
                    S.op("act", lambda e, pN=pN, Ap=Ap: e.copy(out=Ap.t[:, :, :], in_=pN.t[:, :].rearrange("p (h n) -> p h n", h=4)), reads=[pN.r], writes=[Ap.r])
                    if step < 5:
                        S.op("act", lambda e, pNt=pNt, Apt=Apt: e.copy(out=Apt.t[:, :, :], in_=pNt.t[:, :].rearrange("p (h n) -> p h n", h=4)), reads=[pNt.r], writes=[Apt.r])
                    pQ = self.psf()
                    for hl in range(4):
                        self.mm_group(pQ.t[:, hl * 128:(hl + 1) * 128], pQ.r, [(Ap.t[:, hl, :], Q.t[:, hl, :], [Ap.r, Q.r])])
                    S.op("dve", lambda e, pQ=pQ, Q=Q: e.tensor_tensor(out=Q.t[:, :, :], in0=Q.t[:, :, :], in1=pQ.t[:, :].rearrange("p (h n) -> p h n", h=4), op=ALU.add),
                         reads=[pQ.r, Q.r], writes=[Q.r])
                pK = self.psf()
                for hl in range(4):
                    h = hh * 4 + hl
                    self.mm_group(pK.t[:, hl * 64:(hl + 1) * 64], pK.r, [(Mall.t[:, hl, 256:384], Vrow.t[:, h * 64:(h + 1) * 64], [Mall.r, Vrow.r])])
                S.op("act", lambda e, pK=pK, hh=hh: e.copy(out=X.t[:, hh * 4:(hh + 1) * 4, 64:128], in_=pK.t[:, 0:256].rearrange("p (h n) -> p h n", h=4)),
                     reads=[pK.r], writes=[X_.r])
                pZ = self.psf()
                for hl in range(4):
                    h = hh * 4 + hl
                    self.mm_group(pZ.t[:, hl * 128:(hl + 1) * 128], pZ.r, [(X.t[:, h, :], Q.t[:, hl, :], [X_.r, Q.r])])
                S.op("act", lambda e, pZ=pZ, AZ=AZ: e.copy(out=AZ.t[:, :, :], in_=pZ.t[:, :].rearrange("p (h n) -> p h n", h=4)), reads=[pZ.r], writes=[AZ.r])
                pZr = self.psf()
                for hl in range(4):
                    h = hh * 4 + hl
                    self.mm_group(pZr.t[:, hl * 64:(hl + 1) * 64], pZr.r, [(AZ.t[:, hl, :], ST.t[:, h, :], [AZ.r, ST_.r])])
                S.op("dve", lambda e, pZr=pZr, hh=hh: e.tensor_copy(out=Zr.t[:, hh * 256:(hh + 1) * 256], in_=pZr.t[:, 0:256]), reads=[pZr.r], writes=[Zr_.r])
                pY = self.psf()
                for hl in range(4):
                    h = hh * 4 + hl
                    hs = slice(h * 64, (h + 1) * 64)
                    self.mm_group(pY.t[:, hl * 64:(hl + 1) * 64], pY.r, [(AR.t[:, h, 128:256], ST.t[0:64, h, :], [AR.r, ST_.r]),
                                                                      (Mall.t[:, hl, 128:256], Zr.t[:, hs], [Mall.r, Zr_.r]),
                                                                      (Mall.t[:, hl, 384:512], Vrow.t[:, hs], [Mall.r, Vrow.r])])
                S.op("act", lambda e, pY=pY, hh=hh: e.copy(out=Y.t[:, hh * 4:(hh + 1) * 4, :], in_=pY.t[:, 0:256].rearrange("p (h n) -> p h n", h=4)),
                     reads=[pY.r], writes=[Y_.r])
                pD = self.psf()
                for hl in range(4):
                    h = hh * 4 + hl
                    hs = slice(h * 64, (h + 1) * 64)
                    self.mm_group(pD.t[0:64, hl * 64:(hl + 1) * 64], pD.r, [(BEr.t[:, hs], Zr.t[:, hs], [BEr.r, Zr_.r]), (KEr.t[:, hs], Vrow.t[:, hs], [KEr.r, Vrow.r])])
                S.op("dve", lambda e, hh=hh: e.tensor_tensor(out=ST.t[0:64, hh * 4:(hh + 1) * 4, :], in0=ST.t[0:64, hh * 4:(hh + 1) * 4, :],
                                                             in1=sd.t[:, hh * 4:(hh + 1) * 4].unsqueeze(2).to_broadcast([64, 4, 64]), op=ALU.mult), reads=[ST_.r, sd.r], writes=[ST_.r])
                S.op("dve", lambda e, hh=hh, pD=pD: e.tensor_tensor(out=ST.t[0:64, hh * 4:(hh + 1) * 4, :], in0=ST.t[0:64, hh * 4:(hh + 1) * 4, :],
                                                                    in1=pD.t[0:64, 0:256].rearrange("p (h n) -> p h n", h=4), op=ALU.add), reads=[ST_.r, pD.r], writes=[ST_.r])
                recs.append(S.end_record())
                self.psf_sub = None
            S.emit_interleaved(recs)
            b8 = lambda c0: st8.t[:, c0:c0 + 8].unsqueeze(2).to_broadcast([128, 8, 64])
            S.op("dve", lambda e: e.tensor_reduce(out=st8.t[:, 0:8], in_=Y.t[:, :, :], axis=AX.X, op=ALU.add), reads=[Yh[0].r, Yh[1].r], writes=[st8.r])
            S.op("dve", lambda e: e.tensor_scalar(out=st8.t[:, 0:8], in0=st8.t[:, 0:8], scalar1=1.0 / 64.0, scalar2=None, op0=ALU.mult), reads=[st8.r], writes=[st8.r])
            S.op("dve", lambda e: e.tensor_tensor(out=Yc.t[:, :, :], in0=Y.t[:, :, :], in1=b8(0), op=ALU.subtract), reads=[Yh[0].r, Yh[1].r, st8.r], writes=[Yc.r])
            S.op("dve", lambda e: e.tensor_tensor(out=Y.t[:, :, :], in0=Yc.t[:, :, :], in1=Yc.t[:, :, :], op=ALU.mult), reads=[Yc.r], writes=[Yh[0].r, Yh[1].r])
            S.op("dve", lambda e: e.tensor_reduce(out=st8.t[:, 8:16], in_=Y.t[:, :, :], axis=AX.X, op=ALU.add), reads=[Yh[0].r, Yh[1].r], writes=[st8.r])
            S.op("dve", lambda e: e.tensor_scalar(out=st8.t[:, 8:16], in0=st8.t[:, 8:16], scalar1=1.0 / 64.0, scalar2=64e-5, op0=ALU.mult, op1=ALU.add), reads=[st8.r], writes=[st8.r])
            S.op("act", lambda e: e.activation(out=st8.t[:, 8:16], in_=st8.t[:, 8:16], func=AF.Sqrt), reads=[st8.r], writes=[st8.r])
            S.op("dve", lambda e: e.reciprocal(out=st8.t[:, 8:16], in_=st8.t[:, 8:16]), reads=[st8.r], writes=[st8.r])
            S.op("dve", lambda e: e.tensor_tensor(out=Yc.t[:, :, :], in0=Yc.t[:, :, :], in1=b8(8), op=ALU.mult), reads=[Yc.r, st8.r], writes=[Yc.r])
            Yc2 = Yc.t[:, :, :].rearrange("p h n -> p (h n)")
            S.op("dve", lambda e: e.tensor_tensor(out=Yc2, in0=Yc2, in1=rrow.t[:, 0:512], op=ALU.mult), reads=[Yc.r, rrow.r], writes=[Yc.r])
            S.op("dve", lambda e: e.tensor_tensor(out=Yc2, in0=Yc2, in1=rrow.t[:, 512:1024], op=ALU.add), reads=[Yc.r, rrow.r], writes=[Yc.r])
            S.op("dve", lambda e: e.tensor_tensor(out=Y.t[:, :, :], in0=Vrow.t[:, :].rearrange("p (h n) -> p h n", h=8),
                                                  in1=st8.t[:, 16:32:2].unsqueeze(2).to_broadcast([128, 8, 64]), op=ALU.mult), reads=[Vrow.r, st8.r], writes=[Yh[0].r, Yh[1].r])
            S.op("dve", lambda e: e.tensor_tensor(out=Yc.t[:, :, :], in0=Yc.t[:, :, :], in1=Y.t[:, :, :], op=ALU.add), reads=[Yc.r, Yh[0].r, Yh[1].r], writes=[Yc.r])
            S.op("dve", lambda e: e.tensor_tensor(out=yb.t[:, :], in0=Yc2, in1=GR.t[:, :], op=ALU.mult), reads=[Yc.r, GR.r], writes=[yb.r])
            pT = self.psb()
            for n in range(4):
                S.op("pe", lambda e, n=n, pT=pT: e.transpose(pT.t[:, n * 128:(n + 1) * 128], yb.t[:, n * 128:(n + 1) * 128], self.identb.t[:]),
                     reads=[yb.r, self.identb.r], writes=[pT.r], signal=(n == 3))
            S.op("act", lambda e, pT=pT: e.copy(out=mixedT.t[:, :, :], in_=pT.t[:, 0:512].rearrange("p (k n) -> p k n", k=4)), reads=[pT.r], writes=[mixedT.r])
            self.out_proj(mixedT, Wout, ht, resbuf, blk, nk=4)


def host_layout(inputs, NL):
    f = lambda a: np.ascontiguousarray(np.asarray(a, dtype=np.float32))
    m = {}
    for k in ("norm_mix_g", "norm_xattn_g", "norm_ffn_g", "xattn_wq", "xattn_wkv", "xattn_wo", "ffn_w_up", "ffn_w_down", "w_mix_out"):
        m[k] = f(inputs[k])
    m["mem_norm_g"] = f(inputs["mem_norm_g"]).reshape(1, D)
    m["final_norm_g"] = f(inputs["final_norm_g"]).reshape(1, D)
    cw = f(inputs["ffn_conv_w"])
    cb = f(inputs["ffn_conv_b"])
    pad = NCH_FF * 128 - DFF
    v = np.concatenate([cw, cb[:, None, :]], axis=1)
    v = np.pad(v, ((0, 0), (0, 0), (0, pad)))
    v = v.reshape(NL, 4, NCH_FF, 128).transpose(0, 3, 2, 1)
    m["ffn_vec"] = np.ascontiguousarray(v.reshape(NL, 128, NCH_FF * 4))
    NE = (NL + 1) // 2
    m["ab_w_in"] = f(inputs["ab_w_in"])
    m["gla_wa"] = f(inputs["gla_w_alpha2"])
    m["gla_vec"] = np.ascontiguousarray(f(inputs["gla_b_alpha"]).reshape(NE, 4, 64).transpose(0, 2, 1))
    m["gla_ng"] = np.ascontiguousarray(f(inputs["gla_norm_g"]).reshape(NE, 4, 128).transpose(0, 2, 1))
    m["rw_w2"] = f(inputs["rwkv_w2"]); m["rw_a2"] = f(inputs["rwkv_a2"]); m["rw_g2"] = f(inputs["rwkv_g2"])
    mu = f(inputs["rwkv_mu"])
    kht = lambda a: a.reshape(NE, 8, 64).transpose(0, 2, 1)
    vecs = [inputs["rwkv_w0"], inputs["rwkv_a0"], inputs["rwkv_k_k"], inputs["rwkv_k_a"], inputs["rwkv_r_k"],
            mu[:, 0:512], mu[:, 576:1088], mu[:, 1088:1600]]
    m["rw_vec"] = np.ascontiguousarray(np.stack([kht(f(v)) for v in vecs], axis=2).reshape(NE, 64, 64))
    ms = np.zeros((NE, 128, 4), np.float32)
    ms[:, 0:64, 0] = mu[:, 512:576]; ms[:, 0:64, 1] = mu[:, 1600:1664]; ms[:, :, 2] = mu[:, 1664:1792]
    m["rw_mu_s"] = ms
    row = np.concatenate([f(inputs["rwkv_ln_g"]), f(inputs["rwkv_ln_b"])], axis=1)
    m["rw_row"] = np.ascontiguousarray(np.broadcast_to(row[:, None, :], (NE, 128, 1024)))
    NO = NL // 2
    m["cd_w_in"] = f(inputs["cd_w_in"])
    gw = f(inputs["lru_gate_w"])
    m["lru_gw"] = np.ascontiguousarray(gw.transpose(0, 3, 1, 2, 4).reshape(NO, 128, 1024))
    qkv = f(inputs["mlstm_qkv_w"])
    bd = np.zeros((NO, 3, 4, 128, 128), np.float32)
    q5 = qkv.reshape(NO, 3, 4, 32, 4, 4)
    for b in range(32):
        bd[:, :, :, 4 * b:4 * b + 4, 4 * b:4 * b + 4] = q5[:, :, :, b]
    m["ml_bd"] = np.ascontiguousarray(bd.transpose(0, 3, 1, 2, 4).reshape(NO, 128, 12 * 128))
    fm = lambda a: a.reshape(NO, -1, 4, 128).transpose(0, 3, 2, 1)
    lcw = f(inputs["lru_conv_w"]); lcb = f(inputs["lru_conv_b"]); lgb = f(inputs["lru_gate_b"]); lam = f(inputs["lru_lambda"])
    lv = np.concatenate([lcw, lcb[:, None], lgb, lam[:, None]], axis=1)
    m["lru_vec"] = np.ascontiguousarray(fm(lv).reshape(NO, 128, 32))
    mcw = f(inputs["mlstm_conv_w"]); mcb = f(inputs["mlstm_conv_b"])
    mv = np.concatenate([mcw, mcb[:, None]], axis=1)
    m["ml_vec"] = np.ascontiguousarray(fm(mv).reshape(NO, 128, 20))
    row = np.concatenate([f(inputs["mlstm_b_if"]), f(inputs["mlstm_norm_g"])], axis=1)
    m["ml_row"] = np.ascontiguousarray(np.broadcast_to(row[:, None, :], (NO, 128, 520)))
    return m


_CACHE = {}


def kernel(**inputs):
    x = np.asarray(inputs["x"], dtype=np.float32)
    mem = np.asarray(inputs["mem"], dtype=np.float32)
    B, T, _ = x.shape
    NL = inputs["norm_mix_g"].shape[0]
    key = (T, NL)
    if key not in _CACHE:
        _CACHE[key] = KB(T, NL).build()
    nc = _CACHE[key]
    shared = host_layout(inputs, NL)
    in_maps = []
    for b in range(B):
        mm = dict(shared)
        mm["x"] = np.ascontiguousarray(x[b])
        mm["mem"] = np.ascontiguousarray(mem[b])
        in_maps.append(mm)
    res = run_bass_kernel_spmd(nc, in_maps, core_ids=list(range(B)))
    return np.stack([np.asarray(r["out"], dtype=np.float32) for r in res.results], axis=0)
```

```python
import contextlib
import os
import numpy as np
import concourse.bass as bass
import concourse.mybir as mybir
from concourse.bass_utils import run_bass_kernel_spmd

F32 = mybir.dt.float32
BF16 = mybir.dt.bfloat16
AF = mybir.ActivationFunctionType
ALU = mybir.AluOpType
AX = mybir.AxisListType

ENGS = ("pe", "act", "dve", "pool", "sp")
D = 1024
DFF = 2752
NCH_FF = 22
EPS = 1e-6


class Res:
    __slots__ = ("name", "last_w", "readers", "dsem", "dcount", "psum")

    def __init__(self, name):
        self.name = name
        self.psum = False
        self.last_w = None
        self.readers = {}
        self.dsem = None
        self.dcount = 0


class Sched:
    def __init__(self, nc, stack):
        self.nc = nc
        self.stack = stack
        self.q = {e: [] for e in ENGS}
        self.cnt = {e: 0 for e in ENGS}
        self.pending = {e: False for e in ENGS}
        self.seen = {e: {} for e in ENGS}
        self.semh = {}
        for e in ENGS:
            self.semh["E_" + e] = stack.enter_context(nc.semaphore("sem_" + e))
        self.nsem = len(ENGS)
        self.ninstr = 0
        self.rec = None

    def _dsem(self, res):
        if res.dsem is None:
            key = "D_%d" % self.nsem
            self.semh[key] = self.stack.enter_context(self.nc.semaphore("d%d" % self.nsem))
            self.nsem += 1
            res.dsem = key
        return res.dsem

    def _wait(self, eng, deps):
        seen = self.seen[eng]
        for sk, v in deps.items():
            if seen.get(sk, 0) < v:
                seen[sk] = v
                h = self.semh[sk]
                self.q[eng].append(lambda e, h=h, v=v: e.wait_ge(h, v))
                self.ninstr += 1

    def _deps(self, own, reads, writes):
        deps = {}

        def add(ev):
            if ev[1] > deps.get(ev[0], 0):
                deps[ev[0]] = ev[1]
        for r in reads:
            if r.last_w is not None:
                add(r.last_w)
            if r.psum:
                for sk, v in r.readers.items():
                    if sk != own:
                        add((sk, v))
        skip_own = (own == "E_pe")
        for w in writes:
            if w.last_w is not None and not (skip_own and w.last_w[0] == own):
                add(w.last_w)
            for sk, v in w.readers.items():
                if not (skip_own and sk == own):
                    add((sk, v))
        return deps

    def _record(self, ev, reads, writes):
        for r in reads:
            if r.readers.get(ev[0], 0) < ev[1]:
                r.readers[ev[0]] = ev[1]
        for w in writes:
            w.last_w = ev
            w.readers = {}

    def begin_record(self):
        self.rec = []

    def end_record(self):
        r, self.rec = self.rec, None
        units, cur = [], []
        for it in r:
            cur.append(it)
            if it[0] == "dma" or it[5]:
                units.append(cur)
                cur = []
        assert not cur
        return units

    def emit_interleaved(self, lists):
        idx = [0] * len(lists)
        while True:
            best, bf = -1, 2.0
            for i, l in enumerate(lists):
                if idx[i] < len(l):
                    fr = idx[i] / len(l)
                    if fr < bf:
                        best, bf = i, fr
            if best < 0:
                break
            for it in lists[best][idx[best]]:
                if it[0] == "op":
                    self.op(it[1], it[2], it[3], it[4], it[5])
                else:
                    self.dma(it[1], it[2], it[3], it[4], it[5], it[6], **it[7])
            idx[best] += 1

    def op(self, eng, fn, reads=(), writes=(), signal=True):
        if self.rec is not None:
            self.rec.append(("op", eng, fn, tuple(reads), tuple(writes), signal))
            return None
        own = "E_" + eng
        self._wait(eng, self._deps(own, reads, writes))
        val = self.cnt[eng] + 1
        ev = (own, val)
        if signal:
            self.cnt[eng] = val
            self.pending[eng] = False
            h = self.semh[own]
            self.q[eng].append(lambda e, fn=fn, h=h: fn(e).then_inc(h, 1))
        else:
            self.pending[eng] = True
            self.q[eng].append(lambda e, fn=fn: fn(e))
        self.ninstr += 1
        self._record(ev, reads, writes)
        return ev

    def dma(self, qeng, out_ap, in_ap, sb_res, reads=(), writes=(), **kw):
        if self.rec is not None:
            self.rec.append(("dma", qeng, out_ap, in_ap, sb_res, tuple(reads), tuple(writes), kw))
            return None
        self._wait(qeng, self._deps("__none__", reads, writes))
        sk = self._dsem(sb_res)
        sb_res.dcount += 16
        ev = (sk, sb_res.dcount)
        h = self.semh[sk]
        self.q[qeng].append(
            lambda e, o=out_ap, i=in_ap, h=h, kw=kw: e.dma_start(out=o, in_=i, **kw).then_inc(h, 16))
        self.ninstr += 1
        self._record(ev, reads, writes)
        return ev

    def finish(self, final_events):
        for e in ENGS:
            assert not self.pending[e], "engine %s has unsignalled trailing ops" % e
        allev = {}
        for sk, v in list(final_events) + [("E_" + e, self.cnt[e]) for e in ENGS if self.cnt[e] > 0]:
            if v > allev.get(sk, 0):
                allev[sk] = v
        self._wait("sp", allev)
        nc = self.nc
        with nc.Block() as block:
            @block.tensor
            def _(eng):
                for f in self.q["pe"]:
                    f(eng)

            @block.scalar
            def _(eng):
                for f in self.q["act"]:
                    f(eng)

            @block.vector
            def _(eng):
                for f in self.q["dve"]:
                    f(eng)

            @block.gpsimd
            def _(eng):
                for f in self.q["pool"]:
                    f(eng)

            @block.sync
            def _(eng):
                for f in self.q["sp"]:
                    f(eng)


class T_:
    __slots__ = ("t", "r")

    def __init__(self, t, r):
        self.t = t
        self.r = r

    def __getitem__(self, k):
        return self.t[k]


ARENA_COLS = 84000


class KB:
    def __init__(self, T, NL, plan=None, final_norm=True):
        self.T = T
        self.NL = NL
        self.NB = T // 128
        self.plan = plan
        self.final_norm = final_norm
        self.nc = bass.Bass("TRN2", target_bir_lowering=False)
        self.din = {}
        self.uid = 0

    def inp(self, name, shape):
        self.din[name] = self.nc.dram_tensor(name, list(shape), F32, kind="ExternalInput").ap()
        return self.din[name]

    def sb(self, name, shape, dt=F32):
        t = self.st.enter_context(self.nc.sbuf_tensor(name, list(shape), dt))
        return T_(t, Res(name))

    def ps(self, name, shape, dt=F32):
        t = self.st.enter_context(self.nc.psum_tensor(name, list(shape), dt))
        r = Res(name)
        r.psum = True
        return T_(t, r)

    def psf(self):
        pool = self.psf_sub if getattr(self, "psf_sub", None) else self.psf_pool
        self.psf_i += 1
        return pool[self.psf_i % len(pool)]

    def psb(self):
        p = self.psb_pool[self.psb_i % len(self.psb_pool)]
        self.psb_i += 1
        return p

    def arena_reset(self):
        carry = {}
        for r in self.arena_live:
            evs = dict(r.readers)
            if r.last_w is not None:
                evs[r.last_w[0]] = max(evs.get(r.last_w[0], 0), r.last_w[1])
            for k, v in evs.items():
                if v > carry.get(k, 0):
                    carry[k] = v
        for k, v in self.arena_carry.items():
            if v > carry.get(k, 0):
                carry[k] = v
        self.arena_carry = carry
        self.arena_live = []
        self.arena_off = 0

    def arena_alloc(self, name, kch, ncols):
        n = kch * ncols
        assert self.arena_off + n <= ARENA_COLS, (name, self.arena_off, n)
        ap = self.arena.t[:, self.arena_off:self.arena_off + n].rearrange("p (k n) -> p k n", k=kch)
        self.arena_off += n
        r = Res(name)
        r.readers = dict(self.arena_carry)
        self.arena_live.append(r)
        return T_(ap, r)

    def abuf(self, name, shape, dt=BF16):
        n = int(np.prod(shape[1:]))
        ncols = n if dt == BF16 else 2 * n
        self.arena_off += self.arena_off % 2
        assert self.arena_off + ncols <= ARENA_COLS, (name, self.arena_off, ncols)
        ap = self.arena.t[:, self.arena_off:self.arena_off + ncols]
        self.arena_off += ncols
        if dt != BF16:
            ap = ap.bitcast(dt)
        if len(shape) == 3:
            ap = ap.rearrange("p (k n) -> p k n", k=shape[1])
        elif len(shape) == 4:
            ap = ap.rearrange("p (a b n) -> p a b n", a=shape[1], b=shape[2])
        if shape[0] != 128:
            ap = ap[0:shape[0]]
        r = Res(name)
        r.readers = dict(self.arena_carry)
        self.arena_live.append(r)
        return T_(ap, r)

    def new_ht(self):
        ht = self.ht[self.ht_i % len(self.ht)]
        self.ht_i += 1
        return ht

    def load_w(self, src, dst, K, N, col0=0, scale=None, src_col0=0):
        S = self.S
        nk = (K + 127) // 128
        CB = 1024
        for kc in range(nk):
            rows = min(128, K - kc * 128)
            for c0 in range(0, N, CB):
                cw = min(CB, N - c0)
                stg = self.stg[self.stg_i % len(self.stg)]
                self.stg_i += 1
                S.dma("sp", stg.t[:rows, :cw], src[kc * 128:kc * 128 + rows, src_col0 + c0:src_col0 + c0 + cw],
                      stg.r, writes=[stg.r])
                o = dst.t[:rows, kc, col0 + c0:col0 + c0 + cw]
                eng = ("pool", "dve")[self.cast_i % 2] if self.cast_both else "pool"
                self.cast_i += 1
                if scale is None:
                    S.op(eng, lambda e, o=o, i=stg.t[:rows, :cw]: e.tensor_copy(out=o, in_=i),
                         reads=[stg.r], writes=[dst.r])
                else:
                    sc = scale.t[:rows, c0:c0 + cw]
                    S.op(eng, lambda e, o=o, i=stg.t[:rows, :cw], sc=sc: e.tensor_tensor(out=o, in0=i, in1=sc, op=ALU.mult),
                         reads=[stg.r, scale.r], writes=[dst.r])

    def load_small(self, src_ap, dst, dst_ap=None):
        self.S.dma("sp", dst.t[:] if dst_ap is None else dst_ap, src_ap, dst.r, writes=[dst.r])

    def load_h(self, src, blk, dst):
        self.S.dma("sp", dst.t[:, :], src[blk * 128:(blk + 1) * 128, :], dst.r,
                   reads=[self.hres[id(src)][blk]] if id(src) in self.hres else [], writes=[dst.r])

    def store_h(self, dstd, blk, src, ap=None):
        self.S.dma("act", dstd[blk * 128:(blk + 1) * 128, :], src.t[:, :] if ap is None else ap, src.r,
                   reads=[src.r], writes=[self.hres[id(dstd)][blk]])

    def rstd_of(self, hap, hres):
        S = self.S
        ss = self.ss[self.ss_i % len(self.ss)]
        self.ss_i += 1
        junk = self.junk
        S.op("act", lambda e: e.activation(out=junk.t[:], in_=hap, func=AF.Square, accum_out=ss.t[:, 0:1]),
             reads=[hres], writes=[junk.r, ss.r])
        S.op("dve", lambda e: e.tensor_scalar(out=ss.t[:, 1:2], in0=ss.t[:, 0:1], scalar1=1.0 / D, scalar2=EPS,
                                              op0=ALU.mult, op1=ALU.add), reads=[ss.r], writes=[ss.r])
        S.op("act", lambda e: e.activation(out=ss.t[:, 2:3], in_=ss.t[:, 1:2], func=AF.Sqrt), reads=[ss.r], writes=[ss.r])
        S.op("dve", lambda e: e.reciprocal(out=ss.t[:, 3:4], in_=ss.t[:, 2:3]), reads=[ss.r], writes=[ss.r])
        return ss

    def norm_T(self, hap, hres, gb, xnT, col0):
        S = self.S
        ss = self.rstd_of(hap, hres)
        xn = self.xn[self.xn_i % len(self.xn)]
        self.xn_i += 1
        S.op("dve", lambda e: e.scalar_tensor_tensor(out=xn.t[:], in0=hap, scalar=ss.t[:, 3:4], in1=gb.t[:],
                                                     op0=ALU.mult, op1=ALU.mult),
             reads=[hres, ss.r, gb.r], writes=[xn.r])
        pT = self.psb()
        for kc in range(8):
            S.op("pe", lambda e, kc=kc: e.transpose(pT.t[:, kc * 128:(kc + 1) * 128], xn.t[:, kc * 128:(kc + 1) * 128],
                                                     self.identb.t[:]),
                 reads=[xn.r, self.identb.r], writes=[pT.r], signal=(kc == 7))
        S.op("act", lambda e: e.copy(out=xnT.t[:, :, col0:col0 + 128],
                                     in_=pT.t[:, 0:1024].rearrange("p (k n) -> p k n", k=8)),
             reads=[pT.r], writes=[xnT.r])
        return xn

    def mm_group(self, out_ap, out_res, pairs):
        n = len(pairs)
        for i, (l, r, rs) in enumerate(pairs):
            self.S.op("pe", lambda e, l=l, r=r, i=i: e.matmul(out_ap, lhsT=l, rhs=r, start=(i == 0), stop=(i == n - 1)),
                      reads=rs, writes=[out_res], signal=(i == n - 1))

    def phase_ffn(self, l, src, dst):
        S = self.S
        d = self.din
        TT = 256
        self.arena_reset()
        Wup = self.arena_alloc("wup", 8, 2 * DFF)
        Wdn = self.arena_alloc("wdn", NCH_FF, D)
        self.load_small(d["ffn_vec"][l], self.fvec)
        self.load_small(d["norm_ffn_g"][l:l + 1, :].partition_broadcast(128), self.gb)
        self.load_w(d["ffn_w_up"][l], Wup, D, 2 * DFF)
        self.load_w(d["ffn_w_down"][l], Wdn, DFF, D)
        xnT = self.abuf("xnT_f", [128, 8, TT + 2])
        actT = self.abuf("actT", [128, NCH_FF, TT])
        self.cv = [self.abuf("cv%d" % i, [128, 2 * TT], F32) for i in range(2)]
        self.cv_i = 0
        S.op("pool", lambda e: e.memset(xnT.t[:, :, 0:2], 0.0), writes=[xnT.r])
        fv = self.fvec
        for t0 in range(0, self.T, TT):
            nb = TT // 128
            hts = [self.new_ht() for b in range(nb)]
            for b in range(nb):
                self.load_h(src, t0 // 128 + b, hts[b])
                self.norm_T(hts[b].t[:, :], hts[b].r, self.gb, xnT, 2 + b * 128)
            for c in range(NCH_FF):
                cs = min(128, DFF - c * 128)
                pu = self.psf()
                pg = self.psf()
                self.mm_group(pu.t[:cs, 0:TT + 2], pu.r,
                              [(Wup.t[:, kc, c * 128:c * 128 + cs], xnT.t[:, kc, 0:TT + 2], [Wup.r, xnT.r]) for kc in range(8)])
                self.mm_group(pg.t[:cs, 0:TT + 2], pg.r,
                              [(Wup.t[:, kc, DFF + c * 128:DFF + c * 128 + cs], xnT.t[:, kc, 0:TT + 2], [Wup.r, xnT.r]) for kc in range(8)])
                cv = self.cv[self.cv_i % len(self.cv)]
                self.cv_i += 1
                w = lambda j, c=c, cs=cs: fv.t[:cs, c * 4 + j:c * 4 + j + 1]
                S.op("dve", lambda e, cs=cs, pu=pu, cv=cv, w=w: e.tensor_scalar(
                    out=cv.t[:cs, 0:TT], in0=pu.t[:cs, 2:TT + 2], scalar1=w(2), scalar2=w(3), op0=ALU.mult, op1=ALU.add),
                    reads=[pu.r, fv.r], writes=[cv.r])
                S.op("dve", lambda e, cs=cs, pu=pu, cv=cv, w=w: e.scalar_tensor_tensor(
                    out=cv.t[:cs, 0:TT], in0=pu.t[:cs, 1:TT + 1], scalar=w(1), in1=cv.t[:cs, 0:TT], op0=ALU.mult, op1=ALU.add),
                    reads=[pu.r, fv.r, cv.r], writes=[cv.r])
                S.op("dve", lambda e, cs=cs, pu=pu, cv=cv, w=w: e.scalar_tensor_tensor(
                    out=cv.t[:cs, 0:TT], in0=pu.t[:cs, 0:TT], scalar=w(0), in1=cv.t[:cs, 0:TT], op0=ALU.mult, op1=ALU.add),
                    reads=[pu.r, fv.r, cv.r], writes=[cv.r])
                S.op("act", lambda e, cs=cs, cv=cv: e.activation(out=cv.t[:cs, TT:2 * TT], in_=cv.t[:cs, 0:TT], func=AF.Silu),
                     reads=[cv.r], writes=[cv.r])
                S.op("dve", lambda e, cs=cs, cv=cv, pg=pg, c=c: e.tensor_tensor(
                    out=actT.t[:cs, c, 0:TT], in0=cv.t[:cs, TT:2 * TT], in1=pg.t[:cs, 2:TT + 2], op=ALU.mult),
                    reads=[cv.r, pg.r], writes=[actT.r])
            S.op("pool", lambda e: e.tensor_copy(out=xnT.t[:, :, 0:2], in_=xnT.t[:, :, TT:TT + 2]), reads=[xnT.r], writes=[xnT.r])
            for b in range(nb):
                for half in range(2):
                    po = self.psf()
                    prs = []
                    for c in range(NCH_FF):
                        cs = min(128, DFF - c * 128)
                        prs.append((actT.t[:cs, c, b * 128:(b + 1) * 128], Wdn.t[:cs, c, half * 512:(half + 1) * 512], [actT.r, Wdn.r]))
                    self.mm_group(po.t[:, :], po.r, prs)
                    ht = hts[b]
                    S.op("dve", lambda e, half=half, po=po, ht=ht: e.tensor_tensor(
                        out=ht.t[:, half * 512:(half + 1) * 512], in0=ht.t[:, half * 512:(half + 1) * 512], in1=po.t[:, :], op=ALU.add),
                        reads=[ht.r, po.r], writes=[ht.r])
                self.store_h(dst, t0 // 128 + b, hts[b])

    def prep_mem(self):
        d = self.din
        self.load_small(d["mem_norm_g"][0:1, :].partition_broadcast(128), self.gb)
        for b in range(2):
            ht = self.new_ht()
            self.S.dma("sp", ht.t[:, :], d["mem"][b * 128:(b + 1) * 128, :], ht.r, writes=[ht.r])
            self.norm_T(ht.t[:, :], ht.r, self.gb, self.memnT, b * 128)

    def phase_xattn(self, l, src, dst):
        S = self.S
        d = self.din
        TT = 512
        self.arena_reset()
        Wq = self.arena_alloc("wq", 8, D)
        Wkv = self.arena_alloc("wkv", 8, 2 * D)
        Wo = self.arena_alloc("wo", 8, D)
        self.load_small(d["norm_xattn_g"][l:l + 1, :].partition_broadcast(128), self.gb)
        self.load_w(d["xattn_wkv"][l], Wkv, D, 2 * D)
        self.load_w(d["xattn_wq"][l], Wq, D, D)
        self.load_w(d["xattn_wo"][l], Wo, D, D)
        memnT = self.memnT
        KT = self.abuf("KT", [128, 8, 256])
        Vr = self.abuf("Vr", [128, 2, D])
        xnT = self.abuf("xnT_x", [128, 8, TT])
        qT = self.abuf("qT", [128, 8, TT])
        pTs = self.abuf("pTs", [128, 8, TT])
        oT = self.abuf("oT", [128, 8, TT])
        self.ex = [self.abuf("ex%d" % i, [128, 512], F32) for i in range(2)]
        self.pb = [self.abuf("pb%d" % i, [128, 512]) for i in range(2)]
        self.ex_i = self.pb_i = 0
        for c in range(8):
            pk = self.psf()
            self.mm_group(pk.t[:, 0:256], pk.r, [(Wkv.t[:, kc, c * 128:(c + 1) * 128], memnT.t[:, kc, :], [Wkv.r, memnT.r]) for kc in range(8)])
            S.op("act", lambda e, c=c, pk=pk: e.copy(out=KT.t[:, c, :], in_=pk.t[:, 0:256]), reads=[pk.r], writes=[KT.r])
        for mb in range(2):
            for half in range(2):
                pv = self.psf()
                self.mm_group(pv.t[:, :], pv.r, [(memnT.t[:, kc, mb * 128:(mb + 1) * 128], Wkv.t[:, kc, D + half * 512:D + (half + 1) * 512],
                                                  [Wkv.r, memnT.r]) for kc in range(8)])
                S.op("act", lambda e, mb=mb, half=half, pv=pv: e.copy(out=Vr.t[:, mb, half * 512:(half + 1) * 512], in_=pv.t[:, :]),
                     reads=[pv.r], writes=[Vr.r])
        for t0 in range(0, self.T, TT):
            nb = min(TT, self.T - t0) // 128
            ntok = nb * 128
            hts = []
            for b in range(nb):
                ht = self.new_ht()
                hts.append(ht)
                self.load_h(src, t0 // 128 + b, ht)
                self.norm_T(ht.t[:, :], ht.r, self.gb, xnT, b * 128)
            for c in range(8):
                pq = self.psf()
                self.mm_group(pq.t[:, 0:ntok], pq.r, [(Wq.t[:, kc, c * 128:(c + 1) * 128], xnT.t[:, kc, 0:ntok], [Wq.r, xnT.r]) for kc in range(8)])
                S.op("act", lambda e, c=c, pq=pq: e.mul(out=qT.t[:, c, 0:ntok], in_=pq.t[:, 0:ntok], mul=1.0 / 16.0),
                     reads=[pq.r], writes=[qT.r])
            for b in range(nb):
                for hp in range(2):
                    psc = self.psf()
                    for hh in range(2):
                        h = hp * 2 + hh
                        self.mm_group(psc.t[:, hh * 256:(hh + 1) * 256], psc.r,
                                      [(qT.t[:, 2 * h + dd, b * 128:(b + 1) * 128], KT.t[:, 2 * h + dd, :], [qT.r, KT.r]) for dd in range(2)])
                    sm = self.smx[self.smx_i % len(self.smx)]
                    self.smx_i += 1
                    p3 = psc.t[:, :].rearrange("p (h m) -> p h m", h=2)
                    S.op("dve", lambda e, sm=sm, p3=p3: e.tensor_reduce(out=sm.t[:, 0:2], in_=p3, axis=AX.X, op=ALU.max),
                         reads=[psc.r], writes=[sm.r])
                    ex = self.ex[self.ex_i % len(self.ex)]
                    self.ex_i += 1
                    e3 = ex.t[:, :].rearrange("p (h m) -> p h m", h=2)
                    S.op("dve", lambda e, sm=sm, p3=p3, e3=e3: e.tensor_tensor(out=e3, in0=p3, in1=sm.t[:, 0:2].unsqueeze(2).to_broadcast([128, 2, 256]),
                                                                           op=ALU.subtract), reads=[psc.r, sm.r], writes=[ex.r])
                    S.op("act", lambda e, ex=ex: e.activation(out=ex.t[:, :], in_=ex.t[:, :], func=AF.Exp), reads=[ex.r], writes=[ex.r])
                    S.op("dve", lambda e, sm=sm, e3=e3: e.tensor_reduce(out=sm.t[:, 2:4], in_=e3, axis=AX.X, op=ALU.add),
                         reads=[ex.r], writes=[sm.r])
                    S.op("dve", lambda e, sm=sm: e.reciprocal(out=sm.t[:, 4:6], in_=sm.t[:, 2:4]), reads=[sm.r], writes=[sm.r])
                    pb = self.pb[self.pb_i % len(self.pb)]
                    self.pb_i += 1
                    S.op("dve", lambda e, sm=sm, e3=e3, pb=pb: e.tensor_tensor(
                        out=pb.t[:, :].rearrange("p (h m) -> p h m", h=2), in0=e3,
                        in1=sm.t[:, 4:6].unsqueeze(2).to_broadcast([128, 2, 256]), op=ALU.mult), reads=[ex.r, sm.r], writes=[pb.r])
                    pT = self.psb()
                    for i in range(4):
                        S.op("pe", lambda e, i=i, pT=pT, pb=pb: e.transpose(pT.t[:, i * 128:(i + 1) * 128], pb.t[:, i * 128:(i + 1) * 128], self.identb.t[:]),
                             reads=[pb.r, self.identb.r], writes=[pT.r], signal=(i == 3))
                    S.op("act", lambda e, pT=pT, hp=hp, b=b: e.copy(out=pTs.t[:, hp * 4:(hp + 1) * 4, b * 128:(b + 1) * 128],
                                                                   in_=pT.t[:, 0:512].rearrange("p (k n) -> p k n", k=4)),
                         reads=[pT.r], writes=[pTs.r])
            for h in range(4):
                for dd in range(2):
                    po = self.psf()
                    self.mm_group(po.t[:, 0:ntok], po.r, [(Vr.t[:, mb, h * 256 + dd * 128:h * 256 + (dd + 1) * 128], pTs.t[:, h * 2 + mb, 0:ntok],
                                                           [Vr.r, pTs.r]) for mb in range(2)])
                    S.op("act", lambda e, h=h, dd=dd, po=po: e.copy(out=oT.t[:, 2 * h + dd, 0:ntok], in_=po.t[:, 0:ntok]), reads=[po.r], writes=[oT.r])
            for b in range(nb):
                ht = hts[b]
                for half in range(2):
                    pw = self.psf()
                    self.mm_group(pw.t[:, :], pw.r, [(oT.t[:, kc, b * 128:(b + 1) * 128], Wo.t[:, kc, half * 512:(half + 1) * 512], [oT.r, Wo.r]) for kc in range(8)])
                    S.op("dve", lambda e, ht=ht, half=half, pw=pw: e.tensor_tensor(
                        out=ht.t[:, half * 512:(half + 1) * 512], in0=ht.t[:, half * 512:(half + 1) * 512], in1=pw.t[:, :], op=ALU.add),
                        reads=[ht.r, pw.r], writes=[ht.r])
                self.store_h(dst, t0 // 128 + b, ht)

    def phase_final(self, src, dst):
        S = self.S
        d = self.din
        self.load_small(d["final_norm_g"][0:1, :].partition_broadcast(128), self.gb)
        for blk in range(self.NB):
            ht = self.new_ht()
            self.load_h(src, blk, ht)
            ss = self.rstd_of(ht.t[:, :], ht.r)
            S.op("dve", lambda e, ht=ht, ss=ss: e.scalar_tensor_tensor(out=ht.t[:, :], in0=ht.t[:, :], scalar=ss.t[:, 3:4], in1=self.gb.t[:],
                                                                 op0=ALU.mult, op1=ALU.mult), reads=[ht.r, ss.r, self.gb.r], writes=[ht.r])
            self.store_h(dst, blk, ht)

    def build(self):
        nc = self.nc
        T, NL = self.T, self.NL
        inp = self.inp
        inp("x", [T, D]); inp("mem", [256, D]); inp("mem_norm_g", [1, D]); inp("final_norm_g", [1, D])
        inp("norm_mix_g", [NL, D]); inp("norm_xattn_g", [NL, D]); inp("norm_ffn_g", [NL, D])
        inp("xattn_wq", [NL, D, D]); inp("xattn_wkv", [NL, D, 2 * D]); inp("xattn_wo", [NL, D, D])
        inp("ffn_w_up", [NL, D, 2 * DFF]); inp("ffn_w_down", [NL, DFF, D]); inp("ffn_vec", [NL, 128, NCH_FF * 4])
        inp("w_mix_out", [NL, D, D])
        self.decl_mixer_inputs()
        out = nc.dram_tensor("out", [T, D], F32, kind="ExternalOutput").ap()
        hbuf = nc.dram_tensor("hbuf", [T, D], F32, kind="Internal").ap()
        hbuf2 = nc.dram_tensor("hbuf2", [T, D], F32, kind="Internal").ap()
        self.hres = {id(hbuf): [Res("h%d" % i) for i in range(self.NB)], id(hbuf2): [Res("g%d" % i) for i in range(self.NB)],
                     id(out): [Res("o%d" % i) for i in range(self.NB)]}
        with contextlib.ExitStack() as st:
            self.st = st
            self.S = S = Sched(nc, st)
            self.psf_pool = [self.ps("psf%d" % i, [128, 512]) for i in range(6)]
            self.psb_pool = [self.ps("psb%d" % i, [128, 1024], BF16) for i in range(2)]
            self.psf_i = self.psb_i = 0
            self.arena = self.sb("arena", [128, ARENA_COLS], BF16)
            self.arena_live, self.arena_carry, self.arena_off = [], {}, 0
            self.stg = [self.sb("stg%d" % i, [128, 1024]) for i in range(2)]
            self.stg_i = self.cast_i = 0
            self.cast_both = True
            self.gb = self.sb("gb", [128, D])
            self.ht = [self.sb("ht%d" % i, [128, D]) for i in range(4)]
            self.ht_i = 0
            self.ss = [self.sb("ss%d" % i, [128, 4]) for i in range(4)]
            self.ss_i = 0
            self.junk = self.sb("junk", [128, D], BF16)
            self.xn = [self.sb("xn%d" % i, [128, D], BF16) for i in range(2)]
            self.xn_i = 0
            self.identb = self.sb("identb", [128, 128], BF16)
            self.identf = self.sb("identf", [128, 128])
            S.op("pool", lambda e: e.memset(self.identf.t[:], 0.0), writes=[self.identf.r])
            S.op("pool", lambda e: e.affine_select(out=self.identf.t[:], in_=self.identf.t[:], pattern=[[-1, 128]], compare_op=ALU.not_equal,
                                                   fill=1.0, base=0, channel_multiplier=1), reads=[self.identf.r], writes=[self.identf.r])
            S.op("pool", lambda e: e.tensor_copy(out=self.identb.t[:], in_=self.identf.t[:]), reads=[self.identf.r], writes=[self.identb.r])
            self.memnT = self.sb("memnT", [128, 8, 256], BF16)
            self.fvec = self.sb("fvec", [128, NCH_FF * 4])
            self.smx = [self.sb("smx%d" % i, [128, 6]) for i in range(2)]
            self.smx_i = 0
            self.alloc_mixer_bufs()

            plan = self.plan
            if plan is None:
                plan = []
                for l in range(NL):
                    plan += [("M", l), ("X", l), ("F", l)]
            if any(p[0] == "X" for p in plan):
                self.prep_mem()
            cur = self.din["x"]
            for (kind, l) in plan:
                x_in = cur is self.din["x"]
                tgt = hbuf if x_in else cur
                if kind == "M" and l % 2 == 0:
                    oth = hbuf2 if cur is hbuf else hbuf
                    self.phase_gla(l, cur, oth)
                    if "norwkv" not in os.environ.get("KDBG", ""):
                        self.phase_rwkv(l, cur, oth)
                    tgt = oth
                elif kind == "M":
                    self.phase_mixer_odd(l, cur, tgt)
                elif kind == "X":
                    self.phase_xattn(l, cur, tgt)
                elif kind == "F":
                    self.phase_ffn(l, cur, tgt)
                cur = tgt
            if self.final_norm:
                self.phase_final(cur, out)
            else:
                for blk in range(self.NB):
                    ht = self.new_ht()
                    self.load_h(cur, blk, ht)
                    self.store_h(out, blk, ht)
            finals = []
            for r in [t.r for t in self.ht]:
                if r.dsem is not None:
                    finals.append((r.dsem, r.dcount))
            S.finish(finals)
            self.ninstr = S.ninstr
        return nc

    def phase_mixer_odd(self, l, src, dst):
        S = self.S
        d = self.din
        j = l // 2
        self.arena_reset()
        Wcd = self.arena_alloc("wcd", 8, 2056)
        Wout = self.arena_alloc("wout", 8, D)
        Wg = self.arena_alloc("wg", 1, 1024)
        Wbd = self.arena_alloc("wbd", 1, 1536)
        self.load_small(d["norm_mix_g"][l:l + 1, :].partition_broadcast(128), self.gb)
        self.load_w(d["cd_w_in"][j], Wcd, D, 2056)
        self.load_w(d["w_mix_out"][l], Wout, D, D)
        self.load_w(d["lru_gw"][j], Wg, 128, 8 * 128)
        self.load_w(d["ml_bd"][j], Wbd, 128, 12 * 128)
        Wg2 = T_(Wg.t[:, 0, :], Wg.r)
        Wbd2 = T_(Wbd.t[:, 0, :], Wbd.r)
        lv = self.abuf("lru_vec", [128, 4, 8], F32)
        mv = self.abuf("ml_vec", [128, 4, 5], F32)
        mrow = self.abuf("ml_row", [128, 8 + 512], F32)
        self.load_small(d["lru_vec"][j], lv)
        self.load_small(d["ml_vec"][j], mv)
        self.load_small(d["ml_row"][j], mrow)
        c8 = self.abuf("c8", [128, 4, 2], F32)
        tmp4 = self.abuf("tmp4", [128, 4], F32)
        S.op("act", lambda e: e.activation(out=tmp4.t[:, :], in_=lv.t[:, :, 7], func=AF.Exp, scale=-1.0), reads=[lv.r], writes=[tmp4.r])
        S.op("act", lambda e: e.activation(out=tmp4.t[:, :], in_=tmp4.t[:, :], func=AF.Ln, bias=1.0, scale=1.0), reads=[tmp4.r], writes=[tmp4.r])
        S.op("dve", lambda e: e.tensor_scalar(out=c8.t[:, :, 0], in0=tmp4.t[:, :], scalar1=-8.0, scalar2=None, op0=ALU.mult), reads=[tmp4.r], writes=[c8.r])
        S.op("dve", lambda e: e.tensor_scalar(out=c8.t[:, :, 1], in0=tmp4.t[:, :], scalar1=-16.0, scalar2=None, op0=ALU.mult), reads=[tmp4.r], writes=[c8.r])
        HL = 4
        xnT = self.abuf("xnT_m", [128, 8, HL + 128])
        mixedT = self.abuf("mixedT", [128, 8, 128])
        lcar = self.abuf("lcar", [128, 4], F32)
        Cst = self.abuf("Cst", [128, 4, 132], F32)
        Cb = self.abuf("Cb", [128, 4, 132])
        S.op("pool", lambda e: e.memset(xnT.t[:, :, 0:HL], 0.0), writes=[xnT.r])
        S.op("pool", lambda e: e.memset(lcar.t[:, :], 0.0), writes=[lcar.r])
        S.op("pool", lambda e: e.memset(Cst.t[:, :, :], 0.0), writes=[Cst.r])
        S.op("pool", lambda e: e.memset(Cb.t[:, :, :], 0.0), writes=[Cb.r])
        NT = 4
        f32t = [self.abuf("mo_f%d" % i, [128, 128], F32) for i in range(12)]
        bft = [self.abuf("mo_b%d" % i, [128, 128]) for i in range(10)]
        vaug = [self.abuf("vaug%d" % i, [128, 136]) for i in range(2)]
        nrow = [self.abuf("nrow%d" % i, [128, 132], F32) for i in range(2)]
        yrow = self.abuf("yrow", [128, 512])
        gt = self.abuf("gt", [128, 24], F32)
        opre = self.abuf("opre", [128, 512], F32)
        cnt = [0, 0, 0, 0]

        pools = {"F": f32t[0:8], "B": bft[0:2]}

        def F():
            cnt[0] += 1
            return pools["F"][cnt[0] % len(pools["F"])]

        def B():
            cnt[1] += 1
            return pools["B"][cnt[1] % len(pools["B"])]

        for blk in range(self.NB):
            ht = self.new_ht()
            self.load_h(src, blk, ht)
            self.norm_T(ht.t[:, :], ht.r, self.gb, xnT, HL)
            xw = xnT.t[:, :, 0:HL + 128]
            xc_ = xnT.t[:, :, HL:HL + 128]
            DBG = os.environ.get("KDBG", "")
            if "nolru" in DBG:
                S.op("pool", lambda e: e.memset(mixedT.t[:, 0:4, :], 0.0), writes=[mixedT.r])
            recA = []
            for n in range(0 if "nolru" in DBG else 4):
                S.begin_record()
                self.psf_sub = self.psf_pool[0:3]
                pools["F"], pools["B"] = f32t[0:8], bft[0:2]
                px = self.psf()
                self.mm_group(px.t[:, 0:HL + 128], px.r, [(Wcd.t[:, kc, n * 128:(n + 1) * 128], xnT.t[:, kc, 0:HL + 128], [Wcd.r, xnT.r]) for kc in range(8)])
                xc = F()
                w = lambda q, n=n: lv.t[:, n, q:q + 1]
                S.op("dve", lambda e, px=px, xc=xc, w=w: e.tensor_scalar(out=xc.t[:, :], in0=px.t[:, HL:HL + 128], scalar1=w(3), scalar2=w(4), op0=ALU.mult, op1=ALU.add),
                     reads=[px.r, lv.r], writes=[xc.r])
                for q in range(3):
                    S.op("dve", lambda e, px=px, xc=xc, w=w, q=q: e.scalar_tensor_tensor(out=xc.t[:, :], in0=px.t[:, HL - 3 + q:HL - 3 + q + 128], scalar=w(q), in1=xc.t[:, :],
                                                                                       op0=ALU.mult, op1=ALU.add), reads=[px.r, lv.r, xc.r], writes=[xc.r])
                xcb = B()
                S.op("act", lambda e, xc=xc, xcb=xcb: e.copy(out=xcb.t[:, :], in_=xc.t[:, :]), reads=[xc.r], writes=[xcb.r])
                pr = self.psf()
                self.mm_group(pr.t[:, 0:128], pr.r, [(Wg2.t[:, (0 * 4 + n) * 128:(0 * 4 + n + 1) * 128], xcb.t[:, :], [Wg.r, xcb.r])])
                self.mm_group(pr.t[:, 128:256], pr.r, [(Wg2.t[:, (1 * 4 + n) * 128:(1 * 4 + n + 1) * 128], xcb.t[:, :], [Wg.r, xcb.r])])
                rg = F(); ig = F()
                S.op("act", lambda e, pr=pr, rg=rg, w=w: e.activation(out=rg.t[:, :], in_=pr.t[:, 0:128], func=AF.Sigmoid, bias=w(5), scale=1.0),
                     reads=[pr.r, lv.r], writes=[rg.r])
                S.op("act", lambda e, pr=pr, ig=ig, w=w: e.activation(out=ig.t[:, :], in_=pr.t[:, 128:256], func=AF.Sigmoid, bias=w(6), scale=1.0),
                     reads=[pr.r, lv.r], writes=[ig.r])
                a = F(); a2 = F()
                S.op("act", lambda e, rg=rg, a=a, n=n: e.activation(out=a.t[:, :], in_=rg.t[:, :], func=AF.Exp, scale=c8.t[:, n, 0:1]), reads=[rg.r, c8.r], writes=[a.r])
                S.op("act", lambda e, rg=rg, a2=a2, n=n: e.activation(out=a2.t[:, :], in_=rg.t[:, :], func=AF.Exp, scale=c8.t[:, n, 1:2]), reads=[rg.r, c8.r], writes=[a2.r])
                S.op("dve", lambda e, a2=a2: e.tensor_scalar(out=a2.t[:, :], in0=a2.t[:, :], scalar1=-1.0, scalar2=1.0, op0=ALU.mult, op1=ALU.add), reads=[a2.r], writes=[a2.r])
                S.op("act", lambda e, a2=a2: e.activation(out=a2.t[:, :], in_=a2.t[:, :], func=AF.Sqrt), reads=[a2.r], writes=[a2.r])
                S.op("dve", lambda e, ig=ig, xc=xc: e.tensor_tensor(out=ig.t[:, :], in0=ig.t[:, :], in1=xc.t[:, :], op=ALU.mult), reads=[ig.r, xc.r], writes=[ig.r])
                S.op("dve", lambda e, ig=ig, a2=a2: e.tensor_tensor(out=ig.t[:, :], in0=ig.t[:, :], in1=a2.t[:, :], op=ALU.mult), reads=[ig.r, a2.r], writes=[ig.r])
                hl = F()
                S.op("dve", lambda e, hl=hl, a=a, ig=ig, n=n: e.tensor_tensor_scan(out=hl.t[:, :], data0=a.t[:, :], data1=ig.t[:, :], initial=lcar.t[:, n:n + 1],
                                                                                 op0=ALU.mult, op1=ALU.add), reads=[a.r, ig.r, lcar.r], writes=[hl.r])
                S.op("pool", lambda e, hl=hl, n=n: e.tensor_copy(out=lcar.t[:, n:n + 1], in_=hl.t[:, 127:128]), reads=[hl.r], writes=[lcar.r])
                pg = self.psf()
                self.mm_group(pg.t[:, 0:128], pg.r, [(Wcd.t[:, kc, 512 + n * 128:512 + (n + 1) * 128], xnT.t[:, kc, HL:HL + 128], [Wcd.r, xnT.r]) for kc in range(8)])
                g = F(); g3 = F()
                S.op("act", lambda e, pg=pg, g=g: e.copy(out=g.t[:, :], in_=pg.t[:, 0:128]), reads=[pg.r], writes=[g.r])
                S.op("dve", lambda e, g=g, g3=g3: e.tensor_tensor(out=g3.t[:, :], in0=g.t[:, :], in1=g.t[:, :], op=ALU.mult), reads=[g.r], writes=[g3.r])
                S.op("dve", lambda e, g3=g3: e.tensor_scalar(out=g3.t[:, :], in0=g3.t[:, :], scalar1=0.044715, scalar2=1.0, op0=ALU.mult, op1=ALU.add), reads=[g3.r], writes=[g3.r])
                S.op("dve", lambda e, g=g, g3=g3: e.tensor_tensor(out=g3.t[:, :], in0=g3.t[:, :], in1=g.t[:, :], op=ALU.mult), reads=[g.r, g3.r], writes=[g3.r])
                S.op("act", lambda e, g3=g3: e.activation(out=g3.t[:, :], in_=g3.t[:, :], func=AF.Sigmoid, scale=1.5957691216), reads=[g3.r], writes=[g3.r])
                S.op("dve", lambda e, g=g, g3=g3: e.tensor_tensor(out=g3.t[:, :], in0=g3.t[:, :], in1=g.t[:, :], op=ALU.mult), reads=[g.r, g3.r], writes=[g3.r])
                S.op("dve", lambda e, hl=hl, g3=g3, n=n: e.tensor_tensor(out=mixedT.t[:, n, :], in0=g3.t[:, :], in1=hl.t[:, :], op=ALU.mult), reads=[hl.r, g3.r], writes=[mixedT.r])
                recA.append(S.end_record())
                self.psf_sub = None
            pools["F"], pools["B"] = f32t[8:12], bft[2:10]
            if "nomlstm" in DBG:
                for rA in recA:
                    S.emit_interleaved([rA])
                S.op("pool", lambda e: e.memset(mixedT.t[:, 4:8, :], 0.0), writes=[mixedT.r])
                S.op("pool", lambda e: e.tensor_copy(out=xnT.t[:, :, 0:HL], in_=xnT.t[:, :, 128:128 + HL]), reads=[xnT.r], writes=[xnT.r])
                self.out_proj(mixedT, Wout, ht, dst, blk)
                continue
            pgt = self.psf()
            self.mm_group(pgt.t[:, 0:8], pgt.r, [(xnT.t[:, kc, HL:HL + 128], Wcd.t[:, kc, 2048:2056], [Wcd.r, xnT.r]) for kc in range(8)])
            S.op("dve", lambda e, pgt=pgt: e.tensor_tensor(out=gt.t[:, 0:8], in0=pgt.t[:, 0:8], in1=mrow.t[:, 0:8], op=ALU.add), reads=[pgt.r, mrow.r], writes=[gt.r])
            S.op("act", lambda e: e.activation(out=gt.t[:, 8:12], in_=gt.t[:, 4:8], func=AF.Exp, scale=-1.0), reads=[gt.r], writes=[gt.r])
            S.op("act", lambda e: e.activation(out=gt.t[:, 8:12], in_=gt.t[:, 8:12], func=AF.Ln, bias=1.0, scale=1.0), reads=[gt.r], writes=[gt.r])
            pc = self.psf()
            self.mm_group(pc.t[:, 0:4], pc.r, [(self.Uf.t[:, :], gt.t[:, 8:12], [self.Uf.r, gt.r])])
            self.mm_group(pc.t[:, 4:8], pc.r, [(self.onesf.t[:, :], gt.t[:, 8:12], [self.onesf.r, gt.r])])
            S.op("dve", lambda e, pc=pc: e.tensor_tensor(out=gt.t[:, 12:16], in0=pc.t[:, 0:4], in1=gt.t[:, 0:4], op=ALU.add), reads=[pc.r, gt.r], writes=[gt.r])
            S.op("act", lambda e: e.activation(out=gt.t[:, 12:16], in_=gt.t[:, 12:16], func=AF.Exp), reads=[gt.r], writes=[gt.r])
            S.op("act", lambda e, pc=pc: e.activation(out=gt.t[:, 16:24], in_=pc.t[:, 0:8], func=AF.Exp, scale=-1.0), reads=[pc.r, gt.r], writes=[gt.r])
            po = self.psf()
            self.mm_group(po.t[:, :], po.r, [(xnT.t[:, kc, HL:HL + 128], Wcd.t[:, kc, 1536:2048], [Wcd.r, xnT.r]) for kc in range(8)])
            S.op("act", lambda e, po=po: e.activation(out=opre.t[:, :], in_=po.t[:, :], func=AF.Sigmoid), reads=[po.r], writes=[opre.r])
            S.op("dve", lambda e: e.tensor_tensor(out=opre.t[:, :], in0=opre.t[:, :], in1=mrow.t[:, 8:520], op=ALU.mult), reads=[opre.r, mrow.r], writes=[opre.r])
            for n in range(4):
                S.begin_record()
                self.psf_sub = self.psf_pool[3:6]
                px = self.psf()
                self.mm_group(px.t[:, 0:HL + 128], px.r, [(Wcd.t[:, kc, 1024 + n * 128:1024 + (n + 1) * 128], xnT.t[:, kc, 0:HL + 128], [Wcd.r, xnT.r]) for kc in range(8)])
                xc = F()
                w = lambda q, n=n: mv.t[:, n, q:q + 1]
                S.op("dve", lambda e, px=px, xc=xc, w=w: e.tensor_scalar(out=xc.t[:, :], in0=px.t[:, HL:HL + 128], scalar1=w(3), scalar2=w(4), op0=ALU.mult, op1=ALU.add),
                     reads=[px.r, mv.r], writes=[xc.r])
                for q in range(3):
                    S.op("dve", lambda e, px=px, xc=xc, w=w, q=q: e.scalar_tensor_tensor(out=xc.t[:, :], in0=px.t[:, HL - 3 + q:HL - 3 + q + 128], scalar=w(q), in1=xc.t[:, :],
                                                                                       op0=ALU.mult, op1=ALU.add), reads=[px.r, mv.r, xc.r], writes=[xc.r])
                xcm = B(); mxb = B()
                S.op("act", lambda e, xc=xc, xcm=xcm: e.activation(out=xcm.t[:, :], in_=xc.t[:, :], func=AF.Silu), reads=[xc.r], writes=[xcm.r])
                S.op("act", lambda e, px=px, mxb=mxb: e.copy(out=mxb.t[:, :], in_=px.t[:, HL:HL + 128]), reads=[px.r], writes=[mxb.r])
                wq = Wbd2.t[:, (0 * 4 + n) * 128:(0 * 4 + n + 1) * 128]
                wk = Wbd2.t[:, (1 * 4 + n) * 128:(1 * 4 + n + 1) * 128]
                wv = Wbd2.t[:, (2 * 4 + n) * 128:(2 * 4 + n + 1) * 128]
                pqk = self.psf()
                self.mm_group(pqk.t[:, 0:128], pqk.r, [(wq, xcm.t[:, :], [Wbd.r, xcm.r])])
                self.mm_group(pqk.t[:, 128:256], pqk.r, [(wk, xcm.t[:, :], [Wbd.r, xcm.r])])
                self.mm_group(pqk.t[:, 256:384], pqk.r, [(xcm.t[:, :], wk, [Wbd.r, xcm.r])])
                self.mm_group(pqk.t[:, 384:512], pqk.r, [(mxb.t[:, :], wv, [Wbd.r, mxb.r])])
                qT = B(); kT = B(); kr = B()
                S.op("act", lambda e, pqk=pqk, qT=qT: e.copy(out=qT.t[:, :], in_=pqk.t[:, 0:128]), reads=[pqk.r], writes=[qT.r])
                S.op("act", lambda e, pqk=pqk, kT=kT: e.mul(out=kT.t[:, :], in_=pqk.t[:, 128:256], mul=128.0 ** -0.5), reads=[pqk.r], writes=[kT.r])
                S.op("act", lambda e, pqk=pqk, kr=kr: e.mul(out=kr.t[:, :], in_=pqk.t[:, 256:384], mul=128.0 ** -0.5), reads=[pqk.r], writes=[kr.r])
                va = vaug[cnt[2] % 2]; cnt[2] += 1
                S.op("act", lambda e, pqk=pqk, va=va, n=n: e.activation(out=va.t[:, 0:128], in_=pqk.t[:, 384:512], func=AF.Identity, bias=0.0, scale=gt.t[:, 12 + n:13 + n]),
                     reads=[pqk.r, gt.r], writes=[va.r])
                S.op("dve", lambda e, va=va, n=n: e.tensor_tensor(out=va.t[:, 128:136], in0=self.onesf.t[:, 0:8], in1=gt.t[:, 12 + n:13 + n].to_broadcast([128, 8]), op=ALU.mult),
                     reads=[gt.r, self.onesf.r], writes=[va.r])
                psT = self.psf()
                self.mm_group(psT.t[:, 0:128], psT.r, [(kT.t[:, :], qT.t[:, :], [kT.r, qT.r])])
                scT = B()
                S.op("dve", lambda e, psT=psT, scT=scT: e.tensor_tensor(out=scT.t[:, :], in0=psT.t[:, 0:128], in1=self.Uf.t[:, :], op=ALU.mult),
                     reads=[psT.r, self.Uf.r], writes=[scT.r])
                pn = self.psf()
                self.mm_group(pn.t[:, 0:130], pn.r, [(scT.t[:, :], va.t[:, 0:130], [scT.r, va.r]), (qT.t[:, :], Cb.t[:, n, 0:130], [qT.r, Cb.r])])
                nr = nrow[cnt[3] % 2]; cnt[3] += 1
                S.op("dve", lambda e, pn=pn, nr=nr, n=n: e.tensor_tensor(out=nr.t[:, 0:129], in0=pn.t[:, 0:129], in1=gt.t[:, 16 + n:17 + n].to_broadcast([128, 129]), op=ALU.mult),
                     reads=[pn.r, gt.r], writes=[nr.r])
                S.op("act", lambda e, nr=nr: e.activation(out=nr.t[:, 129:130], in_=nr.t[:, 128:129], func=AF.Abs), reads=[nr.r], writes=[nr.r])
                S.op("dve", lambda e, nr=nr: e.tensor_scalar(out=nr.t[:, 129:130], in0=nr.t[:, 129:130], scalar1=1.0, scalar2=None, op0=ALU.max),
                     reads=[nr.r], writes=[nr.r])
                S.op("dve", lambda e, nr=nr: e.reciprocal(out=nr.t[:, 130:131], in_=nr.t[:, 129:130]), reads=[nr.r], writes=[nr.r])
                hh = F()
                S.op("dve", lambda e, nr=nr, hh=hh: e.tensor_tensor(out=hh.t[:, :], in0=nr.t[:, 0:128], in1=nr.t[:, 130:131].to_broadcast([128, 128]), op=ALU.mult),
                     reads=[nr.r], writes=[hh.r])
                jk = F()
                S.op("act", lambda e, hh=hh, jk=jk, nr=nr: e.activation(out=jk.t[:, :], in_=hh.t[:, :], func=AF.Square, accum_out=nr.t[:, 131:132]),
                     reads=[hh.r], writes=[jk.r, nr.r])
                S.op("dve", lambda e, nr=nr: e.tensor_scalar(out=nr.t[:, 129:130], in0=nr.t[:, 131:132], scalar1=1.0 / 128.0, scalar2=EPS, op0=ALU.mult, op1=ALU.add),
                     reads=[nr.r], writes=[nr.r])
                S.op("act", lambda e, nr=nr: e.activation(out=nr.t[:, 129:130], in_=nr.t[:, 129:130], func=AF.Sqrt), reads=[nr.r], writes=[nr.r])
                S.op("dve", lambda e, nr=nr: e.reciprocal(out=nr.t[:, 130:131], in_=nr.t[:, 129:130]), reads=[nr.r], writes=[nr.r])
                S.op("dve", lambda e, nr=nr, hh=hh, n=n: e.scalar_tensor_tensor(out=yrow.t[:, n * 128:(n + 1) * 128], in0=hh.t[:, :], scalar=nr.t[:, 130:131],
                                                                             in1=opre.t[:, n * 128:(n + 1) * 128], op0=ALU.mult, op1=ALU.mult),
                     reads=[nr.r, hh.r, opre.r], writes=[yrow.r])
                pC = self.psf()
                self.mm_group(pC.t[:, 0:130], pC.r, [(kr.t[:, :], va.t[:, 0:130], [kr.r, va.r])])
                S.op("dve", lambda e, pC=pC, n=n: e.tensor_tensor(out=Cst.t[:, n, 0:129], in0=Cst.t[:, n, 0:129], in1=pC.t[:, 0:129], op=ALU.add),
                     reads=[pC.r, Cst.r], writes=[Cst.r])
                S.op("dve", lambda e, n=n: e.tensor_tensor(out=Cst.t[:, n, 0:129], in0=Cst.t[:, n, 0:129], in1=gt.t[:, 20 + n:21 + n].to_broadcast([128, 129]), op=ALU.mult),
                     reads=[Cst.r, gt.r], writes=[Cst.r])
                S.op("act", lambda e, n=n: e.copy(out=Cb.t[:, n, 0:130], in_=Cst.t[:, n, 0:130]), reads=[Cst.r], writes=[Cb.r])
                recB = S.end_record()
                self.psf_sub = None
                S.emit_interleaved([recA[n], recB] if n < len(recA) else [recB])
            pT = self.psb()
            for n in range(4):
                S.op("pe", lambda e, n=n, pT=pT: e.transpose(pT.t[:, n * 128:(n + 1) * 128], yrow.t[:, n * 128:(n + 1) * 128], self.identb.t[:]),
                     reads=[yrow.r, self.identb.r], writes=[pT.r], signal=(n == 3))
            S.op("act", lambda e, pT=pT: e.copy(out=mixedT.t[:, 4:8, :], in_=pT.t[:, 0:512].rearrange("p (k n) -> p k n", k=4)), reads=[pT.r], writes=[mixedT.r])
            S.op("pool", lambda e: e.tensor_copy(out=xnT.t[:, :, 0:HL], in_=xnT.t[:, :, 128:128 + HL]), reads=[xnT.r], writes=[xnT.r])
            self.out_proj(mixedT, Wout, ht, dst, blk)

    def out_proj(self, mixedT, Wout, ht, dst, blk, nk=8):
        S = self.S
        for half in range(2):
            pw = self.psf()
            self.mm_group(pw.t[:, :], pw.r, [(mixedT.t[:, kc, :], Wout.t[:, kc, half * 512:(half + 1) * 512], [mixedT.r, Wout.r]) for kc in range(nk)])
            S.op("dve", lambda e, ht=ht, half=half, pw=pw: e.tensor_tensor(
                out=ht.t[:, half * 512:(half + 1) * 512], in0=ht.t[:, half * 512:(half + 1) * 512], in1=pw.t[:, :], op=ALU.add),
                reads=[ht.r, pw.r], writes=[ht.r])
        self.store_h(dst, blk, ht)

    def phase_mixer(self, l, src, dst):
        if l % 2 == 1:
            self.phase_mixer_odd(l, src, dst)
        else:
            self.phase_mixer_even(l, src, dst)

    def decl_mixer_inputs(self):
        NE, NO = (self.NL + 1) // 2, self.NL // 2
        inp = self.inp
        inp("cd_w_in", [NO, D, 2056]); inp("lru_gw", [NO, 128, 1024]); inp("ml_bd", [NO, 128, 12 * 128])
        inp("lru_vec", [NO, 128, 32]); inp("ml_vec", [NO, 128, 20]); inp("ml_row", [NO, 128, 520])
        self.decl_even_inputs(NE)

    def alloc_mixer_bufs(self):
        S = self.S
        self.Uf = self.sb("Uf", [128, 128])
        self.onesf = self.sb("onesf", [128, 128])
        S.op("pool", lambda e: e.memset(self.onesf.t[:], 1.0), writes=[self.onesf.r])
        S.op("pool", lambda e: e.memset(self.Uf.t[:], 1.0), writes=[self.Uf.r])
        S.op("pool", lambda e: e.affine_select(out=self.Uf.t[:], in_=self.Uf.t[:], pattern=[[1, 128]], compare_op=ALU.is_ge,
                                               fill=0.0, base=0, channel_multiplier=-1), reads=[self.Uf.r], writes=[self.Uf.r])
        self.alloc_even_bufs()

    def decl_even_inputs(self, NE):
        inp = self.inp
        inp("ab_w_in", [NE, D, 3344]); inp("gla_wa", [NE, 16, 256]); inp("gla_vec", [NE, 64, 4]); inp("gla_ng", [NE, 128, 4])
        inp("rw_w2", [NE, 64, 512]); inp("rw_a2", [NE, 64, 512]); inp("rw_g2", [NE, 128, 512])
        inp("rw_vec", [NE, 64, 64]); inp("rw_mu_s", [NE, 128, 4]); inp("rw_row", [NE, 128, 1024])

    def alloc_even_bufs(self):
        S = self.S
        self.Us = self.sb("Us", [128, 128])
        self.Ls = self.sb("Ls", [128, 128])
        S.op("pool", lambda e: e.memset(self.Us.t[:], 1.0), writes=[self.Us.r])
        S.op("pool", lambda e: e.affine_select(out=self.Us.t[:], in_=self.Us.t[:], pattern=[[1, 128]], compare_op=ALU.is_gt,
                                               fill=0.0, base=0, channel_multiplier=-1), reads=[self.Us.r], writes=[self.Us.r])
        S.op("pool", lambda e: e.memset(self.Ls.t[:], 1.0), writes=[self.Ls.r])
        S.op("pool", lambda e: e.affine_select(out=self.Ls.t[:], in_=self.Ls.t[:], pattern=[[-1, 128]], compare_op=ALU.is_gt,
                                               fill=0.0, base=0, channel_multiplier=1), reads=[self.Ls.r], writes=[self.Ls.r])

    def phase_gla(self, l, src, dst):
        S = self.S
        d = self.din
        j = l // 2
        DBG = os.environ.get("KDBG", "")
        self.arena_reset()
        Wg = self.arena_alloc("w_gla", 8, 1552)
        Wout = self.arena_alloc("wout_a", 4, D)
        Wa = self.arena_alloc("w_alpha", 1, 256)
        self.load_small(d["norm_mix_g"][l:l + 1, :].partition_broadcast(128), self.gb)
        self.load_w(d["ab_w_in"][j], Wg, D, 1552)
        self.load_w(d["w_mix_out"][l][0:512, :], Wout, 512, D)
        self.load_w(d["gla_wa"][j], Wa, 16, 256)
        gvec = self.abuf("gla_vec", [64, 4], F32)
        gng = self.abuf("gla_ng", [128, 4], F32)
        self.load_small(d["gla_vec"][j], gvec)
        self.load_small(d["gla_ng"][j], gng)
        xnT = self.abuf("xnT_e", [128, 8, 128])
        mixedT = self.abuf("mixedT", [128, 4, 128])
        Sg = self.abuf("Sg", [64, 4, 128], F32)
        Sgb = self.abuf("Sgb", [64, 4, 128])
        S.op("pool", lambda e: e.memset(Sg.t[:, :, :], 0.0), writes=[Sg.r])
        S.op("pool", lambda e: e.memset(Sgb.t[:, :, :], 0.0), writes=[Sgb.r])
        gq = self.abuf("gq", [64, 4, 128], F32)
        gk = self.abuf("gk", [64, 4, 128], F32)
        gx = self.abuf("gx", [64, 4, 128], F32)
        gcs = self.abuf("gcs", [64, 4, 128], F32)
        ge = self.abuf("ge", [64, 4, 128], F32)
        gsd = self.abuf("gsd", [64, 8], F32)
        qdec = self.abuf("qdec", [64, 4, 128])
        kinv = self.abuf("kinv", [64, 4, 128])
        kend = self.abuf("kend", [64, 4, 128])
        kendr = self.abuf("kendr", [128, 256])
        vrow = self.abuf("g_vrow", [128, 512])
        alrT = self.abuf("alrT", [16, 128])
        scT = self.abuf("g_scT", [128, 4, 128])
        osq = self.abuf("g_osq", [128, 512], F32)
        orst = self.abuf("g_orst", [128, 512], F32)
        gsil = self.abuf("g_sil", [128, 4, 128], F32)
        for blk in range(self.NB):
            ht = self.new_ht()
            self.load_h(src, blk, ht)
            xn_cur = self.norm_T(ht.t[:, :], ht.r, self.gb, xnT, 0)
            xw = [Wg.r, xnT.r]
            pq = self.psf(); pk = self.psf()
            for h in range(4):
                self.mm_group(pq.t[0:64, h * 128:(h + 1) * 128], pq.r, [(Wg.t[:, kc, h * 64:(h + 1) * 64], xnT.t[:, kc, :], xw) for kc in range(8)])
            for h in range(4):
                self.mm_group(pk.t[0:64, h * 128:(h + 1) * 128], pk.r, [(Wg.t[:, kc, 256 + h * 64:256 + (h + 1) * 64], xnT.t[:, kc, :], xw) for kc in range(8)])
            S.op("act", lambda e, pq=pq: e.copy(out=gq.t[:, :, :], in_=pq.t[0:64, :].rearrange("p (h n) -> p h n", h=4)), reads=[pq.r], writes=[gq.r])
            S.op("act", lambda e, pk=pk: e.copy(out=gk.t[:, :, :], in_=pk.t[0:64, :].rearrange("p (h n) -> p h n", h=4)), reads=[pk.r], writes=[gk.r])
            pv = self.psf()
            self.mm_group(pv.t[:, :], pv.r, [(xnT.t[:, kc, :], Wg.t[:, kc, 512:1024], xw) for kc in range(8)])
            S.op("act", lambda e, pv=pv: e.copy(out=vrow.t[:, :], in_=pv.t[:, :]), reads=[pv.r], writes=[vrow.r])
            pg = self.psf()
            for h in range(4):
                self.mm_group(pg.t[:, h * 128:(h + 1) * 128], pg.r, [(Wg.t[:, kc, 1024 + h * 128:1024 + (h + 1) * 128], xnT.t[:, kc, :], xw) for kc in range(8)])
            S.op("act", lambda e, pg=pg: e.activation(out=gsil.t[:, :, :], in_=pg.t[:, :].rearrange("p (h n) -> p h n", h=4), func=AF.Silu), reads=[pg.r], writes=[gsil.r])
            pa = self.psf()
            self.mm_group(pa.t[0:16, 0:128], pa.r, [(Wg.t[:, kc, 1536:1552], xnT.t[:, kc, :], xw) for kc in range(8)])
            S.op("act", lambda e, pa=pa: e.copy(out=alrT.t[:, :], in_=pa.t[0:16, 0:128]), reads=[pa.r], writes=[alrT.r])
            px = self.psf()
            for h in range(4):
                self.mm_group(px.t[0:64, h * 128:(h + 1) * 128], px.r, [(Wa.t[0:16, 0, h * 64:(h + 1) * 64], alrT.t[:, :], [Wa.r, alrT.r])])
            S.op("dve", lambda e, px=px: e.tensor_tensor(out=gx.t[:, :, :], in0=px.t[0:64, :].rearrange("p (h n) -> p h n", h=4),
                                                         in1=gvec.t[:, 0:4].unsqueeze(2).to_broadcast([64, 4, 128]), op=ALU.add), reads=[px.r, gvec.r], writes=[gx.r])
            S.op("act", lambda e: e.activation(out=gx.t[:, :, :], in_=gx.t[:, :, :], func=AF.Exp, scale=-1.0), reads=[gx.r], writes=[gx.r])
            S.op("act", lambda e: e.activation(out=gx.t[:, :, :], in_=gx.t[:, :, :], func=AF.Ln, bias=1.0, scale=1.0), reads=[gx.r], writes=[gx.r])
            for h in range(4):
                S.op("dve", lambda e, h=h: e.tensor_tensor_scan(out=gcs.t[:, h, :], data0=self.onesf.t[0:64, :], data1=gx.t[:, h, :], initial=0.0,
                                                                op0=ALU.mult, op1=ALU.add), reads=[gx.r, self.onesf.r], writes=[gcs.r])
            S.op("act", lambda e: e.activation(out=ge.t[:, :, :], in_=gcs.t[:, :, :], func=AF.Exp, scale=-1.0 / 16.0), reads=[gcs.r], writes=[ge.r])
            S.op("dve", lambda e: e.scalar_tensor_tensor(out=qdec.t[:, :, :], in0=gq.t[:, :, :], scalar=0.125, in1=ge.t[:, :, :], op0=ALU.mult, op1=ALU.mult),
                 reads=[gq.r, ge.r], writes=[qdec.r])
            S.op("act", lambda e: e.copy(out=gsd.t[:, 0:4], in_=ge.t[:, :, 127]), reads=[ge.r], writes=[gsd.r])
            S.op("act", lambda e: e.activation(out=ge.t[:, :, :], in_=gcs.t[:, :, :], func=AF.Exp, scale=1.0 / 16.0), reads=[gcs.r], writes=[ge.r])
            S.op("dve", lambda e: e.tensor_tensor(out=kinv.t[:, :, :], in0=gk.t[:, :, :], in1=ge.t[:, :, :], op=ALU.mult), reads=[gk.r, ge.r], writes=[kinv.r])
            S.op("dve", lambda e: e.tensor_tensor(out=gx.t[:, :, :], in0=gcs.t[:, :, :], in1=gcs.t[:, :, 127:128].to_broadcast([64, 4, 128]), op=ALU.subtract),
                 reads=[gcs.r], writes=[gx.r])
            S.op("act", lambda e: e.activation(out=ge.t[:, :, :], in_=gx.t[:, :, :], func=AF.Exp, scale=1.0 / 16.0), reads=[gx.r], writes=[ge.r])
            S.op("dve", lambda e: e.tensor_tensor(out=kend.t[:, :, :], in0=gk.t[:, :, :], in1=ge.t[:, :, :], op=ALU.mult), reads=[gk.r, ge.r], writes=[kend.r])
            pT = self.psb()
            for h in range(4):
                S.op("pe", lambda e, h=h, pT=pT: e.transpose(pT.t[:, h * 64:(h + 1) * 64], kend.t[:, h, :], self.identb.t[0:64, 0:64]),
                     reads=[kend.r, self.identb.r], writes=[pT.r], signal=(h == 3))
            S.op("act", lambda e, pT=pT: e.copy(out=kendr.t[:, :], in_=pT.t[:, 0:256]), reads=[pT.r], writes=[kendr.r])
            psc = self.psf()
            for h in range(4):
                self.mm_group(psc.t[:, h * 128:(h + 1) * 128], psc.r, [(kinv.t[:, h, :], qdec.t[:, h, :], [kinv.r, qdec.r])])
            S.op("dve", lambda e, psc=psc: e.tensor_tensor(out=scT.t[:, :, :], in0=psc.t[:, :].rearrange("p (h n) -> p h n", h=4),
                                                           in1=self.Uf.t[:, :].unsqueeze(1).to_broadcast([128, 4, 128]), op=ALU.mult),
                 reads=[psc.r, self.Uf.r], writes=[scT.r])
            po = self.psf()
            for h in range(4):
                self.mm_group(po.t[:, h * 128:(h + 1) * 128], po.r, [(vrow.t[:, h * 128:(h + 1) * 128], scT.t[:, h, :], [vrow.r, scT.r]),
                                                                  (Sgb.t[:, h, :], qdec.t[:, h, :], [Sgb.r, qdec.r])])
            pS = self.psf()
            for h in range(4):
                self.mm_group(pS.t[0:64, h * 128:(h + 1) * 128], pS.r, [(kendr.t[:, h * 64:(h + 1) * 64], vrow.t[:, h * 128:(h + 1) * 128], [kendr.r, vrow.r])])
            S.op("dve", lambda e: e.tensor_tensor(out=Sg.t[:, :, :], in0=Sg.t[:, :, :], in1=gsd.t[:, 0:4].unsqueeze(2).to_broadcast([64, 4, 128]), op=ALU.mult),
                 reads=[Sg.r, gsd.r], writes=[Sg.r])
            S.op("dve", lambda e, pS=pS: e.tensor_tensor(out=Sg.t[:, :, :], in0=Sg.t[:, :, :], in1=pS.t[0:64, :].rearrange("p (h n) -> p h n", h=4), op=ALU.add),
                 reads=[Sg.r, pS.r], writes=[Sg.r])
            S.op("act", lambda e: e.copy(out=Sgb.t[:, :, :], in_=Sg.t[:, :, :]), reads=[Sg.r], writes=[Sgb.r])
            S.op("act", lambda e, po=po: e.activation(out=osq.t[:, :], in_=po.t[:, :], func=AF.Square), reads=[po.r], writes=[osq.r])
            pss = self.psf()
            self.mm_group(pss.t[:, :], pss.r, [(self.onesf.t[:, :], osq.t[:, :], [self.onesf.r, osq.r])])
            S.op("dve", lambda e, pss=pss: e.tensor_scalar(out=orst.t[:, :], in0=pss.t[:, :], scalar1=1.0 / 128.0, scalar2=EPS, op0=ALU.mult, op1=ALU.add),
                 reads=[pss.r], writes=[orst.r])
            S.op("act", lambda e: e.activation(out=orst.t[:, :], in_=orst.t[:, :], func=AF.Sqrt), reads=[orst.r], writes=[orst.r])
            S.op("dve", lambda e: e.reciprocal(out=orst.t[:, :], in_=orst.t[:, :]), reads=[orst.r], writes=[orst.r])
            S.op("act", lambda e, po=po: e.copy(out=osq.t[:, :], in_=po.t[:, :]), reads=[po.r, osq.r], writes=[osq.r])
            S.op("dve", lambda e: e.tensor_tensor(out=osq.t[:, :], in0=osq.t[:, :], in1=orst.t[:, :], op=ALU.mult), reads=[osq.r, orst.r], writes=[osq.r])
            S.op("dve", lambda e: e.tensor_tensor(out=gsil.t[:, :, :], in0=gsil.t[:, :, :], in1=gng.t[:, 0:4].unsqueeze(2).to_broadcast([128, 4, 128]), op=ALU.mult),
                 reads=[gsil.r, gng.r], writes=[gsil.r])
            S.op("dve", lambda e: e.tensor_tensor(out=mixedT.t[:, 0:4, :], in0=osq.t[:, :].rearrange("p (h n) -> p h n", h=4), in1=gsil.t[:, :, :], op=ALU.mult),
                 reads=[osq.r, gsil.r], writes=[mixedT.r])
            self.out_proj(mixedT, Wout, ht, dst, blk, nk=4)

    def phase_rwkv(self, l, src, resbuf):
        S = self.S
        d = self.din
        j = l // 2
        CW = 0.6065306597126334
        self.arena_reset()
        Wr = self.arena_alloc("w_rwkv", 8, 1792)
        Wout = self.arena_alloc("wout_b", 4, D)
        W2b = self.arena_alloc("rw_w2", 1, 512)
        A2b = self.arena_alloc("rw_a2", 1, 512)
        G2b = self.arena_alloc("rw_g2", 1, 512)
        self.load_small(d["norm_mix_g"][l:l + 1, :].partition_broadcast(128), self.gb)
        self.load_w(d["ab_w_in"][j], Wr, D, 1792, src_col0=1552)
        self.load_w(d["w_mix_out"][l][512:1024, :], Wout, 512, D)
        self.load_w(d["rw_w2"][j], W2b, 64, 512)
        self.load_w(d["rw_a2"][j], A2b, 64, 512)
        self.load_w(d["rw_g2"][j], G2b, 128, 512)
        rvec = self.abuf("rw_vec", [64, 8, 8], F32)
        mus = self.abuf("rw_mus", [128, 4], F32)
        rrow = self.abuf("rw_row", [128, 1024], F32)
        self.load_small(d["rw_vec"][j], rvec)
        self.load_small(d["rw_mu_s"][j], mus)
        self.load_small(d["rw_row"][j], rrow)
        vb = lambda i: rvec.t[:, i, :].unsqueeze(2).to_broadcast([64, 8, 128])
        xnT = self.abuf("xnT_r", [128, 8, 128])
        mixedT = self.abuf("mixedT_r", [128, 4, 128])
        K3 = lambda name: self.abuf(name, [64, 8, 128], F32)
        Pr = self.abuf("Pr", [64, 8, 129], F32); Pk = self.abuf("Pk", [64, 8, 129], F32); Pv = self.abuf("Pv", [64, 8, 129], F32)
        Pl = self.abuf("Pl", [128, 3, 129], F32)
        for P in (Pr, Pk, Pv, Pl):
            S.op("pool", lambda e, P=P: e.memset(P.t[:, :, :], 0.0), writes=[P.r])
        R = K3("R"); Kt = K3("K"); KK = K3("KK"); AS = K3("AS"); LW = K3("LW"); CS = K3("CS"); E = K3("E"); T1 = K3("T1")
        BI = K3("BI"); KI = K3("KI")
        AR = self.abuf("AR", [64, 8, 256], F32)
        ltmp = self.abuf("ltmp", [128, 128], F32)
        lor = self.abuf("lor", [128, 3, 128])
        sd = self.abuf("rsd", [64, 8], F32)
        Vrow = self.abuf("Vrow", [128, 512], F32); BEr = self.abuf("BEr", [128, 512], F32); KEr = self.abuf("KEr", [128, 512], F32)
        X = self.abuf("X", [128, 8, 128], F32)
        AZs = [self.abuf("AZ%d" % i, [128, 4, 128], F32) for i in range(2)]
        Zr = self.abuf("Zr", [128, 512], F32)
        Y = self.abuf("Y", [128, 8, 64], F32); Yc = self.abuf("Yc", [128, 8, 64], F32)
        GR = self.abuf("GR", [128, 512], F32)
        yb = self.abuf("yb", [128, 512])
        st8 = self.abuf("st8", [128, 32], F32)
        ST = self.abuf("ST", [128, 8, 64], F32)
        Malls = [self.abuf("Mall%d" % i, [128, 4, 512], F32) for i in range(2)]
        chs = [[self.abuf("ch%d_%d" % (k, i), [128, 4, 128], F32) for i in range(2)] for k in range(2)]
        Xh = [T_(X.t, Res("Xh%d" % i)) for i in range(2)]
        Zrh = [T_(Zr.t, Res("Zrh%d" % i)) for i in range(2)]
        Yh = [T_(Y.t, Res("Yh%d" % i)) for i in range(2)]
        STh = [T_(ST.t, Res("STh%d" % i)) for i in range(2)]
        MASK4 = self.abuf("MASK4", [128, 512], F32)
        S.op("pool", lambda e: e.memset(ST.t[:, :, :], 0.0), writes=[STh[0].r, STh[1].r])
        S.op("pool", lambda e: e.tensor_copy(out=ST.t[64:128, :, :], in_=self.identf.t[64:128, 64:128].unsqueeze(1).to_broadcast([64, 8, 64])),
             reads=[self.identf.r, STh[0].r, STh[1].r], writes=[STh[0].r, STh[1].r])
        for q, msk in enumerate((self.Us, self.Uf, self.Us, self.Uf)):
            S.op("pool", lambda e, q=q, msk=msk: e.tensor_copy(out=MASK4.t[:, q * 128:(q + 1) * 128], in_=msk.t[:, :]), reads=[msk.r, MASK4.r], writes=[MASK4.r])
        onesK = self.onesf.t[0:64, 0:64]
        v3 = lambda t: t.t[:, :, :]

        for blk in range(self.NB):
            htA = self.new_ht()
            self.load_h(src, blk, htA)
            self.norm_T(htA.t[:, :], htA.r, self.gb, xnT, 0)
            ht = self.new_ht()
            self.load_h(resbuf, blk, ht)
            xw = [Wr.r, xnT.r]
            for (c0, P) in ((0, Pr), (576, Pk), (1088, Pv)):
                for hh in range(2):
                    pp = self.psf()
                    for hl in range(4):
                        h = hh * 4 + hl
                        self.mm_group(pp.t[0:64, hl * 128:(hl + 1) * 128], pp.r, [(Wr.t[:, kc, c0 + h * 64:c0 + (h + 1) * 64], xnT.t[:, kc, :], xw) for kc in range(8)])
                    S.op("act", lambda e, pp=pp, P=P, hh=hh: e.copy(out=P.t[:, hh * 4:(hh + 1) * 4, 1:129], in_=pp.t[0:64, :].rearrange("p (h n) -> p h n", h=4)),
                         reads=[pp.r], writes=[P.r])
            pl = self.psf()
            self.mm_group(pl.t[0:64, 0:128], pl.r, [(Wr.t[:, kc, 512:576], xnT.t[:, kc, :], xw) for kc in range(8)])
            self.mm_group(pl.t[0:64, 128:256], pl.r, [(Wr.t[:, kc, 1600:1664], xnT.t[:, kc, :], xw) for kc in range(8)])
            self.mm_group(pl.t[:, 256:384], pl.r, [(Wr.t[:, kc, 1664:1792], xnT.t[:, kc, :], xw) for kc in range(8)])
            S.op("act", lambda e, pl=pl: e.copy(out=Pl.t[0:64, 0:2, 1:129], in_=pl.t[0:64, 0:256].rearrange("p (h n) -> p h n", h=2)), reads=[pl.r], writes=[Pl.r])
            S.op("act", lambda e, pl=pl: e.copy(out=Pl.t[:, 2, 1:129], in_=pl.t[:, 256:384]), reads=[pl.r, Pl.r], writes=[Pl.r])
            for (P, i, out) in ((Pr, 5, R), (Pk, 6, Kt), (Pv, 7, E)):
                S.op("dve", lambda e, P=P: e.tensor_tensor(out=T1.t[:, :, :], in0=P.t[:, :, 0:128], in1=P.t[:, :, 1:129], op=ALU.subtract), reads=[P.r], writes=[T1.r])
                S.op("dve", lambda e, i=i: e.tensor_tensor(out=T1.t[:, :, :], in0=T1.t[:, :, :], in1=vb(i), op=ALU.mult), reads=[T1.r, rvec.r], writes=[T1.r])
                S.op("dve", lambda e, P=P, out=out: e.tensor_tensor(out=out.t[:, :, :], in0=T1.t[:, :, :], in1=P.t[:, :, 1:129], op=ALU.add), reads=[T1.r, P.r], writes=[out.r])
                S.op("act", lambda e, P=P: e.copy(out=P.t[:, :, 0:1], in_=P.t[:, :, 128:129]), reads=[P.r], writes=[P.r])
            pt = self.psf()
            for h in range(8):
                S.op("pe", lambda e, h=h, pt=pt: e.transpose(pt.t[:, h * 64:(h + 1) * 64], E.t[:, h, :], self.identf.t[0:64, 0:64]),
                     reads=[E.r, self.identf.r], writes=[pt.r], signal=(h == 7))
            S.op("act", lambda e, pt=pt: e.copy(out=Vrow.t[:, :], in_=pt.t[:, :]), reads=[pt.r], writes=[Vrow.r])
            for i, (rows, fn) in enumerate(((64, AF.Tanh), (64, AF.Identity), (128, AF.Sigmoid))):
                S.op("dve", lambda e, i=i, rows=rows: e.tensor_tensor(out=ltmp.t[0:rows, :], in0=Pl.t[0:rows, i, 0:128], in1=Pl.t[0:rows, i, 1:129], op=ALU.subtract),
                     reads=[Pl.r], writes=[ltmp.r])
                S.op("dve", lambda e, i=i, rows=rows: e.scalar_tensor_tensor(out=ltmp.t[0:rows, :], in0=ltmp.t[0:rows, :], scalar=mus.t[0:rows, i:i + 1],
                                                                         in1=Pl.t[0:rows, i, 1:129], op0=ALU.mult, op1=ALU.add),
                     reads=[ltmp.r, mus.r, Pl.r], writes=[ltmp.r])
                S.op("act", lambda e, i=i, rows=rows, fn=fn: e.activation(out=lor.t[0:rows, i, :], in_=ltmp.t[0:rows, :], func=fn), reads=[ltmp.r], writes=[lor.r])
            S.op("act", lambda e: e.copy(out=Pl.t[:, :, 0:1], in_=Pl.t[:, :, 128:129]), reads=[Pl.r], writes=[Pl.r])
            for (Wl, li, vi, OUT) in ((W2b, 0, 0, LW), (A2b, 1, 1, AS)):
                for hh in range(2):
                    pp = self.psf()
                    for hl in range(4):
                        h = hh * 4 + hl
                        self.mm_group(pp.t[0:64, hl * 128:(hl + 1) * 128], pp.r, [(Wl.t[0:64, 0, h * 64:(h + 1) * 64], lor.t[0:64, li, :], [Wl.r, lor.r])])
                    S.op("dve", lambda e, pp=pp, hh=hh, vi=vi, OUT=OUT: e.tensor_tensor(
                        out=OUT.t[:, hh * 4:(hh + 1) * 4, :], in0=pp.t[0:64, :].rearrange("p (h n) -> p h n", h=4),
                        in1=rvec.t[:, vi, hh * 4:(hh + 1) * 4].unsqueeze(2).to_broadcast([64, 4, 128]), op=ALU.add), reads=[pp.r, rvec.r], writes=[OUT.r])
                S.op("act", lambda e, OUT=OUT: e.activation(out=OUT.t[:, :, :], in_=OUT.t[:, :, :], func=AF.Sigmoid), reads=[OUT.r], writes=[OUT.r])
            pgr = self.psf()
            self.mm_group(pgr.t[:, :], pgr.r, [(lor.t[:, 2, :], G2b.t[:, 0, :], [lor.r, G2b.r])])
            S.op("act", lambda e, pgr=pgr: e.copy(out=GR.t[:, :], in_=pgr.t[:, :]), reads=[pgr.r], writes=[GR.r])
            for h in range(8):
                S.op("dve", lambda e, h=h: e.tensor_tensor_scan(out=CS.t[:, h, :], data0=self.onesf.t[0:64, :], data1=LW.t[:, h, :], initial=0.0,
                                                                op0=ALU.mult, op1=ALU.add), reads=[LW.r, self.onesf.r], writes=[CS.r])
            S.op("dve", lambda e: e.tensor_tensor(out=v3(KK), in0=v3(Kt), in1=vb(2), op=ALU.mult), reads=[Kt.r, rvec.r], writes=[KK.r])
            S.op("dve", lambda e: e.tensor_tensor(out=v3(T1), in0=v3(KK), in1=v3(KK), op=ALU.mult), reads=[KK.r], writes=[T1.r])
            for hh in range(2):
                pp = self.psf()
                self.mm_group(pp.t[0:64, :], pp.r, [(onesK, T1.t[:, hh * 4:(hh + 1) * 4, :], [self.onesf.r, T1.r])])
                S.op("act", lambda e, pp=pp, hh=hh: e.activation(out=E.t[:, hh * 4:(hh + 1) * 4, :], in_=pp.t[0:64, :].rearrange("p (h n) -> p h n", h=4), func=AF.Sqrt),
                     reads=[pp.r], writes=[E.r])
            S.op("dve", lambda e: e.tensor_scalar(out=v3(E), in0=v3(E), scalar1=1e-6, scalar2=None, op0=ALU.max), reads=[E.r], writes=[E.r])
            S.op("dve", lambda e: e.reciprocal(out=v3(E), in_=v3(E)), reads=[E.r], writes=[E.r])
            S.op("dve", lambda e: e.tensor_tensor(out=v3(KK), in0=v3(KK), in1=v3(E), op=ALU.mult), reads=[KK.r, E.r], writes=[KK.r])
            S.op("dve", lambda e: e.tensor_scalar(out=v3(T1), in0=v3(AS), scalar1=-1.0, scalar2=None, op0=ALU.add), reads=[AS.r], writes=[T1.r])
            S.op("dve", lambda e: e.tensor_tensor(out=v3(T1), in0=v3(T1), in1=vb(3), op=ALU.mult), reads=[T1.r, rvec.r], writes=[T1.r])
            S.op("dve", lambda e: e.tensor_scalar(out=v3(T1), in0=v3(T1), scalar1=1.0, scalar2=None, op0=ALU.add), reads=[T1.r], writes=[T1.r])
            S.op("dve", lambda e: e.tensor_tensor(out=v3(Kt), in0=v3(Kt), in1=v3(T1), op=ALU.mult), reads=[Kt.r, T1.r], writes=[Kt.r])
            S.op("dve", lambda e: e.tensor_tensor(out=v3(T1), in0=v3(R), in1=v3(Kt), op=ALU.mult), reads=[R.r, Kt.r], writes=[T1.r])
            S.op("dve", lambda e: e.tensor_tensor(out=v3(T1), in0=v3(T1), in1=vb(4), op=ALU.mult), reads=[T1.r, rvec.r], writes=[T1.r])
            pb = self.psf()
            for h in range(8):
                self.mm_group(pb.t[:, h * 2:h * 2 + 2], pb.r, [(T1.t[:, h, :], self.onesf.t[0:64, 0:2], [T1.r, self.onesf.r])])
            S.op("act", lambda e, pb=pb: e.copy(out=st8.t[:, 16:32], in_=pb.t[:, 0:16]), reads=[pb.r], writes=[st8.r])
            S.op("dve", lambda e: e.tensor_tensor(out=v3(AS), in0=v3(AS), in1=v3(KK), op=ALU.mult), reads=[AS.r, KK.r], writes=[AS.r])
            S.op("act", lambda e: e.activation(out=v3(E), in_=v3(CS), func=AF.Exp, scale=-CW), reads=[CS.r], writes=[E.r])
            S.op("dve", lambda e: e.tensor_tensor(out=AR.t[:, :, 128:256], in0=v3(R), in1=v3(E), op=ALU.mult), reads=[R.r, E.r], writes=[AR.r])
            S.op("dve", lambda e: e.tensor_tensor(out=v3(T1), in0=v3(CS), in1=v3(LW), op=ALU.subtract), reads=[CS.r, LW.r], writes=[T1.r])
            S.op("act", lambda e: e.activation(out=v3(E), in_=v3(T1), func=AF.Exp, scale=-CW), reads=[T1.r], writes=[E.r])
            S.op("dve", lambda e: e.scalar_tensor_tensor(out=AR.t[:, :, 0:128], in0=v3(KK), scalar=-1.0, in1=v3(E), op0=ALU.mult, op1=ALU.mult),
                 reads=[KK.r, E.r], writes=[AR.r])
            S.op("act", lambda e: e.activation(out=v3(E), in_=v3(CS), func=AF.Exp, scale=CW), reads=[CS.r], writes=[E.r])
            S.op("dve", lambda e: e.tensor_tensor(out=v3(BI), in0=v3(AS), in1=v3(E), op=ALU.mult), reads=[AS.r, E.r], writes=[BI.r])
            S.op("dve", lambda e: e.tensor_tensor(out=v3(KI), in0=v3(Kt), in1=v3(E), op=ALU.mult), reads=[Kt.r, E.r], writes=[KI.r])
            S.op("act", lambda e: e.activation(out=sd.t[:, :], in_=CS.t[:, :, 127], func=AF.Exp, scale=-CW), reads=[CS.r], writes=[sd.r])
            S.op("dve", lambda e: e.tensor_tensor(out=v3(T1), in0=v3(CS), in1=CS.t[:, :, 127:128].to_broadcast([64, 8, 128]), op=ALU.subtract), reads=[CS.r], writes=[T1.r])
            S.op("act", lambda e: e.activation(out=v3(E), in_=v3(T1), func=AF.Exp, scale=CW), reads=[T1.r], writes=[E.r])
            S.op("dve", lambda e: e.tensor_tensor(out=v3(AS), in0=v3(AS), in1=v3(E), op=ALU.mult), reads=[AS.r, E.r], writes=[AS.r])
            S.op("dve", lambda e: e.tensor_tensor(out=v3(Kt), in0=v3(Kt), in1=v3(E), op=ALU.mult), reads=[Kt.r, E.r], writes=[Kt.r])
            for (srcap, srcr, dstap, dstt) in ((lambda h: AR.t[:, h, 0:128], AR.r, X.t[:, :, 0:64], None),
                                              (lambda h: AS.t[:, h, :], AS.r, BEr.t[:, :].rearrange("p (h n) -> p h n", h=8), BEr),
                                              (lambda h: Kt.t[:, h, :], Kt.r, KEr.t[:, :].rearrange("p (h n) -> p h n", h=8), KEr)):
                pt = self.psf()
                for h in range(8):
                    S.op("pe", lambda e, h=h, pt=pt, srcap=srcap: e.transpose(pt.t[:, h * 64:(h + 1) * 64], srcap(h), self.identf.t[0:64, 0:64]),
                         reads=[srcr, self.identf.r], writes=[pt.r], signal=(h == 7))
                S.op("act", lambda e, pt=pt, dstap=dstap: e.copy(out=dstap, in_=pt.t[:, :].rearrange("p (h n) -> p h n", h=8)), reads=[pt.r],
                     writes=[dstt.r] if dstt is not None else [Xh[0].r, Xh[1].r])
            recs = []
            for hh in range(2):
                S.begin_record()
                self.psf_sub = self.psf_pool[3 * hh:3 * hh + 3]
                Mall, ch, AZ = Malls[hh], chs[hh], AZs[hh]
                X_, Zr_, Y_, ST_ = Xh[hh], Zrh[hh], Yh[hh], STh[hh]
                for hl in range(4):
                    h = hh * 4 + hl
                    pM = self.psf()
                    self.mm_group(pM.t[:, 0:256], pM.r, [(BI.t[:, h, :], AR.t[:, h, :], [BI.r, AR.r])])
                    self.mm_group(pM.t[:, 256:512], pM.r, [(KI.t[:, h, :], AR.t[:, h, :], [KI.r, AR.r])])
                    S.op("dve", lambda e, pM=pM, hl=hl, Mall=Mall: e.tensor_tensor(out=Mall.t[:, hl, :], in0=pM.t[:, :], in1=MASK4.t[:, :], op=ALU.mult),
                         reads=[pM.r, MASK4.r], writes=[Mall.r])
                pA = self.psf()
                for hl in range(4):
                    h = hh * 4 + hl
                    self.mm_group(pA.t[:, hl * 128:(hl + 1) * 128], pA.r, [(AR.t[:, h, 0:128], BI.t[:, h, :], [AR.r, BI.r])])
                A0 = ch[0]
                S.op("dve", lambda e, pA=pA, A0=A0: e.tensor_tensor(out=A0.t[:, :, :], in0=pA.t[:, :].rearrange("p (h n) -> p h n", h=4),
                                                             in1=self.Ls.t[:, :].unsqueeze(1).to_broadcast([128, 4, 128]), op=ALU.mult), reads=[pA.r, self.Ls.r], writes=[A0.r])
                Ap, Apt = A0, T_(Mall.t[:, :, 0:128], Mall.r)
                Q = ch[1]
                S.op("dve", lambda e, Q=Q, Mall=Mall: e.tensor_tensor(out=Q.t[:, :, :], in0=Mall.t[:, :, 0:128], in1=self.identf.t[:, :].unsqueeze(1).to_broadcast([128, 4, 128]), op=ALU.add),
                     reads=[Mall.r, self.identf.r], writes=[Q.r])
                for step in range(6):
                    pN = self.psf()
                    for hl in range(4):
                        self.mm_group(pN.t[:, hl * 128:(hl + 1) * 128], pN.r, [(Apt.t[:, hl, :], Ap.t[:, hl, :], [Apt.r, Ap.r])])
                    if step < 5:
                        pNt = self.psf()
                        for hl in range(4):
                            self.mm_group(pNt.t[:, hl * 128:(hl + 1) * 128], pNt.r, [(Ap.t[:, hl, :], Apt.t[:, hl, :], [Apt.r, Ap.r])])
                    S.op("act", lambda e, pN=pN, Ap=Ap: e.copy(out=Ap.t[:, :, :], in_=pN.t[:, :].rearrange("p (h n) -> p h n", h=4)), reads=[pN.r], writes=[Ap.r])
                    if step < 5:
                        S.op("act", lambda e, pNt=pNt, Apt=Apt: e.copy(out=Apt.t[:, :, :], in_=pNt.t[:, :].rearrange("p (h n) -> p h n", h=4)), reads=[pNt.r], writes=[Apt.r])
                    pQ = self.psf()
                    for hl in range(4):
                        self.mm_group(pQ.t[:, hl * 128:(hl + 1) * 128], pQ.r, [(Ap.t[:, hl, :], Q.t[:, hl, :], [Ap.r, Q.r])])
                    S.op("dve", lambda e, pQ=pQ, Q=Q: e.tensor_tensor(out=Q.t[:, :, :], in0=Q.t[:, :, :], in1=pQ.t[:, :].rearrange("p (h n) -> p h n", h=4), op=ALU.add),
                         reads=[pQ.r, Q.r], writes=[Q.r])
                pK = self.psf()
                for hl in range(4):
                    h = hh * 4 + hl
                    self.mm_group(pK.t[:, hl * 64:(hl + 1) * 64], pK.r, [(Mall.t[:, hl, 256:384], Vrow.t[:, h * 64:(h + 1) * 64], [Mall.r, Vrow.r])])
                S.op("act", lambda e, pK=pK, hh=hh: e.copy(out=X.t[:, hh * 4:(hh + 1) * 4, 64:128], in_=pK.t[:, 0:256].rearrange("p (h n) -> p h n", h=4)),
                     reads=[pK.r], writes=[X_.r])
                pZ = self.psf()
                for hl in range(4):
                    h = hh * 4 + hl
                    self.mm_group(pZ.t[:, hl * 128:(hl + 1) * 128], pZ.r, [(X.t[:, h, :], Q.t[:, hl, :], [X_.r, Q.r])])
                S.op("act", lambda e, pZ=pZ, AZ=AZ: e.copy(out=AZ.t[:, :, :], in_=pZ.t[:, :].rearrange("p (h n) -> p h n", h=4)), reads=[pZ.r], writes=[AZ.r])
                pZr = self.psf()
                for hl in range(4):
                    h = hh * 4 + hl
                    self.mm_group(pZr.t[:, hl * 64:(hl + 1) * 64], pZr.r, [(AZ.t[:, hl, :], ST.t[:, h, :], [AZ.r, ST_.r])])
                S.op("dve", lambda e, pZr=pZr, hh=hh: e.tensor_copy(out=Zr.t[:, hh * 256:(hh + 1) * 256], in_=pZr.t[:, 0:256]), reads=[pZr.r], writes=[Zr_.r])
                pY = self.psf()
                for hl in range(4):
                    h = hh * 4 + hl
                    hs = slice(h * 64, (h + 1) * 64)
                    self.mm_group(pY.t[:, hl * 64:(hl + 1) * 64], pY.r, [(AR.t[:, h, 128:256], ST.t[0:64, h, :], [AR.r, ST_.r]),
                                                                      (Mall.t[:, hl, 128:256], Zr.t[:, hs], [Mall.r, Zr_.r]),
                                                                      (Mall.t[:, hl, 384:512], Vrow.t[:, hs], [Mall.r, Vrow.r])])
                S.op("act", lambda e, pY=pY, hh=hh: e.copy(out=Y.t[:, hh * 4:(hh + 1) * 4, :], in_=pY.t[:, 0:256].rearrange("p (h n) -> p h n", h=4)),
                     reads=[pY.r], writes=[Y_.r])
                pD = self.psf()
                for hl in range(4):
                    h = hh * 4 + hl
                    hs = slice(h * 64, (h + 1) * 64)
                    self.mm_group(pD.t[0:64, hl * 64:(hl + 1) * 64], pD.r, [(BEr.t[:, hs], Zr.t[:, hs], [BEr.r, Zr_.r]), (KEr.t[:, hs], Vrow.t[:, hs], [KEr.r, Vrow.r])])
                S.op("dve", lambda e, hh=hh: e.tensor_tensor(out=ST.t[0:64, hh * 4:(hh + 1) * 4, :], in0=ST.t[0:64, hh * 4:(hh + 1) * 4, :],
                                                             in1=sd.t[:, hh * 4:(hh + 1) * 4].unsqueeze(2).to_broadcast([64, 4, 64]), op=ALU.mult), reads=[ST_.r, sd.r], writes=[ST_.r])
                S.op("dve", lambda e, hh=hh, pD=pD: e.tensor_tensor(out=ST.t[0:64, hh * 4:(hh + 1) * 4, :], in0=ST.t[0:64, hh * 4:(hh + 1) * 4, :],
                                                                    in1=pD.t[0:64, 0:256].rearrange("p (h n) -> p h n", h=4), op=ALU.add), reads=[ST_.r, pD.r], writes=[ST_.r])
                recs.append(S.end_record())
                self.psf_sub = None
            S.emit_interleaved(recs)
            b8 = lambda c0: st8.t[:, c0:c0 + 8].unsqueeze(2).to_broadcast([128, 8, 64])
            S.op("dve", lambda e: e.tensor_reduce(out=st8.t[:, 0:8], in_=Y.t[:, :, :], axis=AX.X, op=ALU.add), reads=[Yh[0].r, Yh[1].r], writes=[st8.r])
            S.op("dve", lambda e: e.tensor_scalar(out=st8.t[:, 0:8], in0=st8.t[:, 0:8], scalar1=1.0 / 64.0, scalar2=None, op0=ALU.mult), reads=[st8.r], writes=[st8.r])
            S.op("dve", lambda e: e.tensor_tensor(out=Yc.t[:, :, :], in0=Y.t[:, :, :], in1=b8(0), op=ALU.subtract), reads=[Yh[0].r, Yh[1].r, st8.r], writes=[Yc.r])
            S.op("dve", lambda e: e.tensor_tensor(out=Y.t[:, :, :], in0=Yc.t[:, :, :], in1=Yc.t[:, :, :], op=ALU.mult), reads=[Yc.r], writes=[Yh[0].r, Yh[1].r])
            S.op("dve", lambda e: e.tensor_reduce(out=st8.t[:, 8:16], in_=Y.t[:, :, :], axis=AX.X, op=ALU.add), reads=[Yh[0].r, Yh[1].r], writes=[st8.r])
            S.op("dve", lambda e: e.tensor_scalar(out=st8.t[:, 8:16], in0=st8.t[:, 8:16], scalar1=1.0 / 64.0, scalar2=64e-5, op0=ALU.mult, op1=ALU.add), reads=[st8.r], writes=[st8.r])
            S.op("act", lambda e: e.activation(out=st8.t[:, 8:16], in_=st8.t[:, 8:16], func=AF.Sqrt), reads=[st8.r], writes=[st8.r])
            S.op("dve", lambda e: e.reciprocal(out=st8.t[:, 8:16], in_=st8.t[:, 8:16]), reads=[st8.r], writes=[st8.r])
            S.op("dve", lambda e: e.tensor_tensor(out=Yc.t[:, :, :], in0=Yc.t[:, :, :], in1=b8(8), op=ALU.mult), reads=[Yc.r, st8.r], writes=[Yc.r])
            Yc2 = Yc.t[:, :, :].rearrange("p h n -> p (h n)")
            S.op("dve", lambda e: e.tensor_tensor(out=Yc2, in0=Yc2, in1=rrow.t[:, 0:512], op=ALU.mult), reads=[Yc.r, rrow.r], writes=[Yc.r])
            S.op("dve", lambda e: e.tensor_tensor(out=Yc2, in0=Yc2, in1=rrow.t[:, 512:1024], op=ALU.add), reads=[Yc.r, rrow.r], writes=[Yc.r])
            S.op("dve", lambda e: e.tensor_tensor(out=Y.t[:, :, :], in0=Vrow.t[:, :].rearrange("p (h n) -> p h n", h=8),
                                                  in1=st8.t[:, 16:32:2].unsqueeze(2).to_broadcast([128, 8, 64]), op=ALU.mult), reads=[Vrow.r, st8.r], writes=[Yh[0].r, Yh[1].r])
            S.op("dve", lambda e: e.tensor_tensor(out=Yc.t[:, :, :], in0=Yc.t[:, :, :], in1=Y.t[:, :, :], op=ALU.add), reads=[Yc.r, Yh[0].r, Yh[1].r], writes=[Yc.r])
            S.op("dve", lambda e: e.tensor_tensor(out=yb.t[:, :], in0=Yc2, in1=GR.t[:, :], op=ALU.mult), reads=[Yc.r, GR.r], writes=[yb.r])
            pT = self.psb()
            for n in range(4):
                S.op("pe", lambda e, n=n, pT=pT: e.transpose(pT.t[:, n * 128:(n + 1) * 128], yb.t[:, n * 128:(n + 1) * 128], self.identb.t[:]),
                     reads=[yb.r, self.identb.r], writes=[pT.r], signal=(n == 3))
            S.op("act", lambda e, pT=pT: e.copy(out=mixedT.t[:, :, :], in_=pT.t[:, 0:512].rearrange("p (k n) -> p k n", k=4)), reads=[pT.r], writes=[mixedT.r])
            self.out_proj(mixedT, Wout, ht, resbuf, blk, nk=4)


def host_layout(inputs, NL):
    f = lambda a: np.ascontiguousarray(np.asarray(a, dtype=np.float32))
    m = {}
    for k in ("norm_mix_g", "norm_xattn_g", "norm_ffn_g", "xattn_wq", "xattn_wkv", "xattn_wo", "ffn_w_up", "ffn_w_down", "w_mix_out"):
        m[k] = f(inputs[k])
    m["mem_norm_g"] = f(inputs["mem_norm_g"]).reshape(1, D)
    m["final_norm_g"] = f(inputs["final_norm_g"]).reshape(1, D)
    cw = f(inputs["ffn_conv_w"])
    cb = f(inputs["ffn_conv_b"])
    pad = NCH_FF * 128 - DFF
    v = np.concatenate([cw, cb[:, None, :]], axis=1)
    v = np.pad(v, ((0, 0), (0, 0), (0, pad)))
    v = v.reshape(NL, 4, NCH_FF, 128).transpose(0, 3, 2, 1)
    m["ffn_vec"] = np.ascontiguousarray(v.reshape(NL, 128, NCH_FF * 4))
    NE = (NL + 1) // 2
    m["ab_w_in"] = f(inputs["ab_w_in"])
    m["gla_wa"] = f(inputs["gla_w_alpha2"])
    m["gla_vec"] = np.ascontiguousarray(f(inputs["gla_b_alpha"]).reshape(NE, 4, 64).transpose(0, 2, 1))
    m["gla_ng"] = np.ascontiguousarray(f(inputs["gla_norm_g"]).reshape(NE, 4, 128).transpose(0, 2, 1))
    m["rw_w2"] = f(inputs["rwkv_w2"]); m["rw_a2"] = f(inputs["rwkv_a2"]); m["rw_g2"] = f(inputs["rwkv_g2"])
    mu = f(inputs["rwkv_mu"])
    kht = lambda a: a.reshape(NE, 8, 64).transpose(0, 2, 1)
    vecs = [inputs["rwkv_w0"], inputs["rwkv_a0"], inputs["rwkv_k_k"], inputs["rwkv_k_a"], inputs["rwkv_r_k"],
            mu[:, 0:512], mu[:, 576:1088], mu[:, 1088:1600]]
    m["rw_vec"] = np.ascontiguousarray(np.stack([kht(f(v)) for v in vecs], axis=2).reshape(NE, 64, 64))
    ms = np.zeros((NE, 128, 4), np.float32)
    ms[:, 0:64, 0] = mu[:, 512:576]; ms[:, 0:64, 1] = mu[:, 1600:1664]; ms[:, :, 2] = mu[:, 1664:1792]
    m["rw_mu_s"] = ms
    row = np.concatenate([f(inputs["rwkv_ln_g"]), f(inputs["rwkv_ln_b"])], axis=1)
    m["rw_row"] = np.ascontiguousarray(np.broadcast_to(row[:, None, :], (NE, 128, 1024)))
    NO = NL // 2
    m["cd_w_in"] = f(inputs["cd_w_in"])
    gw = f(inputs["lru_gate_w"])
    m["lru_gw"] = np.ascontiguousarray(gw.transpose(0, 3, 1, 2, 4).reshape(NO, 128, 1024))
    qkv = f(inputs["mlstm_qkv_w"])
    bd = np.zeros((NO, 3, 4, 128, 128), np.float32)
    q5 = qkv.reshape(NO, 3, 4, 32, 4, 4)
    for b in range(32):
        bd[:, :, :, 4 * b:4 * b + 4, 4 * b:4 * b + 4] = q5[:, :, :, b]
    m["ml_bd"] = np.ascontiguousarray(bd.transpose(0, 3, 1, 2, 4).reshape(NO, 128, 12 * 128))
    fm = lambda a: a.reshape(NO, -1, 4, 128).transpose(0, 3, 2, 1)
    lcw = f(inputs["lru_conv_w"]); lcb = f(inputs["lru_conv_b"]); lgb = f(inputs["lru_gate_b"]); lam = f(inputs["lru_lambda"])
    lv = np.concatenate([lcw, lcb[:, None], lgb, lam[:, None]], axis=1)
    m["lru_vec"] = np.ascontiguousarray(fm(lv).reshape(NO, 128, 32))
    mcw = f(inputs["mlstm_conv_w"]); mcb = f(inputs["mlstm_conv_b"])
    mv = np.concatenate([mcw, mcb[:, None]], axis=1)
    m["ml_vec"] = np.ascontiguousarray(fm(mv).reshape(NO, 128, 20))
    row = np.concatenate([f(inputs["mlstm_b_if"]), f(inputs["mlstm_norm_g"])], axis=1)
    m["ml_row"] = np.ascontiguousarray(np.broadcast_to(row[:, None, :], (NO, 128, 520)))
    return m


_CACHE = {}


def kernel(**inputs):
    x = np.asarray(inputs["x"], dtype=np.float32)
    mem = np.asarray(inputs["mem"], dtype=np.float32)
    B, T, _ = x.shape
    NL = inputs["norm_mix_g"].shape[0]
    key = (T, NL)
    if key not in _CACHE:
        _CACHE[key] = KB(T, NL).build()
    nc = _CACHE[key]
    shared = host_layout(inputs, NL)
    in_maps = []
    for b in range(B):
        mm = dict(shared)
        mm["x"] = np.ascontiguousarray(x[b])
        mm["mem"] = np.ascontiguousarray(mem[b])
        in_maps.append(mm)
    res = run_bass_kernel_spmd(nc, in_maps, core_ids=list(range(B)))
    return np.stack([np.asarray(r["out"], dtype=np.float32) for r in res.results], axis=0)
```
